# Optimizing a Trainium2 kernel written in Bass

```python
import math
import jax, jax.numpy as jnp
from jax import lax
import numpy as np

D_MODEL = 2048
BATCH = 16
SEQ = 2048
DEPTH = 2
DEC_BATCH = 8
DEC_SEQ = 4096
PAST_LEN = 128

N_BRANCH = 4
W_BR = D_MODEL // N_BRANCH
D_MIX = N_BRANCH * W_BR
H_GDN = 4
DK_GDN = W_BR // H_GDN
GDN_CONV = 5
GDN_CHUNK = 64
H_DIFF = 4
DV_DIFF = W_BR // H_DIFF
DQK_DIFF = DV_DIFF // 2
ROT_DIM = DQK_DIFF // 4
ROPE_THETA = 500000.0
Q_BLOCK = 128
CONV_W = 3
H_MEM = 4
D_MEM_HEAD = W_BR // H_MEM
N_MEM = 256
EPS = 1e-6

IN_SPLITS = (3 * W_BR, 2 * H_GDN, 2 * H_GDN, W_BR,
             W_BR, W_BR, W_BR, W_BR,
             W_BR, W_BR, W_BR, W_BR,
             W_BR, W_BR)
IN_COLS = sum(IN_SPLITS)
IN_SPLIT_IDX = tuple(int(i) for i in np.cumsum(IN_SPLITS)[:-1])

kernel_name = 'hybrid_gdn_diffattn_shortconv_memory_encoder'


def _rmsnorm(x, w):
    xf = x.astype(jnp.float32)
    y = xf * lax.rsqrt(jnp.mean(xf * xf, axis=-1, keepdims=True) + EPS)
    return (y * w.astype(jnp.float32)).astype(x.dtype)


def _l2norm(x):
    return x * lax.rsqrt(jnp.sum(x * x, axis=-1, keepdims=True) + EPS)


def _dwconv(x, w):
    width = w.shape[0]
    pad = width // 2
    S = x.shape[1]
    xp = jnp.pad(x, ((0, 0), (pad, pad), (0, 0)))
    out = xp[:, 0:S, :] * w[0]
    for i in range(1, width):
        out = out + xp[:, i:i + S, :] * w[i]
    return out


def _rope_tables(S):
    inv = ROPE_THETA ** (-jnp.arange(0, ROT_DIM, 2, dtype=jnp.float32) / ROT_DIM)
    ang = jnp.arange(S, dtype=jnp.float32)[:, None] * inv[None, :]
    return jnp.cos(ang), jnp.sin(ang)


def _partial_rope(x, cos, sin):
    half = ROT_DIM // 2
    c = cos[None, :, None, None, :].astype(x.dtype)
    s = sin[None, :, None, None, :].astype(x.dtype)
    x1, x2, rest = x[..., :half], x[..., half:ROT_DIM], x[..., ROT_DIM:]
    return jnp.concatenate([x1 * c - x2 * s, x2 * c + x1 * s, rest], axis=-1)


def _gated_delta_chunked(q, k, v, g, beta):
    Bn, S, H, DK = q.shape
    DV = v.shape[-1]
    N = S // GDN_CHUNK

    def to_chunks(t):
        t = t.reshape((Bn, N, GDN_CHUNK, H) + t.shape[3:])
        return jnp.moveaxis(t, 3, 1)

    q, k, v, g, beta = (to_chunks(t) for t in (q, k, v, g, beta))
    g = jnp.cumsum(g, axis=-1)
    idx = jnp.arange(GDN_CHUNK)
    tri_incl = idx[:, None] >= idx[None, :]
    tri_strict = idx[:, None] > idx[None, :]
    decay = jnp.exp(jnp.where(tri_incl, g[..., :, None] - g[..., None, :], -jnp.inf))
    kb = k * beta[..., None]
    a_low = jnp.where(tri_strict, jnp.einsum('bhncd,bhnsd->bhncs', kb, k) * decay, 0.0)
    lhs = a_low + jnp.eye(GDN_CHUNK, dtype=q.dtype)
    u = lax.linalg.triangular_solve(lhs, v * beta[..., None], left_side=True, lower=True, unit_diagonal=True)
    w = lax.linalg.triangular_solve(lhs, kb * jnp.exp(g)[..., None], left_side=True, lower=True, unit_diagonal=True)
    attn_intra = jnp.einsum('bhncd,bhnsd->bhncs', q, k) * decay

    def step(state, inp):
        q_c, k_c, u_c, w_c, g_c, a_c = inp
        v_new = u_c - jnp.einsum('bhcd,bhde->bhce', w_c, state)
        o = (jnp.einsum('bhcd,bhde->bhce', q_c * jnp.exp(g_c)[..., None], state)
             + jnp.einsum('bhcs,bhse->bhce', a_c, v_new))
        g_last = g_c[..., -1]
        state = (state * jnp.exp(g_last)[..., None, None]
                 + jnp.einsum('bhcd,bhce->bhde', k_c * jnp.exp(g_last[..., None] - g_c)[..., None], v_new))
        return state, o

    xs = tuple(jnp.moveaxis(t, 2, 0) for t in (q, k, u, w, g, attn_intra))
    s0 = jnp.zeros((Bn, H, DK, DV), q.dtype)
    _, o = lax.scan(step, s0, xs)
    o = jnp.moveaxis(o, 0, 2)
    return jnp.moveaxis(o, 1, 3).reshape(Bn, S, H, DV)


def _gdn_branch(qkv, dec, bet, z, conv, A_log, dt_bias, norm_w):
    Bn, S, _ = qkv.shape
    f32 = jnp.float32
    qkv = jax.nn.silu(_dwconv(qkv, conv)).astype(f32)
    q, k, v = (t.reshape(Bn, S, H_GDN, DK_GDN) for t in jnp.split(qkv, 3, axis=-1))
    q = _l2norm(q) * DK_GDN ** -0.5
    k = _l2norm(k)
    beta = jax.nn.sigmoid(bet.astype(f32)).reshape(Bn, S, 2, H_GDN)
    g = -jnp.exp(A_log.astype(f32)) * jax.nn.softplus(dec.astype(f32).reshape(Bn, S, 2, H_GDN) + dt_bias.astype(f32))
    o_fwd = _gated_delta_chunked(q, k, v, g[:, :, 0], beta[:, :, 0])
    rev = lambda t: jnp.flip(t, axis=1)
    o_bwd = rev(_gated_delta_chunked(rev(q), rev(k), rev(v), rev(g[:, :, 1]), rev(beta[:, :, 1])))
    o = _rmsnorm(o_fwd + o_bwd, norm_w) * jax.nn.silu(z.astype(f32).reshape(Bn, S, H_GDN, DK_GDN))
    return o.reshape(Bn, S, W_BR).astype(z.dtype)


def _diff_attention(q, k, v, lam):
    Bn, S = q.shape[:2]
    nb = S // Q_BLOCK
    qb = jnp.moveaxis(q.reshape(Bn, nb, Q_BLOCK, H_DIFF, 2, DQK_DIFF), 1, 0)
    scale = DQK_DIFF ** -0.5

    def block(qi):
        s = jnp.einsum('bqhjd,bkhjd->bhjqk', qi, k).astype(jnp.float32) * scale
        p = jax.nn.softmax(s, axis=-1)
        p = (p[:, :, 0] - lam * p[:, :, 1]).astype(v.dtype)
        return jnp.einsum('bhqk,bkhe->bqhe', p, v)

    o = lax.map(block, qb)
    return jnp.moveaxis(o, 0, 1).reshape(Bn, S, H_DIFF, DV_DIFF)


def _diff_branch(q, k, v, z, lam_p, norm_w, lambda_init, cos, sin):
    Bn, S, _ = q.shape
    q = _partial_rope(q.reshape(Bn, S, H_DIFF, 2, DQK_DIFF), cos, sin)
    k = _partial_rope(k.reshape(Bn, S, H_DIFF, 2, DQK_DIFF), cos, sin)
    v = v.reshape(Bn, S, H_DIFF, DV_DIFF)
    lp = lam_p.astype(jnp.float32)
    lam = jnp.exp(jnp.sum(lp[0] * lp[1])) - jnp.exp(jnp.sum(lp[2] * lp[3])) + lambda_init
    o = _diff_attention(q, k, v, lam)
    o = _rmsnorm(o, norm_w) * (1.0 - lambda_init)
    return (o * jax.nn.silu(z).reshape(Bn, S, H_DIFF, DV_DIFF)).reshape(Bn, S, W_BR)


def _conv_branch(bg, cg, xin, z, w):
    return bg * _dwconv(cg * xin, w) * jax.nn.silu(z)


def _memory_branch(q, z, mem, norm_m, w_kv):
    Bn, S, _ = q.shape
    M = mem.shape[1]
    k, v = jnp.split(_rmsnorm(mem, norm_m) @ w_kv, 2, axis=-1)
    k = k.reshape(Bn, M, H_MEM, D_MEM_HEAD)
    v = v.reshape(Bn, M, H_MEM, D_MEM_HEAD)
    q = q.reshape(Bn, S, H_MEM, D_MEM_HEAD)
    s = jnp.einsum('bshd,bmhd->bhsm', q, k).astype(jnp.float32) * D_MEM_HEAD ** -0.5
    p = jax.nn.softmax(s, axis=-1).astype(v.dtype)
    o = jnp.einsum('bhsm,bmhd->bshd', p, v).reshape(Bn, S, W_BR)
    return o * jax.nn.silu(z)


def _trunk(x, mem, norm_pre, norm_post, norm_mem, w_in, gdn_conv, gdn_A_log, gdn_dt_bias, gdn_norm,
           diff_lambda, diff_norm, conv_w, w_mem_kv, w_out):
    cos, sin = _rope_tables(x.shape[1])
    for l in range(DEPTH):
        lambda_init = 0.8 - 0.6 * math.exp(-0.3 * l)
        h = _rmsnorm(x, norm_pre[l])
        (a_qkv, a_dec, a_beta, a_z, b_q, b_k, b_v, b_z,
         c_b, c_c, c_x, c_z, m_q, m_z) = jnp.split(h @ w_in[l], IN_SPLIT_IDX, axis=-1)
        y_a = _gdn_branch(a_qkv, a_dec, a_beta, a_z, gdn_conv[l], gdn_A_log[l], gdn_dt_bias[l], gdn_norm[l])
        y_b = _diff_branch(b_q, b_k, b_v, b_z, diff_lambda[l], diff_norm[l], lambda_init, cos, sin)
        y_c = _conv_branch(c_b, c_c, c_x, c_z, conv_w[l])
        y_m = _memory_branch(m_q, m_z, mem, norm_mem[l], w_mem_kv[l])
        y = jnp.concatenate([y_a, y_b, y_c, y_m], axis=-1) @ w_out[l]
        x = x + _rmsnorm(y, norm_post[l])
    return x


def setup_inputs(seed: int = 0) -> dict:
    key = jax.random.key(seed)
    ks = jax.random.split(key, 20)
    nrm = jax.random.normal
    dt = jnp.exp(jax.random.uniform(ks[10], (DEPTH, 2, H_GDN), minval=math.log(1e-3), maxval=math.log(1e-1)))
    return {
        'x_prompt': nrm(ks[0], (BATCH, SEQ, D_MODEL), jnp.float32),
        'x_sample': nrm(ks[1], (DEC_BATCH, DEC_SEQ, D_MODEL), jnp.float32),
        'mem_prompt': nrm(ks[2], (BATCH, N_MEM, D_MODEL), jnp.float32),
        'mem_sample': nrm(ks[3], (DEC_BATCH, N_MEM, D_MODEL), jnp.float32),
        'norm_pre': 1.0 + 0.01 * nrm(ks[4], (DEPTH, D_MODEL), jnp.float32),
        'norm_post': 1.0 + 0.01 * nrm(ks[5], (DEPTH, D_MODEL), jnp.float32),
        'norm_mem': 1.0 + 0.01 * nrm(ks[6], (DEPTH, D_MODEL), jnp.float32),
        'w_in': nrm(ks[7], (DEPTH, D_MODEL, IN_COLS), jnp.float32) * D_MODEL ** -0.5,
        'gdn_conv': nrm(ks[8], (DEPTH, GDN_CONV, 3 * W_BR), jnp.float32) * GDN_CONV ** -0.5,
        'gdn_A_log': jnp.log(jax.random.uniform(ks[9], (DEPTH, 2, H_GDN), minval=1.0, maxval=16.0)),
        'gdn_dt_bias': jnp.log(jnp.expm1(dt)),
        'gdn_norm': 1.0 + 0.01 * nrm(ks[11], (DEPTH, DK_GDN), jnp.float32),
        'diff_lambda': 0.1 * nrm(ks[12], (DEPTH, 4, DQK_DIFF), jnp.float32),
        'diff_norm': 1.0 + 0.01 * nrm(ks[13], (DEPTH, DV_DIFF), jnp.float32),
        'conv_w': nrm(ks[14], (DEPTH, CONV_W, W_BR), jnp.float32) * CONV_W ** -0.5,
        'w_mem_kv': nrm(ks[15], (DEPTH, D_MODEL, 2 * W_BR), jnp.float32) * D_MODEL ** -0.5,
        'w_out': nrm(ks[16], (DEPTH, D_MIX, D_MODEL), jnp.float32) * D_MIX ** -0.5,
    }


def reference(x_prompt, x_sample, mem_prompt, mem_sample, norm_pre, norm_post, norm_mem, w_in, gdn_conv,
              gdn_A_log, gdn_dt_bias, gdn_norm, diff_lambda, diff_norm, conv_w, w_mem_kv, w_out):
    y_prompt = _trunk(x_prompt, mem_prompt, norm_pre, norm_post, norm_mem, w_in, gdn_conv, gdn_A_log,
                      gdn_dt_bias, gdn_norm, diff_lambda, diff_norm, conv_w, w_mem_kv, w_out)
    y_sample = _trunk(x_sample, mem_sample, norm_pre, norm_post, norm_mem, w_in, gdn_conv, gdn_A_log,
                      gdn_dt_bias, gdn_norm, diff_lambda, diff_norm, conv_w, w_mem_kv, w_out)
    return (y_prompt, y_sample)
```

```python
import math
from contextlib import ExitStack
import numpy as np
import ml_dtypes
import concourse.bass as bass
import concourse.mybir as mybir
from concourse.bass_utils import run_bass_kernel_spmd

F32 = mybir.dt.float32
BF16 = mybir.dt.bfloat16
F32R = mybir.dt.float32r
AF = mybir.ActivationFunctionType
ALU = mybir.AluOpType

D = 2048
W_BR = 512
IN_COLS = 7184
N_MEM = 256
EPS = 1e-6
NEG = 30000.0
C_AQKV, C_ADEC, C_AZ = 0, 1536, 1552
C_BQ, C_BK, C_BV, C_BZ = 2064, 2576, 3088, 3600
C_CB, C_CC, C_CX, C_CZ = 4112, 4624, 5136, 5648
C_MQ, C_MZ = 6160, 6672


class Res:
    __slots__ = ("w", "r", "wx", "name")

    def __init__(self, name=""):
        self.w = {}
        self.r = {}
        self.wx = {}
        self.name = name


class Slot:
    __slots__ = ("si", "val")


class Eng:
    RING = 12
    CAP = 30000

    def __init__(self, kb, name, obj):
        self.kb = kb
        self.name = name
        self.o = obj
        self.si = kb.newsem()
        self.cnt = 0
        self.known = {}
        self.ring = []
        self.ri = 0

    def _deps(self, reads, writes, dwrites):
        deps = {}

        def add(d):
            for k, v in d.items():
                if deps.get(k, 0) < v:
                    deps[k] = v
        for r in reads:
            add(r.w)
        for w in writes:
            add(w.w)
            add(w.r)
        for w in dwrites:
            add(w.r)
            add(w.wx)
        return deps

    def _wait(self, deps):
        for k, v in deps.items():
            if self.name == "pe" and k == self.si:
                continue
            if self.known.get(k, 0) < v:
                self.o.wait_ge(self.kb.sems[k], v)
                self.known[k] = v

    def _post(self, ev, reads, writes, dwrites):
        k, v = ev
        for r in reads:
            r.r[k] = v
        for w in writes:
            w.w = {k: v}
            w.wx = {k: v}
            w.r = {}
        for w in dwrites:
            w.w[k] = v

    def op(self, fn, reads=(), writes=(), dwrites=()):
        self._wait(self._deps(reads, writes, dwrites))
        ins = fn()
        self.cnt += 1
        ins.then_inc(self.kb.sems[self.si], 1)
        self._post((self.si, self.cnt), reads, writes, dwrites)
        if self.cnt >= self.CAP:
            self.si = self.kb.newsem()
            self.cnt = 0
        return ins

    def dma(self, out, in_, reads=(), writes=(), dwrites=()):
        self._wait(self._deps(reads, writes, dwrites))
        if len(self.ring) < self.RING:
            s = Slot()
            s.si = self.kb.newsem()
            s.val = 0
            self.ring.append(s)
        s = self.ring[self.ri % self.RING]
        self.ri += 1
        if s.val >= self.CAP:
            self._wait({s.si: s.val})
            s.si = self.kb.newsem()
            s.val = 0
        if s.val > 0:
            self._wait({s.si: s.val})
        ins = self.o.dma_start(out=out, in_=in_)
        s.val += 16
        ins.then_inc(self.kb.sems[s.si], 16)
        self._post((s.si, s.val), reads, writes, dwrites)
        return ins

    def last_events(self):
        ev = {}
        if self.cnt > 0:
            ev[self.si] = self.cnt
        for s in self.ring:
            if s.val > 0:
                ev[s.si] = s.val
        return ev


class Rot:
    def __init__(self, items):
        self.items = items
        self.i = 0

    def next(self):
        it = self.items[self.i % len(self.items)]
        self.i += 1
        return it


class KB:
    def __init__(self, nc, es):
        self.nc = nc
        self.es = es
        self.sems = []
        self.pe = Eng(self, "pe", nc.tensor)
        self.act = Eng(self, "act", nc.scalar)
        self.dve = Eng(self, "dve", nc.vector)
        self.pool = Eng(self, "pool", nc.gpsimd)
        self.sp = Eng(self, "sp", nc.sync)
        self.engs = [self.pe, self.act, self.dve, self.pool, self.sp]
        self.ew = Rot([self.act, self.dve])

    def newsem(self):
        h = self.es.enter_context(self.nc.semaphore(f"s{len(self.sems)}"))
        self.sems.append(h)
        return len(self.sems) - 1

    def barrier(self):
        ev = {}
        for e in self.engs:
            ev.update(e.last_events())
        for e in self.engs:
            e._wait(dict(ev))


def _consts(smax):
    c = {}
    eye = np.eye(128, dtype=np.float32)
    c["ident_bf"] = eye.astype(ml_dtypes.bfloat16)
    c["ones_bf"] = np.ones((128, 128), np.float32).astype(ml_dtypes.bfloat16)
    idx = np.arange(128)
    same = (idx[:, None] // 64) == (idx[None, :] // 64)
    lf = (same & (idx[:, None] <= idx[None, :])).astype(np.float32)
    lb = (same & (idx[:, None] >= idx[None, :])).astype(np.float32)
    bo = same.astype(np.float32)
    mf = np.where(same & (idx[:, None] >= idx[None, :]), 0.0, NEG).astype(np.float32)
    mb = np.where(same & (idx[:, None] <= idx[None, :]), 0.0, NEG).astype(np.float32)
    nodiag = (1.0 - eye).astype(np.float32)
    f32c = np.concatenate([eye, lf, lb, bo, np.tile(mf, (1, 4)), np.tile(mb, (1, 4)), np.tile(nodiag, (1, 4)),
                           np.tile(eye, (1, 4))], axis=1)
    c["f32c"] = np.ascontiguousarray(f32c)
    pt = np.zeros((128, 128), np.float32)
    for j in range(2):
        b = 64 * j
        for i in range(8):
            pt[b + 8 + i, b + i] = -1.0
            pt[b + i, b + 8 + i] = 1.0
    c["pt_bf"] = pt.astype(ml_dtypes.bfloat16)
    inv = (np.float32(500000.0) ** (-np.arange(0, 16, 2, dtype=np.float32) / np.float32(16))).astype(np.float32)
    ang = (np.arange(smax, dtype=np.float32)[:, None] * inv[None, :]).astype(np.float32)
    cos = np.cos(ang.astype(np.float64)).astype(np.float32).T
    sin = np.sin(ang.astype(np.float64)).astype(np.float32).T
    cf = np.ones((128, smax), np.float32)
    sf = np.zeros((128, smax), np.float32)
    for j in range(2):
        b = 64 * j
        cf[b:b + 8] = cos
        cf[b + 8:b + 16] = cos
        sf[b:b + 8] = sin
        sf[b + 8:b + 16] = sin
    c["rope_c"] = cf
    c["rope_s"] = sf
    return c


def build(seqs, depth, debug=False, phases="a,g1,g2,g3,b,m,c,o"):
    nc = bass.Bass("TRN2", target_bir_lowering=False)
    ntok = sum(seqs)
    nseq = len(seqs)
    smax = max(seqs)
    offs = [sum(seqs[:i]) for i in range(nseq)]

    def din(name, shape, dt=F32):
        return nc.dram_tensor(name, list(shape), dt, kind="ExternalInput").ap()

    def dscr(name, shape, dt):
        kind = "ExternalOutput" if debug else "Internal"
        return nc.dram_tensor(name, list(shape), dt, kind=kind).ap()

    x_in = din("x", [ntok, D])
    mem_in = din("mem", [nseq * N_MEM, D])
    norm_pre = din("norm_pre", [depth, D])
    norm_post = din("norm_post", [depth, D])
    norm_mem = din("norm_mem", [depth, D])
    w_in = din("w_in", [depth, D, IN_COLS])
    gdn_conv = din("gdn_conv", [depth, 5, 1536])
    gdn_A_log = din("gdn_A_log", [depth, 2, 4])
    gdn_dt_bias = din("gdn_dt_bias", [depth, 2, 4])
    gdn_norm = din("gdn_norm", [depth, 128])
    diff_lambda = din("diff_lambda", [depth, 4, 64])
    diff_norm = din("diff_norm", [depth, 128])
    conv_w = din("conv_w", [depth, 3, 512])
    w_mem_kv = din("w_mem_kv", [depth, D, 1024])
    w_out = din("w_out", [depth, D, D])
    c_ident_bf = din("ident_bf", [128, 128], BF16)
    c_ones_bf = din("ones_bf", [128, 128], BF16)
    c_pt_bf = din("pt_bf", [128, 128], BF16)
    c_f32c = din("f32c", [128, 4 * 128 + 4 * 512])
    c_rope_c = din("rope_c", [128, smax])
    c_rope_s = din("rope_s", [128, smax])
    y_out = nc.dram_tensor("y", [ntok, D], F32, kind="ExternalOutput").ap()

    WIN = dscr("WIN", [depth, 56, 128, 16, 128], BF16)
    WDB = dscr("WDB", [depth, 128, 16, 16], BF16)
    WOUT = dscr("WOUT", [depth, 128, 16, D], BF16)
    WKV = dscr("WKV", [depth, 128, 16, 1024], BF16)
    X1 = dscr("X1", [ntok, D], F32)
    QKVT = dscr("QKVT", [1536, smax], BF16)
    DBt = dscr("DBt", [smax, 16], F32)
    AZ = dscr("AZ", [smax, 512], BF16)
    BQT = dscr("BQT", [512, smax], BF16)
    BKT = dscr("BKT", [512, smax], BF16)
    BV = dscr("BV", [smax, 512], BF16)
    BZT = dscr("BZT", [512, smax], BF16)
    CUT = dscr("CUT", [512, smax], BF16)
    CGT = dscr("CGT", [512, smax], BF16)
    MQT = dscr("MQT", [512, smax], BF16)
    MZT = dscr("MZT", [512, smax], BF16)
    GQT = dscr("GQT", [512, smax], BF16)
    GKT = dscr("GKT", [512, smax], BF16)
    GK = dscr("GK", [smax, 512], BF16)
    GV = dscr("GV", [smax, 512], BF16)
    OD = [dscr("OF", [smax, 512], F32), dscr("OB", [smax, 512], F32)]
    YT = dscr("YT", [D, smax], BF16)

    es = ExitStack()
    with es:
        kb = KB(nc, es)
        pe, act, dve, pool, sp = kb.pe, kb.act, kb.dve, kb.pool, kb.sp
        ARENA_W = 40000
        arena = es.enter_context(nc.sbuf_tensor("arena", [128, ARENA_W], F32))
        rbuf = es.enter_context(nc.sbuf_tensor("rbuf", [128, 10 * 512 + 128], F32R))
        cst = es.enter_context(nc.sbuf_tensor("cst", [128, 7800], F32))
        banks = [es.enter_context(nc.psum_tensor(f"bank{i}", [128, 512], F32)) for i in range(8)]
        bankR = [Res(f"bank{i}") for i in range(8)]

        coff = [0]

        def calloc(words):
            o = coff[0]
            coff[0] += words
            assert coff[0] <= 7800
            return cst[:, o:o + words]

        RC = Res("consts")
        f32c = calloc(4 * 128 + 4 * 512)
        ident_f = f32c[:, 0:128]
        Lf = f32c[:, 128:256]
        Lb = f32c[:, 256:384]
        mask4 = [f32c[:, 512:1024], f32c[:, 1024:1536]]
        nodiag4 = f32c[:, 1536:2048]
        ident4 = f32c[:, 2048:2560]
        Lmat = [Lf, Lb]
        ident_bf = calloc(64).bitcast(BF16)
        ones_bf = calloc(64).bitcast(BF16)
        pt_bf = calloc(64).bitcast(BF16)
        eps_t = calloc(1)
        eps128_t = calloc(1)
        one_t = calloc(1)
        sp.dma(f32c, c_f32c, writes=[RC])
        sp.dma(ident_bf, c_ident_bf, dwrites=[RC])
        sp.dma(ones_bf, c_ones_bf, dwrites=[RC])
        sp.dma(pt_bf, c_pt_bf, dwrites=[RC])
        pool.op(lambda: nc.gpsimd.memset(eps_t, EPS), dwrites=[RC])
        pool.op(lambda: nc.gpsimd.memset(eps128_t, EPS * 128.0), dwrites=[RC])
        pool.op(lambda: nc.gpsimd.memset(one_t, 1.0), dwrites=[RC])
        ind_t = calloc(2)
        RI = Res("ind")
        pool.op(lambda: nc.gpsimd.memset(ind_t, 0.0), writes=[RI])
        pool.op(lambda: nc.gpsimd.memset(ind_t[0:64, 0:1], 1.0), writes=[RI])
        pool.op(lambda: nc.gpsimd.memset(ind_t[64:128, 1:2], 1.0), writes=[RI])
        ident_r = rbuf[:, 10 * 512:10 * 512 + 128]
        dve.op(lambda: nc.vector.tensor_copy(out=ident_r, in_=ident_f), reads=[RC], dwrites=[RC])
        RL = Res("layerparams")
        npre_b = calloc(D)
        npost_b = calloc(D)
        gconv = calloc(60)
        cconv = calloc(12)
        gnorm_b = calloc(128)
        alog_b = calloc(8)
        dtb_b = calloc(8)
        negA_b = calloc(8)
        dnorm_c = calloc(1)
        dnorm_s = calloc(1)
        lam_b = calloc(256)
        lam_t = calloc(256)
        lam_s = calloc(2)
        lam_e = calloc(2)
        neglam = calloc(1)

        RW = Res("weights")
        for l in range(depth):
            src = w_in[l][:, 0:1536].rearrange("(kc p) (ch c) -> ch p kc c", p=128, c=128)
            chunk_cols = [c0 for c0 in range(0, 1536, 128)] + [c0 for c0 in range(C_AZ, IN_COLS, 128)]
            assert len(chunk_cols) == 56
            for ci, c0 in enumerate(chunk_cols):
                srcc = w_in[l][:, c0:c0 + 128].rearrange("(kc p) c -> p kc c", p=128)
                pool.dma(WIN[l, ci], srcc, dwrites=[RW])
            pool.dma(WDB[l], w_in[l][:, C_ADEC:C_ADEC + 16].rearrange("(kc p) c -> p kc c", p=128), dwrites=[RW])
            for q4 in range(4):
                pool.dma(WOUT[l][:, :, q4 * 512:(q4 + 1) * 512],
                         w_out[l][:, q4 * 512:(q4 + 1) * 512].rearrange("(kc p) c -> p kc c", p=128), dwrites=[RW])
            for q4 in range(2):
                pool.dma(WKV[l][:, :, q4 * 512:(q4 + 1) * 512],
                         w_mem_kv[l][:, q4 * 512:(q4 + 1) * 512].rearrange("(kc p) c -> p kc c", p=128), dwrites=[RW])

        def chunk_index(col):
            if col < 1536:
                return col // 128
            return 12 + (col - C_AZ) // 128

        class Carve:
            def __init__(self):
                self.o = 0

            def f32(self, words, shape=None):
                ap = arena[:, self.o:self.o + words]
                self.o += words
                assert self.o <= ARENA_W, self.o
                if shape is not None:
                    ap = ap.rearrange("p (a b) -> p a b", a=shape[0]) if len(shape) == 2 else ap
                return ap

            def bf(self, elems, shape=None):
                words = (elems + 1) // 2
                ap = arena[:, self.o:self.o + words].bitcast(BF16)
                self.o += words
                assert self.o <= ARENA_W, self.o
                if shape is not None and len(shape) == 2:
                    ap = ap.rearrange("p (a b) -> p a b", a=shape[0])
                return ap

            def r32(self, words, shape=None):
                ap = arena[:, self.o:self.o + words].bitcast(F32R)
                self.o += words
                assert self.o <= ARENA_W, self.o
                if shape is not None and len(shape) == 2:
                    ap = ap.rearrange("p (a b) -> p a b", a=shape[0])
                return ap

        V = nc.vector
        G = nc.gpsimd
        A_ = nc.scalar
        T = nc.tensor

        def bank_bf(i):
            return banks[i][:, :].bitcast(BF16)

        def load_layer_params(l):
            kb.barrier()
            sp.dma(npre_b, norm_pre[l].partition_broadcast(128), writes=[RL])
            sp.dma(npost_b, norm_post[l].partition_broadcast(128), dwrites=[RL])
            sp.dma(gnorm_b, gdn_norm[l].partition_broadcast(128), dwrites=[RL])
            sp.dma(alog_b, gdn_A_log[l].rearrange("a h -> (a h)").partition_broadcast(128), dwrites=[RL])
            sp.dma(dtb_b, gdn_dt_bias[l].rearrange("a h -> (a h)").partition_broadcast(128), dwrites=[RL])
            sp.dma(lam_b, diff_lambda[l].rearrange("a d -> (a d)").partition_broadcast(128), dwrites=[RL])
            with nc.allow_non_contiguous_dma(reason="tiny parameter transposes"):
                for wi in range(5):
                    sp.dma(gconv.rearrange("p (c w) -> p c w", w=5)[:, :, wi],
                           gdn_conv[l, wi].rearrange("(c p) -> p c", p=128), dwrites=[RL])
                for wi in range(3):
                    sp.dma(cconv.rearrange("p (c w) -> p c w", w=3)[:, :, wi],
                           conv_w[l, wi].rearrange("(c p) -> p c", p=128), dwrites=[RL])
                sp.dma(dnorm_c, diff_norm[l].rearrange("(p o) -> p o", o=1), dwrites=[RL])
            lam_init = 0.8 - 0.6 * math.exp(-0.3 * l)
            act.op(lambda: A_.activation(out=negA_b, in_=alog_b, func=AF.Exp), reads=[RL], dwrites=[RL])
            dve.op(lambda: V.tensor_scalar(out=negA_b, in0=negA_b, scalar1=-1.0, scalar2=None, op0=ALU.mult),
                   reads=[RL], dwrites=[RL])
            dve.op(lambda: V.tensor_scalar(out=dnorm_s, in0=dnorm_c, scalar1=1.0 - lam_init, scalar2=None,
                                           op0=ALU.mult), reads=[RL], dwrites=[RL])
            lb3 = lam_b.rearrange("p (a d) -> p a d", d=64)
            lt3 = lam_t.rearrange("p (a d) -> p a d", d=64)
            dve.op(lambda: V.tensor_tensor(out=lt3[:, 0, :], in0=lb3[:, 0, :], in1=lb3[:, 1, :], op=ALU.mult),
                   reads=[RL], dwrites=[RL])
            dve.op(lambda: V.tensor_tensor(out=lt3[:, 1, :], in0=lb3[:, 2, :], in1=lb3[:, 3, :], op=ALU.mult),
                   reads=[RL], dwrites=[RL])
            dve.op(lambda: V.reduce_sum(out=lam_s[:, 0:1], in_=lt3[:, 0, :], axis=mybir.AxisListType.X),
                   reads=[RL], dwrites=[RL])
            dve.op(lambda: V.reduce_sum(out=lam_s[:, 1:2], in_=lt3[:, 1, :], axis=mybir.AxisListType.X),
                   reads=[RL], dwrites=[RL])
            act.op(lambda: A_.activation(out=lam_e, in_=lam_s, func=AF.Exp), reads=[RL], dwrites=[RL])
            dve.op(lambda: V.tensor_tensor(out=neglam, in0=lam_e[:, 1:2], in1=lam_e[:, 0:1], op=ALU.subtract),
                   reads=[RL], dwrites=[RL])
            dve.op(lambda: V.tensor_scalar(out=neglam, in0=neglam, scalar1=-lam_init, scalar2=None, op0=ALU.add),
                   reads=[RL], dwrites=[RL])
            kb.barrier()

        def rms_rows(cv, xt, xtR, rstd, tmpR, width):
            junk, ss, rms = cv
            act.op(lambda: A_.activation(out=junk, in_=xt, func=AF.Square, accum_out=ss),
                   reads=[xtR], writes=[tmpR])
            act.op(lambda: A_.activation(out=rms, in_=ss, func=AF.Sqrt, scale=1.0 / width, bias=eps_t),
                   reads=[tmpR, RC], dwrites=[tmpR])
            dve.op(lambda: V.reciprocal(out=rstd, in_=rms), reads=[tmpR], dwrites=[tmpR])

        def phase_a(l, si, xsrc, xsrcR):
            S = seqs[si]
            TB = min(S, 1024)
            NT = min(512, TB)
            kb.barrier()
            cv = Carve()
            xts = [(cv.f32(D), Res()) for _ in range(2)]
            junk = cv.bf(D)
            hbs = [(cv.bf(D), Res()) for _ in range(2)]
            hT = cv.bf(16 * TB, (16, TB))
            hTR = Res("hT")
            ring = Rot([(cv.bf(16 * 128, (16, 128)), Res()) for _ in range(8)])
            wides = [(cv.bf(16 * 512, (16, 512)), Res("wide")) for _ in range(2)]
            wdb = cv.bf(16 * 16, (16, 16))
            wdbR = Res("wdb")
            stg = Rot([(cv.f32(512), Res()) for _ in range(8)])
            ropeC = [(cv.f32(512), Res()) for _ in range(2)]
            ropeS = [(cv.f32(512), Res()) for _ in range(2)]
            small = cv.f32(8)
            smallR = Res()
            pb = Rot([2, 3, 4, 5, 6, 7])

            order = [c * 128 for c in range(12)]
            order += [col + h * 128 for col in (C_BQ, C_BK) for h in range(4)]
            order += [col + c * 128 for col in (C_BZ, C_MQ, C_MZ) for c in range(4)]
            order += [col + c * 128 for c in range(4) for col in (C_CB, C_CC, C_CX, C_CZ)]
            LOOK = 4
            pq = {"next": 0, "ready": []}

            def _issue():
                col = order[pq["next"] % len(order)]
                pq["next"] += 1
                w, wr = ring.next()
                sp.dma(w, WIN[l, chunk_index(col)], reads=[RW], writes=[wr])
                pq["ready"].append((col, w, wr))

            def load_chunk(col):
                while len(pq["ready"]) < 1:
                    _issue()
                c0, w, wr = pq["ready"].pop(0)
                assert c0 == col, (c0, col)
                while len(pq["ready"]) < LOOK and pq["next"] < pq["limit"]:
                    _issue()
                return w, wr
            pq["limit"] = len(order) * (S // TB)

            def fm_mm(w, wr, n, bi=None):
                bi = pb.next() if bi is None else bi
                for kc in range(16):
                    pe.op(lambda kc=kc: T.matmul(banks[bi][:, 0:NT], w[:, kc, :], hT[:, kc, n * NT:(n + 1) * NT],
                                                 start=(kc == 0), stop=(kc == 15)),
                          reads=[wr, hTR], writes=[bankR[bi]] if kc == 0 else (), dwrites=() if kc == 0 else [bankR[bi]])
                return bi

            for b0 in range(0, S, TB):
                tok0 = offs[si] + b0
                for i in range(TB // 128):
                    xt, xtR = xts[i % 2]
                    hb, hbR = hbs[i % 2]
                    sp.dma(xt, xsrc[tok0 + i * 128: tok0 + (i + 1) * 128, :], reads=[xsrcR], writes=[xtR])
                    rms_rows((junk, small[:, 0:1], small[:, 1:2]), xt, xtR, small[:, 2:3], smallR, D)
                    dve.op(lambda: V.scalar_tensor_tensor(out=hb, in0=xt, scalar=small[:, 2:3], in1=npre_b,
                                                          op0=ALU.mult, op1=ALU.mult),
                           reads=[xtR, smallR, RL], writes=[hbR])
                    for half in range(2):
                        tpb = bank_bf(half)
                        for kk in range(8):
                            kc = half * 8 + kk
                            pe.op(lambda kc=kc, kk=kk, tpb=tpb: T.transpose(out=tpb[:, kk * 128:(kk + 1) * 128],
                                                                            in_=hb[:, kc * 128:(kc + 1) * 128],
                                                                            identity=ident_bf),
                                  reads=[hbR, RC], writes=[bankR[half]] if kk == 0 else (),
                                  dwrites=() if kk == 0 else [bankR[half]])
                        e = kb.ew.next()
                        dst = hT[:, half * 8:(half + 1) * 8, i * 128:(i + 1) * 128]
                        srcv = tpb.rearrange("p (k t) -> p k t", k=8)
                        if e is act:
                            act.op(lambda: A_.copy(out=dst, in_=srcv), reads=[bankR[half]], dwrites=[hTR])
                        else:
                            dve.op(lambda: V.tensor_copy(out=dst, in_=srcv), reads=[bankR[half]], dwrites=[hTR])

                def store_fm(dst, row0, n, sv, svR):
                    sp.dma(dst[row0:row0 + 128, b0 + n * NT: b0 + (n + 1) * NT], sv, reads=[svR])

                def evac_copy_bf(bi, func=None):
                    sv, svR = stg.next()
                    svb = sv.bitcast(BF16)[:, 0:NT]
                    if func is not None:
                        act.op(lambda: A_.activation(out=svb, in_=banks[bi][:, 0:NT], func=func),
                               reads=[bankR[bi]], writes=[svR])
                    else:
                        e = kb.ew.next()
                        if e is act:
                            act.op(lambda: A_.copy(out=svb, in_=banks[bi][:, 0:NT]), reads=[bankR[bi]], writes=[svR])
                        else:
                            dve.op(lambda: V.tensor_copy(out=svb, in_=banks[bi][:, 0:NT]), reads=[bankR[bi]],
                                   writes=[svR])
                    return svb, svR

                nsub = TB // NT
                for c in range(12):
                    w, wr = load_chunk(c * 128)
                    for n in range(nsub):
                        bi = fm_mm(w, wr, n)
                        svb, svR = evac_copy_bf(bi)
                        store_fm(QKVT, c * 128, n, svb, svR)
                sp.dma(wdb, WDB[l], reads=[RW], writes=[wdbR])
                for wi_, col_ in enumerate((C_AZ, C_BV)):
                    wide_, wideR_ = wides[wi_]
                    for q4 in range(4):
                        sp.dma(wide_[:, :, q4 * 128:(q4 + 1) * 128], WIN[l, chunk_index(col_ + q4 * 128)], reads=[RW],
                               writes=[wideR_] if q4 == 0 else (), dwrites=() if q4 == 0 else [wideR_])
                for _ in range(LOOK):
                    if len(pq["ready"]) < LOOK and pq["next"] < pq["limit"]:
                        _issue()
                for i in range(TB // 128):
                    bi = pb.next()
                    for kc in range(16):
                        pe.op(lambda kc=kc: T.matmul(banks[bi][:, 0:16], hT[:, kc, i * 128:(i + 1) * 128],
                                                     wdb[:, kc, :], start=(kc == 0), stop=(kc == 15)),
                              reads=[wdbR, hTR], writes=[bankR[bi]] if kc == 0 else (),
                              dwrites=() if kc == 0 else [bankR[bi]])
                    sv, svR = stg.next()
                    dve.op(lambda: V.tensor_copy(out=sv[:, 0:16], in_=banks[bi][:, 0:16]), reads=[bankR[bi]],
                           writes=[svR])
                    sp.dma(DBt[b0 + i * 128: b0 + (i + 1) * 128, :], sv[:, 0:16], reads=[svR])

                def wide_tm(wsel, dst, func):
                    wide, wideR = wides[wsel]
                    for i in range(TB // 128):
                        bi = pb.next()
                        for kc in range(16):
                            pe.op(lambda kc=kc: T.matmul(banks[bi][:, :], hT[:, kc, i * 128:(i + 1) * 128],
                                                         wide[:, kc, :], start=(kc == 0), stop=(kc == 15)),
                                  reads=[wideR, hTR], writes=[bankR[bi]] if kc == 0 else (),
                                  dwrites=() if kc == 0 else [bankR[bi]])
                        sv, svR = stg.next()
                        svb = sv.bitcast(BF16)[:, 0:512]
                        if func is not None:
                            act.op(lambda: A_.activation(out=svb, in_=banks[bi][:, :], func=func), reads=[bankR[bi]],
                                   writes=[svR])
                        else:
                            dve.op(lambda: V.tensor_copy(out=svb, in_=banks[bi][:, :]), reads=[bankR[bi]], writes=[svR])
                        sp.dma(dst[b0 + i * 128: b0 + (i + 1) * 128, :], svb, reads=[svR])

                wide_tm(0, AZ, AF.Silu)
                wide_tm(1, BV, None)
                assert nsub <= 2
                for n in range(nsub):
                    p0 = b0 + n * NT
                    sp.dma(ropeC[n][0][:, 0:NT], c_rope_c[:, p0:p0 + NT], writes=[ropeC[n][1]])
                    sp.dma(ropeS[n][0][:, 0:NT], c_rope_s[:, p0:p0 + NT], writes=[ropeS[n][1]])
                for qk, (col, dst) in enumerate(((C_BQ, BQT), (C_BK, BKT))):
                    for h in range(4):
                        w, wr = load_chunk(col + h * 128)
                        for n in range(nsub):
                            rc, rcR = ropeC[n]
                            rs, rsR = ropeS[n]
                            bi = fm_mm(w, wr, n)
                            sv, svR = stg.next()
                            qsb = sv.bitcast(BF16)[:, 0:NT]
                            act.op(lambda: A_.copy(out=qsb, in_=banks[bi][:, 0:NT]), reads=[bankR[bi]], writes=[svR])
                            b2 = pb.next()
                            pe.op(lambda: T.matmul(banks[b2][:, 0:NT], pt_bf, qsb, start=True, stop=True),
                                  reads=[svR, RC], writes=[bankR[b2]])
                            t1, t1R = stg.next()
                            dve.op(lambda: V.tensor_tensor(out=t1[:, 0:NT], in0=banks[b2][:, 0:NT], in1=rs[:, 0:NT],
                                                           op=ALU.mult), reads=[bankR[b2], rsR], writes=[t1R])
                            t2, t2R = stg.next()
                            pool.op(lambda: G.tensor_tensor(out=t2[:, 0:NT], in0=qsb, in1=rc[:, 0:NT], op=ALU.mult),
                                    reads=[svR, rcR], writes=[t2R])
                            o, oR = stg.next()
                            ob = o.bitcast(BF16)[:, 0:NT]
                            dve.op(lambda: V.tensor_tensor(out=ob, in0=t1[:, 0:NT], in1=t2[:, 0:NT], op=ALU.add),
                                   reads=[t1R, t2R], writes=[oR])
                            store_fm(dst, h * 128, n, ob, oR)
                for col, dst, func in ((C_BZ, BZT, AF.Silu), (C_MQ, MQT, None), (C_MZ, MZT, AF.Silu)):
                    for c in range(4):
                        w, wr = load_chunk(col + c * 128)
                        for n in range(nsub):
                            bi = fm_mm(w, wr, n)
                            svb, svR = evac_copy_bf(bi, func)
                            store_fm(dst, c * 128, n, svb, svR)
                for c in range(4):
                    ws = [load_chunk(col + c * 128) for col in (C_CB, C_CC, C_CX, C_CZ)]
                    for n in range(nsub):
                        bis = [fm_mm(w, wr, n) for (w, wr) in ws]
                        ccs, ccR = stg.next()
                        act.op(lambda: A_.copy(out=ccs[:, 0:NT], in_=banks[bis[1]][:, 0:NT]), reads=[bankR[bis[1]]],
                               writes=[ccR])
                        u, uR = stg.next()
                        ub = u.bitcast(BF16)[:, 0:NT]
                        dve.op(lambda: V.tensor_tensor(out=ub, in0=banks[bis[2]][:, 0:NT], in1=ccs[:, 0:NT],
                                                       op=ALU.mult), reads=[bankR[bis[2]], ccR], writes=[uR])
                        store_fm(CUT, c * 128, n, ub, uR)
                        sz, szR = stg.next()
                        act.op(lambda: A_.activation(out=sz[:, 0:NT], in_=banks[bis[3]][:, 0:NT], func=AF.Silu),
                               reads=[bankR[bis[3]]], writes=[szR])
                        g, gR = stg.next()
                        gb = g.bitcast(BF16)[:, 0:NT]
                        dve.op(lambda: V.tensor_tensor(out=gb, in0=banks[bis[0]][:, 0:NT], in1=sz[:, 0:NT],
                                                       op=ALU.mult), reads=[bankR[bis[0]], szR], writes=[gR])
                        store_fm(CGT, c * 128, n, gb, gR)
            kb.barrier()

        def phase_g1(l, si):
            S = seqs[si]
            TG = min(S, 512)
            kb.barrier()
            cv = Carve()
            xin = Rot([(cv.bf(TG + 4), Res()) for _ in range(2)])
            acc = Rot([(cv.f32(TG), Res()) for _ in range(2)])
            ys = Rot([(cv.f32(TG), Res()) for _ in range(2)])
            sq = Rot([(cv.bf(TG), Res()) for _ in range(2)])
            rn = Rot([(cv.f32(TG), Res()) for _ in range(2)])
            yn = Rot([(cv.bf(TG), Res()) for _ in range(3)])
            tk = Rot([(cv.bf(TG), Res()) for _ in range(2)])
            pb = Rot([0, 1, 2, 3])
            pt = Rot([4, 5, 6, 7])
            nt = TG // 128
            items = [(b0, c) for b0 in range(0, S, TG) for c in range(12)]

            def pre(b0, c):
                xi, xiR = xin.next()
                lo = max(0, b0 - 2)
                hi = min(S, b0 + TG + 2)
                pool.op(lambda: G.memset(xi, 0.0), writes=[xiR])
                sp.dma(xi[:, lo - (b0 - 2): hi - (b0 - 2)], QKVT[c * 128:(c + 1) * 128, lo:hi], dwrites=[xiR])
                return xi, xiR
            def chunk(b0, c, xi, xiR):
                a, aR = acc.next()
                dve.op(lambda: V.tensor_scalar(out=a, in0=xi[:, 0:TG], scalar1=gconv[:, c * 5:c * 5 + 1],
                                               scalar2=None, op0=ALU.mult), reads=[xiR, RL], writes=[aR])
                for wi in range(1, 5):
                    dve.op(lambda wi=wi: V.scalar_tensor_tensor(out=a, in0=xi[:, wi:wi + TG],
                                                                scalar=gconv[:, c * 5 + wi:c * 5 + wi + 1], in1=a,
                                                                op0=ALU.mult, op1=ALU.add),
                           reads=[xiR, RL], writes=[aR])
                h = c % 4
                if c < 8:
                    y, yR = ys.next()
                    act.op(lambda: A_.activation(out=y, in_=a, func=AF.Silu), reads=[aR], writes=[yR])
                    s2, s2R = sq.next()
                    act.op(lambda: A_.activation(out=s2, in_=y, func=AF.Square), reads=[yR], writes=[s2R])
                    bi = pb.next()
                    pe.op(lambda: T.matmul(banks[bi][:, 0:TG], ones_bf, s2, start=True, stop=True),
                          reads=[s2R, RC], writes=[bankR[bi]])
                    r, rR = rn.next()
                    if c < 4:
                        act.op(lambda: A_.activation(out=r, in_=banks[bi][:, 0:TG], func=AF.Sqrt, scale=128.0,
                                                     bias=eps128_t), reads=[bankR[bi], RC], writes=[rR])
                    else:
                        act.op(lambda: A_.activation(out=r, in_=banks[bi][:, 0:TG], func=AF.Sqrt, scale=1.0,
                                                     bias=eps_t), reads=[bankR[bi], RC], writes=[rR])
                    yield
                    dve.op(lambda: V.reciprocal(out=r, in_=r), reads=[rR], writes=[rR])
                    o, oR = yn.next()
                    pool.op(lambda: G.tensor_tensor(out=o, in0=y, in1=r, op=ALU.mult), reads=[yR, rR],
                            writes=[oR])
                    dst = GQT if c < 4 else GKT
                    sp.dma(dst[h * 128:(h + 1) * 128, b0:b0 + TG], o, reads=[oR])
                else:
                    o, oR = yn.next()
                    act.op(lambda: A_.activation(out=o, in_=a, func=AF.Silu), reads=[aR], writes=[oR])
                    yield
                if c >= 4:
                    bt = pt.next()
                    tpb = bank_bf(bt)
                    for i in range(nt):
                        pe.op(lambda i=i: T.transpose(out=tpb[:, i * 128:(i + 1) * 128],
                                                      in_=o[:, i * 128:(i + 1) * 128], identity=ident_bf),
                              reads=[oR, RC], writes=[bankR[bt]] if i == 0 else (),
                              dwrites=() if i == 0 else [bankR[bt]])
                    t, tR = tk.next()
                    act.op(lambda: A_.copy(out=t, in_=tpb[:, 0:TG]), reads=[bankR[bt]], writes=[tR])
                    dst = GK if c < 8 else GV
                    sp.dma(dst[b0:b0 + TG, h * 128:(h + 1) * 128].rearrange("(i p) d -> p i d", p=128),
                           t.rearrange("p (i d) -> p i d", d=128), reads=[tR])
            pend = pre(*items[0])
            prev = None
            for ii, (b0, c) in enumerate(items):
                xi, xiR = pend
                if ii + 1 < len(items):
                    pend = pre(*items[ii + 1])
                g_ = chunk(b0, c, xi, xiR)
                next(g_)
                if prev is not None:
                    for _ in prev:
                        pass
                prev = g_
            for _ in prev:
                pass
            kb.barrier()

        def phase_g2(l, si):
            S = seqs[si]
            ntile = S // 128
            kb.barrier()
            cv = Carve()

            def mk(fn, *a):
                return (fn(*a), Res())

            def alloc_set(d):
                B = {}
                B["qT"] = mk(cv.bf, 512, (4, 128))
                B["kT"] = mk(cv.bf, 512, (4, 128))
                B["ktok"] = mk(cv.bf, 512, (4, 128))
                B["vtok"] = mk(cv.bf, 512, (4, 128))
                B["db"] = mk(cv.f32, 16)
                B["sm"] = mk(cv.f32, 64)
                B["E"] = mk(cv.f32, 512, (4, 128))
                B["En"] = mk(cv.f32, 512, (4, 128))
                B["EG"] = mk(cv.f32, 512, (4, 128))
                rv = lambda i: rbuf[:, (5 * d + i) * 512:(5 * d + i + 1) * 512].rearrange("p (a b) -> p a b", a=4)
                B["Xs"] = [(rv(0), Res()), (rv(1), Res())]
                B["Ys"] = [(rv(2), Res()), (rv(3), Res())]
                B["R"] = rv(4), Res()
                B["Rb"] = mk(cv.bf, 512, (4, 128))
                B["attn"] = mk(cv.bf, 512, (4, 128))
                B["attnT"] = mk(cv.bf, 512, (4, 128))
                B["vb"] = mk(cv.bf, 512, (4, 128))
                B["kbg"] = mk(cv.bf, 512, (4, 128))
                B["kg"] = mk(cv.bf, 512, (4, 128))
                B["kg1"] = mk(cv.bf, 512, (4, 128))
                B["qgT"] = mk(cv.bf, 512, (4, 128))
                B["u"] = mk(cv.f32, 512, (4, 128))
                B["wT"] = mk(cv.bf, 512, (4, 128))
                B["vn"] = mk(cv.bf, 512, (4, 128))
                B["ost"] = mk(cv.f32, 512)
                B["S32"] = mk(cv.f32, 512, (4, 128))
                B["Sbf"] = mk(cv.bf, 512, (4, 128))
                return B
            bufs = [alloc_set(0), alloc_set(1)]
            for d in range(2):
                pool.op(lambda d=d: G.memset(bufs[d]["S32"][0], 0.0), writes=[bufs[d]["S32"][1]])
                pool.op(lambda d=d: G.memset(bufs[d]["Sbf"][0], 0.0), writes=[bufs[d]["Sbf"][1]])
                pool.op(lambda d=d: G.memset(bufs[d]["vn"][0], 0.0), writes=[bufs[d]["vn"][1]])
            def unit(d, ti, B):
                qT, qTR = B["qT"]
                kT, kTR = B["kT"]
                ktok, ktokR = B["ktok"]
                vtok, vtokR = B["vtok"]
                db, dbR = B["db"]
                sm, smR = B["sm"]
                E, ER = B["E"]
                En, EnR = B["En"]
                EG, EGR = B["EG"]
                Xs = B["Xs"]
                Ys = B["Ys"]
                R, RR = B["R"]
                Rb, RbR = B["Rb"]
                attn, attnR = B["attn"]
                attnT, attnTR = B["attnT"]
                vb, vbR = B["vb"]
                kbg, kbgR = B["kbg"]
                kg, kgR = B["kg"]
                kg1, kg1R = B["kg1"]
                qgT, qgTR = B["qgT"]
                u, uR = B["u"]
                wT, wTR = B["wT"]
                vn, vnR = B["vn"]
                ost, ostR = B["ost"]
                t0 = ti * 128
                bk = [4 * d + i_ for i_ in range(4)]
                s32, s32R = B["S32"]
                sbf, sbfR = B["Sbf"]
                sp.dma(qT, GQT[:, t0:t0 + 128].rearrange("(h p) t -> p h t", p=128), writes=[qTR])
                sp.dma(kT, GKT[:, t0:t0 + 128].rearrange("(h p) t -> p h t", p=128), writes=[kTR])
                sp.dma(ktok, GK[t0:t0 + 128, :].rearrange("t (h e) -> t h e", e=128), writes=[ktokR])
                sp.dma(vtok, GV[t0:t0 + 128, :].rearrange("t (h e) -> t h e", e=128), writes=[vtokR])
                sp.dma(db, DBt[t0:t0 + 128, :], writes=[dbR])
                dec = db[:, d * 4:d * 4 + 4]
                bet = db[:, 8 + d * 4:8 + d * 4 + 4]
                beta, nbeta, ee, g, gc, dl, kgf, egc, kbgf, spin = [sm[:, 4 * i:4 * i + 4] for i in range(10)]
                act.op(lambda: A_.activation(out=beta, in_=bet, func=AF.Sigmoid), reads=[dbR], writes=[smR])
                dve.op(lambda: V.tensor_scalar(out=nbeta, in0=beta, scalar1=-1.0, scalar2=None, op0=ALU.mult),
                       reads=[smR], dwrites=[smR])
                dve.op(lambda: V.tensor_tensor(out=spin, in0=dec, in1=dtb_b[:, d * 4:d * 4 + 4], op=ALU.add),
                       reads=[dbR, RL, smR], dwrites=[smR])
                act.op(lambda: A_.activation(out=ee, in_=spin, func=AF.Exp), reads=[smR], dwrites=[smR])
                act.op(lambda: A_.activation(out=ee, in_=ee, func=AF.Ln, bias=one_t), reads=[smR, RC],
                       dwrites=[smR])
                dve.op(lambda: V.tensor_tensor(out=g, in0=ee, in1=negA_b[:, d * 4:d * 4 + 4], op=ALU.mult),
                       reads=[smR, RL], dwrites=[smR])
                pe.op(lambda: T.matmul(banks[bk[0]][:, 0:4], Lmat[d], g, start=True, stop=True), reads=[smR, RC],
                      writes=[bankR[bk[0]]])
                pe.op(lambda: T.matmul(banks[bk[0]][:, 4:8], f32c[:, 384:512], g, start=True, stop=True),
                      reads=[smR, RC], dwrites=[bankR[bk[0]]])
                dve.op(lambda: V.tensor_copy(out=gc, in_=banks[bk[0]][:, 0:4]), reads=[bankR[bk[0]], smR], dwrites=[smR])
                dve.op(lambda: V.tensor_tensor(out=dl, in0=banks[bk[0]][:, 4:8], in1=gc, op=ALU.subtract),
                       reads=[bankR[bk[0]], smR], dwrites=[smR])
                act.op(lambda: A_.activation(out=kgf, in_=dl, func=AF.Exp), reads=[smR], dwrites=[smR])
                act.op(lambda: A_.activation(out=egc, in_=gc, func=AF.Exp), reads=[smR], dwrites=[smR])
                dve.op(lambda: V.tensor_tensor(out=kbgf, in0=egc, in1=beta, op=ALU.mult), reads=[smR],
                       dwrites=[smR])
                yield
                for h in range(4):
                    pe.op(lambda h=h: T.matmul(banks[bk[1]][:, h * 128:(h + 1) * 128],
                                               g[:, h:h + 1].to_broadcast([128, 128]), Lmat[d],
                                               start=True, stop=True), reads=[smR, RC],
                          writes=[bankR[bk[1]]] if h == 0 else (), dwrites=() if h == 0 else [bankR[bk[1]]])
                pe.op(lambda: T.matmul(banks[bk[2]][:, :], ident_f, mask4[d], start=True, stop=False), reads=[RC],
                      writes=[bankR[bk[2]]])
                for h in range(4):
                    pe.op(lambda h=h: T.matmul(banks[bk[2]][:, h * 128:(h + 1) * 128],
                                               g[:, h:h + 1].to_broadcast([128, 128]), Lmat[d],
                                               start=False, stop=(h == 3)), reads=[smR, RC], dwrites=[bankR[bk[2]]])
                for h in range(4):
                    act.op(lambda h=h: A_.activation(out=E[:, h, :], in_=banks[bk[2]][:, h * 128:(h + 1) * 128],
                                                     func=AF.Exp, scale=-1.0, bias=gc[:, h:h + 1]),
                           reads=[bankR[bk[2]], smR], writes=[ER] if h == 0 else (), dwrites=() if h == 0 else [ER])
                act.op(lambda: A_.activation(out=EG.rearrange("p h c -> p (h c)"), in_=banks[bk[1]][:, :], func=AF.Exp),
                       reads=[bankR[bk[1]]], writes=[EGR])
                pool.op(lambda: G.tensor_tensor(out=En.rearrange("p h c -> p (h c)"),
                                                in0=E.rearrange("p h c -> p (h c)"), in1=nodiag4, op=ALU.mult),
                        reads=[ER, RC], writes=[EnR])
                yield
                for h in range(4):
                    pe.op(lambda h=h: T.matmul(banks[bk[3]][:, h * 128:(h + 1) * 128], kT[:, h, :], kT[:, h, :],
                                               start=True, stop=True), reads=[kTR],
                          writes=[bankR[bk[3]]] if h == 0 else (), dwrites=() if h == 0 else [bankR[bk[3]]])
                for h in range(4):
                    pe.op(lambda h=h: T.matmul(banks[bk[0]][:, h * 128:(h + 1) * 128], qT[:, h, :], kT[:, h, :],
                                               start=True, stop=True), reads=[kTR, qTR],
                          writes=[bankR[bk[0]]] if h == 0 else (), dwrites=() if h == 0 else [bankR[bk[0]]])
                X0, X0R = Xs[0]
                Y0, Y0R = Ys[0]
                for h in range(4):
                    dve.op(lambda h=h: V.scalar_tensor_tensor(out=X0[:, h, :], in0=banks[bk[3]][:, h * 128:(h + 1) * 128],
                                                              scalar=nbeta[:, h:h + 1], in1=En[:, h, :],
                                                              op0=ALU.mult, op1=ALU.mult),
                           reads=[bankR[bk[3]], smR, EnR], writes=[X0R] if h == 0 else (),
                           dwrites=() if h == 0 else [X0R])
                dve.op(lambda: V.tensor_tensor(out=attn.rearrange("p h c -> p (h c)"), in0=banks[bk[0]][:, :],
                                               in1=E.rearrange("p h c -> p (h c)"), op=ALU.mult),
                       reads=[bankR[bk[0]], ER], writes=[attnR])
                yield
                b5r = banks[bk[1]][:, :].bitcast(F32R)
                for h in range(4):
                    pe.op(lambda h=h: T.matmul(banks[bk[1]][:, h * 128:(h + 1) * 128], X0[:, h, :], ident_r,
                                               start=True, stop=True),
                          reads=[X0R, RC],
                          writes=[bankR[bk[1]]] if h == 0 else (), dwrites=() if h == 0 else [bankR[bk[1]]])
                b6 = bank_bf(bk[2])
                for h in range(4):
                    pe.op(lambda h=h: T.transpose(out=b6[:, h * 128:(h + 1) * 128], in_=attn[:, h, :],
                                                  identity=ident_bf), reads=[attnR, RC],
                          writes=[bankR[bk[2]]] if h == 0 else (), dwrites=() if h == 0 else [bankR[bk[2]]])
                act.op(lambda: A_.copy(out=Y0.rearrange("p h c -> p (h c)"), in_=banks[bk[1]][:, :]),
                       reads=[bankR[bk[1]]], writes=[Y0R])
                dve.op(lambda: V.tensor_tensor(out=R.rearrange("p h c -> p (h c)"),
                                               in0=Y0.rearrange("p h c -> p (h c)").bitcast(F32),
                                               in1=ident4, op=ALU.add), reads=[Y0R, RC], writes=[RR])
                act.op(lambda: A_.copy(out=attnT.rearrange("p h c -> p (h c)"), in_=b6[:, 0:512]),
                       reads=[bankR[bk[2]]], writes=[attnTR])


                yield
                def mm4(bi, lhs, lhsR, rhs, rhsR):
                    for h in range(4):
                        pe.op(lambda h=h: T.matmul(banks[bi][:, h * 128:(h + 1) * 128], lhs[:, h, :], rhs[:, h, :],
                                                   start=True, stop=True), reads=[lhsR, rhsR],
                              writes=[bankR[bi]] if h == 0 else (), dwrites=() if h == 0 else [bankR[bi]])

                cur = 0
                for lev in range(1, 6):
                    Xc, XcR = Xs[cur]
                    Yc, YcR = Ys[cur]
                    Xn, XnR = Xs[1 - cur]
                    Yn, YnR = Ys[1 - cur]
                    if lev >= 2:
                        pass
                    if lev >= 2:
                        mm4(bk[3], Xc, XcR, R, RR)
                        dve.op(lambda: V.tensor_tensor(out=R.rearrange("p h c -> p (h c)"), in0=banks[bk[3]][:, :],
                                                       in1=R.rearrange("p h c -> p (h c)").bitcast(F32),
                                                       op=ALU.add), reads=[bankR[bk[3]], RR], writes=[RR])
                    mm4(bk[1], Yc, YcR, Xc, XcR)
                    act.op(lambda Xn=Xn: A_.copy(out=Xn.rearrange("p h c -> p (h c)"), in_=banks[bk[1]][:, :]),
                           reads=[bankR[bk[1]]], writes=[XnR])
                    if lev <= 4:
                        mm4(bk[2], Xc, XcR, Yc, YcR)
                        dve.op(lambda Yn=Yn: V.tensor_copy(out=Yn.rearrange("p h c -> p (h c)"), in_=banks[bk[2]][:, :]),
                               reads=[bankR[bk[2]]], writes=[YnR])
                    cur = 1 - cur
                    yield
                Xc, XcR = Xs[cur]
                mm4(bk[3], Xc, XcR, R, RR)
                dve.op(lambda: V.tensor_tensor(out=R.rearrange("p h c -> p (h c)"), in0=banks[bk[3]][:, :],
                                               in1=R.rearrange("p h c -> p (h c)").bitcast(F32), op=ALU.add),
                       reads=[bankR[bk[3]], RR], writes=[RR])
                act.op(lambda: A_.copy(out=Rb.rearrange("p h c -> p (h c)"),
                                       in_=R.rearrange("p h c -> p (h c)").bitcast(F32)), reads=[RR],
                       writes=[RbR])
                yield
                kgf0, kgf1 = sm[:, 40:44], sm[:, 44:48]
                dve.op(lambda: V.tensor_scalar(out=kgf0, in0=kgf, scalar1=ind_t[:, 0:1], scalar2=None, op0=ALU.mult),
                       reads=[smR, RI], dwrites=[smR])
                dve.op(lambda: V.tensor_scalar(out=kgf1, in0=kgf, scalar1=ind_t[:, 1:2], scalar2=None, op0=ALU.mult),
                       reads=[smR, RI], dwrites=[smR])
                for h in range(4):
                    act.op(lambda h=h: A_.activation(out=vb[:, h, :], in_=vtok[:, h, :], func=AF.Copy,
                                                     scale=beta[:, h:h + 1]), reads=[vtokR, smR],
                           writes=[vbR] if h == 0 else (), dwrites=() if h == 0 else [vbR])
                    dve.op(lambda h=h: V.tensor_scalar(out=kbg[:, h, :], in0=ktok[:, h, :],
                                                       scalar1=kbgf[:, h:h + 1], scalar2=None, op0=ALU.mult),
                           reads=[ktokR, smR], writes=[kbgR] if h == 0 else (), dwrites=() if h == 0 else [kbgR])
                    pool.op(lambda h=h: G.tensor_scalar(out=kg[:, h, :], in0=ktok[:, h, :],
                                                        scalar1=kgf0[:, h:h + 1], scalar2=1.0, op0=ALU.mult,
                                                        op1=ALU.mult),
                            reads=[ktokR, smR], writes=[kgR] if h == 0 else (), dwrites=() if h == 0 else [kgR])
                    dve.op(lambda h=h: V.tensor_scalar(out=kg1[:, h, :], in0=ktok[:, h, :],
                                                       scalar1=kgf1[:, h:h + 1], scalar2=None, op0=ALU.mult),
                           reads=[ktokR, smR], writes=[kg1R] if h == 0 else (), dwrites=() if h == 0 else [kg1R])
                pool.op(lambda: G.tensor_tensor(out=qgT.rearrange("p h c -> p (h c)"),
                                                in0=qT.rearrange("p h c -> p (h c)"),
                                                in1=EG.rearrange("p h c -> p (h c)"), op=ALU.mult),
                        reads=[qTR, EGR], writes=[qgTR])
                mm4(bk[3], Rb, RbR, vb, vbR)
                act.op(lambda: A_.copy(out=u.rearrange("p h c -> p (h c)"), in_=banks[bk[3]][:, :]), reads=[bankR[bk[3]]],
                       writes=[uR])
                mm4(bk[0], kbg, kbgR, Rb, RbR)
                dve.op(lambda: V.tensor_copy(out=wT.rearrange("p h c -> p (h c)"), in_=banks[bk[0]][:, :]),
                       reads=[bankR[bk[0]]], writes=[wTR])
                yield
                for step in range(2):
                    r0 = (0, 64)[step] if d == 0 else (64, 0)[step]
                    rows = slice(r0, r0 + 64)
                    for h in range(4):
                        pe.op(lambda h=h: T.matmul(banks[bk[0]][:, h * 128:(h + 1) * 128], wT[:, h, :],
                                                   sbf[:, h, :], start=True, stop=True), reads=[wTR, sbfR],
                              writes=[bankR[bk[0]]] if h == 0 else (), dwrites=() if h == 0 else [bankR[bk[0]]])
                    dve.op(lambda: V.tensor_tensor(out=vn[rows].rearrange("p h c -> p (h c)"),
                                                   in0=u[rows].rearrange("p h c -> p (h c)"),
                                                   in1=banks[bk[0]][rows, :], op=ALU.subtract),
                           reads=[bankR[bk[0]], uR], writes=[vnR])
                    for h in range(4):
                        pe.op(lambda h=h: T.matmul(banks[bk[1]][:, h * 128:(h + 1) * 128], qgT[:, h, :],
                                                   sbf[:, h, :], start=True, stop=False), reads=[qgTR, sbfR],
                              writes=[bankR[bk[1]]] if h == 0 else (), dwrites=() if h == 0 else [bankR[bk[1]]])
                        pe.op(lambda h=h: T.matmul(banks[bk[1]][:, h * 128:(h + 1) * 128], attnT[:, h, :],
                                                   vn[:, h, :], start=False, stop=True), reads=[attnTR, vnR],
                              dwrites=[bankR[bk[1]]])
                    for h in range(4):
                        pe.op(lambda h=h: T.matmul(banks[bk[2]][:, h * 128:(h + 1) * 128], (kg, kg1)[r0 // 64][:, h, :],
                                                   vn[:, h, :], start=True, stop=True), reads=[kgR, kg1R, vnR],
                              writes=[bankR[bk[2]]] if h == 0 else (), dwrites=() if h == 0 else [bankR[bk[2]]])
                    col = r0 + 63 if d == 0 else r0
                    for h in range(4):
                        dve.op(lambda h=h: V.scalar_tensor_tensor(out=s32[:, h, :], in0=s32[:, h, :],
                                                                  scalar=EG[:, h, col:col + 1],
                                                                  in1=banks[bk[2]][:, h * 128:(h + 1) * 128],
                                                                  op0=ALU.mult, op1=ALU.add),
                               reads=[bankR[bk[2]], EGR], writes=[s32R] if h == 0 else (),
                               dwrites=() if h == 0 else [s32R])
                    act.op(lambda: A_.copy(out=sbf.rearrange("p h c -> p (h c)"),
                                           in_=s32.rearrange("p h c -> p (h c)")), reads=[s32R], writes=[sbfR])
                    act.op(lambda: A_.copy(out=ost[rows, :], in_=banks[bk[1]][rows, :]), reads=[bankR[bk[1]]],
                           writes=[ostR] if step == 0 else (), dwrites=() if step == 0 else [ostR])
                    yield
                sp.dma(OD[d][t0:t0 + 128, :], ost, reads=[ostR])
            for it in range(ntile):
                alive = [unit(0, it, bufs[0]), unit(1, ntile - 1 - it, bufs[1])]
                while alive:
                    for g_ in list(alive):
                        try:
                            next(g_)
                        except StopIteration:
                            alive.remove(g_)
            kb.barrier()

        def phase_g3(l, si):
            S = seqs[si]
            kb.barrier()
            cv = Carve()
            ofs = Rot([(cv.f32(512), Res()) for _ in range(2)])
            obs = Rot([(cv.f32(512), Res()) for _ in range(2)])
            azs = Rot([(cv.bf(512), Res()) for _ in range(2)])
            junk = cv.f32(128)
            sm, smR = cv.f32(16), Res()
            ys = Rot([(cv.f32(512), Res()) for _ in range(2)])
            ybs = Rot([(cv.bf(512), Res()) for _ in range(2)])
            sts = Rot([(cv.bf(512), Res()) for _ in range(2)])
            pb = Rot([0, 1])
            def pre_g3(ti):
                t0 = ti * 128
                of, ofR = ofs.next()
                ob, obR = obs.next()
                az, azR = azs.next()
                sp.dma(of, OD[0][t0:t0 + 128, :], writes=[ofR])
                sp.dma(ob, OD[1][t0:t0 + 128, :], writes=[obR])
                sp.dma(az, AZ[t0:t0 + 128, :], writes=[azR])
                return of, ofR, ob, obR, az, azR
            pend = pre_g3(0)
            for ti in range(S // 128):
                t0 = ti * 128
                of, ofR, ob, obR, az, azR = pend
                if ti + 1 < S // 128:
                    pend = pre_g3(ti + 1)
                pool.op(lambda: G.tensor_tensor(out=of, in0=of, in1=ob, op=ALU.add), reads=[obR], writes=[ofR])
                for h in range(4):
                    act.op(lambda h=h: A_.activation(out=junk, in_=of[:, h * 128:(h + 1) * 128], func=AF.Square,
                                                     accum_out=sm[:, h:h + 1]), reads=[ofR], writes=[smR])
                act.op(lambda: A_.activation(out=sm[:, 4:8], in_=sm[:, 0:4], func=AF.Sqrt, scale=1.0 / 128, bias=eps_t),
                       reads=[smR, RC], dwrites=[smR])
                dve.op(lambda: V.reciprocal(out=sm[:, 8:12], in_=sm[:, 4:8]), reads=[smR], dwrites=[smR])
                y, yR = ys.next()
                for h in range(4):
                    dve.op(lambda h=h: V.scalar_tensor_tensor(out=y[:, h * 128:(h + 1) * 128],
                                                              in0=of[:, h * 128:(h + 1) * 128],
                                                              scalar=sm[:, 8 + h:9 + h], in1=gnorm_b, op0=ALU.mult,
                                                              op1=ALU.mult), reads=[ofR, smR, RL],
                           writes=[yR] if h == 0 else (), dwrites=() if h == 0 else [yR])
                yb, ybR = ybs.next()
                pool.op(lambda: G.tensor_tensor(out=yb, in0=y, in1=az, op=ALU.mult), reads=[yR, azR], writes=[ybR])
                bi = pb.next()
                tpb = bank_bf(bi)
                for h in range(4):
                    pe.op(lambda h=h: T.transpose(out=tpb[:, h * 128:(h + 1) * 128], in_=yb[:, h * 128:(h + 1) * 128],
                                                  identity=ident_bf), reads=[ybR, RC],
                          writes=[bankR[bi]] if h == 0 else (), dwrites=() if h == 0 else [bankR[bi]])
                st, stR = sts.next()
                act.op(lambda: A_.copy(out=st, in_=tpb[:, 0:512]), reads=[bankR[bi]], writes=[stR])
                sp.dma(YT[0:512, t0:t0 + 128].rearrange("(h e) t -> e h t", e=128),
                       st.rearrange("p (h t) -> p h t", t=128), reads=[stR])
            kb.barrier()

        def softmax_pv(items, QB, scale, bsc, pts):
            n = len(items)
            sc = [None] * n

            def qk(i):
                it = items[i]
                bs = bsc.next()
                sc[i] = bs
                pe.op(lambda: T.matmul(banks[bs][:, 0:QB], it["kT"], it["q"], start=True, stop=True),
                      reads=[it["kR"], it["qR"]], writes=[bankR[bs]])
            for i in range(min(2, n)):
                qk(i)
            for i in range(n):
                it = items[i]
                bs = sc[i]
                bo, bd = it["bo"], it["bd"]
                p, pR = pts.next()
                act.op(lambda: A_.activation(out=p[:, 0:QB], in_=banks[bs][:, 0:QB], func=AF.Exp, scale=scale),
                       reads=[bankR[bs]], writes=[pR])
                pe.op(lambda: T.matmul(banks[bo][:, 0:QB], it["v"], p[:, 0:QB], start=it["first"], stop=it["last"]),
                      reads=[it["vR"], pR], writes=[bankR[bo]] if it["first"] else (),
                      dwrites=() if it["first"] else [bankR[bo]])
                pe.op(lambda: T.matmul(banks[bd][:, 0:QB], ones_bf, p[:, 0:QB], start=it["first"], stop=it["last"]),
                      reads=[RC, pR], writes=[bankR[bd]] if it["first"] else (),
                      dwrites=() if it["first"] else [bankR[bd]])
                if i + 2 < n:
                    qk(i + 2)

        def phase_b(l, si):
            S = seqs[si]
            QB = min(512, S)
            nkc = S // 128
            kb.barrier()
            cv = Carve()
            kTs = Rot([(cv.bf(S), (cv.bf(S), cv.bf(S)), cv.bf(S, (S // 128, 128)), Res()) for _ in range(2)])
            for (_k, (qz0, qz1), _v, kvR0) in kTs.items:
                pool.op(lambda: G.memset(qz0[64:128, :], 0.0), writes=[kvR0])
                pool.op(lambda: G.memset(qz1[0:64, :], 0.0), dwrites=[kvR0])
            zts = Rot([(cv.bf(QB), Res()) for _ in range(2)])
            pts = Rot([(cv.bf(512), Res()) for _ in range(4)])
            tmp = Rot([(cv.f32(512), Res()) for _ in range(6)])
            sqs = Rot([(cv.bf(512), Res()) for _ in range(2)])
            outs = Rot([(cv.bf(512), Res()) for _ in range(2)])
            bsc = Rot([0, 1, 7])

            def load_head(h):
                kTh, qTh, vh, kvR = kTs.next()
                sp.dma(kTh, BKT[h * 128:(h + 1) * 128, 0:S], writes=[kvR])
                sp.dma(qTh[0][0:64, :], BQT[h * 128:h * 128 + 64, 0:S], dwrites=[kvR])
                sp.dma(qTh[1][64:128, :], BQT[h * 128 + 64:(h + 1) * 128, 0:S], dwrites=[kvR])
                sp.dma(vh, BV[0:S, h * 128:(h + 1) * 128].rearrange("(i p) e -> p i e", p=128), dwrites=[kvR])
                return kTh, qTh, vh, kvR

            def load_z(h, qb):
                zt, ztR = zts.next()
                sp.dma(zt, BZT[h * 128:(h + 1) * 128, qb * QB:(qb + 1) * QB], writes=[ztR])
                return zt, ztR
            nqb = S // QB
            hq = [(h, qb) for h in range(4) for qb in range(nqb)]
            head_next = load_head(0)
            z_next = load_z(0, 0)
            for ii, (h, qb) in enumerate(hq):
                if qb == 0:
                    kTh, qTh, vh, kvR = head_next
                    if h + 1 < 4:
                        head_next = load_head(h + 1)
                zt, ztR = z_next
                if ii + 1 < len(hq):
                    z_next = load_z(*hq[ii + 1])
                if True:
                    q0 = qb * QB
                    items = []
                    for j in range(2):
                        for kc in range(nkc):
                            items.append(dict(kT=kTh[:, kc * 128:(kc + 1) * 128], q=qTh[j][:, q0:q0 + QB],
                                              v=vh[:, kc, :], kR=kvR, qR=kvR, vR=kvR, bo=2 + j, bd=4 + j,
                                              first=(kc == 0), last=(kc == nkc - 1)))
                    softmax_pv(items, QB, 0.125, bsc, pts)
                    r0, r0R = tmp.next()
                    r1, r1R = tmp.next()
                    dve.op(lambda: V.reciprocal(out=r0[:, 0:QB], in_=banks[4][:, 0:QB]), reads=[bankR[4]], writes=[r0R])
                    dve.op(lambda: V.reciprocal(out=r1[:, 0:QB], in_=banks[5][:, 0:QB]), reads=[bankR[5]], writes=[r1R])
                    o0, o0R = tmp.next()
                    o1, o1R = tmp.next()
                    dve.op(lambda: V.tensor_tensor(out=o0[:, 0:QB], in0=banks[2][:, 0:QB], in1=r0[:, 0:QB], op=ALU.mult),
                           reads=[bankR[2], r0R], writes=[o0R])
                    dve.op(lambda: V.tensor_tensor(out=o1[:, 0:QB], in0=banks[3][:, 0:QB], in1=r1[:, 0:QB], op=ALU.mult),
                           reads=[bankR[3], r1R], writes=[o1R])
                    dve.op(lambda: V.scalar_tensor_tensor(out=o0[:, 0:QB], in0=o1[:, 0:QB], scalar=neglam,
                                                          in1=o0[:, 0:QB], op0=ALU.mult, op1=ALU.add),
                           reads=[o1R, RL], writes=[o0R])
                    sq, sqR = sqs.next()
                    pool.op(lambda: G.tensor_tensor(out=sq[:, 0:QB], in0=o0[:, 0:QB], in1=o0[:, 0:QB], op=ALU.mult),
                            reads=[o0R], writes=[sqR])
                    pe.op(lambda: T.matmul(banks[6][:, 0:QB], ones_bf, sq[:, 0:QB], start=True, stop=True),
                          reads=[sqR, RC], writes=[bankR[6]])
                    rn, rnR = tmp.next()
                    act.op(lambda: A_.activation(out=rn[:, 0:QB], in_=banks[6][:, 0:QB], func=AF.Ln, scale=1.0 / 128,
                                                 bias=eps_t), reads=[bankR[6], RC], writes=[rnR])
                    act.op(lambda: A_.activation(out=rn[:, 0:QB], in_=rn[:, 0:QB], func=AF.Exp, scale=-0.5),
                           reads=[rnR], writes=[rnR])
                    dve.op(lambda: V.scalar_tensor_tensor(out=o0[:, 0:QB], in0=o0[:, 0:QB], scalar=dnorm_s,
                                                          in1=rn[:, 0:QB], op0=ALU.mult, op1=ALU.mult),
                           reads=[rnR, RL], writes=[o0R])
                    ot, otR = outs.next()
                    pool.op(lambda: G.tensor_tensor(out=ot[:, 0:QB], in0=o0[:, 0:QB], in1=zt, op=ALU.mult),
                            reads=[o0R, ztR], writes=[otR])
                    sp.dma(YT[512 + h * 128: 512 + (h + 1) * 128, q0:q0 + QB], ot[:, 0:QB], reads=[otR])
            kb.barrier()

        def phase_m(l, si):
            S = seqs[si]
            QB = min(512, S)
            kb.barrier()
            cv = Carve()
            wkv = cv.bf(16 * 1024, (16, 1024))
            wkvR = Res()
            nmem_b = cv.f32(D)
            xts = Rot([(cv.f32(D), Res()) for _ in range(2)])
            junk = cv.bf(D)
            hbs = Rot([(cv.bf(D), Res()) for _ in range(2)])
            memT = cv.bf(16 * 256, (16, 256))
            memTR = Res()
            small, smallR = cv.f32(8), Res()
            kmT, kmTR = cv.bf(4 * 256, (4, 256)), Res()
            vm, vmR = cv.bf(2 * 512, (2, 512)), Res()
            qts = Rot([(cv.bf(QB), cv.bf(QB), Res()) for _ in range(2)])
            pts = Rot([(cv.bf(512), Res()) for _ in range(3)])
            tmp = Rot([(cv.f32(512), Res()) for _ in range(4)])
            outs = Rot([(cv.bf(512), Res()) for _ in range(2)])
            sp.dma(wkv, WKV[l], reads=[RW], writes=[wkvR])
            sp.dma(nmem_b, norm_mem[l].partition_broadcast(128), writes=[smallR])
            for i in range(2):
                xt, xtR = xts.next()
                hb, hbR = hbs.next()
                sp.dma(xt, mem_in[si * N_MEM + i * 128: si * N_MEM + (i + 1) * 128, :], writes=[xtR])
                rms_rows((junk, small[:, 0:1], small[:, 1:2]), xt, xtR, small[:, 2:3], smallR, D)
                dve.op(lambda: V.scalar_tensor_tensor(out=hb, in0=xt, scalar=small[:, 2:3], in1=nmem_b,
                                                      op0=ALU.mult, op1=ALU.mult), reads=[xtR, smallR], writes=[hbR])
                for half in range(2):
                    tpb = bank_bf(half)
                    for kk in range(8):
                        kc = half * 8 + kk
                        pe.op(lambda kc=kc, kk=kk, tpb=tpb: T.transpose(out=tpb[:, kk * 128:(kk + 1) * 128],
                                                                        in_=hb[:, kc * 128:(kc + 1) * 128],
                                                                        identity=ident_bf), reads=[hbR, RC],
                              writes=[bankR[half]] if kk == 0 else (), dwrites=() if kk == 0 else [bankR[half]])
                    dve.op(lambda: V.tensor_copy(out=memT[:, half * 8:(half + 1) * 8, i * 128:(i + 1) * 128],
                                                 in_=tpb.rearrange("p (k t) -> p k t", k=8)), reads=[bankR[half]],
                           dwrites=[memTR])
            for h in range(4):
                bi = 2 + h
                for kc in range(16):
                    pe.op(lambda kc=kc: T.matmul(banks[bi][:, 0:256], wkv[:, kc, h * 128:(h + 1) * 128], memT[:, kc, :],
                                                 start=(kc == 0), stop=(kc == 15)), reads=[wkvR, memTR],
                          writes=[bankR[bi]] if kc == 0 else (), dwrites=() if kc == 0 else [bankR[bi]])
                act.op(lambda: A_.copy(out=kmT[:, h, :], in_=banks[bi][:, 0:256]), reads=[bankR[bi]], dwrites=[kmTR])
            for mt in range(2):
                bi = 6 + mt
                for kc in range(16):
                    pe.op(lambda kc=kc: T.matmul(banks[bi][:, :], memT[:, kc, mt * 128:(mt + 1) * 128],
                                                 wkv[:, kc, 512:1024], start=(kc == 0), stop=(kc == 15)),
                          reads=[wkvR, memTR], writes=[bankR[bi]] if kc == 0 else (),
                          dwrites=() if kc == 0 else [bankR[bi]])
                dve.op(lambda: V.tensor_copy(out=vm[:, mt, :], in_=banks[bi][:, :]), reads=[bankR[bi]], dwrites=[vmR])
            bsc = Rot([0, 1, 7])
            bo = Rot([2, 3])
            bdn = Rot([4, 5])
            scale = 128.0 ** -0.5
            for h in range(4):
                for qb in range(S // QB):
                    q0 = qb * QB
                    qt, zt, qR = qts.next()
                    sp.dma(qt, MQT[h * 128:(h + 1) * 128, q0:q0 + QB], writes=[qR])
                    sp.dma(zt, MZT[h * 128:(h + 1) * 128, q0:q0 + QB], dwrites=[qR])
                    b_o = bo.next()
                    b_d = bdn.next()
                    items = [dict(kT=kmT[:, h, kc * 128:(kc + 1) * 128], q=qt, v=vm[:, kc, h * 128:(h + 1) * 128],
                                  kR=kmTR, qR=qR, vR=vmR, bo=b_o, bd=b_d, first=(kc == 0), last=(kc == 1))
                             for kc in range(2)]
                    softmax_pv(items, QB, scale, bsc, pts)
                    r, rR = tmp.next()
                    dve.op(lambda: V.reciprocal(out=r[:, 0:QB], in_=banks[b_d][:, 0:QB]), reads=[bankR[b_d]],
                           writes=[rR])
                    o, oR = tmp.next()
                    dve.op(lambda: V.tensor_tensor(out=o[:, 0:QB], in0=banks[b_o][:, 0:QB], in1=r[:, 0:QB],
                                                   op=ALU.mult), reads=[bankR[b_o], rR], writes=[oR])
                    ot, otR = outs.next()
                    pool.op(lambda: G.tensor_tensor(out=ot[:, 0:QB], in0=o[:, 0:QB], in1=zt, op=ALU.mult),
                            reads=[oR, qR], writes=[otR])
                    sp.dma(YT[1536 + h * 128: 1536 + (h + 1) * 128, q0:q0 + QB], ot[:, 0:QB], reads=[otR])
            kb.barrier()

        def phase_c(l, si):
            S = seqs[si]
            TC = min(S, 1024)
            kb.barrier()
            cv = Carve()
            us = Rot([(cv.bf(TC + 2), Res()) for _ in range(2)])
            gs = Rot([(cv.bf(TC), Res()) for _ in range(2)])
            acc = Rot([(cv.f32(TC), Res()) for _ in range(2)])
            outs = Rot([(cv.bf(TC), Res()) for _ in range(2)])
            for c in range(4):
                for b0 in range(0, S, TC):
                    ut, utR = us.next()
                    gt, gtR = gs.next()
                    lo = max(0, b0 - 1)
                    hi = min(S, b0 + TC + 1)
                    pool.op(lambda: G.memset(ut, 0.0), writes=[utR])
                    sp.dma(ut[:, lo - (b0 - 1): hi - (b0 - 1)], CUT[c * 128:(c + 1) * 128, lo:hi], dwrites=[utR])
                    sp.dma(gt, CGT[c * 128:(c + 1) * 128, b0:b0 + TC], writes=[gtR])
                    a, aR = acc.next()
                    dve.op(lambda: V.tensor_scalar(out=a, in0=ut[:, 0:TC], scalar1=cconv[:, c * 3:c * 3 + 1],
                                                   scalar2=None, op0=ALU.mult), reads=[utR, RL], writes=[aR])
                    for wi in (1, 2):
                        dve.op(lambda wi=wi: V.scalar_tensor_tensor(out=a, in0=ut[:, wi:wi + TC],
                                                                    scalar=cconv[:, c * 3 + wi:c * 3 + wi + 1], in1=a,
                                                                    op0=ALU.mult, op1=ALU.add), reads=[utR, RL],
                               writes=[aR])
                    o, oR = outs.next()
                    pool.op(lambda: G.tensor_tensor(out=o, in0=a, in1=gt, op=ALU.mult), reads=[aR, gtR], writes=[oR])
                    sp.dma(YT[1024 + c * 128: 1024 + (c + 1) * 128, b0:b0 + TC], o, reads=[oR])
            kb.barrier()

        def phase_o(l, si, xsrc, xsrcR, xdst, xdstR):
            S = seqs[si]
            kb.barrier()
            cv = Carve()
            wo = cv.bf(16 * D, (16, D))
            woR = Res()
            yts = Rot([(cv.bf(16 * 128, (16, 128)), Res()) for _ in range(2)])
            y32 = Rot([(cv.f32(D), Res()) for _ in range(2)])
            xts = Rot([(cv.f32(D), Res()) for _ in range(2)])
            junk = cv.bf(D)
            small, smallR = cv.f32(8), Res()
            for q4 in range(4):
                sp.dma(wo[:, :, q4 * 512:(q4 + 1) * 512], WOUT[l][:, :, q4 * 512:(q4 + 1) * 512], reads=[RW],
                       dwrites=[woR])
            pbo = Rot([[0, 1, 2, 3], [4, 5, 6, 7]])
            def pre_o(ti):
                t0 = ti * 128
                yt, ytR = yts.next()
                sp.dma(yt, YT[:, t0:t0 + 128].rearrange("(k p) t -> p k t", p=128), writes=[ytR])
                xt, xtR = xts.next()
                sp.dma(xt, xsrc[offs[si] + t0: offs[si] + t0 + 128, :], reads=[xsrcR], writes=[xtR])
                return yt, ytR, xt, xtR
            pend = pre_o(0)
            for ti in range(S // 128):
                t0 = ti * 128
                yt, ytR, xt, xtR = pend
                if ti + 1 < S // 128:
                    pend = pre_o(ti + 1)
                bs = pbo.next()
                yv, yvR = y32.next()
                for cb in range(4):
                    bi = bs[cb]
                    for kc in range(16):
                        pe.op(lambda kc=kc: T.matmul(banks[bi][:, :], yt[:, kc, :], wo[:, kc, cb * 512:(cb + 1) * 512],
                                                     start=(kc == 0), stop=(kc == 15)), reads=[ytR, woR],
                              writes=[bankR[bi]] if kc == 0 else (), dwrites=() if kc == 0 else [bankR[bi]])
                    e = kb.ew.next()
                    if e is act:
                        act.op(lambda: A_.copy(out=yv[:, cb * 512:(cb + 1) * 512], in_=banks[bi][:, :]),
                               reads=[bankR[bi]], writes=[yvR] if cb == 0 else (), dwrites=() if cb == 0 else [yvR])
                    else:
                        dve.op(lambda: V.tensor_copy(out=yv[:, cb * 512:(cb + 1) * 512], in_=banks[bi][:, :]),
                               reads=[bankR[bi]], writes=[yvR] if cb == 0 else (), dwrites=() if cb == 0 else [yvR])
                rms_rows((junk, small[:, 0:1], small[:, 1:2]), yv, yvR, small[:, 2:3], smallR, D)
                dve.op(lambda: V.scalar_tensor_tensor(out=yv, in0=yv, scalar=small[:, 2:3], in1=npost_b, op0=ALU.mult,
                                                      op1=ALU.mult), reads=[smallR, RL], writes=[yvR])
                pool.op(lambda: G.tensor_tensor(out=yv, in0=yv, in1=xt, op=ALU.add), reads=[xtR], writes=[yvR])
                sp.dma(xdst[offs[si] + t0: offs[si] + t0 + 128, :], yv, reads=[yvR], dwrites=[xdstR])
            kb.barrier()

        XinR = Res("xin")
        X1R = Res("x1")
        YoR = Res("yout")
        for l in range(depth):
            load_layer_params(l)
            if depth == 1:
                xsrc, xsrcR, xdst, xdstR = x_in, XinR, y_out, YoR
            elif l == 0:
                xsrc, xsrcR, xdst, xdstR = x_in, XinR, X1, X1R
            else:
                xsrc, xsrcR, xdst, xdstR = X1, X1R, y_out, YoR
            for si in range(nseq):
                ph = phases.split(",")
                if "a" in ph:
                    phase_a(l, si, xsrc, xsrcR)
                if "g1" in ph:
                    phase_g1(l, si)
                if "g2" in ph:
                    phase_g2(l, si)
                if "g3" in ph:
                    phase_g3(l, si)
                if "b" in ph:
                    phase_b(l, si)
                if "m" in ph:
                    phase_m(l, si)
                if "c" in ph:
                    phase_c(l, si)
                if "o" in ph:
                    phase_o(l, si, xsrc, xsrcR, xdst, xdstR)
        kb.barrier()
    return nc


_CACHE = {}


def _run(seqs, depth, core_inputs, debug=False):
    key = (tuple(seqs), depth, debug)
    if key not in _CACHE:
        _CACHE[key] = build(list(seqs), depth, debug)
    nc = _CACHE[key]
    res = run_bass_kernel_spmd(nc, core_inputs, core_ids=list(range(len(core_inputs))))
    return res


def kernel(x_prompt, x_sample, mem_prompt, mem_sample, norm_pre, norm_post, norm_mem, w_in, gdn_conv,
           gdn_A_log, gdn_dt_bias, gdn_norm, diff_lambda, diff_norm, conv_w, w_mem_kv, w_out):
    n = 8
    f = lambda a: np.ascontiguousarray(np.asarray(a, dtype=np.float32))
    x_prompt, x_sample, mem_prompt, mem_sample = f(x_prompt), f(x_sample), f(mem_prompt), f(mem_sample)
    B, S, _ = x_prompt.shape
    DB, DS, _ = x_sample.shape
    pb = B // n
    db = DB // n
    seqs = [S] * pb + [DS] * db
    depth = np.asarray(norm_pre).shape[0]
    consts = _consts(max(seqs))
    shared = dict(norm_pre=f(norm_pre), norm_post=f(norm_post), norm_mem=f(norm_mem), w_in=f(w_in),
                  gdn_conv=f(gdn_conv), gdn_A_log=f(gdn_A_log), gdn_dt_bias=f(gdn_dt_bias), gdn_norm=f(gdn_norm),
                  diff_lambda=f(diff_lambda), diff_norm=f(diff_norm), conv_w=f(conv_w), w_mem_kv=f(w_mem_kv),
                  w_out=f(w_out), **consts)
    in_maps = []
    for c in range(n):
        xs = [x_prompt[c * pb + i] for i in range(pb)] + [x_sample[c * db + i] for i in range(db)]
        ms = [mem_prompt[c * pb + i] for i in range(pb)] + [mem_sample[c * db + i] for i in range(db)]
        m = dict(shared)
        m["x"] = np.ascontiguousarray(np.concatenate(xs, axis=0))
        m["mem"] = np.ascontiguousarray(np.concatenate(ms, axis=0))
        in_maps.append(m)
    res = _run(seqs, depth, in_maps)
    y_prompt = np.empty((B, S, D), np.float32)
    y_sample = np.empty((DB, DS, D), np.float32)
    for c in range(n):
        y = res.results[c]["y"]
        o = 0
        for i in range(pb):
            y_prompt[c * pb + i] = y[o:o + S]
            o += S
        for i in range(db):
            y_sample[c * db + i] = y[o:o + DS]
            o += DS
    return (y_prompt, y_sample)
```

```python
import math
from contextlib import ExitStack
import numpy as np
import ml_dtypes
import concourse.bass as bass
import concourse.mybir as mybir
from concourse.bass_utils import run_bass_kernel_spmd

F32 = mybir.dt.float32
BF16 = mybir.dt.bfloat16
F32R = mybir.dt.float32r
AF = mybir.ActivationFunctionType
ALU = mybir.AluOpType

D = 2048
W_BR = 512
IN_COLS = 7184
N_MEM = 256
EPS = 1e-6
NEG = 32768.0
C_AQKV, C_ADEC, C_AZ = 0, 1536, 1552
C_BQ, C_BK, C_BV, C_BZ = 2064, 2576, 3088, 3600
C_CB, C_CC, C_CX, C_CZ = 4112, 4624, 5136, 5648
C_MQ, C_MZ = 6160, 6672


class Res:
    __slots__ = ("w", "r", "wx", "name")

    def __init__(self, name=""):
        self.w = {}
        self.r = {}
        self.wx = {}
        self.name = name


class Slot:
    __slots__ = ("si", "val")


class Eng:
    RING = 12
    CAP = 30000

    def __init__(self, kb, name, obj):
        self.kb = kb
        self.name = name
        self.o = obj
        self.si = kb.newsem()
        self.cnt = 0
        self.known = {}
        self.ring = []
        self.ri = 0

    def _deps(self, reads, writes, dwrites):
        deps = {}

        def add(d):
            for k, v in d.items():
                if deps.get(k, 0) < v:
                    deps[k] = v
        for r in reads:
            add(r.w)
        for w in writes:
            add(w.w)
            add(w.r)
        for w in dwrites:
            add(w.r)
            add(w.wx)
        return deps

    def _wait(self, deps):
        for k, v in deps.items():
            if self.name == "pe" and k == self.si:
                continue
            if self.known.get(k, 0) < v:
                self.o.wait_ge(self.kb.sems[k], v)
                self.known[k] = v

    def _post(self, ev, reads, writes, dwrites):
        k, v = ev
        for r in reads:
            r.r[k] = v
        for w in writes:
            w.w = {k: v}
            w.wx = {k: v}
            w.r = {}
        for w in dwrites:
            w.w[k] = v

    def op(self, fn, reads=(), writes=(), dwrites=()):
        self._wait(self._deps(reads, writes, dwrites))
        ins = fn()
        self.cnt += 1
        ins.then_inc(self.kb.sems[self.si], 1)
        self._post((self.si, self.cnt), reads, writes, dwrites)
        if self.cnt >= self.CAP:
            self.si = self.kb.newsem()
            self.cnt = 0
        return ins

    def dma(self, out, in_, reads=(), writes=(), dwrites=()):
        self._wait(self._deps(reads, writes, dwrites))
        if len(self.ring) < self.RING:
            s = Slot()
            s.si = self.kb.newsem()
            s.val = 0
            self.ring.append(s)
        s = self.ring[self.ri % self.RING]
        self.ri += 1
        if s.val >= self.CAP:
            self._wait({s.si: s.val})
            s.si = self.kb.newsem()
            s.val = 0
        if s.val > 0:
            self._wait({s.si: s.val})
        ins = self.o.dma_start(out=out, in_=in_)
        s.val += 16
        ins.then_inc(self.kb.sems[s.si], 16)
        self._post((s.si, s.val), reads, writes, dwrites)
        return ins

    def last_events(self):
        ev = {}
        if self.cnt > 0:
            ev[self.si] = self.cnt
        for s in self.ring:
            if s.val > 0:
                ev[s.si] = s.val
        return ev


class Rot:
    def __init__(self, items):
        self.items = items
        self.i = 0

    def next(self):
        it = self.items[self.i % len(self.items)]
        self.i += 1
        return it


class KB:
    def __init__(self, nc, es):
        self.nc = nc
        self.es = es
        self.sems = []
        self.pe = Eng(self, "pe", nc.tensor)
        self.act = Eng(self, "act", nc.scalar)
        self.dve = Eng(self, "dve", nc.vector)
        self.pool = Eng(self, "pool", nc.gpsimd)
        self.sp = Eng(self, "sp", nc.sync)
        self.engs = [self.pe, self.act, self.dve, self.pool, self.sp]
        self.ew = Rot([self.act, self.dve])

    def newsem(self):
        h = self.es.enter_context(self.nc.semaphore(f"s{len(self.sems)}"))
        self.sems.append(h)
        return len(self.sems) - 1

    def barrier(self):
        ev = {}
        for e in self.engs:
            ev.update(e.last_events())
        for e in self.engs:
            e._wait(dict(ev))


def _consts(smax):
    c = {}
    eye = np.eye(128, dtype=np.float32)
    c["ident_bf"] = eye.astype(ml_dtypes.bfloat16)
    c["ones_bf"] = np.ones((128, 128), np.float32).astype(ml_dtypes.bfloat16)
    idx = np.arange(128)
    same = (idx[:, None] // 64) == (idx[None, :] // 64)
    lf = (same & (idx[:, None] <= idx[None, :])).astype(np.float32)
    lb = (same & (idx[:, None] >= idx[None, :])).astype(np.float32)
    bo = same.astype(np.float32)
    mf = np.where(same & (idx[:, None] >= idx[None, :]), 0.0, NEG).astype(np.float32)
    mb = np.where(same & (idx[:, None] <= idx[None, :]), 0.0, NEG).astype(np.float32)
    nodiag = (1.0 - eye).astype(np.float32)
    f32c = np.concatenate([eye, lf, lb, bo, np.tile(mf, (1, 4)), np.tile(mb, (1, 4)), np.tile(nodiag, (1, 4)),
                           np.tile(eye, (1, 4))], axis=1)
    c["f32c"] = np.ascontiguousarray(f32c)
    pt = np.zeros((128, 128), np.float32)
    for j in range(2):
        b = 64 * j
        for i in range(8):
            pt[b + 8 + i, b + i] = -1.0
            pt[b + i, b + 8 + i] = 1.0
    c["pt_bf"] = pt.astype(ml_dtypes.bfloat16)
    inv = (np.float32(500000.0) ** (-np.arange(0, 16, 2, dtype=np.float32) / np.float32(16))).astype(np.float32)
    ang = (np.arange(smax, dtype=np.float32)[:, None] * inv[None, :]).astype(np.float32)
    cos = np.cos(ang.astype(np.float64)).astype(np.float32).T
    sin = np.sin(ang.astype(np.float64)).astype(np.float32).T
    cf = np.ones((128, smax), np.float32)
    sf = np.zeros((128, smax), np.float32)
    for j in range(2):
        b = 64 * j
        cf[b:b + 8] = cos
        cf[b + 8:b + 16] = cos
        sf[b:b + 8] = sin
        sf[b + 8:b + 16] = sin
    c["rope_c"] = cf
    c["rope_s"] = sf
    return c


def build(seqs, depth, debug=False, phases="a,g1,g2,g3,b,m,c,o"):
    nc = bass.Bass("TRN2", target_bir_lowering=False)
    ntok = sum(seqs)
    nseq = len(seqs)
    smax = max(seqs)
    offs = [sum(seqs[:i]) for i in range(nseq)]

    def din(name, shape, dt=F32):
        return nc.dram_tensor(name, list(shape), dt, kind="ExternalInput").ap()

    def dscr(name, shape, dt):
        kind = "ExternalOutput" if debug else "Internal"
        return nc.dram_tensor(name, list(shape), dt, kind=kind).ap()

    x_in = din("x", [ntok, D])
    mem_in = din("mem", [nseq * N_MEM, D])
    norm_pre = din("norm_pre", [depth, D])
    norm_post = din("norm_post", [depth, D])
    norm_mem = din("norm_mem", [depth, D])
    w_in = din("w_in", [depth, D, IN_COLS])
    gdn_conv = din("gdn_conv", [depth, 5, 1536])
    gdn_A_log = din("gdn_A_log", [depth, 2, 4])
    gdn_dt_bias = din("gdn_dt_bias", [depth, 2, 4])
    gdn_norm = din("gdn_norm", [depth, 128])
    diff_lambda = din("diff_lambda", [depth, 4, 64])
    diff_norm = din("diff_norm", [depth, 128])
    conv_w = din("conv_w", [depth, 3, 512])
    w_mem_kv = din("w_mem_kv", [depth, D, 1024])
    w_out = din("w_out", [depth, D, D])
    c_ident_bf = din("ident_bf", [128, 128], BF16)
    c_ones_bf = din("ones_bf", [128, 128], BF16)
    c_pt_bf = din("pt_bf", [128, 128], BF16)
    c_f32c = din("f32c", [128, 4 * 128 + 4 * 512])
    c_rope_c = din("rope_c", [128, smax])
    c_rope_s = din("rope_s", [128, smax])
    y_out = nc.dram_tensor("y", [ntok, D], F32, kind="ExternalOutput").ap()

    WIN = dscr("WIN", [depth, 56, 128, 16, 128], BF16)
    WDB = dscr("WDB", [depth, 128, 16, 16], BF16)
    WOUT = dscr("WOUT", [depth, 128, 16, D], BF16)
    WKV = dscr("WKV", [depth, 128, 16, 1024], BF16)
    X1 = dscr("X1", [ntok, D], F32)
    QKVT = dscr("QKVT", [1536, smax], BF16)
    DBt = dscr("DBt", [smax, 16], F32)
    AZ = dscr("AZ", [smax, 512], BF16)
    BQT = dscr("BQT", [512, smax], BF16)
    BKT = dscr("BKT", [512, smax], BF16)
    BV = dscr("BV", [smax, 512], BF16)
    BZT = dscr("BZT", [512, smax], BF16)
    CUT = dscr("CUT", [512, smax], BF16)
    CGT = dscr("CGT", [512, smax], BF16)
    MQT = dscr("MQT", [512, smax], BF16)
    MZT = dscr("MZT", [512, smax], BF16)
    GQT = dscr("GQT", [512, smax], BF16)
    GKT = dscr("GKT", [512, smax], BF16)
    GK = dscr("GK", [smax, 512], BF16)
    GV = dscr("GV", [smax, 512], BF16)
    OD = [dscr("OF", [smax, 512], F32), dscr("OB", [smax, 512], F32)]
    YT = dscr("YT", [D, smax], BF16)

    es = ExitStack()
    with es:
        kb = KB(nc, es)
        pe, act, dve, pool, sp = kb.pe, kb.act, kb.dve, kb.pool, kb.sp
        ARENA_W = 38200
        arena = es.enter_context(nc.sbuf_tensor("arena", [128, ARENA_W], F32))
        rbuf = es.enter_context(nc.sbuf_tensor("rbuf", [128, 10 * 512 + 128 + 1424], F32R))
        cst = es.enter_context(nc.sbuf_tensor("cst", [128, 7800], F32))
        banks = [es.enter_context(nc.psum_tensor(f"bank{i}", [128, 512], F32)) for i in range(8)]
        bankR = [Res(f"bank{i}") for i in range(8)]

        coff = [0]

        def calloc(words):
            o = coff[0]
            coff[0] += words
            assert coff[0] <= 7800
            return cst[:, o:o + words]

        RC = Res("consts")
        f32c = calloc(4 * 128 + 4 * 512)
        ident_f = f32c[:, 0:128]
        Lf = f32c[:, 128:256]
        Lb = f32c[:, 256:384]
        mask4 = [f32c[:, 512:1024], f32c[:, 1024:1536]]
        nodiag4 = f32c[:, 1536:2048]
        ident4 = f32c[:, 2048:2560]
        Lmat = [Lf, Lb]
        ident_bf = calloc(64).bitcast(BF16)
        ones_bf = calloc(64).bitcast(BF16)
        pt_bf = calloc(64).bitcast(BF16)
        eps_t = calloc(1)
        eps128_t = calloc(1)
        one_t = calloc(1)
        sp.dma(f32c, c_f32c, writes=[RC])
        sp.dma(ident_bf, c_ident_bf, dwrites=[RC])
        sp.dma(ones_bf, c_ones_bf, dwrites=[RC])
        sp.dma(pt_bf, c_pt_bf, dwrites=[RC])
        pool.op(lambda: nc.gpsimd.memset(eps_t, EPS), dwrites=[RC])
        pool.op(lambda: nc.gpsimd.memset(eps128_t, EPS * 128.0), dwrites=[RC])
        pool.op(lambda: nc.gpsimd.memset(one_t, 1.0), dwrites=[RC])
        ind_t = calloc(2)
        RI = Res("ind")
        pool.op(lambda: nc.gpsimd.memset(ind_t, 0.0), writes=[RI])
        pool.op(lambda: nc.gpsimd.memset(ind_t[0:64, 0:1], 1.0), writes=[RI])
        pool.op(lambda: nc.gpsimd.memset(ind_t[64:128, 1:2], 1.0), writes=[RI])
        ident_r = rbuf[:, 10 * 512:10 * 512 + 128]
        dve.op(lambda: nc.vector.tensor_copy(out=ident_r, in_=ident_f), reads=[RC], dwrites=[RC])
        rb0 = 10 * 512 + 128
        Lr = [rbuf[:, rb0:rb0 + 128], rbuf[:, rb0 + 128:rb0 + 256]]
        Bor = rbuf[:, rb0 + 256:rb0 + 384]
        maskr = [rbuf[:, rb0 + 384:rb0 + 896], rbuf[:, rb0 + 896:rb0 + 1408]]
        g_r = [rbuf[:, rb0 + 1408:rb0 + 1412], rbuf[:, rb0 + 1412:rb0 + 1416]]
        dve.op(lambda: nc.vector.tensor_copy(out=rbuf[:, rb0:rb0 + 384], in_=f32c[:, 128:512]), reads=[RC], dwrites=[RC])
        dve.op(lambda: nc.vector.tensor_copy(out=rbuf[:, rb0 + 384:rb0 + 1408], in_=f32c[:, 512:1536]), reads=[RC],
               dwrites=[RC])
        RL = Res("layerparams")
        npre_b = calloc(D)
        npost_b = calloc(D)
        gconv = calloc(60)
        cconv = calloc(12)
        gnorm_b = calloc(128)
        alog_b = calloc(8)
        dtb_b = calloc(8)
        negA_b = calloc(8)
        dnorm_c = calloc(1)
        dnorm_s = calloc(1)
        lam_b = calloc(256)
        lam_t = calloc(256)
        lam_s = calloc(2)
        lam_e = calloc(2)
        neglam = calloc(1)

        RW = Res("weights")
        for l in range(depth):
            src = w_in[l][:, 0:1536].rearrange("(kc p) (ch c) -> ch p kc c", p=128, c=128)
            chunk_cols = [c0 for c0 in range(0, 1536, 128)] + [c0 for c0 in range(C_AZ, IN_COLS, 128)]
            assert len(chunk_cols) == 56
            for ci, c0 in enumerate(chunk_cols):
                srcc = w_in[l][:, c0:c0 + 128].rearrange("(kc p) c -> p kc c", p=128)
                pool.dma(WIN[l, ci], srcc, dwrites=[RW])
            pool.dma(WDB[l], w_in[l][:, C_ADEC:C_ADEC + 16].rearrange("(kc p) c -> p kc c", p=128), dwrites=[RW])
            for q4 in range(4):
                pool.dma(WOUT[l][:, :, q4 * 512:(q4 + 1) * 512],
                         w_out[l][:, q4 * 512:(q4 + 1) * 512].rearrange("(kc p) c -> p kc c", p=128), dwrites=[RW])
            for q4 in range(2):
                pool.dma(WKV[l][:, :, q4 * 512:(q4 + 1) * 512],
                         w_mem_kv[l][:, q4 * 512:(q4 + 1) * 512].rearrange("(kc p) c -> p kc c", p=128), dwrites=[RW])

        def chunk_index(col):
            if col < 1536:
                return col // 128
            return 12 + (col - C_AZ) // 128

        class Carve:
            def __init__(self):
                self.o = 0

            def f32(self, words, shape=None):
                ap = arena[:, self.o:self.o + words]
                self.o += words
                assert self.o <= ARENA_W, self.o
                if shape is not None:
                    ap = ap.rearrange("p (a b) -> p a b", a=shape[0]) if len(shape) == 2 else ap
                return ap

            def bf(self, elems, shape=None):
                words = (elems + 1) // 2
                ap = arena[:, self.o:self.o + words].bitcast(BF16)
                self.o += words
                assert self.o <= ARENA_W, self.o
                if shape is not None and len(shape) == 2:
                    ap = ap.rearrange("p (a b) -> p a b", a=shape[0])
                return ap

            def r32(self, words, shape=None):
                ap = arena[:, self.o:self.o + words].bitcast(F32R)
                self.o += words
                assert self.o <= ARENA_W, self.o
                if shape is not None and len(shape) == 2:
                    ap = ap.rearrange("p (a b) -> p a b", a=shape[0])
                return ap

        V = nc.vector
        G = nc.gpsimd
        A_ = nc.scalar
        T = nc.tensor

        def bank_bf(i):
            return banks[i][:, :].bitcast(BF16)

        def load_layer_params(l):
            kb.barrier()
            sp.dma(npre_b, norm_pre[l].partition_broadcast(128), writes=[RL])
            sp.dma(npost_b, norm_post[l].partition_broadcast(128), dwrites=[RL])
            sp.dma(gnorm_b, gdn_norm[l].partition_broadcast(128), dwrites=[RL])
            sp.dma(alog_b, gdn_A_log[l].rearrange("a h -> (a h)").partition_broadcast(128), dwrites=[RL])
            sp.dma(dtb_b, gdn_dt_bias[l].rearrange("a h -> (a h)").partition_broadcast(128), dwrites=[RL])
            sp.dma(lam_b, diff_lambda[l].rearrange("a d -> (a d)").partition_broadcast(128), dwrites=[RL])
            with nc.allow_non_contiguous_dma(reason="tiny parameter transposes"):
                for wi in range(5):
                    sp.dma(gconv.rearrange("p (c w) -> p c w", w=5)[:, :, wi],
                           gdn_conv[l, wi].rearrange("(c p) -> p c", p=128), dwrites=[RL])
                for wi in range(3):
                    sp.dma(cconv.rearrange("p (c w) -> p c w", w=3)[:, :, wi],
                           conv_w[l, wi].rearrange("(c p) -> p c", p=128), dwrites=[RL])
                sp.dma(dnorm_c, diff_norm[l].rearrange("(p o) -> p o", o=1), dwrites=[RL])
            lam_init = 0.8 - 0.6 * math.exp(-0.3 * l)
            act.op(lambda: A_.activation(out=negA_b, in_=alog_b, func=AF.Exp), reads=[RL], dwrites=[RL])
            dve.op(lambda: V.tensor_scalar(out=negA_b, in0=negA_b, scalar1=-1.0, scalar2=None, op0=ALU.mult),
                   reads=[RL], dwrites=[RL])
            dve.op(lambda: V.tensor_scalar(out=dnorm_s, in0=dnorm_c, scalar1=1.0 - lam_init, scalar2=None,
                                           op0=ALU.mult), reads=[RL], dwrites=[RL])
            lb3 = lam_b.rearrange("p (a d) -> p a d", d=64)
            lt3 = lam_t.rearrange("p (a d) -> p a d", d=64)
            dve.op(lambda: V.tensor_tensor(out=lt3[:, 0, :], in0=lb3[:, 0, :], in1=lb3[:, 1, :], op=ALU.mult),
                   reads=[RL], dwrites=[RL])
            dve.op(lambda: V.tensor_tensor(out=lt3[:, 1, :], in0=lb3[:, 2, :], in1=lb3[:, 3, :], op=ALU.mult),
                   reads=[RL], dwrites=[RL])
            dve.op(lambda: V.reduce_sum(out=lam_s[:, 0:1], in_=lt3[:, 0, :], axis=mybir.AxisListType.X),
                   reads=[RL], dwrites=[RL])
            dve.op(lambda: V.reduce_sum(out=lam_s[:, 1:2], in_=lt3[:, 1, :], axis=mybir.AxisListType.X),
                   reads=[RL], dwrites=[RL])
            act.op(lambda: A_.activation(out=lam_e, in_=lam_s, func=AF.Exp), reads=[RL], dwrites=[RL])
            dve.op(lambda: V.tensor_tensor(out=neglam, in0=lam_e[:, 1:2], in1=lam_e[:, 0:1], op=ALU.subtract),
                   reads=[RL], dwrites=[RL])
            dve.op(lambda: V.tensor_scalar(out=neglam, in0=neglam, scalar1=-lam_init, scalar2=None, op0=ALU.add),
                   reads=[RL], dwrites=[RL])
            kb.barrier()

        def rms_rows(cv, xt, xtR, rstd, tmpR, width):
            junk, ss, rms = cv
            act.op(lambda: A_.activation(out=junk, in_=xt, func=AF.Square, accum_out=ss),
                   reads=[xtR], writes=[tmpR])
            act.op(lambda: A_.activation(out=rms, in_=ss, func=AF.Sqrt, scale=1.0 / width, bias=eps_t),
                   reads=[tmpR, RC], dwrites=[tmpR])
            dve.op(lambda: V.reciprocal(out=rstd, in_=rms), reads=[tmpR], dwrites=[tmpR])

        def phase_a(l, si, xsrc, xsrcR):
            S = seqs[si]
            TB = min(S, 1024)
            NT = min(512, TB)
            kb.barrier()
            cv = Carve()
            xts = [(cv.f32(D), Res()) for _ in range(2)]
            junk = cv.bf(D)
            hbs = [(cv.bf(D), Res()) for _ in range(2)]
            hT = cv.bf(16 * TB, (16, TB))
            hTR = Res("hT")
            ring = Rot([(cv.bf(16 * 128, (16, 128)), Res()) for _ in range(8)])
            wides = [(cv.bf(16 * 512, (16, 512)), Res("wide")) for _ in range(2)]
            wdb = cv.bf(16 * 16, (16, 16))
            wdbR = Res("wdb")
            stg = Rot([(cv.f32(512), Res()) for _ in range(8)])
            ropeC = [(cv.f32(512), Res()) for _ in range(2)]
            ropeS = [(cv.f32(512), Res()) for _ in range(2)]
            small = cv.f32(8)
            smallR = Res()
            pb = Rot([2, 3, 4, 5, 6, 7])

            order = [c * 128 for c in range(12)]
            order += [col + h * 128 for col in (C_BQ, C_BK) for h in range(4)]
            order += [col + c * 128 for col in (C_BZ, C_MQ, C_MZ) for c in range(4)]
            order += [col + c * 128 for c in range(4) for col in (C_CB, C_CC, C_CX, C_CZ)]
            LOOK = 4
            pq = {"next": 0, "ready": []}

            def _issue():
                col = order[pq["next"] % len(order)]
                pq["next"] += 1
                w, wr = ring.next()
                sp.dma(w, WIN[l, chunk_index(col)], reads=[RW], writes=[wr])
                pq["ready"].append((col, w, wr))

            def load_chunk(col):
                while len(pq["ready"]) < 1:
                    _issue()
                c0, w, wr = pq["ready"].pop(0)
                assert c0 == col, (c0, col)
                while len(pq["ready"]) < LOOK and pq["next"] < pq["limit"]:
                    _issue()
                return w, wr
            pq["limit"] = len(order) * (S // TB)

            def fm_mm(w, wr, n, bi=None):
                bi = pb.next() if bi is None else bi
                for kc in range(16):
                    pe.op(lambda kc=kc: T.matmul(banks[bi][:, 0:NT], w[:, kc, :], hT[:, kc, n * NT:(n + 1) * NT],
                                                 start=(kc == 0), stop=(kc == 15)),
                          reads=[wr, hTR], writes=[bankR[bi]] if kc == 0 else (), dwrites=() if kc == 0 else [bankR[bi]])
                return bi

            for b0 in range(0, S, TB):
                tok0 = offs[si] + b0
                for i in range(TB // 128):
                    xt, xtR = xts[i % 2]
                    hb, hbR = hbs[i % 2]
                    sp.dma(xt, xsrc[tok0 + i * 128: tok0 + (i + 1) * 128, :], reads=[xsrcR], writes=[xtR])
                    rms_rows((junk, small[:, 0:1], small[:, 1:2]), xt, xtR, small[:, 2:3], smallR, D)
                    dve.op(lambda: V.scalar_tensor_tensor(out=hb, in0=xt, scalar=small[:, 2:3], in1=npre_b,
                                                          op0=ALU.mult, op1=ALU.mult),
                           reads=[xtR, smallR, RL], writes=[hbR])
                    for half in range(2):
                        tpb = bank_bf(half)
                        for kk in range(8):
                            kc = half * 8 + kk
                            pe.op(lambda kc=kc, kk=kk, tpb=tpb: T.transpose(out=tpb[:, kk * 128:(kk + 1) * 128],
                                                                            in_=hb[:, kc * 128:(kc + 1) * 128],
                                                                            identity=ident_bf),
                                  reads=[hbR, RC], writes=[bankR[half]] if kk == 0 else (),
                                  dwrites=() if kk == 0 else [bankR[half]])
                        e = kb.ew.next()
                        dst = hT[:, half * 8:(half + 1) * 8, i * 128:(i + 1) * 128]
                        srcv = tpb.rearrange("p (k t) -> p k t", k=8)
                        if e is act:
                            act.op(lambda: A_.copy(out=dst, in_=srcv), reads=[bankR[half]], dwrites=[hTR])
                        else:
                            dve.op(lambda: V.tensor_copy(out=dst, in_=srcv), reads=[bankR[half]], dwrites=[hTR])

                def store_fm(dst, row0, n, sv, svR):
                    sp.dma(dst[row0:row0 + 128, b0 + n * NT: b0 + (n + 1) * NT], sv, reads=[svR])

                def evac_copy_bf(bi, func=None):
                    sv, svR = stg.next()
                    svb = sv.bitcast(BF16)[:, 0:NT]
                    if func is not None:
                        act.op(lambda: A_.activation(out=svb, in_=banks[bi][:, 0:NT], func=func),
                               reads=[bankR[bi]], writes=[svR])
                    else:
                        e = kb.ew.next()
                        if e is act:
                            act.op(lambda: A_.copy(out=svb, in_=banks[bi][:, 0:NT]), reads=[bankR[bi]], writes=[svR])
                        else:
                            dve.op(lambda: V.tensor_copy(out=svb, in_=banks[bi][:, 0:NT]), reads=[bankR[bi]],
                                   writes=[svR])
                    return svb, svR

                nsub = TB // NT
                for c in range(12):
                    w, wr = load_chunk(c * 128)
                    for n in range(nsub):
                        bi = fm_mm(w, wr, n)
                        svb, svR = evac_copy_bf(bi)
                        store_fm(QKVT, c * 128, n, svb, svR)
                sp.dma(wdb, WDB[l], reads=[RW], writes=[wdbR])
                for wi_, col_ in enumerate((C_AZ, C_BV)):
                    wide_, wideR_ = wides[wi_]
                    for q4 in range(4):
                        sp.dma(wide_[:, :, q4 * 128:(q4 + 1) * 128], WIN[l, chunk_index(col_ + q4 * 128)], reads=[RW],
                               writes=[wideR_] if q4 == 0 else (), dwrites=() if q4 == 0 else [wideR_])
                for _ in range(LOOK):
                    if len(pq["ready"]) < LOOK and pq["next"] < pq["limit"]:
                        _issue()
                for i in range(TB // 128):
                    bi = pb.next()
                    for kc in range(16):
                        pe.op(lambda kc=kc: T.matmul(banks[bi][:, 0:16], hT[:, kc, i * 128:(i + 1) * 128],
                                                     wdb[:, kc, :], start=(kc == 0), stop=(kc == 15)),
                              reads=[wdbR, hTR], writes=[bankR[bi]] if kc == 0 else (),
                              dwrites=() if kc == 0 else [bankR[bi]])
                    sv, svR = stg.next()
                    dve.op(lambda: V.tensor_copy(out=sv[:, 0:16], in_=banks[bi][:, 0:16]), reads=[bankR[bi]],
                           writes=[svR])
                    sp.dma(DBt[b0 + i * 128: b0 + (i + 1) * 128, :], sv[:, 0:16], reads=[svR])

                def wide_tm(wsel, dst, func):
                    wide, wideR = wides[wsel]
                    for i in range(TB // 128):
                        bi = pb.next()
                        for kc in range(16):
                            pe.op(lambda kc=kc: T.matmul(banks[bi][:, :], hT[:, kc, i * 128:(i + 1) * 128],
                                                         wide[:, kc, :], start=(kc == 0), stop=(kc == 15)),
                                  reads=[wideR, hTR], writes=[bankR[bi]] if kc == 0 else (),
                                  dwrites=() if kc == 0 else [bankR[bi]])
                        sv, svR = stg.next()
                        svb = sv.bitcast(BF16)[:, 0:512]
                        if func is not None:
                            act.op(lambda: A_.activation(out=svb, in_=banks[bi][:, :], func=func), reads=[bankR[bi]],
                                   writes=[svR])
                        else:
                            dve.op(lambda: V.tensor_copy(out=svb, in_=banks[bi][:, :]), reads=[bankR[bi]], writes=[svR])
                        sp.dma(dst[b0 + i * 128: b0 + (i + 1) * 128, :], svb, reads=[svR])

                wide_tm(0, AZ, AF.Silu)
                wide_tm(1, BV, None)
                assert nsub <= 2
                for n in range(nsub):
                    p0 = b0 + n * NT
                    sp.dma(ropeC[n][0][:, 0:NT], c_rope_c[:, p0:p0 + NT], writes=[ropeC[n][1]])
                    sp.dma(ropeS[n][0][:, 0:NT], c_rope_s[:, p0:p0 + NT], writes=[ropeS[n][1]])
                for qk, (col, dst) in enumerate(((C_BQ, BQT), (C_BK, BKT))):
                    for h in range(4):
                        w, wr = load_chunk(col + h * 128)
                        for n in range(nsub):
                            rc, rcR = ropeC[n]
                            rs, rsR = ropeS[n]
                            bi = fm_mm(w, wr, n)
                            sv, svR = stg.next()
                            qsb = sv.bitcast(BF16)[:, 0:NT]
                            act.op(lambda: A_.copy(out=qsb, in_=banks[bi][:, 0:NT]), reads=[bankR[bi]], writes=[svR])
                            b2 = pb.next()
                            pe.op(lambda: T.matmul(banks[b2][:, 0:NT], pt_bf, qsb, start=True, stop=True),
                                  reads=[svR, RC], writes=[bankR[b2]])
                            t1, t1R = stg.next()
                            dve.op(lambda: V.tensor_tensor(out=t1[:, 0:NT], in0=banks[b2][:, 0:NT], in1=rs[:, 0:NT],
                                                           op=ALU.mult), reads=[bankR[b2], rsR], writes=[t1R])
                            t2, t2R = stg.next()
                            pool.op(lambda: G.tensor_tensor(out=t2[:, 0:NT], in0=qsb, in1=rc[:, 0:NT], op=ALU.mult),
                                    reads=[svR, rcR], writes=[t2R])
                            o, oR = stg.next()
                            ob = o.bitcast(BF16)[:, 0:NT]
                            dve.op(lambda: V.tensor_tensor(out=ob, in0=t1[:, 0:NT], in1=t2[:, 0:NT], op=ALU.add),
                                   reads=[t1R, t2R], writes=[oR])
                            store_fm(dst, h * 128, n, ob, oR)
                for col, dst, func in ((C_BZ, BZT, AF.Silu), (C_MQ, MQT, None), (C_MZ, MZT, AF.Silu)):
                    for c in range(4):
                        w, wr = load_chunk(col + c * 128)
                        for n in range(nsub):
                            bi = fm_mm(w, wr, n)
                            svb, svR = evac_copy_bf(bi, func)
                            store_fm(dst, c * 128, n, svb, svR)
                for c in range(4):
                    ws = [load_chunk(col + c * 128) for col in (C_CB, C_CC, C_CX, C_CZ)]
                    for n in range(nsub):
                        bis = [fm_mm(w, wr, n) for (w, wr) in ws]
                        ccs, ccR = stg.next()
                        act.op(lambda: A_.copy(out=ccs[:, 0:NT], in_=banks[bis[1]][:, 0:NT]), reads=[bankR[bis[1]]],
                               writes=[ccR])
                        u, uR = stg.next()
                        ub = u.bitcast(BF16)[:, 0:NT]
                        dve.op(lambda: V.tensor_tensor(out=ub, in0=banks[bis[2]][:, 0:NT], in1=ccs[:, 0:NT],
                                                       op=ALU.mult), reads=[bankR[bis[2]], ccR], writes=[uR])
                        store_fm(CUT, c * 128, n, ub, uR)
                        sz, szR = stg.next()
                        act.op(lambda: A_.activation(out=sz[:, 0:NT], in_=banks[bis[3]][:, 0:NT], func=AF.Silu),
                               reads=[bankR[bis[3]]], writes=[szR])
                        g, gR = stg.next()
                        gb = g.bitcast(BF16)[:, 0:NT]
                        dve.op(lambda: V.tensor_tensor(out=gb, in0=banks[bis[0]][:, 0:NT], in1=sz[:, 0:NT],
                                                       op=ALU.mult), reads=[bankR[bis[0]], szR], writes=[gR])
                        store_fm(CGT, c * 128, n, gb, gR)
            kb.barrier()

        def phase_g1(l, si):
            S = seqs[si]
            TG = min(S, 512)
            kb.barrier()
            cv = Carve()
            xin = Rot([(cv.bf(TG + 4), Res()) for _ in range(2)])
            acc = Rot([(cv.f32(TG), Res()) for _ in range(2)])
            ys = Rot([(cv.f32(TG), Res()) for _ in range(2)])
            sq = Rot([(cv.bf(TG), Res()) for _ in range(2)])
            rn = Rot([(cv.f32(TG), Res()) for _ in range(2)])
            yn = Rot([(cv.bf(TG), Res()) for _ in range(3)])
            tk = Rot([(cv.bf(TG), Res()) for _ in range(2)])
            pb = Rot([0, 1, 2, 3])
            pt = Rot([4, 5, 6, 7])
            nt = TG // 128
            items = [(b0, c) for b0 in range(0, S, TG) for c in range(12)]

            def pre(b0, c):
                xi, xiR = xin.next()
                lo = max(0, b0 - 2)
                hi = min(S, b0 + TG + 2)
                pool.op(lambda: G.memset(xi, 0.0), writes=[xiR])
                sp.dma(xi[:, lo - (b0 - 2): hi - (b0 - 2)], QKVT[c * 128:(c + 1) * 128, lo:hi], dwrites=[xiR])
                return xi, xiR
            def chunk(b0, c, xi, xiR):
                a, aR = acc.next()
                dve.op(lambda: V.tensor_scalar(out=a, in0=xi[:, 0:TG], scalar1=gconv[:, c * 5:c * 5 + 1],
                                               scalar2=None, op0=ALU.mult), reads=[xiR, RL], writes=[aR])
                for wi in range(1, 5):
                    dve.op(lambda wi=wi: V.scalar_tensor_tensor(out=a, in0=xi[:, wi:wi + TG],
                                                                scalar=gconv[:, c * 5 + wi:c * 5 + wi + 1], in1=a,
                                                                op0=ALU.mult, op1=ALU.add),
                           reads=[xiR, RL], writes=[aR])
                h = c % 4
                if c < 8:
                    y, yR = ys.next()
                    act.op(lambda: A_.activation(out=y, in_=a, func=AF.Silu), reads=[aR], writes=[yR])
                    s2, s2R = sq.next()
                    act.op(lambda: A_.activation(out=s2, in_=y, func=AF.Square), reads=[yR], writes=[s2R])
                    bi = pb.next()
                    pe.op(lambda: T.matmul(banks[bi][:, 0:TG], ones_bf, s2, start=True, stop=True),
                          reads=[s2R, RC], writes=[bankR[bi]])
                    r, rR = rn.next()
                    if c < 4:
                        act.op(lambda: A_.activation(out=r, in_=banks[bi][:, 0:TG], func=AF.Sqrt, scale=128.0,
                                                     bias=eps128_t), reads=[bankR[bi], RC], writes=[rR])
                    else:
                        act.op(lambda: A_.activation(out=r, in_=banks[bi][:, 0:TG], func=AF.Sqrt, scale=1.0,
                                                     bias=eps_t), reads=[bankR[bi], RC], writes=[rR])
                    yield
                    dve.op(lambda: V.reciprocal(out=r, in_=r), reads=[rR], writes=[rR])
                    o, oR = yn.next()
                    pool.op(lambda: G.tensor_tensor(out=o, in0=y, in1=r, op=ALU.mult), reads=[yR, rR],
                            writes=[oR])
                    dst = GQT if c < 4 else GKT
                    sp.dma(dst[h * 128:(h + 1) * 128, b0:b0 + TG], o, reads=[oR])
                else:
                    o, oR = yn.next()
                    act.op(lambda: A_.activation(out=o, in_=a, func=AF.Silu), reads=[aR], writes=[oR])
                    yield
                if c >= 4:
                    bt = pt.next()
                    tpb = bank_bf(bt)
                    for i in range(nt):
                        pe.op(lambda i=i: T.transpose(out=tpb[:, i * 128:(i + 1) * 128],
                                                      in_=o[:, i * 128:(i + 1) * 128], identity=ident_bf),
                              reads=[oR, RC], writes=[bankR[bt]] if i == 0 else (),
                              dwrites=() if i == 0 else [bankR[bt]])
                    t, tR = tk.next()
                    act.op(lambda: A_.copy(out=t, in_=tpb[:, 0:TG]), reads=[bankR[bt]], writes=[tR])
                    dst = GK if c < 8 else GV
                    sp.dma(dst[b0:b0 + TG, h * 128:(h + 1) * 128].rearrange("(i p) d -> p i d", p=128),
                           t.rearrange("p (i d) -> p i d", d=128), reads=[tR])
            pend = pre(*items[0])
            prev = None
            for ii, (b0, c) in enumerate(items):
                xi, xiR = pend
                if ii + 1 < len(items):
                    pend = pre(*items[ii + 1])
                g_ = chunk(b0, c, xi, xiR)
                next(g_)
                if prev is not None:
                    for _ in prev:
                        pass
                prev = g_
            for _ in prev:
                pass
            kb.barrier()

        def phase_g2(l, si):
            S = seqs[si]
            ntile = S // 128
            kb.barrier()
            cv = Carve()

            def mk(fn, *a):
                return (fn(*a), Res())

            def alloc_set(d):
                B = {}
                B["qT"] = mk(cv.bf, 512, (4, 128))
                B["kT"] = mk(cv.bf, 512, (4, 128))
                B["ktok"] = mk(cv.bf, 512, (4, 128))
                B["vtok"] = mk(cv.bf, 512, (4, 128))
                B["db"] = mk(cv.f32, 16)
                B["sm"] = mk(cv.f32, 64)
                B["E"] = mk(cv.f32, 512, (4, 128))
                B["En"] = mk(cv.f32, 512, (4, 128))
                B["EG"] = mk(cv.f32, 512, (4, 128))
                rv = lambda i: rbuf[:, (5 * d + i) * 512:(5 * d + i + 1) * 512].rearrange("p (a b) -> p a b", a=4)
                B["Xs"] = [(rv(0), Res()), (rv(1), Res())]
                B["Ys"] = [(rv(2), Res()), (rv(3), Res())]
                B["R"] = rv(4), Res()
                B["Rb"] = mk(cv.bf, 512, (4, 128))
                B["attn"] = mk(cv.bf, 512, (4, 128))
                B["attnT"] = mk(cv.bf, 512, (4, 128))
                B["vb"] = mk(cv.bf, 512, (4, 128))
                B["kbg"] = mk(cv.bf, 512, (4, 128))
                B["kg"] = mk(cv.bf, 512, (4, 128))
                B["kg1"] = mk(cv.bf, 512, (4, 128))
                B["qgT"] = mk(cv.bf, 512, (4, 128))
                B["u"] = mk(cv.f32, 512, (4, 128))
                B["wT"] = mk(cv.bf, 512, (4, 128))
                B["vn"] = mk(cv.bf, 512, (4, 128))
                B["ost"] = mk(cv.f32, 512)
                B["S32"] = mk(cv.f32, 512, (4, 128))
                B["Sbf"] = mk(cv.bf, 512, (4, 128))
                return B
            bufs = [alloc_set(0), alloc_set(1)]
            for d in range(2):
                pool.op(lambda d=d: G.memset(bufs[d]["S32"][0], 0.0), writes=[bufs[d]["S32"][1]])
                pool.op(lambda d=d: G.memset(bufs[d]["Sbf"][0], 0.0), writes=[bufs[d]["Sbf"][1]])
                pool.op(lambda d=d: G.memset(bufs[d]["vn"][0], 0.0), writes=[bufs[d]["vn"][1]])
            def unit(d, ti, B):
                qT, qTR = B["qT"]
                kT, kTR = B["kT"]
                ktok, ktokR = B["ktok"]
                vtok, vtokR = B["vtok"]
                db, dbR = B["db"]
                sm, smR = B["sm"]
                E, ER = B["E"]
                En, EnR = B["En"]
                EG, EGR = B["EG"]
                Xs = B["Xs"]
                Ys = B["Ys"]
                R, RR = B["R"]
                Rb, RbR = B["Rb"]
                attn, attnR = B["attn"]
                attnT, attnTR = B["attnT"]
                vb, vbR = B["vb"]
                kbg, kbgR = B["kbg"]
                kg, kgR = B["kg"]
                kg1, kg1R = B["kg1"]
                qgT, qgTR = B["qgT"]
                u, uR = B["u"]
                wT, wTR = B["wT"]
                vn, vnR = B["vn"]
                ost, ostR = B["ost"]
                t0 = ti * 128
                bk = [4 * d + i_ for i_ in range(4)]
                s32, s32R = B["S32"]
                sbf, sbfR = B["Sbf"]
                sp.dma(qT, GQT[:, t0:t0 + 128].rearrange("(h p) t -> p h t", p=128), writes=[qTR])
                sp.dma(kT, GKT[:, t0:t0 + 128].rearrange("(h p) t -> p h t", p=128), writes=[kTR])
                sp.dma(ktok, GK[t0:t0 + 128, :].rearrange("t (h e) -> t h e", e=128), writes=[ktokR])
                sp.dma(vtok, GV[t0:t0 + 128, :].rearrange("t (h e) -> t h e", e=128), writes=[vtokR])
                sp.dma(db, DBt[t0:t0 + 128, :], writes=[dbR])
                dec = db[:, d * 4:d * 4 + 4]
                bet = db[:, 8 + d * 4:8 + d * 4 + 4]
                beta, nbeta, ee, g, gc, dl, kgf, egc, kbgf, spin = [sm[:, 4 * i:4 * i + 4] for i in range(10)]
                act.op(lambda: A_.activation(out=beta, in_=bet, func=AF.Sigmoid), reads=[dbR], writes=[smR])
                dve.op(lambda: V.tensor_scalar(out=nbeta, in0=beta, scalar1=-1.0, scalar2=None, op0=ALU.mult),
                       reads=[smR], dwrites=[smR])
                dve.op(lambda: V.tensor_tensor(out=spin, in0=dec, in1=dtb_b[:, d * 4:d * 4 + 4], op=ALU.add),
                       reads=[dbR, RL, smR], dwrites=[smR])
                act.op(lambda: A_.activation(out=ee, in_=spin, func=AF.Exp), reads=[smR], dwrites=[smR])
                act.op(lambda: A_.activation(out=ee, in_=ee, func=AF.Ln, bias=one_t), reads=[smR, RC],
                       dwrites=[smR])
                g = g_r[d]
                dve.op(lambda: V.tensor_tensor(out=g, in0=ee, in1=negA_b[:, d * 4:d * 4 + 4], op=ALU.mult),
                       reads=[smR, RL], dwrites=[smR])
                pe.op(lambda: T.matmul(banks[bk[0]][:, 0:4], Lr[d], g, start=True, stop=True), reads=[smR, RC],
                      writes=[bankR[bk[0]]])
                pe.op(lambda: T.matmul(banks[bk[0]][:, 4:8], Bor, g, start=True, stop=True),
                      reads=[smR, RC], dwrites=[bankR[bk[0]]])
                dve.op(lambda: V.tensor_copy(out=gc, in_=banks[bk[0]][:, 0:4]), reads=[bankR[bk[0]], smR], dwrites=[smR])
                dve.op(lambda: V.tensor_tensor(out=dl, in0=banks[bk[0]][:, 4:8], in1=gc, op=ALU.subtract),
                       reads=[bankR[bk[0]], smR], dwrites=[smR])
                act.op(lambda: A_.activation(out=kgf, in_=dl, func=AF.Exp), reads=[smR], dwrites=[smR])
                act.op(lambda: A_.activation(out=egc, in_=gc, func=AF.Exp), reads=[smR], dwrites=[smR])
                dve.op(lambda: V.tensor_tensor(out=kbgf, in0=egc, in1=beta, op=ALU.mult), reads=[smR],
                       dwrites=[smR])
                yield
                for h in range(4):
                    pe.op(lambda h=h: T.matmul(banks[bk[1]][:, h * 128:(h + 1) * 128],
                                               g[:, h:h + 1].to_broadcast([128, 128]), Lr[d],
                                               start=True, stop=True), reads=[smR, RC],
                          writes=[bankR[bk[1]]] if h == 0 else (), dwrites=() if h == 0 else [bankR[bk[1]]])
                pe.op(lambda: T.matmul(banks[bk[2]][:, :], ident_r, maskr[d], start=True, stop=False), reads=[RC],
                      writes=[bankR[bk[2]]])
                for h in range(4):
                    pe.op(lambda h=h: T.matmul(banks[bk[2]][:, h * 128:(h + 1) * 128],
                                               g[:, h:h + 1].to_broadcast([128, 128]), Lr[d],
                                               start=False, stop=(h == 3)), reads=[smR, RC], dwrites=[bankR[bk[2]]])
                for h in range(4):
                    act.op(lambda h=h: A_.activation(out=E[:, h, :], in_=banks[bk[2]][:, h * 128:(h + 1) * 128],
                                                     func=AF.Exp, scale=-1.0, bias=gc[:, h:h + 1]),
                           reads=[bankR[bk[2]], smR], writes=[ER] if h == 0 else (), dwrites=() if h == 0 else [ER])
                act.op(lambda: A_.activation(out=EG.rearrange("p h c -> p (h c)"), in_=banks[bk[1]][:, :], func=AF.Exp),
                       reads=[bankR[bk[1]]], writes=[EGR])
                pool.op(lambda: G.tensor_tensor(out=En.rearrange("p h c -> p (h c)"),
                                                in0=E.rearrange("p h c -> p (h c)"), in1=nodiag4, op=ALU.mult),
                        reads=[ER, RC], writes=[EnR])
                yield
                for h in range(4):
                    pe.op(lambda h=h: T.matmul(banks[bk[3]][:, h * 128:(h + 1) * 128], kT[:, h, :], kT[:, h, :],
                                               start=True, stop=True), reads=[kTR],
                          writes=[bankR[bk[3]]] if h == 0 else (), dwrites=() if h == 0 else [bankR[bk[3]]])
                for h in range(4):
                    pe.op(lambda h=h: T.matmul(banks[bk[0]][:, h * 128:(h + 1) * 128], qT[:, h, :], kT[:, h, :],
                                               start=True, stop=True), reads=[kTR, qTR],
                          writes=[bankR[bk[0]]] if h == 0 else (), dwrites=() if h == 0 else [bankR[bk[0]]])
                X0, X0R = Xs[0]
                Y0, Y0R = Ys[0]
                for h in range(4):
                    dve.op(lambda h=h: V.scalar_tensor_tensor(out=X0[:, h, :], in0=banks[bk[3]][:, h * 128:(h + 1) * 128],
                                                              scalar=nbeta[:, h:h + 1], in1=En[:, h, :],
                                                              op0=ALU.mult, op1=ALU.mult),
                           reads=[bankR[bk[3]], smR, EnR], writes=[X0R] if h == 0 else (),
                           dwrites=() if h == 0 else [X0R])
                dve.op(lambda: V.tensor_tensor(out=attn.rearrange("p h c -> p (h c)"), in0=banks[bk[0]][:, :],
                                               in1=E.rearrange("p h c -> p (h c)"), op=ALU.mult),
                       reads=[bankR[bk[0]], ER], writes=[attnR])
                yield
                b5r = banks[bk[1]][:, :].bitcast(F32R)
                for h in range(4):
                    pe.op(lambda h=h: T.matmul(banks[bk[1]][:, h * 128:(h + 1) * 128], X0[:, h, :], ident_r,
                                               start=True, stop=True),
                          reads=[X0R, RC],
                          writes=[bankR[bk[1]]] if h == 0 else (), dwrites=() if h == 0 else [bankR[bk[1]]])
                b6 = bank_bf(bk[2])
                for h in range(4):
                    pe.op(lambda h=h: T.transpose(out=b6[:, h * 128:(h + 1) * 128], in_=attn[:, h, :],
                                                  identity=ident_bf), reads=[attnR, RC],
                          writes=[bankR[bk[2]]] if h == 0 else (), dwrites=() if h == 0 else [bankR[bk[2]]])
                act.op(lambda: A_.copy(out=Y0.rearrange("p h c -> p (h c)"), in_=banks[bk[1]][:, :]),
                       reads=[bankR[bk[1]]], writes=[Y0R])
                dve.op(lambda: V.tensor_tensor(out=R.rearrange("p h c -> p (h c)"),
                                               in0=Y0.rearrange("p h c -> p (h c)").bitcast(F32),
                                               in1=ident4, op=ALU.add), reads=[Y0R, RC], writes=[RR])
                act.op(lambda: A_.copy(out=attnT.rearrange("p h c -> p (h c)"), in_=b6[:, 0:512]),
                       reads=[bankR[bk[2]]], writes=[attnTR])


                yield
                def mm4(bi, lhs, lhsR, rhs, rhsR):
                    for h in range(4):
                        pe.op(lambda h=h: T.matmul(banks[bi][:, h * 128:(h + 1) * 128], lhs[:, h, :], rhs[:, h, :],
                                                   start=True, stop=True), reads=[lhsR, rhsR],
                              writes=[bankR[bi]] if h == 0 else (), dwrites=() if h == 0 else [bankR[bi]])

                cur = 0
                for lev in range(1, 6):
                    Xc, XcR = Xs[cur]
                    Yc, YcR = Ys[cur]
                    Xn, XnR = Xs[1 - cur]
                    Yn, YnR = Ys[1 - cur]
                    if lev >= 2:
                        pass
                    if lev >= 2:
                        mm4(bk[3], Xc, XcR, R, RR)
                        dve.op(lambda: V.tensor_tensor(out=R.rearrange("p h c -> p (h c)"), in0=banks[bk[3]][:, :],
                                                       in1=R.rearrange("p h c -> p (h c)").bitcast(F32),
                                                       op=ALU.add), reads=[bankR[bk[3]], RR], writes=[RR])
                    mm4(bk[1], Yc, YcR, Xc, XcR)
                    act.op(lambda Xn=Xn: A_.copy(out=Xn.rearrange("p h c -> p (h c)"), in_=banks[bk[1]][:, :]),
                           reads=[bankR[bk[1]]], writes=[XnR])
                    if lev <= 4:
                        mm4(bk[2], Xc, XcR, Yc, YcR)
                        dve.op(lambda Yn=Yn: V.tensor_copy(out=Yn.rearrange("p h c -> p (h c)"), in_=banks[bk[2]][:, :]),
                               reads=[bankR[bk[2]]], writes=[YnR])
                    cur = 1 - cur
                    yield
                Xc, XcR = Xs[cur]
                mm4(bk[3], Xc, XcR, R, RR)
                dve.op(lambda: V.tensor_tensor(out=R.rearrange("p h c -> p (h c)"), in0=banks[bk[3]][:, :],
                                               in1=R.rearrange("p h c -> p (h c)").bitcast(F32), op=ALU.add),
                       reads=[bankR[bk[3]], RR], writes=[RR])
                act.op(lambda: A_.copy(out=Rb.rearrange("p h c -> p (h c)"),
                                       in_=R.rearrange("p h c -> p (h c)").bitcast(F32)), reads=[RR],
                       writes=[RbR])
                yield
                kgf0, kgf1 = sm[:, 40:44], sm[:, 44:48]
                dve.op(lambda: V.tensor_scalar(out=kgf0, in0=kgf, scalar1=ind_t[:, 0:1], scalar2=None, op0=ALU.mult),
                       reads=[smR, RI], dwrites=[smR])
                dve.op(lambda: V.tensor_scalar(out=kgf1, in0=kgf, scalar1=ind_t[:, 1:2], scalar2=None, op0=ALU.mult),
                       reads=[smR, RI], dwrites=[smR])
                for h in range(4):
                    act.op(lambda h=h: A_.activation(out=vb[:, h, :], in_=vtok[:, h, :], func=AF.Copy,
                                                     scale=beta[:, h:h + 1]), reads=[vtokR, smR],
                           writes=[vbR] if h == 0 else (), dwrites=() if h == 0 else [vbR])
                    dve.op(lambda h=h: V.tensor_scalar(out=kbg[:, h, :], in0=ktok[:, h, :],
                                                       scalar1=kbgf[:, h:h + 1], scalar2=None, op0=ALU.mult),
                           reads=[ktokR, smR], writes=[kbgR] if h == 0 else (), dwrites=() if h == 0 else [kbgR])
                    pool.op(lambda h=h: G.tensor_scalar(out=kg[:, h, :], in0=ktok[:, h, :],
                                                        scalar1=kgf0[:, h:h + 1], scalar2=1.0, op0=ALU.mult,
                                                        op1=ALU.mult),
                            reads=[ktokR, smR], writes=[kgR] if h == 0 else (), dwrites=() if h == 0 else [kgR])
                    dve.op(lambda h=h: V.tensor_scalar(out=kg1[:, h, :], in0=ktok[:, h, :],
                                                       scalar1=kgf1[:, h:h + 1], scalar2=None, op0=ALU.mult),
                           reads=[ktokR, smR], writes=[kg1R] if h == 0 else (), dwrites=() if h == 0 else [kg1R])
                pool.op(lambda: G.tensor_tensor(out=qgT.rearrange("p h c -> p (h c)"),
                                                in0=qT.rearrange("p h c -> p (h c)"),
                                                in1=EG.rearrange("p h c -> p (h c)"), op=ALU.mult),
                        reads=[qTR, EGR], writes=[qgTR])
                mm4(bk[3], Rb, RbR, vb, vbR)
                act.op(lambda: A_.copy(out=u.rearrange("p h c -> p (h c)"), in_=banks[bk[3]][:, :]), reads=[bankR[bk[3]]],
                       writes=[uR])
                mm4(bk[0], kbg, kbgR, Rb, RbR)
                dve.op(lambda: V.tensor_copy(out=wT.rearrange("p h c -> p (h c)"), in_=banks[bk[0]][:, :]),
                       reads=[bankR[bk[0]]], writes=[wTR])
                yield
                for step in range(2):
                    r0 = (0, 64)[step] if d == 0 else (64, 0)[step]
                    rows = slice(r0, r0 + 64)
                    for h in range(4):
                        pe.op(lambda h=h: T.matmul(banks[bk[0]][:, h * 128:(h + 1) * 128], wT[:, h, :],
                                                   sbf[:, h, :], start=True, stop=True), reads=[wTR, sbfR],
                              writes=[bankR[bk[0]]] if h == 0 else (), dwrites=() if h == 0 else [bankR[bk[0]]])
                    dve.op(lambda: V.tensor_tensor(out=vn[rows].rearrange("p h c -> p (h c)"),
                                                   in0=u[rows].rearrange("p h c -> p (h c)"),
                                                   in1=banks[bk[0]][rows, :], op=ALU.subtract),
                           reads=[bankR[bk[0]], uR], writes=[vnR])
                    for h in range(4):
                        pe.op(lambda h=h: T.matmul(banks[bk[1]][:, h * 128:(h + 1) * 128], qgT[:, h, :],
                                                   sbf[:, h, :], start=True, stop=False), reads=[qgTR, sbfR],
                              writes=[bankR[bk[1]]] if h == 0 else (), dwrites=() if h == 0 else [bankR[bk[1]]])
                        pe.op(lambda h=h: T.matmul(banks[bk[1]][:, h * 128:(h + 1) * 128], attnT[:, h, :],
                                                   vn[:, h, :], start=False, stop=True), reads=[attnTR, vnR],
                              dwrites=[bankR[bk[1]]])
                    for h in range(4):
                        pe.op(lambda h=h: T.matmul(banks[bk[2]][:, h * 128:(h + 1) * 128], (kg, kg1)[r0 // 64][:, h, :],
                                                   vn[:, h, :], start=True, stop=True), reads=[kgR, kg1R, vnR],
                              writes=[bankR[bk[2]]] if h == 0 else (), dwrites=() if h == 0 else [bankR[bk[2]]])
                    col = r0 + 63 if d == 0 else r0
                    for h in range(4):
                        dve.op(lambda h=h: V.scalar_tensor_tensor(out=s32[:, h, :], in0=s32[:, h, :],
                                                                  scalar=EG[:, h, col:col + 1],
                                                                  in1=banks[bk[2]][:, h * 128:(h + 1) * 128],
                                                                  op0=ALU.mult, op1=ALU.add),
                               reads=[bankR[bk[2]], EGR], writes=[s32R] if h == 0 else (),
                               dwrites=() if h == 0 else [s32R])
                    act.op(lambda: A_.copy(out=sbf.rearrange("p h c -> p (h c)"),
                                           in_=s32.rearrange("p h c -> p (h c)")), reads=[s32R], writes=[sbfR])
                    act.op(lambda: A_.copy(out=ost[rows, :], in_=banks[bk[1]][rows, :]), reads=[bankR[bk[1]]],
                           writes=[ostR] if step == 0 else (), dwrites=() if step == 0 else [ostR])
                    yield
                sp.dma(OD[d][t0:t0 + 128, :], ost, reads=[ostR])
            for it in range(ntile):
                alive = [unit(0, it, bufs[0]), unit(1, ntile - 1 - it, bufs[1])]
                while alive:
                    for g_ in list(alive):
                        try:
                            next(g_)
                        except StopIteration:
                            alive.remove(g_)
            kb.barrier()

        def phase_g3(l, si):
            S = seqs[si]
            kb.barrier()
            cv = Carve()
            ofs = Rot([(cv.f32(512), Res()) for _ in range(2)])
            obs = Rot([(cv.f32(512), Res()) for _ in range(2)])
            azs = Rot([(cv.bf(512), Res()) for _ in range(2)])
            junk = cv.f32(128)
            sm, smR = cv.f32(16), Res()
            ys = Rot([(cv.f32(512), Res()) for _ in range(2)])
            ybs = Rot([(cv.bf(512), Res()) for _ in range(2)])
            sts = Rot([(cv.bf(512), Res()) for _ in range(2)])
            pb = Rot([0, 1])
            def pre_g3(ti):
                t0 = ti * 128
                of, ofR = ofs.next()
                ob, obR = obs.next()
                az, azR = azs.next()
                sp.dma(of, OD[0][t0:t0 + 128, :], writes=[ofR])
                sp.dma(ob, OD[1][t0:t0 + 128, :], writes=[obR])
                sp.dma(az, AZ[t0:t0 + 128, :], writes=[azR])
                return of, ofR, ob, obR, az, azR
            pend = pre_g3(0)
            for ti in range(S // 128):
                t0 = ti * 128
                of, ofR, ob, obR, az, azR = pend
                if ti + 1 < S // 128:
                    pend = pre_g3(ti + 1)
                pool.op(lambda: G.tensor_tensor(out=of, in0=of, in1=ob, op=ALU.add), reads=[obR], writes=[ofR])
                for h in range(4):
                    act.op(lambda h=h: A_.activation(out=junk, in_=of[:, h * 128:(h + 1) * 128], func=AF.Square,
                                                     accum_out=sm[:, h:h + 1]), reads=[ofR], writes=[smR])
                act.op(lambda: A_.activation(out=sm[:, 4:8], in_=sm[:, 0:4], func=AF.Sqrt, scale=1.0 / 128, bias=eps_t),
                       reads=[smR, RC], dwrites=[smR])
                dve.op(lambda: V.reciprocal(out=sm[:, 8:12], in_=sm[:, 4:8]), reads=[smR], dwrites=[smR])
                y, yR = ys.next()
                for h in range(4):
                    dve.op(lambda h=h: V.scalar_tensor_tensor(out=y[:, h * 128:(h + 1) * 128],
                                                              in0=of[:, h * 128:(h + 1) * 128],
                                                              scalar=sm[:, 8 + h:9 + h], in1=gnorm_b, op0=ALU.mult,
                                                              op1=ALU.mult), reads=[ofR, smR, RL],
                           writes=[yR] if h == 0 else (), dwrites=() if h == 0 else [yR])
                yb, ybR = ybs.next()
                pool.op(lambda: G.tensor_tensor(out=yb, in0=y, in1=az, op=ALU.mult), reads=[yR, azR], writes=[ybR])
                bi = pb.next()
                tpb = bank_bf(bi)
                for h in range(4):
                    pe.op(lambda h=h: T.transpose(out=tpb[:, h * 128:(h + 1) * 128], in_=yb[:, h * 128:(h + 1) * 128],
                                                  identity=ident_bf), reads=[ybR, RC],
                          writes=[bankR[bi]] if h == 0 else (), dwrites=() if h == 0 else [bankR[bi]])
                st, stR = sts.next()
                act.op(lambda: A_.copy(out=st, in_=tpb[:, 0:512]), reads=[bankR[bi]], writes=[stR])
                sp.dma(YT[0:512, t0:t0 + 128].rearrange("(h e) t -> e h t", e=128),
                       st.rearrange("p (h t) -> p h t", t=128), reads=[stR])
            kb.barrier()

        def softmax_pv(items, QB, scale, bsc, pts):
            n = len(items)
            sc = [None] * n

            def qk(i):
                it = items[i]
                bs = bsc.next()
                sc[i] = bs
                pe.op(lambda: T.matmul(banks[bs][:, 0:QB], it["kT"], it["q"], start=True, stop=True),
                      reads=[it["kR"], it["qR"]], writes=[bankR[bs]])
            for i in range(min(2, n)):
                qk(i)
            for i in range(n):
                it = items[i]
                bs = sc[i]
                bo, bd = it["bo"], it["bd"]
                p, pR = pts.next()
                act.op(lambda: A_.activation(out=p[:, 0:QB], in_=banks[bs][:, 0:QB], func=AF.Exp, scale=scale),
                       reads=[bankR[bs]], writes=[pR])
                pe.op(lambda: T.matmul(banks[bo][:, 0:QB], it["v"], p[:, 0:QB], start=it["first"], stop=it["last"]),
                      reads=[it["vR"], pR], writes=[bankR[bo]] if it["first"] else (),
                      dwrites=() if it["first"] else [bankR[bo]])
                pe.op(lambda: T.matmul(banks[bd][:, 0:QB], ones_bf, p[:, 0:QB], start=it["first"], stop=it["last"]),
                      reads=[RC, pR], writes=[bankR[bd]] if it["first"] else (),
                      dwrites=() if it["first"] else [bankR[bd]])
                if i + 2 < n:
                    qk(i + 2)

        def phase_b(l, si):
            S = seqs[si]
            QB = min(512, S)
            nkc = S // 128
            kb.barrier()
            cv = Carve()
            kTs = Rot([(cv.bf(S), (cv.bf(S), cv.bf(S)), cv.bf(S, (S // 128, 128)), Res()) for _ in range(2)])
            for (_k, (qz0, qz1), _v, kvR0) in kTs.items:
                pool.op(lambda: G.memset(qz0[64:128, :], 0.0), writes=[kvR0])
                pool.op(lambda: G.memset(qz1[0:64, :], 0.0), dwrites=[kvR0])
            zts = Rot([(cv.bf(QB), Res()) for _ in range(2)])
            pts = Rot([(cv.bf(512), Res()) for _ in range(4)])
            tmp = Rot([(cv.f32(512), Res()) for _ in range(6)])
            sqs = Rot([(cv.bf(512), Res()) for _ in range(2)])
            outs = Rot([(cv.bf(512), Res()) for _ in range(2)])
            bsc = Rot([0, 1, 7])

            def load_head(h):
                kTh, qTh, vh, kvR = kTs.next()
                sp.dma(kTh, BKT[h * 128:(h + 1) * 128, 0:S], writes=[kvR])
                sp.dma(qTh[0][0:64, :], BQT[h * 128:h * 128 + 64, 0:S], dwrites=[kvR])
                sp.dma(qTh[1][64:128, :], BQT[h * 128 + 64:(h + 1) * 128, 0:S], dwrites=[kvR])
                sp.dma(vh, BV[0:S, h * 128:(h + 1) * 128].rearrange("(i p) e -> p i e", p=128), dwrites=[kvR])
                return kTh, qTh, vh, kvR

            def load_z(h, qb):
                zt, ztR = zts.next()
                sp.dma(zt, BZT[h * 128:(h + 1) * 128, qb * QB:(qb + 1) * QB], writes=[ztR])
                return zt, ztR
            nqb = S // QB
            hq = [(h, qb) for h in range(4) for qb in range(nqb)]
            head_next = load_head(0)
            z_next = load_z(0, 0)
            for ii, (h, qb) in enumerate(hq):
                if qb == 0:
                    kTh, qTh, vh, kvR = head_next
                    if h + 1 < 4:
                        head_next = load_head(h + 1)
                zt, ztR = z_next
                if ii + 1 < len(hq):
                    z_next = load_z(*hq[ii + 1])
                if True:
                    q0 = qb * QB
                    items = []
                    for j in range(2):
                        for kc in range(nkc):
                            items.append(dict(kT=kTh[:, kc * 128:(kc + 1) * 128], q=qTh[j][:, q0:q0 + QB],
                                              v=vh[:, kc, :], kR=kvR, qR=kvR, vR=kvR, bo=2 + j, bd=4 + j,
                                              first=(kc == 0), last=(kc == nkc - 1)))
                    softmax_pv(items, QB, 0.125, bsc, pts)
                    r0, r0R = tmp.next()
                    r1, r1R = tmp.next()
                    dve.op(lambda: V.reciprocal(out=r0[:, 0:QB], in_=banks[4][:, 0:QB]), reads=[bankR[4]], writes=[r0R])
                    dve.op(lambda: V.reciprocal(out=r1[:, 0:QB], in_=banks[5][:, 0:QB]), reads=[bankR[5]], writes=[r1R])
                    o0, o0R = tmp.next()
                    o1, o1R = tmp.next()
                    dve.op(lambda: V.tensor_tensor(out=o0[:, 0:QB], in0=banks[2][:, 0:QB], in1=r0[:, 0:QB], op=ALU.mult),
                           reads=[bankR[2], r0R], writes=[o0R])
                    dve.op(lambda: V.tensor_tensor(out=o1[:, 0:QB], in0=banks[3][:, 0:QB], in1=r1[:, 0:QB], op=ALU.mult),
                           reads=[bankR[3], r1R], writes=[o1R])
                    dve.op(lambda: V.scalar_tensor_tensor(out=o0[:, 0:QB], in0=o1[:, 0:QB], scalar=neglam,
                                                          in1=o0[:, 0:QB], op0=ALU.mult, op1=ALU.add),
                           reads=[o1R, RL], writes=[o0R])
                    sq, sqR = sqs.next()
                    pool.op(lambda: G.tensor_tensor(out=sq[:, 0:QB], in0=o0[:, 0:QB], in1=o0[:, 0:QB], op=ALU.mult),
                            reads=[o0R], writes=[sqR])
                    pe.op(lambda: T.matmul(banks[6][:, 0:QB], ones_bf, sq[:, 0:QB], start=True, stop=True),
                          reads=[sqR, RC], writes=[bankR[6]])
                    rn, rnR = tmp.next()
                    act.op(lambda: A_.activation(out=rn[:, 0:QB], in_=banks[6][:, 0:QB], func=AF.Ln, scale=1.0 / 128,
                                                 bias=eps_t), reads=[bankR[6], RC], writes=[rnR])
                    act.op(lambda: A_.activation(out=rn[:, 0:QB], in_=rn[:, 0:QB], func=AF.Exp, scale=-0.5),
                           reads=[rnR], writes=[rnR])
                    dve.op(lambda: V.scalar_tensor_tensor(out=o0[:, 0:QB], in0=o0[:, 0:QB], scalar=dnorm_s,
                                                          in1=rn[:, 0:QB], op0=ALU.mult, op1=ALU.mult),
                           reads=[rnR, RL], writes=[o0R])
                    ot, otR = outs.next()
                    pool.op(lambda: G.tensor_tensor(out=ot[:, 0:QB], in0=o0[:, 0:QB], in1=zt, op=ALU.mult),
                            reads=[o0R, ztR], writes=[otR])
                    sp.dma(YT[512 + h * 128: 512 + (h + 1) * 128, q0:q0 + QB], ot[:, 0:QB], reads=[otR])
            kb.barrier()

        def phase_m(l, si):
            S = seqs[si]
            QB = min(512, S)
            kb.barrier()
            cv = Carve()
            wkv = cv.bf(16 * 1024, (16, 1024))
            wkvR = Res()
            nmem_b = cv.f32(D)
            xts = Rot([(cv.f32(D), Res()) for _ in range(2)])
            junk = cv.bf(D)
            hbs = Rot([(cv.bf(D), Res()) for _ in range(2)])
            memT = cv.bf(16 * 256, (16, 256))
            memTR = Res()
            small, smallR = cv.f32(8), Res()
            kmT, kmTR = cv.bf(4 * 256, (4, 256)), Res()
            vm, vmR = cv.bf(2 * 512, (2, 512)), Res()
            qts = Rot([(cv.bf(QB), cv.bf(QB), Res()) for _ in range(2)])
            pts = Rot([(cv.bf(512), Res()) for _ in range(3)])
            tmp = Rot([(cv.f32(512), Res()) for _ in range(4)])
            outs = Rot([(cv.bf(512), Res()) for _ in range(2)])
            sp.dma(wkv, WKV[l], reads=[RW], writes=[wkvR])
            sp.dma(nmem_b, norm_mem[l].partition_broadcast(128), writes=[smallR])
            for i in range(2):
                xt, xtR = xts.next()
                hb, hbR = hbs.next()
                sp.dma(xt, mem_in[si * N_MEM + i * 128: si * N_MEM + (i + 1) * 128, :], writes=[xtR])
                rms_rows((junk, small[:, 0:1], small[:, 1:2]), xt, xtR, small[:, 2:3], smallR, D)
                dve.op(lambda: V.scalar_tensor_tensor(out=hb, in0=xt, scalar=small[:, 2:3], in1=nmem_b,
                                                      op0=ALU.mult, op1=ALU.mult), reads=[xtR, smallR], writes=[hbR])
                for half in range(2):
                    tpb = bank_bf(half)
                    for kk in range(8):
                        kc = half * 8 + kk
                        pe.op(lambda kc=kc, kk=kk, tpb=tpb: T.transpose(out=tpb[:, kk * 128:(kk + 1) * 128],
                                                                        in_=hb[:, kc * 128:(kc + 1) * 128],
                                                                        identity=ident_bf), reads=[hbR, RC],
                              writes=[bankR[half]] if kk == 0 else (), dwrites=() if kk == 0 else [bankR[half]])
                    dve.op(lambda: V.tensor_copy(out=memT[:, half * 8:(half + 1) * 8, i * 128:(i + 1) * 128],
                                                 in_=tpb.rearrange("p (k t) -> p k t", k=8)), reads=[bankR[half]],
                           dwrites=[memTR])
            for h in range(4):
                bi = 2 + h
                for kc in range(16):
                    pe.op(lambda kc=kc: T.matmul(banks[bi][:, 0:256], wkv[:, kc, h * 128:(h + 1) * 128], memT[:, kc, :],
                                                 start=(kc == 0), stop=(kc == 15)), reads=[wkvR, memTR],
                          writes=[bankR[bi]] if kc == 0 else (), dwrites=() if kc == 0 else [bankR[bi]])
                act.op(lambda: A_.copy(out=kmT[:, h, :], in_=banks[bi][:, 0:256]), reads=[bankR[bi]], dwrites=[kmTR])
            for mt in range(2):
                bi = 6 + mt
                for kc in range(16):
                    pe.op(lambda kc=kc: T.matmul(banks[bi][:, :], memT[:, kc, mt * 128:(mt + 1) * 128],
                                                 wkv[:, kc, 512:1024], start=(kc == 0), stop=(kc == 15)),
                          reads=[wkvR, memTR], writes=[bankR[bi]] if kc == 0 else (),
                          dwrites=() if kc == 0 else [bankR[bi]])
                dve.op(lambda: V.tensor_copy(out=vm[:, mt, :], in_=banks[bi][:, :]), reads=[bankR[bi]], dwrites=[vmR])
            bsc = Rot([0, 1, 7])
            bo = Rot([2, 3])
            bdn = Rot([4, 5])
            scale = 128.0 ** -0.5
            for h in range(4):
                for qb in range(S // QB):
                    q0 = qb * QB
                    qt, zt, qR = qts.next()
                    sp.dma(qt, MQT[h * 128:(h + 1) * 128, q0:q0 + QB], writes=[qR])
                    sp.dma(zt, MZT[h * 128:(h + 1) * 128, q0:q0 + QB], dwrites=[qR])
                    b_o = bo.next()
                    b_d = bdn.next()
                    items = [dict(kT=kmT[:, h, kc * 128:(kc + 1) * 128], q=qt, v=vm[:, kc, h * 128:(h + 1) * 128],
                                  kR=kmTR, qR=qR, vR=vmR, bo=b_o, bd=b_d, first=(kc == 0), last=(kc == 1))
                             for kc in range(2)]
                    softmax_pv(items, QB, scale, bsc, pts)
                    r, rR = tmp.next()
                    dve.op(lambda: V.reciprocal(out=r[:, 0:QB], in_=banks[b_d][:, 0:QB]), reads=[bankR[b_d]],
                           writes=[rR])
                    o, oR = tmp.next()
                    dve.op(lambda: V.tensor_tensor(out=o[:, 0:QB], in0=banks[b_o][:, 0:QB], in1=r[:, 0:QB],
                                                   op=ALU.mult), reads=[bankR[b_o], rR], writes=[oR])
                    ot, otR = outs.next()
                    pool.op(lambda: G.tensor_tensor(out=ot[:, 0:QB], in0=o[:, 0:QB], in1=zt, op=ALU.mult),
                            reads=[oR, qR], writes=[otR])
                    sp.dma(YT[1536 + h * 128: 1536 + (h + 1) * 128, q0:q0 + QB], ot[:, 0:QB], reads=[otR])
            kb.barrier()

        def phase_c(l, si):
            S = seqs[si]
            TC = min(S, 1024)
            kb.barrier()
            cv = Carve()
            us = Rot([(cv.bf(TC + 2), Res()) for _ in range(2)])
            gs = Rot([(cv.bf(TC), Res()) for _ in range(2)])
            acc = Rot([(cv.f32(TC), Res()) for _ in range(2)])
            outs = Rot([(cv.bf(TC), Res()) for _ in range(2)])
            for c in range(4):
                for b0 in range(0, S, TC):
                    ut, utR = us.next()
                    gt, gtR = gs.next()
                    lo = max(0, b0 - 1)
                    hi = min(S, b0 + TC + 1)
                    pool.op(lambda: G.memset(ut, 0.0), writes=[utR])
                    sp.dma(ut[:, lo - (b0 - 1): hi - (b0 - 1)], CUT[c * 128:(c + 1) * 128, lo:hi], dwrites=[utR])
                    sp.dma(gt, CGT[c * 128:(c + 1) * 128, b0:b0 + TC], writes=[gtR])
                    a, aR = acc.next()
                    dve.op(lambda: V.tensor_scalar(out=a, in0=ut[:, 0:TC], scalar1=cconv[:, c * 3:c * 3 + 1],
                                                   scalar2=None, op0=ALU.mult), reads=[utR, RL], writes=[aR])
                    for wi in (1, 2):
                        dve.op(lambda wi=wi: V.scalar_tensor_tensor(out=a, in0=ut[:, wi:wi + TC],
                                                                    scalar=cconv[:, c * 3 + wi:c * 3 + wi + 1], in1=a,
                                                                    op0=ALU.mult, op1=ALU.add), reads=[utR, RL],
                               writes=[aR])
                    o, oR = outs.next()
                    pool.op(lambda: G.tensor_tensor(out=o, in0=a, in1=gt, op=ALU.mult), reads=[aR, gtR], writes=[oR])
                    sp.dma(YT[1024 + c * 128: 1024 + (c + 1) * 128, b0:b0 + TC], o, reads=[oR])
            kb.barrier()

        def phase_o(l, si, xsrc, xsrcR, xdst, xdstR):
            S = seqs[si]
            kb.barrier()
            cv = Carve()
            wo = cv.bf(16 * D, (16, D))
            woR = Res()
            yts = Rot([(cv.bf(16 * 128, (16, 128)), Res()) for _ in range(2)])
            y32 = Rot([(cv.f32(D), Res()) for _ in range(2)])
            xts = Rot([(cv.f32(D), Res()) for _ in range(2)])
            junk = cv.bf(D)
            small, smallR = cv.f32(8), Res()
            for q4 in range(4):
                sp.dma(wo[:, :, q4 * 512:(q4 + 1) * 512], WOUT[l][:, :, q4 * 512:(q4 + 1) * 512], reads=[RW],
                       dwrites=[woR])
            pbo = Rot([[0, 1, 2, 3], [4, 5, 6, 7]])
            def pre_o(ti):
                t0 = ti * 128
                yt, ytR = yts.next()
                sp.dma(yt, YT[:, t0:t0 + 128].rearrange("(k p) t -> p k t", p=128), writes=[ytR])
                xt, xtR = xts.next()
                sp.dma(xt, xsrc[offs[si] + t0: offs[si] + t0 + 128, :], reads=[xsrcR], writes=[xtR])
                return yt, ytR, xt, xtR
            pend = pre_o(0)
            for ti in range(S // 128):
                t0 = ti * 128
                yt, ytR, xt, xtR = pend
                if ti + 1 < S // 128:
                    pend = pre_o(ti + 1)
                bs = pbo.next()
                yv, yvR = y32.next()
                for cb in range(4):
                    bi = bs[cb]
                    for kc in range(16):
                        pe.op(lambda kc=kc: T.matmul(banks[bi][:, :], yt[:, kc, :], wo[:, kc, cb * 512:(cb + 1) * 512],
                                                     start=(kc == 0), stop=(kc == 15)), reads=[ytR, woR],
                              writes=[bankR[bi]] if kc == 0 else (), dwrites=() if kc == 0 else [bankR[bi]])
                    e = kb.ew.next()
                    if e is act:
                        act.op(lambda: A_.copy(out=yv[:, cb * 512:(cb + 1) * 512], in_=banks[bi][:, :]),
                               reads=[bankR[bi]], writes=[yvR] if cb == 0 else (), dwrites=() if cb == 0 else [yvR])
                    else:
                        dve.op(lambda: V.tensor_copy(out=yv[:, cb * 512:(cb + 1) * 512], in_=banks[bi][:, :]),
                               reads=[bankR[bi]], writes=[yvR] if cb == 0 else (), dwrites=() if cb == 0 else [yvR])
                rms_rows((junk, small[:, 0:1], small[:, 1:2]), yv, yvR, small[:, 2:3], smallR, D)
                dve.op(lambda: V.scalar_tensor_tensor(out=yv, in0=yv, scalar=small[:, 2:3], in1=npost_b, op0=ALU.mult,
                                                      op1=ALU.mult), reads=[smallR, RL], writes=[yvR])
                pool.op(lambda: G.tensor_tensor(out=yv, in0=yv, in1=xt, op=ALU.add), reads=[xtR], writes=[yvR])
                sp.dma(xdst[offs[si] + t0: offs[si] + t0 + 128, :], yv, reads=[yvR], dwrites=[xdstR])
            kb.barrier()

        XinR = Res("xin")
        X1R = Res("x1")
        YoR = Res("yout")
        for l in range(depth):
            load_layer_params(l)
            if depth == 1:
                xsrc, xsrcR, xdst, xdstR = x_in, XinR, y_out, YoR
            elif l == 0:
                xsrc, xsrcR, xdst, xdstR = x_in, XinR, X1, X1R
            else:
                xsrc, xsrcR, xdst, xdstR = X1, X1R, y_out, YoR
            for si in range(nseq):
                ph = phases.split(",")
                if "a" in ph:
                    phase_a(l, si, xsrc, xsrcR)
                if "g1" in ph:
                    phase_g1(l, si)
                if "g2" in ph:
                    phase_g2(l, si)
                if "g3" in ph:
                    phase_g3(l, si)
                if "b" in ph:
                    phase_b(l, si)
                if "m" in ph:
                    phase_m(l, si)
                if "c" in ph:
                    phase_c(l, si)
                if "o" in ph:
                    phase_o(l, si, xsrc, xsrcR, xdst, xdstR)
        kb.barrier()
    return nc


_CACHE = {}


def _run(seqs, depth, core_inputs, debug=False):
    key = (tuple(seqs), depth, debug)
    if key not in _CACHE:
        _CACHE[key] = build(list(seqs), depth, debug)
    nc = _CACHE[key]
    res = run_bass_kernel_spmd(nc, core_inputs, core_ids=list(range(len(core_inputs))))
    return res


def kernel(x_prompt, x_sample, mem_prompt, mem_sample, norm_pre, norm_post, norm_mem, w_in, gdn_conv,
           gdn_A_log, gdn_dt_bias, gdn_norm, diff_lambda, diff_norm, conv_w, w_mem_kv, w_out):
    n = 8
    f = lambda a: np.ascontiguousarray(np.asarray(a, dtype=np.float32))
    x_prompt, x_sample, mem_prompt, mem_sample = f(x_prompt), f(x_sample), f(mem_prompt), f(mem_sample)
    B, S, _ = x_prompt.shape
    DB, DS, _ = x_sample.shape
    pb = B // n
    db = DB // n
    seqs = [S] * pb + [DS] * db
    depth = np.asarray(norm_pre).shape[0]
    consts = _consts(max(seqs))
    shared = dict(norm_pre=f(norm_pre), norm_post=f(norm_post), norm_mem=f(norm_mem), w_in=f(w_in),
                  gdn_conv=f(gdn_conv), gdn_A_log=f(gdn_A_log), gdn_dt_bias=f(gdn_dt_bias), gdn_norm=f(gdn_norm),
                  diff_lambda=f(diff_lambda), diff_norm=f(diff_norm), conv_w=f(conv_w), w_mem_kv=f(w_mem_kv),
                  w_out=f(w_out), **consts)
    in_maps = []
    for c in range(n):
        xs = [x_prompt[c * pb + i] for i in range(pb)] + [x_sample[c * db + i] for i in range(db)]
        ms = [mem_prompt[c * pb + i] for i in range(pb)] + [mem_sample[c * db + i] for i in range(db)]
        m = dict(shared)
        m["x"] = np.ascontiguousarray(np.concatenate(xs, axis=0))
        m["mem"] = np.ascontiguousarray(np.concatenate(ms, axis=0))
        in_maps.append(m)
    res = _run(seqs, depth, in_maps)
    y_prompt = np.empty((B, S, D), np.float32)
    y_sample = np.empty((DB, DS, D), np.float32)
    for c in range(n):
        y = res.results[c]["y"]
        o = 0
        for i in range(pb):
            y_prompt[c * pb + i] = y[o:o + S]
            o += S
        for i in range(db):
            y_sample[c * db + i] = y[o:o + DS]
            o += DS
    return (y_prompt, y_sample)
```

```python
import math
from contextlib import ExitStack
import numpy as np
import ml_dtypes
import concourse.bass as bass
import concourse.mybir as mybir
from concourse.bass_utils import run_bass_kernel_spmd

F32 = mybir.dt.float32
BF16 = mybir.dt.bfloat16
F32R = mybir.dt.float32r
AF = mybir.ActivationFunctionType
ALU = mybir.AluOpType

D = 2048
W_BR = 512
IN_COLS = 7184
N_MEM = 256
EPS = 1e-6
NEG = 32768.0
C_AQKV, C_ADEC, C_AZ = 0, 1536, 1552
C_BQ, C_BK, C_BV, C_BZ = 2064, 2576, 3088, 3600
C_CB, C_CC, C_CX, C_CZ = 4112, 4624, 5136, 5648
C_MQ, C_MZ = 6160, 6672


class Res:
    __slots__ = ("w", "r", "wx", "name")

    def __init__(self, name=""):
        self.w = {}
        self.r = {}
        self.wx = {}
        self.name = name


class Slot:
    __slots__ = ("si", "val")


class Eng:
    RING = 12
    CAP = 30000

    def __init__(self, kb, name, obj):
        self.kb = kb
        self.name = name
        self.o = obj
        self.si = kb.newsem()
        self.cnt = 0
        self.known = {}
        self.ring = []
        self.ri = 0

    def _deps(self, reads, writes, dwrites):
        deps = {}

        def add(d):
            for k, v in d.items():
                if deps.get(k, 0) < v:
                    deps[k] = v
        for r in reads:
            add(r.w)
        for w in writes:
            add(w.w)
            add(w.r)
        for w in dwrites:
            add(w.r)
            add(w.wx)
        return deps

    def _wait(self, deps):
        for k, v in deps.items():
            if self.name == "pe" and k == self.si:
                continue
            if self.known.get(k, 0) < v:
                self.o.wait_ge(self.kb.sems[k], v)
                self.known[k] = v

    def _post(self, ev, reads, writes, dwrites):
        k, v = ev
        for r in reads:
            r.r[k] = v
        for w in writes:
            w.w = {k: v}
            w.wx = {k: v}
            w.r = {}
        for w in dwrites:
            w.w[k] = v

    def op(self, fn, reads=(), writes=(), dwrites=()):
        self._wait(self._deps(reads, writes, dwrites))
        ins = fn()
        self.cnt += 1
        ins.then_inc(self.kb.sems[self.si], 1)
        self._post((self.si, self.cnt), reads, writes, dwrites)
        if self.cnt >= self.CAP:
            self.si = self.kb.newsem()
            self.cnt = 0
        return ins

    def dma(self, out, in_, reads=(), writes=(), dwrites=()):
        self._wait(self._deps(reads, writes, dwrites))
        if len(self.ring) < self.RING:
            s = Slot()
            s.si = self.kb.newsem()
            s.val = 0
            self.ring.append(s)
        s = self.ring[self.ri % self.RING]
        self.ri += 1
        if s.val >= self.CAP:
            self._wait({s.si: s.val})
            s.si = self.kb.newsem()
            s.val = 0
        if s.val > 0:
            self._wait({s.si: s.val})
        ins = self.o.dma_start(out=out, in_=in_)
        s.val += 16
        ins.then_inc(self.kb.sems[s.si], 16)
        self._post((s.si, s.val), reads, writes, dwrites)
        return ins

    def last_events(self):
        ev = {}
        if self.cnt > 0:
            ev[self.si] = self.cnt
        for s in self.ring:
            if s.val > 0:
                ev[s.si] = s.val
        return ev


class Rot:
    def __init__(self, items):
        self.items = items
        self.i = 0

    def next(self):
        it = self.items[self.i % len(self.items)]
        self.i += 1
        return it


class KB:
    def __init__(self, nc, es):
        self.nc = nc
        self.es = es
        self.sems = []
        self.pe = Eng(self, "pe", nc.tensor)
        self.act = Eng(self, "act", nc.scalar)
        self.dve = Eng(self, "dve", nc.vector)
        self.pool = Eng(self, "pool", nc.gpsimd)
        self.sp = Eng(self, "sp", nc.sync)
        self.engs = [self.pe, self.act, self.dve, self.pool, self.sp]
        self.ew = Rot([self.act, self.dve])

    def newsem(self):
        h = self.es.enter_context(self.nc.semaphore(f"s{len(self.sems)}"))
        self.sems.append(h)
        return len(self.sems) - 1

    def barrier(self):
        ev = {}
        for e in self.engs:
            ev.update(e.last_events())
        for e in self.engs:
            e._wait(dict(ev))


def _consts(smax):
    c = {}
    eye = np.eye(128, dtype=np.float32)
    c["ident_bf"] = eye.astype(ml_dtypes.bfloat16)
    c["ones_bf"] = np.ones((128, 128), np.float32).astype(ml_dtypes.bfloat16)
    idx = np.arange(128)
    same = (idx[:, None] // 64) == (idx[None, :] // 64)
    lf = (same & (idx[:, None] <= idx[None, :])).astype(np.float32)
    lb = (same & (idx[:, None] >= idx[None, :])).astype(np.float32)
    bo = same.astype(np.float32)
    mf = np.where(same & (idx[:, None] >= idx[None, :]), 0.0, NEG).astype(np.float32)
    mb = np.where(same & (idx[:, None] <= idx[None, :]), 0.0, NEG).astype(np.float32)
    nodiag = (1.0 - eye).astype(np.float32)
    f32c = np.concatenate([eye, lf, lb, bo, np.tile(mf, (1, 4)), np.tile(mb, (1, 4)), np.tile(nodiag, (1, 4)),
                           np.tile(eye, (1, 4))], axis=1)
    c["f32c"] = np.ascontiguousarray(f32c)
    pt = np.zeros((128, 128), np.float32)
    for j in range(2):
        b = 64 * j
        for i in range(8):
            pt[b + 8 + i, b + i] = -1.0
            pt[b + i, b + 8 + i] = 1.0
    c["pt_bf"] = pt.astype(ml_dtypes.bfloat16)
    inv = (np.float32(500000.0) ** (-np.arange(0, 16, 2, dtype=np.float32) / np.float32(16))).astype(np.float32)
    ang = (np.arange(smax, dtype=np.float32)[:, None] * inv[None, :]).astype(np.float32)
    cos = np.cos(ang.astype(np.float64)).astype(np.float32).T
    sin = np.sin(ang.astype(np.float64)).astype(np.float32).T
    cf = np.ones((128, smax), np.float32)
    sf = np.zeros((128, smax), np.float32)
    for j in range(2):
        b = 64 * j
        cf[b:b + 8] = cos
        cf[b + 8:b + 16] = cos
        sf[b:b + 8] = sin
        sf[b + 8:b + 16] = sin
    c["rope_c"] = cf
    c["rope_s"] = sf
    return c


def build(seqs, depth, debug=False, phases="a,g1,g2,g3,b,m,c,o"):
    nc = bass.Bass("TRN2", target_bir_lowering=False)
    ntok = sum(seqs)
    nseq = len(seqs)
    smax = max(seqs)
    offs = [sum(seqs[:i]) for i in range(nseq)]

    def din(name, shape, dt=F32):
        return nc.dram_tensor(name, list(shape), dt, kind="ExternalInput").ap()

    def dscr(name, shape, dt):
        kind = "ExternalOutput" if debug else "Internal"
        return nc.dram_tensor(name, list(shape), dt, kind=kind).ap()

    x_in = din("x", [ntok, D])
    mem_in = din("mem", [nseq * N_MEM, D])
    norm_pre = din("norm_pre", [depth, D])
    norm_post = din("norm_post", [depth, D])
    norm_mem = din("norm_mem", [depth, D])
    w_in = din("w_in", [depth, D, IN_COLS])
    gdn_conv = din("gdn_conv", [depth, 5, 1536])
    gdn_A_log = din("gdn_A_log", [depth, 2, 4])
    gdn_dt_bias = din("gdn_dt_bias", [depth, 2, 4])
    gdn_norm = din("gdn_norm", [depth, 128])
    diff_lambda = din("diff_lambda", [depth, 4, 64])
    diff_norm = din("diff_norm", [depth, 128])
    conv_w = din("conv_w", [depth, 3, 512])
    w_mem_kv = din("w_mem_kv", [depth, D, 1024])
    w_out = din("w_out", [depth, D, D])
    c_ident_bf = din("ident_bf", [128, 128], BF16)
    c_ones_bf = din("ones_bf", [128, 128], BF16)
    c_pt_bf = din("pt_bf", [128, 128], BF16)
    c_f32c = din("f32c", [128, 4 * 128 + 4 * 512])
    c_rope_c = din("rope_c", [128, smax])
    c_rope_s = din("rope_s", [128, smax])
    y_out = nc.dram_tensor("y", [ntok, D], F32, kind="ExternalOutput").ap()

    WIN = dscr("WIN", [depth, 56, 128, 16, 128], BF16)
    WDB = dscr("WDB", [depth, 128, 16, 16], BF16)
    WOUT = dscr("WOUT", [depth, 128, 16, D], BF16)
    WKV = dscr("WKV", [depth, 128, 16, 1024], BF16)
    X1 = dscr("X1", [ntok, D], F32)
    QKVT = dscr("QKVT", [1536, smax], BF16)
    DBt = dscr("DBt", [smax, 16], F32)
    AZ = dscr("AZ", [smax, 512], BF16)
    BQT = dscr("BQT", [512, smax], BF16)
    BKT = dscr("BKT", [512, smax], BF16)
    BV = dscr("BV", [smax, 512], BF16)
    BZT = dscr("BZT", [512, smax], BF16)
    CUT = dscr("CUT", [512, smax], BF16)
    CGT = dscr("CGT", [512, smax], BF16)
    MQT = dscr("MQT", [512, smax], BF16)
    MZT = dscr("MZT", [512, smax], BF16)
    GQT = dscr("GQT", [512, smax], BF16)
    GKT = dscr("GKT", [512, smax], BF16)
    GK = dscr("GK", [smax, 512], BF16)
    GV = dscr("GV", [smax, 512], BF16)
    OD = [dscr("OF", [smax, 512], F32), dscr("OB", [smax, 512], F32)]
    YT = dscr("YT", [D, smax], BF16)

    es = ExitStack()
    with es:
        kb = KB(nc, es)
        pe, act, dve, pool, sp = kb.pe, kb.act, kb.dve, kb.pool, kb.sp
        ARENA_W = 38200
        arena = es.enter_context(nc.sbuf_tensor("arena", [128, ARENA_W], F32))
        rbuf = es.enter_context(nc.sbuf_tensor("rbuf", [128, 10 * 512 + 128 + 1424], F32R))
        cst = es.enter_context(nc.sbuf_tensor("cst", [128, 7800], F32))
        banks = [es.enter_context(nc.psum_tensor(f"bank{i}", [128, 512], F32)) for i in range(8)]
        bankR = [Res(f"bank{i}") for i in range(8)]

        coff = [0]

        def calloc(words):
            o = coff[0]
            coff[0] += words
            assert coff[0] <= 7800
            return cst[:, o:o + words]

        RC = Res("consts")
        f32c = calloc(4 * 128 + 4 * 512)
        ident_f = f32c[:, 0:128]
        Lf = f32c[:, 128:256]
        Lb = f32c[:, 256:384]
        mask4 = [f32c[:, 512:1024], f32c[:, 1024:1536]]
        nodiag4 = f32c[:, 1536:2048]
        ident4 = f32c[:, 2048:2560]
        Lmat = [Lf, Lb]
        ident_bf = calloc(64).bitcast(BF16)
        ones_bf = calloc(64).bitcast(BF16)
        pt_bf = calloc(64).bitcast(BF16)
        eps_t = calloc(1)
        eps128_t = calloc(1)
        one_t = calloc(1)
        sp.dma(f32c, c_f32c, writes=[RC])
        sp.dma(ident_bf, c_ident_bf, dwrites=[RC])
        sp.dma(ones_bf, c_ones_bf, dwrites=[RC])
        sp.dma(pt_bf, c_pt_bf, dwrites=[RC])
        pool.op(lambda: nc.gpsimd.memset(eps_t, EPS), dwrites=[RC])
        pool.op(lambda: nc.gpsimd.memset(eps128_t, EPS * 128.0), dwrites=[RC])
        pool.op(lambda: nc.gpsimd.memset(one_t, 1.0), dwrites=[RC])
        ind_t = calloc(2)
        RI = Res("ind")
        pool.op(lambda: nc.gpsimd.memset(ind_t, 0.0), writes=[RI])
        pool.op(lambda: nc.gpsimd.memset(ind_t[0:64, 0:1], 1.0), writes=[RI])
        pool.op(lambda: nc.gpsimd.memset(ind_t[64:128, 1:2], 1.0), writes=[RI])
        ident_r = rbuf[:, 10 * 512:10 * 512 + 128]
        dve.op(lambda: nc.vector.tensor_copy(out=ident_r, in_=ident_f), reads=[RC], dwrites=[RC])
        rb0 = 10 * 512 + 128
        Lr = [rbuf[:, rb0:rb0 + 128], rbuf[:, rb0 + 128:rb0 + 256]]
        Bor = rbuf[:, rb0 + 256:rb0 + 384]
        maskr = [rbuf[:, rb0 + 384:rb0 + 896], rbuf[:, rb0 + 896:rb0 + 1408]]
        g_r = [rbuf[:, rb0 + 1408:rb0 + 1412], rbuf[:, rb0 + 1412:rb0 + 1416]]
        dve.op(lambda: nc.vector.tensor_copy(out=rbuf[:, rb0:rb0 + 384], in_=f32c[:, 128:512]), reads=[RC], dwrites=[RC])
        dve.op(lambda: nc.vector.tensor_copy(out=rbuf[:, rb0 + 384:rb0 + 1408], in_=f32c[:, 512:1536]), reads=[RC],
               dwrites=[RC])
        RL = Res("layerparams")
        npre_b = calloc(D)
        npost_b = calloc(D)
        gconv = calloc(60)
        cconv = calloc(12)
        gnorm_b = calloc(128)
        alog_b = calloc(8)
        dtb_b = calloc(8)
        negA_b = calloc(8)
        dnorm_c = calloc(1)
        dnorm_s = calloc(1)
        lam_b = calloc(256)
        lam_t = calloc(256)
        lam_s = calloc(2)
        lam_e = calloc(2)
        neglam = calloc(1)

        RW = Res("weights")
        for l in range(depth):
            src = w_in[l][:, 0:1536].rearrange("(kc p) (ch c) -> ch p kc c", p=128, c=128)
            chunk_cols = [c0 for c0 in range(0, 1536, 128)] + [c0 for c0 in range(C_AZ, IN_COLS, 128)]
            assert len(chunk_cols) == 56
            for ci, c0 in enumerate(chunk_cols):
                srcc = w_in[l][:, c0:c0 + 128].rearrange("(kc p) c -> p kc c", p=128)
                pool.dma(WIN[l, ci], srcc, dwrites=[RW])
            pool.dma(WDB[l], w_in[l][:, C_ADEC:C_ADEC + 16].rearrange("(kc p) c -> p kc c", p=128), dwrites=[RW])
            for q4 in range(4):
                pool.dma(WOUT[l][:, :, q4 * 512:(q4 + 1) * 512],
                         w_out[l][:, q4 * 512:(q4 + 1) * 512].rearrange("(kc p) c -> p kc c", p=128), dwrites=[RW])
            for q4 in range(2):
                pool.dma(WKV[l][:, :, q4 * 512:(q4 + 1) * 512],
                         w_mem_kv[l][:, q4 * 512:(q4 + 1) * 512].rearrange("(kc p) c -> p kc c", p=128), dwrites=[RW])

        def chunk_index(col):
            if col < 1536:
                return col // 128
            return 12 + (col - C_AZ) // 128

        class Carve:
            def __init__(self):
                self.o = 0

            def f32(self, words, shape=None):
                ap = arena[:, self.o:self.o + words]
                self.o += words
                assert self.o <= ARENA_W, self.o
                if shape is not None:
                    ap = ap.rearrange("p (a b) -> p a b", a=shape[0]) if len(shape) == 2 else ap
                return ap

            def bf(self, elems, shape=None):
                words = (elems + 1) // 2
                ap = arena[:, self.o:self.o + words].bitcast(BF16)
                self.o += words
                assert self.o <= ARENA_W, self.o
                if shape is not None and len(shape) == 2:
                    ap = ap.rearrange("p (a b) -> p a b", a=shape[0])
                return ap

            def r32(self, words, shape=None):
                ap = arena[:, self.o:self.o + words].bitcast(F32R)
                self.o += words
                assert self.o <= ARENA_W, self.o
                if shape is not None and len(shape) == 2:
                    ap = ap.rearrange("p (a b) -> p a b", a=shape[0])
                return ap

        V = nc.vector
        G = nc.gpsimd
        A_ = nc.scalar
        T = nc.tensor

        def bank_bf(i):
            return banks[i][:, :].bitcast(BF16)

        def load_layer_params(l):
            kb.barrier()
            sp.dma(npre_b, norm_pre[l].partition_broadcast(128), writes=[RL])
            sp.dma(npost_b, norm_post[l].partition_broadcast(128), dwrites=[RL])
            sp.dma(gnorm_b, gdn_norm[l].partition_broadcast(128), dwrites=[RL])
            sp.dma(alog_b, gdn_A_log[l].rearrange("a h -> (a h)").partition_broadcast(128), dwrites=[RL])
            sp.dma(dtb_b, gdn_dt_bias[l].rearrange("a h -> (a h)").partition_broadcast(128), dwrites=[RL])
            sp.dma(lam_b, diff_lambda[l].rearrange("a d -> (a d)").partition_broadcast(128), dwrites=[RL])
            with nc.allow_non_contiguous_dma(reason="tiny parameter transposes"):
                for wi in range(5):
                    sp.dma(gconv.rearrange("p (c w) -> p c w", w=5)[:, :, wi],
                           gdn_conv[l, wi].rearrange("(c p) -> p c", p=128), dwrites=[RL])
                for wi in range(3):
                    sp.dma(cconv.rearrange("p (c w) -> p c w", w=3)[:, :, wi],
                           conv_w[l, wi].rearrange("(c p) -> p c", p=128), dwrites=[RL])
                sp.dma(dnorm_c, diff_norm[l].rearrange("(p o) -> p o", o=1), dwrites=[RL])
            lam_init = 0.8 - 0.6 * math.exp(-0.3 * l)
            act.op(lambda: A_.activation(out=negA_b, in_=alog_b, func=AF.Exp), reads=[RL], dwrites=[RL])
            dve.op(lambda: V.tensor_scalar(out=negA_b, in0=negA_b, scalar1=-1.0, scalar2=None, op0=ALU.mult),
                   reads=[RL], dwrites=[RL])
            dve.op(lambda: V.tensor_scalar(out=dnorm_s, in0=dnorm_c, scalar1=1.0 - lam_init, scalar2=None,
                                           op0=ALU.mult), reads=[RL], dwrites=[RL])
            lb3 = lam_b.rearrange("p (a d) -> p a d", d=64)
            lt3 = lam_t.rearrange("p (a d) -> p a d", d=64)
            dve.op(lambda: V.tensor_tensor(out=lt3[:, 0, :], in0=lb3[:, 0, :], in1=lb3[:, 1, :], op=ALU.mult),
                   reads=[RL], dwrites=[RL])
            dve.op(lambda: V.tensor_tensor(out=lt3[:, 1, :], in0=lb3[:, 2, :], in1=lb3[:, 3, :], op=ALU.mult),
                   reads=[RL], dwrites=[RL])
            dve.op(lambda: V.reduce_sum(out=lam_s[:, 0:1], in_=lt3[:, 0, :], axis=mybir.AxisListType.X),
                   reads=[RL], dwrites=[RL])
            dve.op(lambda: V.reduce_sum(out=lam_s[:, 1:2], in_=lt3[:, 1, :], axis=mybir.AxisListType.X),
                   reads=[RL], dwrites=[RL])
            act.op(lambda: A_.activation(out=lam_e, in_=lam_s, func=AF.Exp), reads=[RL], dwrites=[RL])
            dve.op(lambda: V.tensor_tensor(out=neglam, in0=lam_e[:, 1:2], in1=lam_e[:, 0:1], op=ALU.subtract),
                   reads=[RL], dwrites=[RL])
            dve.op(lambda: V.tensor_scalar(out=neglam, in0=neglam, scalar1=-lam_init, scalar2=None, op0=ALU.add),
                   reads=[RL], dwrites=[RL])
            kb.barrier()

        def rms_rows(cv, xt, xtR, rstd, tmpR, width):
            junk, ss, rms = cv
            act.op(lambda: A_.activation(out=junk, in_=xt, func=AF.Square, accum_out=ss),
                   reads=[xtR], writes=[tmpR])
            act.op(lambda: A_.activation(out=rms, in_=ss, func=AF.Sqrt, scale=1.0 / width, bias=eps_t),
                   reads=[tmpR, RC], dwrites=[tmpR])
            dve.op(lambda: V.reciprocal(out=rstd, in_=rms), reads=[tmpR], dwrites=[tmpR])

        def phase_a(l, si, xsrc, xsrcR):
            S = seqs[si]
            TB = min(S, 1024)
            NT = min(512, TB)
            kb.barrier()
            cv = Carve()
            xts = [(cv.f32(D), Res()) for _ in range(2)]
            junk = cv.bf(D)
            hbs = [(cv.bf(D), Res()) for _ in range(2)]
            hT = cv.bf(16 * TB, (16, TB))
            hTR = Res("hT")
            ring = Rot([(cv.bf(16 * 128, (16, 128)), Res()) for _ in range(8)])
            wides = [(cv.bf(16 * 512, (16, 512)), Res("wide")) for _ in range(2)]
            wdb = cv.bf(16 * 16, (16, 16))
            wdbR = Res("wdb")
            stg = Rot([(cv.f32(512), Res()) for _ in range(8)])
            ropeC = [(cv.f32(512), Res()) for _ in range(2)]
            ropeS = [(cv.f32(512), Res()) for _ in range(2)]
            small = cv.f32(8)
            smallR = Res()
            pb = Rot([2, 3, 4, 5, 6, 7])

            order = [c * 128 for c in range(12)]
            order += [col + h * 128 for col in (C_BQ, C_BK) for h in range(4)]
            order += [col + c * 128 for col in (C_BZ, C_MQ, C_MZ) for c in range(4)]
            order += [col + c * 128 for c in range(4) for col in (C_CB, C_CC, C_CX, C_CZ)]
            LOOK = 4
            pq = {"next": 0, "ready": []}

            def _issue():
                col = order[pq["next"] % len(order)]
                pq["next"] += 1
                w, wr = ring.next()
                sp.dma(w, WIN[l, chunk_index(col)], reads=[RW], writes=[wr])
                pq["ready"].append((col, w, wr))

            def load_chunk(col):
                while len(pq["ready"]) < 1:
                    _issue()
                c0, w, wr = pq["ready"].pop(0)
                assert c0 == col, (c0, col)
                while len(pq["ready"]) < LOOK and pq["next"] < pq["limit"]:
                    _issue()
                return w, wr
            pq["limit"] = len(order) * (S // TB)

            def fm_mm(w, wr, n, bi=None):
                bi = pb.next() if bi is None else bi
                for kc in range(16):
                    pe.op(lambda kc=kc: T.matmul(banks[bi][:, 0:NT], w[:, kc, :], hT[:, kc, n * NT:(n + 1) * NT],
                                                 start=(kc == 0), stop=(kc == 15)),
                          reads=[wr, hTR], writes=[bankR[bi]] if kc == 0 else (), dwrites=() if kc == 0 else [bankR[bi]])
                return bi

            for b0 in range(0, S, TB):
                tok0 = offs[si] + b0
                for i in range(TB // 128):
                    xt, xtR = xts[i % 2]
                    hb, hbR = hbs[i % 2]
                    sp.dma(xt, xsrc[tok0 + i * 128: tok0 + (i + 1) * 128, :], reads=[xsrcR], writes=[xtR])
                    rms_rows((junk, small[:, 0:1], small[:, 1:2]), xt, xtR, small[:, 2:3], smallR, D)
                    dve.op(lambda: V.scalar_tensor_tensor(out=hb, in0=xt, scalar=small[:, 2:3], in1=npre_b,
                                                          op0=ALU.mult, op1=ALU.mult),
                           reads=[xtR, smallR, RL], writes=[hbR])
                    for half in range(2):
                        tpb = bank_bf(half)
                        for kk in range(8):
                            kc = half * 8 + kk
                            pe.op(lambda kc=kc, kk=kk, tpb=tpb: T.transpose(out=tpb[:, kk * 128:(kk + 1) * 128],
                                                                            in_=hb[:, kc * 128:(kc + 1) * 128],
                                                                            identity=ident_bf),
                                  reads=[hbR, RC], writes=[bankR[half]] if kk == 0 else (),
                                  dwrites=() if kk == 0 else [bankR[half]])
                        e = kb.ew.next()
                        dst = hT[:, half * 8:(half + 1) * 8, i * 128:(i + 1) * 128]
                        srcv = tpb.rearrange("p (k t) -> p k t", k=8)
                        if e is act:
                            act.op(lambda: A_.copy(out=dst, in_=srcv), reads=[bankR[half]], dwrites=[hTR])
                        else:
                            dve.op(lambda: V.tensor_copy(out=dst, in_=srcv), reads=[bankR[half]], dwrites=[hTR])

                def store_fm(dst, row0, n, sv, svR):
                    sp.dma(dst[row0:row0 + 128, b0 + n * NT: b0 + (n + 1) * NT], sv, reads=[svR])

                def evac_copy_bf(bi, func=None):
                    sv, svR = stg.next()
                    svb = sv.bitcast(BF16)[:, 0:NT]
                    if func is not None:
                        act.op(lambda: A_.activation(out=svb, in_=banks[bi][:, 0:NT], func=func),
                               reads=[bankR[bi]], writes=[svR])
                    else:
                        e = kb.ew.next()
                        if e is act:
                            act.op(lambda: A_.copy(out=svb, in_=banks[bi][:, 0:NT]), reads=[bankR[bi]], writes=[svR])
                        else:
                            dve.op(lambda: V.tensor_copy(out=svb, in_=banks[bi][:, 0:NT]), reads=[bankR[bi]],
                                   writes=[svR])
                    return svb, svR

                nsub = TB // NT
                for c in range(12):
                    w, wr = load_chunk(c * 128)
                    for n in range(nsub):
                        bi = fm_mm(w, wr, n)
                        svb, svR = evac_copy_bf(bi)
                        store_fm(QKVT, c * 128, n, svb, svR)
                sp.dma(wdb, WDB[l], reads=[RW], writes=[wdbR])
                for wi_, col_ in enumerate((C_AZ, C_BV)):
                    wide_, wideR_ = wides[wi_]
                    for q4 in range(4):
                        sp.dma(wide_[:, :, q4 * 128:(q4 + 1) * 128], WIN[l, chunk_index(col_ + q4 * 128)], reads=[RW],
                               writes=[wideR_] if q4 == 0 else (), dwrites=() if q4 == 0 else [wideR_])
                for _ in range(LOOK):
                    if len(pq["ready"]) < LOOK and pq["next"] < pq["limit"]:
                        _issue()
                for i in range(TB // 128):
                    bi = pb.next()
                    for kc in range(16):
                        pe.op(lambda kc=kc: T.matmul(banks[bi][:, 0:16], hT[:, kc, i * 128:(i + 1) * 128],
                                                     wdb[:, kc, :], start=(kc == 0), stop=(kc == 15)),
                              reads=[wdbR, hTR], writes=[bankR[bi]] if kc == 0 else (),
                              dwrites=() if kc == 0 else [bankR[bi]])
                    sv, svR = stg.next()
                    dve.op(lambda: V.tensor_copy(out=sv[:, 0:16], in_=banks[bi][:, 0:16]), reads=[bankR[bi]],
                           writes=[svR])
                    sp.dma(DBt[b0 + i * 128: b0 + (i + 1) * 128, :], sv[:, 0:16], reads=[svR])

                def wide_tm(wsel, dst, func):
                    wide, wideR = wides[wsel]
                    for i in range(TB // 128):
                        bi = pb.next()
                        for kc in range(16):
                            pe.op(lambda kc=kc: T.matmul(banks[bi][:, :], hT[:, kc, i * 128:(i + 1) * 128],
                                                         wide[:, kc, :], start=(kc == 0), stop=(kc == 15)),
                                  reads=[wideR, hTR], writes=[bankR[bi]] if kc == 0 else (),
                                  dwrites=() if kc == 0 else [bankR[bi]])
                        sv, svR = stg.next()
                        svb = sv.bitcast(BF16)[:, 0:512]
                        if func is not None:
                            act.op(lambda: A_.activation(out=svb, in_=banks[bi][:, :], func=func), reads=[bankR[bi]],
                                   writes=[svR])
                        else:
                            dve.op(lambda: V.tensor_copy(out=svb, in_=banks[bi][:, :]), reads=[bankR[bi]], writes=[svR])
                        sp.dma(dst[b0 + i * 128: b0 + (i + 1) * 128, :], svb, reads=[svR])

                wide_tm(0, AZ, AF.Silu)
                wide_tm(1, BV, None)
                assert nsub <= 2
                for n in range(nsub):
                    p0 = b0 + n * NT
                    sp.dma(ropeC[n][0][:, 0:NT], c_rope_c[:, p0:p0 + NT], writes=[ropeC[n][1]])
                    sp.dma(ropeS[n][0][:, 0:NT], c_rope_s[:, p0:p0 + NT], writes=[ropeS[n][1]])
                pendr = [None]
                for qk, (col, dst) in enumerate(((C_BQ, BQT), (C_BK, BKT))):
                    for h in range(4):
                        w, wr = load_chunk(col + h * 128)
                        for n in range(nsub):
                            rc, rcR = ropeC[n]
                            rs, rsR = ropeS[n]
                            bi = fm_mm(w, wr, n)
                            if pendr[0] is not None:
                                pendr[0]()
                            sv, svR = stg.next()
                            qsb = sv.bitcast(BF16)[:, 0:NT]
                            act.op(lambda: A_.copy(out=qsb, in_=banks[bi][:, 0:NT]), reads=[bankR[bi]], writes=[svR])

                            def rot_rest(qsb=qsb, svR=svR, rc=rc, rcR=rcR, rs=rs, rsR=rsR, n=n, dst=dst, h=h):
                                b2 = pb.next()
                                pe.op(lambda: T.matmul(banks[b2][:, 0:NT], pt_bf, qsb, start=True, stop=True),
                                      reads=[svR, RC], writes=[bankR[b2]])
                                t1, t1R = stg.next()
                                dve.op(lambda: V.tensor_tensor(out=t1[:, 0:NT], in0=banks[b2][:, 0:NT], in1=rs[:, 0:NT],
                                                               op=ALU.mult), reads=[bankR[b2], rsR], writes=[t1R])
                                t2, t2R = stg.next()
                                pool.op(lambda: G.tensor_tensor(out=t2[:, 0:NT], in0=qsb, in1=rc[:, 0:NT], op=ALU.mult),
                                        reads=[svR, rcR], writes=[t2R])
                                o, oR = stg.next()
                                ob = o.bitcast(BF16)[:, 0:NT]
                                dve.op(lambda: V.tensor_tensor(out=ob, in0=t1[:, 0:NT], in1=t2[:, 0:NT], op=ALU.add),
                                       reads=[t1R, t2R], writes=[oR])
                                store_fm(dst, h * 128, n, ob, oR)
                            pendr[0] = rot_rest
                pendr[0]()
                for col, dst, func in ((C_BZ, BZT, AF.Silu), (C_MQ, MQT, None), (C_MZ, MZT, AF.Silu)):
                    for c in range(4):
                        w, wr = load_chunk(col + c * 128)
                        for n in range(nsub):
                            bi = fm_mm(w, wr, n)
                            svb, svR = evac_copy_bf(bi, func)
                            store_fm(dst, c * 128, n, svb, svR)
                for c in range(4):
                    ws = [load_chunk(col + c * 128) for col in (C_CB, C_CC, C_CX, C_CZ)]
                    for n in range(nsub):
                        bis = [fm_mm(w, wr, n) for (w, wr) in ws]
                        ccs, ccR = stg.next()
                        act.op(lambda: A_.copy(out=ccs[:, 0:NT], in_=banks[bis[1]][:, 0:NT]), reads=[bankR[bis[1]]],
                               writes=[ccR])
                        u, uR = stg.next()
                        ub = u.bitcast(BF16)[:, 0:NT]
                        dve.op(lambda: V.tensor_tensor(out=ub, in0=banks[bis[2]][:, 0:NT], in1=ccs[:, 0:NT],
                                                       op=ALU.mult), reads=[bankR[bis[2]], ccR], writes=[uR])
                        store_fm(CUT, c * 128, n, ub, uR)
                        sz, szR = stg.next()
                        act.op(lambda: A_.activation(out=sz[:, 0:NT], in_=banks[bis[3]][:, 0:NT], func=AF.Silu),
                               reads=[bankR[bis[3]]], writes=[szR])
                        g, gR = stg.next()
                        gb = g.bitcast(BF16)[:, 0:NT]
                        dve.op(lambda: V.tensor_tensor(out=gb, in0=banks[bis[0]][:, 0:NT], in1=sz[:, 0:NT],
                                                       op=ALU.mult), reads=[bankR[bis[0]], szR], writes=[gR])
                        store_fm(CGT, c * 128, n, gb, gR)
            kb.barrier()

        def phase_g1(l, si):
            S = seqs[si]
            TG = min(S, 512)
            kb.barrier()
            cv = Carve()
            xin = Rot([(cv.bf(TG + 4), Res()) for _ in range(2)])
            acc = Rot([(cv.f32(TG), Res()) for _ in range(2)])
            ys = Rot([(cv.f32(TG), Res()) for _ in range(2)])
            sq = Rot([(cv.bf(TG), Res()) for _ in range(2)])
            rn = Rot([(cv.f32(TG), Res()) for _ in range(2)])
            yn = Rot([(cv.bf(TG), Res()) for _ in range(3)])
            tk = Rot([(cv.bf(TG), Res()) for _ in range(2)])
            pb = Rot([0, 1, 2, 3])
            pt = Rot([4, 5, 6, 7])
            nt = TG // 128
            items = [(b0, c) for b0 in range(0, S, TG) for c in range(12)]

            def pre(b0, c):
                xi, xiR = xin.next()
                lo = max(0, b0 - 2)
                hi = min(S, b0 + TG + 2)
                pool.op(lambda: G.memset(xi, 0.0), writes=[xiR])
                sp.dma(xi[:, lo - (b0 - 2): hi - (b0 - 2)], QKVT[c * 128:(c + 1) * 128, lo:hi], dwrites=[xiR])
                return xi, xiR
            def chunk(b0, c, xi, xiR):
                a, aR = acc.next()
                dve.op(lambda: V.tensor_scalar(out=a, in0=xi[:, 0:TG], scalar1=gconv[:, c * 5:c * 5 + 1],
                                               scalar2=None, op0=ALU.mult), reads=[xiR, RL], writes=[aR])
                for wi in range(1, 5):
                    dve.op(lambda wi=wi: V.scalar_tensor_tensor(out=a, in0=xi[:, wi:wi + TG],
                                                                scalar=gconv[:, c * 5 + wi:c * 5 + wi + 1], in1=a,
                                                                op0=ALU.mult, op1=ALU.add),
                           reads=[xiR, RL], writes=[aR])
                h = c % 4
                if c < 8:
                    y, yR = ys.next()
                    act.op(lambda: A_.activation(out=y, in_=a, func=AF.Silu), reads=[aR], writes=[yR])
                    s2, s2R = sq.next()
                    act.op(lambda: A_.activation(out=s2, in_=y, func=AF.Square), reads=[yR], writes=[s2R])
                    bi = pb.next()
                    pe.op(lambda: T.matmul(banks[bi][:, 0:TG], ones_bf, s2, start=True, stop=True),
                          reads=[s2R, RC], writes=[bankR[bi]])
                    r, rR = rn.next()
                    if c < 4:
                        act.op(lambda: A_.activation(out=r, in_=banks[bi][:, 0:TG], func=AF.Sqrt, scale=128.0,
                                                     bias=eps128_t), reads=[bankR[bi], RC], writes=[rR])
                    else:
                        act.op(lambda: A_.activation(out=r, in_=banks[bi][:, 0:TG], func=AF.Sqrt, scale=1.0,
                                                     bias=eps_t), reads=[bankR[bi], RC], writes=[rR])
                    yield
                    dve.op(lambda: V.reciprocal(out=r, in_=r), reads=[rR], writes=[rR])
                    o, oR = yn.next()
                    pool.op(lambda: G.tensor_tensor(out=o, in0=y, in1=r, op=ALU.mult), reads=[yR, rR],
                            writes=[oR])
                    dst = GQT if c < 4 else GKT
                    sp.dma(dst[h * 128:(h + 1) * 128, b0:b0 + TG], o, reads=[oR])
                else:
                    o, oR = yn.next()
                    act.op(lambda: A_.activation(out=o, in_=a, func=AF.Silu), reads=[aR], writes=[oR])
                    yield
                if c >= 4:
                    bt = pt.next()
                    tpb = bank_bf(bt)
                    for i in range(nt):
                        pe.op(lambda i=i: T.transpose(out=tpb[:, i * 128:(i + 1) * 128],
                                                      in_=o[:, i * 128:(i + 1) * 128], identity=ident_bf),
                              reads=[oR, RC], writes=[bankR[bt]] if i == 0 else (),
                              dwrites=() if i == 0 else [bankR[bt]])
                    t, tR = tk.next()
                    act.op(lambda: A_.copy(out=t, in_=tpb[:, 0:TG]), reads=[bankR[bt]], writes=[tR])
                    dst = GK if c < 8 else GV
                    sp.dma(dst[b0:b0 + TG, h * 128:(h + 1) * 128].rearrange("(i p) d -> p i d", p=128),
                           t.rearrange("p (i d) -> p i d", d=128), reads=[tR])
            pend = pre(*items[0])
            prev = None
            for ii, (b0, c) in enumerate(items):
                xi, xiR = pend
                if ii + 1 < len(items):
                    pend = pre(*items[ii + 1])
                g_ = chunk(b0, c, xi, xiR)
                next(g_)
                if prev is not None:
                    for _ in prev:
                        pass
                prev = g_
            for _ in prev:
                pass
            kb.barrier()

        def phase_g2(l, si):
            S = seqs[si]
            ntile = S // 128
            kb.barrier()
            cv = Carve()

            def mk(fn, *a):
                return (fn(*a), Res())

            def alloc_set(d):
                B = {}
                B["qT"] = mk(cv.bf, 512, (4, 128))
                B["kT"] = mk(cv.bf, 512, (4, 128))
                B["ktok"] = mk(cv.bf, 512, (4, 128))
                B["vtok"] = mk(cv.bf, 512, (4, 128))
                B["db"] = mk(cv.f32, 16)
                B["sm"] = mk(cv.f32, 64)
                B["E"] = mk(cv.f32, 512, (4, 128))
                B["En"] = mk(cv.f32, 512, (4, 128))
                B["EG"] = mk(cv.f32, 512, (4, 128))
                rv = lambda i: rbuf[:, (5 * d + i) * 512:(5 * d + i + 1) * 512].rearrange("p (a b) -> p a b", a=4)
                B["Xs"] = [(rv(0), Res()), (rv(1), Res())]
                B["Ys"] = [(rv(2), Res()), (rv(3), Res())]
                B["R"] = rv(4), Res()
                B["Rb"] = mk(cv.bf, 512, (4, 128))
                B["attn"] = mk(cv.bf, 512, (4, 128))
                B["attnT"] = mk(cv.bf, 512, (4, 128))
                B["vb"] = mk(cv.bf, 512, (4, 128))
                B["kbg"] = mk(cv.bf, 512, (4, 128))
                B["kg"] = mk(cv.bf, 512, (4, 128))
                B["kg1"] = mk(cv.bf, 512, (4, 128))
                B["qgT"] = mk(cv.bf, 512, (4, 128))
                B["u"] = mk(cv.f32, 512, (4, 128))
                B["wT"] = mk(cv.bf, 512, (4, 128))
                B["vn"] = mk(cv.bf, 512, (4, 128))
                B["ost"] = mk(cv.f32, 512)
                B["S32"] = mk(cv.f32, 512, (4, 128))
                B["Sbf"] = mk(cv.bf, 512, (4, 128))
                return B
            bufs = [alloc_set(0), alloc_set(1)]
            for d in range(2):
                pool.op(lambda d=d: G.memset(bufs[d]["S32"][0], 0.0), writes=[bufs[d]["S32"][1]])
                pool.op(lambda d=d: G.memset(bufs[d]["Sbf"][0], 0.0), writes=[bufs[d]["Sbf"][1]])
                pool.op(lambda d=d: G.memset(bufs[d]["vn"][0], 0.0), writes=[bufs[d]["vn"][1]])
            def unit(d, ti, B):
                qT, qTR = B["qT"]
                kT, kTR = B["kT"]
                ktok, ktokR = B["ktok"]
                vtok, vtokR = B["vtok"]
                db, dbR = B["db"]
                sm, smR = B["sm"]
                E, ER = B["E"]
                En, EnR = B["En"]
                EG, EGR = B["EG"]
                Xs = B["Xs"]
                Ys = B["Ys"]
                R, RR = B["R"]
                Rb, RbR = B["Rb"]
                attn, attnR = B["attn"]
                attnT, attnTR = B["attnT"]
                vb, vbR = B["vb"]
                kbg, kbgR = B["kbg"]
                kg, kgR = B["kg"]
                kg1, kg1R = B["kg1"]
                qgT, qgTR = B["qgT"]
                u, uR = B["u"]
                wT, wTR = B["wT"]
                vn, vnR = B["vn"]
                ost, ostR = B["ost"]
                t0 = ti * 128
                bk = [4 * d + i_ for i_ in range(4)]
                s32, s32R = B["S32"]
                sbf, sbfR = B["Sbf"]
                sp.dma(qT, GQT[:, t0:t0 + 128].rearrange("(h p) t -> p h t", p=128), writes=[qTR])
                sp.dma(kT, GKT[:, t0:t0 + 128].rearrange("(h p) t -> p h t", p=128), writes=[kTR])
                sp.dma(ktok, GK[t0:t0 + 128, :].rearrange("t (h e) -> t h e", e=128), writes=[ktokR])
                sp.dma(vtok, GV[t0:t0 + 128, :].rearrange("t (h e) -> t h e", e=128), writes=[vtokR])
                sp.dma(db, DBt[t0:t0 + 128, :], writes=[dbR])
                dec = db[:, d * 4:d * 4 + 4]
                bet = db[:, 8 + d * 4:8 + d * 4 + 4]
                beta, nbeta, ee, g, gc, dl, kgf, egc, kbgf, spin = [sm[:, 4 * i:4 * i + 4] for i in range(10)]
                act.op(lambda: A_.activation(out=beta, in_=bet, func=AF.Sigmoid), reads=[dbR], writes=[smR])
                dve.op(lambda: V.tensor_scalar(out=nbeta, in0=beta, scalar1=-1.0, scalar2=None, op0=ALU.mult),
                       reads=[smR], dwrites=[smR])
                dve.op(lambda: V.tensor_tensor(out=spin, in0=dec, in1=dtb_b[:, d * 4:d * 4 + 4], op=ALU.add),
                       reads=[dbR, RL, smR], dwrites=[smR])
                act.op(lambda: A_.activation(out=ee, in_=spin, func=AF.Exp), reads=[smR], dwrites=[smR])
                act.op(lambda: A_.activation(out=ee, in_=ee, func=AF.Ln, bias=one_t), reads=[smR, RC],
                       dwrites=[smR])
                g = g_r[d]
                dve.op(lambda: V.tensor_tensor(out=g, in0=ee, in1=negA_b[:, d * 4:d * 4 + 4], op=ALU.mult),
                       reads=[smR, RL], dwrites=[smR])
                pe.op(lambda: T.matmul(banks[bk[0]][:, 0:4], Lr[d], g, start=True, stop=True), reads=[smR, RC],
                      writes=[bankR[bk[0]]])
                pe.op(lambda: T.matmul(banks[bk[0]][:, 4:8], Bor, g, start=True, stop=True),
                      reads=[smR, RC], dwrites=[bankR[bk[0]]])
                dve.op(lambda: V.tensor_copy(out=gc, in_=banks[bk[0]][:, 0:4]), reads=[bankR[bk[0]], smR], dwrites=[smR])
                dve.op(lambda: V.tensor_tensor(out=dl, in0=banks[bk[0]][:, 4:8], in1=gc, op=ALU.subtract),
                       reads=[bankR[bk[0]], smR], dwrites=[smR])
                act.op(lambda: A_.activation(out=kgf, in_=dl, func=AF.Exp), reads=[smR], dwrites=[smR])
                act.op(lambda: A_.activation(out=egc, in_=gc, func=AF.Exp), reads=[smR], dwrites=[smR])
                dve.op(lambda: V.tensor_tensor(out=kbgf, in0=egc, in1=beta, op=ALU.mult), reads=[smR],
                       dwrites=[smR])
                yield
                for h in range(4):
                    pe.op(lambda h=h: T.matmul(banks[bk[1]][:, h * 128:(h + 1) * 128],
                                               g[:, h:h + 1].to_broadcast([128, 128]), Lr[d],
                                               start=True, stop=True), reads=[smR, RC],
                          writes=[bankR[bk[1]]] if h == 0 else (), dwrites=() if h == 0 else [bankR[bk[1]]])
                pe.op(lambda: T.matmul(banks[bk[2]][:, :], ident_r, maskr[d], start=True, stop=False), reads=[RC],
                      writes=[bankR[bk[2]]])
                for h in range(4):
                    pe.op(lambda h=h: T.matmul(banks[bk[2]][:, h * 128:(h + 1) * 128],
                                               g[:, h:h + 1].to_broadcast([128, 128]), Lr[d],
                                               start=False, stop=(h == 3)), reads=[smR, RC], dwrites=[bankR[bk[2]]])
                for h in range(4):
                    act.op(lambda h=h: A_.activation(out=E[:, h, :], in_=banks[bk[2]][:, h * 128:(h + 1) * 128],
                                                     func=AF.Exp, scale=-1.0, bias=gc[:, h:h + 1]),
                           reads=[bankR[bk[2]], smR], writes=[ER] if h == 0 else (), dwrites=() if h == 0 else [ER])
                act.op(lambda: A_.activation(out=EG.rearrange("p h c -> p (h c)"), in_=banks[bk[1]][:, :], func=AF.Exp),
                       reads=[bankR[bk[1]]], writes=[EGR])
                pool.op(lambda: G.tensor_tensor(out=En.rearrange("p h c -> p (h c)"),
                                                in0=E.rearrange("p h c -> p (h c)"), in1=nodiag4, op=ALU.mult),
                        reads=[ER, RC], writes=[EnR])
                yield
                for h in range(4):
                    pe.op(lambda h=h: T.matmul(banks[bk[3]][:, h * 128:(h + 1) * 128], kT[:, h, :], kT[:, h, :],
                                               start=True, stop=True), reads=[kTR],
                          writes=[bankR[bk[3]]] if h == 0 else (), dwrites=() if h == 0 else [bankR[bk[3]]])
                for h in range(4):
                    pe.op(lambda h=h: T.matmul(banks[bk[0]][:, h * 128:(h + 1) * 128], qT[:, h, :], kT[:, h, :],
                                               start=True, stop=True), reads=[kTR, qTR],
                          writes=[bankR[bk[0]]] if h == 0 else (), dwrites=() if h == 0 else [bankR[bk[0]]])
                X0, X0R = Xs[0]
                Y0, Y0R = Ys[0]
                for h in range(4):
                    dve.op(lambda h=h: V.scalar_tensor_tensor(out=X0[:, h, :], in0=banks[bk[3]][:, h * 128:(h + 1) * 128],
                                                              scalar=nbeta[:, h:h + 1], in1=En[:, h, :],
                                                              op0=ALU.mult, op1=ALU.mult),
                           reads=[bankR[bk[3]], smR, EnR], writes=[X0R] if h == 0 else (),
                           dwrites=() if h == 0 else [X0R])
                dve.op(lambda: V.tensor_tensor(out=attn.rearrange("p h c -> p (h c)"), in0=banks[bk[0]][:, :],
                                               in1=E.rearrange("p h c -> p (h c)"), op=ALU.mult),
                       reads=[bankR[bk[0]], ER], writes=[attnR])
                yield
                b5r = banks[bk[1]][:, :].bitcast(F32R)
                for h in range(4):
                    pe.op(lambda h=h: T.matmul(banks[bk[1]][:, h * 128:(h + 1) * 128], X0[:, h, :], ident_r,
                                               start=True, stop=True),
                          reads=[X0R, RC],
                          writes=[bankR[bk[1]]] if h == 0 else (), dwrites=() if h == 0 else [bankR[bk[1]]])
                b6 = bank_bf(bk[2])
                for h in range(4):
                    pe.op(lambda h=h: T.transpose(out=b6[:, h * 128:(h + 1) * 128], in_=attn[:, h, :],
                                                  identity=ident_bf), reads=[attnR, RC],
                          writes=[bankR[bk[2]]] if h == 0 else (), dwrites=() if h == 0 else [bankR[bk[2]]])
                act.op(lambda: A_.copy(out=Y0.rearrange("p h c -> p (h c)"), in_=banks[bk[1]][:, :]),
                       reads=[bankR[bk[1]]], writes=[Y0R])
                dve.op(lambda: V.tensor_tensor(out=R.rearrange("p h c -> p (h c)"),
                                               in0=Y0.rearrange("p h c -> p (h c)").bitcast(F32),
                                               in1=ident4, op=ALU.add), reads=[Y0R, RC], writes=[RR])
                act.op(lambda: A_.copy(out=attnT.rearrange("p h c -> p (h c)"), in_=b6[:, 0:512]),
                       reads=[bankR[bk[2]]], writes=[attnTR])


                yield
                def mm4(bi, lhs, lhsR, rhs, rhsR):
                    for h in range(4):
                        pe.op(lambda h=h: T.matmul(banks[bi][:, h * 128:(h + 1) * 128], lhs[:, h, :], rhs[:, h, :],
                                                   start=True, stop=True), reads=[lhsR, rhsR],
                              writes=[bankR[bi]] if h == 0 else (), dwrites=() if h == 0 else [bankR[bi]])

                cur = 0
                for lev in range(1, 6):
                    Xc, XcR = Xs[cur]
                    Yc, YcR = Ys[cur]
                    Xn, XnR = Xs[1 - cur]
                    Yn, YnR = Ys[1 - cur]
                    if lev >= 2:
                        pass
                    if lev >= 2:
                        mm4(bk[3], Xc, XcR, R, RR)
                        dve.op(lambda: V.tensor_tensor(out=R.rearrange("p h c -> p (h c)"), in0=banks[bk[3]][:, :],
                                                       in1=R.rearrange("p h c -> p (h c)").bitcast(F32),
                                                       op=ALU.add), reads=[bankR[bk[3]], RR], writes=[RR])
                    mm4(bk[1], Yc, YcR, Xc, XcR)
                    act.op(lambda Xn=Xn: A_.copy(out=Xn.rearrange("p h c -> p (h c)"), in_=banks[bk[1]][:, :]),
                           reads=[bankR[bk[1]]], writes=[XnR])
                    if lev <= 4:
                        mm4(bk[2], Xc, XcR, Yc, YcR)
                        dve.op(lambda Yn=Yn: V.tensor_copy(out=Yn.rearrange("p h c -> p (h c)"), in_=banks[bk[2]][:, :]),
                               reads=[bankR[bk[2]]], writes=[YnR])
                    cur = 1 - cur
                    yield
                Xc, XcR = Xs[cur]
                mm4(bk[3], Xc, XcR, R, RR)
                dve.op(lambda: V.tensor_tensor(out=R.rearrange("p h c -> p (h c)"), in0=banks[bk[3]][:, :],
                                               in1=R.rearrange("p h c -> p (h c)").bitcast(F32), op=ALU.add),
                       reads=[bankR[bk[3]], RR], writes=[RR])
                act.op(lambda: A_.copy(out=Rb.rearrange("p h c -> p (h c)"),
                                       in_=R.rearrange("p h c -> p (h c)").bitcast(F32)), reads=[RR],
                       writes=[RbR])
                yield
                kgf0, kgf1 = sm[:, 40:44], sm[:, 44:48]
                dve.op(lambda: V.tensor_scalar(out=kgf0, in0=kgf, scalar1=ind_t[:, 0:1], scalar2=None, op0=ALU.mult),
                       reads=[smR, RI], dwrites=[smR])
                dve.op(lambda: V.tensor_scalar(out=kgf1, in0=kgf, scalar1=ind_t[:, 1:2], scalar2=None, op0=ALU.mult),
                       reads=[smR, RI], dwrites=[smR])
                for h in range(4):
                    act.op(lambda h=h: A_.activation(out=vb[:, h, :], in_=vtok[:, h, :], func=AF.Copy,
                                                     scale=beta[:, h:h + 1]), reads=[vtokR, smR],
                           writes=[vbR] if h == 0 else (), dwrites=() if h == 0 else [vbR])
                    dve.op(lambda h=h: V.tensor_scalar(out=kbg[:, h, :], in0=ktok[:, h, :],
                                                       scalar1=kbgf[:, h:h + 1], scalar2=None, op0=ALU.mult),
                           reads=[ktokR, smR], writes=[kbgR] if h == 0 else (), dwrites=() if h == 0 else [kbgR])
                    pool.op(lambda h=h: G.tensor_scalar(out=kg[:, h, :], in0=ktok[:, h, :],
                                                        scalar1=kgf0[:, h:h + 1], scalar2=1.0, op0=ALU.mult,
                                                        op1=ALU.mult),
                            reads=[ktokR, smR], writes=[kgR] if h == 0 else (), dwrites=() if h == 0 else [kgR])
                    dve.op(lambda h=h: V.tensor_scalar(out=kg1[:, h, :], in0=ktok[:, h, :],
                                                       scalar1=kgf1[:, h:h + 1], scalar2=None, op0=ALU.mult),
                           reads=[ktokR, smR], writes=[kg1R] if h == 0 else (), dwrites=() if h == 0 else [kg1R])
                pool.op(lambda: G.tensor_tensor(out=qgT.rearrange("p h c -> p (h c)"),
                                                in0=qT.rearrange("p h c -> p (h c)"),
                                                in1=EG.rearrange("p h c -> p (h c)"), op=ALU.mult),
                        reads=[qTR, EGR], writes=[qgTR])
                mm4(bk[3], Rb, RbR, vb, vbR)
                act.op(lambda: A_.copy(out=u.rearrange("p h c -> p (h c)"), in_=banks[bk[3]][:, :]), reads=[bankR[bk[3]]],
                       writes=[uR])
                mm4(bk[0], kbg, kbgR, Rb, RbR)
                dve.op(lambda: V.tensor_copy(out=wT.rearrange("p h c -> p (h c)"), in_=banks[bk[0]][:, :]),
                       reads=[bankR[bk[0]]], writes=[wTR])
                yield
                for step in range(2):
                    r0 = (0, 64)[step] if d == 0 else (64, 0)[step]
                    rows = slice(r0, r0 + 64)
                    for h in range(4):
                        pe.op(lambda h=h: T.matmul(banks[bk[0]][:, h * 128:(h + 1) * 128], wT[:, h, :],
                                                   sbf[:, h, :], start=True, stop=True), reads=[wTR, sbfR],
                              writes=[bankR[bk[0]]] if h == 0 else (), dwrites=() if h == 0 else [bankR[bk[0]]])
                    dve.op(lambda: V.tensor_tensor(out=vn[rows].rearrange("p h c -> p (h c)"),
                                                   in0=u[rows].rearrange("p h c -> p (h c)"),
                                                   in1=banks[bk[0]][rows, :], op=ALU.subtract),
                           reads=[bankR[bk[0]], uR], writes=[vnR])
                    for h in range(4):
                        pe.op(lambda h=h: T.matmul(banks[bk[1]][:, h * 128:(h + 1) * 128], qgT[:, h, :],
                                                   sbf[:, h, :], start=True, stop=False), reads=[qgTR, sbfR],
                              writes=[bankR[bk[1]]] if h == 0 else (), dwrites=() if h == 0 else [bankR[bk[1]]])
                        pe.op(lambda h=h: T.matmul(banks[bk[1]][:, h * 128:(h + 1) * 128], attnT[:, h, :],
                                                   vn[:, h, :], start=False, stop=True), reads=[attnTR, vnR],
                              dwrites=[bankR[bk[1]]])
                    for h in range(4):
                        pe.op(lambda h=h: T.matmul(banks[bk[2]][:, h * 128:(h + 1) * 128], (kg, kg1)[r0 // 64][:, h, :],
                                                   vn[:, h, :], start=True, stop=True), reads=[kgR, kg1R, vnR],
                              writes=[bankR[bk[2]]] if h == 0 else (), dwrites=() if h == 0 else [bankR[bk[2]]])
                    col = r0 + 63 if d == 0 else r0
                    for h in range(4):
                        dve.op(lambda h=h: V.scalar_tensor_tensor(out=s32[:, h, :], in0=s32[:, h, :],
                                                                  scalar=EG[:, h, col:col + 1],
                                                                  in1=banks[bk[2]][:, h * 128:(h + 1) * 128],
                                                                  op0=ALU.mult, op1=ALU.add),
                               reads=[bankR[bk[2]], EGR], writes=[s32R] if h == 0 else (),
                               dwrites=() if h == 0 else [s32R])
                    act.op(lambda: A_.copy(out=sbf.rearrange("p h c -> p (h c)"),
                                           in_=s32.rearrange("p h c -> p (h c)")), reads=[s32R], writes=[sbfR])
                    act.op(lambda: A_.copy(out=ost[rows, :], in_=banks[bk[1]][rows, :]), reads=[bankR[bk[1]]],
                           writes=[ostR] if step == 0 else (), dwrites=() if step == 0 else [ostR])
                    yield
                sp.dma(OD[d][t0:t0 + 128, :], ost, reads=[ostR])
            for it in range(ntile):
                alive = [unit(0, it, bufs[0]), unit(1, ntile - 1 - it, bufs[1])]
                while alive:
                    for g_ in list(alive):
                        try:
                            next(g_)
                        except StopIteration:
                            alive.remove(g_)
            kb.barrier()

        def phase_g3(l, si):
            S = seqs[si]
            kb.barrier()
            cv = Carve()
            ofs = Rot([(cv.f32(512), Res()) for _ in range(2)])
            obs = Rot([(cv.f32(512), Res()) for _ in range(2)])
            azs = Rot([(cv.bf(512), Res()) for _ in range(2)])
            junk = cv.f32(128)
            sm, smR = cv.f32(16), Res()
            ys = Rot([(cv.f32(512), Res()) for _ in range(2)])
            ybs = Rot([(cv.bf(512), Res()) for _ in range(2)])
            sts = Rot([(cv.bf(512), Res()) for _ in range(2)])
            pb = Rot([0, 1])
            def pre_g3(ti):
                t0 = ti * 128
                of, ofR = ofs.next()
                ob, obR = obs.next()
                az, azR = azs.next()
                sp.dma(of, OD[0][t0:t0 + 128, :], writes=[ofR])
                sp.dma(ob, OD[1][t0:t0 + 128, :], writes=[obR])
                sp.dma(az, AZ[t0:t0 + 128, :], writes=[azR])
                return of, ofR, ob, obR, az, azR
            pend = pre_g3(0)
            for ti in range(S // 128):
                t0 = ti * 128
                of, ofR, ob, obR, az, azR = pend
                if ti + 1 < S // 128:
                    pend = pre_g3(ti + 1)
                pool.op(lambda: G.tensor_tensor(out=of, in0=of, in1=ob, op=ALU.add), reads=[obR], writes=[ofR])
                for h in range(4):
                    act.op(lambda h=h: A_.activation(out=junk, in_=of[:, h * 128:(h + 1) * 128], func=AF.Square,
                                                     accum_out=sm[:, h:h + 1]), reads=[ofR], writes=[smR])
                act.op(lambda: A_.activation(out=sm[:, 4:8], in_=sm[:, 0:4], func=AF.Sqrt, scale=1.0 / 128, bias=eps_t),
                       reads=[smR, RC], dwrites=[smR])
                dve.op(lambda: V.reciprocal(out=sm[:, 8:12], in_=sm[:, 4:8]), reads=[smR], dwrites=[smR])
                y, yR = ys.next()
                for h in range(4):
                    dve.op(lambda h=h: V.scalar_tensor_tensor(out=y[:, h * 128:(h + 1) * 128],
                                                              in0=of[:, h * 128:(h + 1) * 128],
                                                              scalar=sm[:, 8 + h:9 + h], in1=gnorm_b, op0=ALU.mult,
                                                              op1=ALU.mult), reads=[ofR, smR, RL],
                           writes=[yR] if h == 0 else (), dwrites=() if h == 0 else [yR])
                yb, ybR = ybs.next()
                pool.op(lambda: G.tensor_tensor(out=yb, in0=y, in1=az, op=ALU.mult), reads=[yR, azR], writes=[ybR])
                bi = pb.next()
                tpb = bank_bf(bi)
                for h in range(4):
                    pe.op(lambda h=h: T.transpose(out=tpb[:, h * 128:(h + 1) * 128], in_=yb[:, h * 128:(h + 1) * 128],
                                                  identity=ident_bf), reads=[ybR, RC],
                          writes=[bankR[bi]] if h == 0 else (), dwrites=() if h == 0 else [bankR[bi]])
                st, stR = sts.next()
                act.op(lambda: A_.copy(out=st, in_=tpb[:, 0:512]), reads=[bankR[bi]], writes=[stR])
                sp.dma(YT[0:512, t0:t0 + 128].rearrange("(h e) t -> e h t", e=128),
                       st.rearrange("p (h t) -> p h t", t=128), reads=[stR])
            kb.barrier()

        def softmax_pv(items, QB, scale, bsc, pts, hook=None):
            n = len(items)
            sc = [None] * n

            def qk(i):
                it = items[i]
                bs = bsc.next()
                sc[i] = bs
                pe.op(lambda: T.matmul(banks[bs][:, 0:QB], it["kT"], it["q"], start=True, stop=True),
                      reads=[it["kR"], it["qR"]], writes=[bankR[bs]])
            for i in range(min(2, n)):
                qk(i)
            for i in range(n):
                it = items[i]
                bs = sc[i]
                bo, bd = it["bo"], it["bd"]
                p, pR = pts.next()
                act.op(lambda: A_.activation(out=p[:, 0:QB], in_=banks[bs][:, 0:QB], func=AF.Exp, scale=scale),
                       reads=[bankR[bs]], writes=[pR])
                pe.op(lambda: T.matmul(banks[bo][:, 0:QB], it["v"], p[:, 0:QB], start=it["first"], stop=it["last"]),
                      reads=[it["vR"], pR], writes=[bankR[bo]] if it["first"] else (),
                      dwrites=() if it["first"] else [bankR[bo]])
                pe.op(lambda: T.matmul(banks[bd][:, 0:QB], ones_bf, p[:, 0:QB], start=it["first"], stop=it["last"]),
                      reads=[RC, pR], writes=[bankR[bd]] if it["first"] else (),
                      dwrites=() if it["first"] else [bankR[bd]])
                if i + 2 < n:
                    qk(i + 2)
                if hook is not None and i == min(3, n - 1):
                    hook()

        def phase_b(l, si):
            S = seqs[si]
            QB = min(512, S)
            nkc = S // 128
            kb.barrier()
            cv = Carve()
            kTs = Rot([(cv.bf(S), (cv.bf(S), cv.bf(S)), cv.bf(S, (S // 128, 128)), Res()) for _ in range(2)])
            for (_k, (qz0, qz1), _v, kvR0) in kTs.items:
                pool.op(lambda: G.memset(qz0[64:128, :], 0.0), writes=[kvR0])
                pool.op(lambda: G.memset(qz1[0:64, :], 0.0), dwrites=[kvR0])
            zts = Rot([(cv.bf(QB), Res()) for _ in range(3)])
            pts = Rot([(cv.bf(512), Res()) for _ in range(4)])
            tmp = Rot([(cv.f32(512), Res()) for _ in range(6)])
            sqs = Rot([(cv.bf(512), Res()) for _ in range(2)])
            outs = Rot([(cv.bf(512), Res()) for _ in range(2)])
            bsc = Rot([0, 1, 7])

            def load_head(h):
                kTh, qTh, vh, kvR = kTs.next()
                sp.dma(kTh, BKT[h * 128:(h + 1) * 128, 0:S], writes=[kvR])
                sp.dma(qTh[0][0:64, :], BQT[h * 128:h * 128 + 64, 0:S], dwrites=[kvR])
                sp.dma(qTh[1][64:128, :], BQT[h * 128 + 64:(h + 1) * 128, 0:S], dwrites=[kvR])
                sp.dma(vh, BV[0:S, h * 128:(h + 1) * 128].rearrange("(i p) e -> p i e", p=128), dwrites=[kvR])
                return kTh, qTh, vh, kvR

            def load_z(h, qb):
                zt, ztR = zts.next()
                sp.dma(zt, BZT[h * 128:(h + 1) * 128, qb * QB:(qb + 1) * QB], writes=[ztR])
                return zt, ztR
            nqb = S // QB
            hq = [(h, qb) for h in range(4) for qb in range(nqb)]
            head_next = load_head(0)
            z_next = load_z(0, 0)
            pend2 = [None]
            for ii, (h, qb) in enumerate(hq):
                if qb == 0:
                    kTh, qTh, vh, kvR = head_next
                    if h + 1 < 4:
                        head_next = load_head(h + 1)
                zt, ztR = z_next
                if ii + 1 < len(hq):
                    z_next = load_z(*hq[ii + 1])
                if True:
                    q0 = qb * QB
                    items = []
                    for j in range(2):
                        for kc in range(nkc):
                            items.append(dict(kT=kTh[:, kc * 128:(kc + 1) * 128], q=qTh[j][:, q0:q0 + QB],
                                              v=vh[:, kc, :], kR=kvR, qR=kvR, vR=kvR, bo=2 + j, bd=4 + j,
                                              first=(kc == 0), last=(kc == nkc - 1)))
                    softmax_pv(items, QB, 0.125, bsc, pts, hook=pend2[0])
                    r0, r0R = tmp.next()
                    r1, r1R = tmp.next()
                    dve.op(lambda: V.reciprocal(out=r0[:, 0:QB], in_=banks[4][:, 0:QB]), reads=[bankR[4]], writes=[r0R])
                    dve.op(lambda: V.reciprocal(out=r1[:, 0:QB], in_=banks[5][:, 0:QB]), reads=[bankR[5]], writes=[r1R])
                    o0, o0R = tmp.next()
                    o1, o1R = tmp.next()
                    dve.op(lambda: V.tensor_tensor(out=o0[:, 0:QB], in0=banks[2][:, 0:QB], in1=r0[:, 0:QB], op=ALU.mult),
                           reads=[bankR[2], r0R], writes=[o0R])
                    dve.op(lambda: V.tensor_tensor(out=o1[:, 0:QB], in0=banks[3][:, 0:QB], in1=r1[:, 0:QB], op=ALU.mult),
                           reads=[bankR[3], r1R], writes=[o1R])
                    dve.op(lambda: V.scalar_tensor_tensor(out=o0[:, 0:QB], in0=o1[:, 0:QB], scalar=neglam,
                                                          in1=o0[:, 0:QB], op0=ALU.mult, op1=ALU.add),
                           reads=[o1R, RL], writes=[o0R])
                    sq, sqR = sqs.next()
                    pool.op(lambda: G.tensor_tensor(out=sq[:, 0:QB], in0=o0[:, 0:QB], in1=o0[:, 0:QB], op=ALU.mult),
                            reads=[o0R], writes=[sqR])
                    def p2(sq=sq, sqR=sqR, o0=o0, o0R=o0R, zt=zt, ztR=ztR, h=h, q0=q0):
                        pe.op(lambda: T.matmul(banks[6][:, 0:QB], ones_bf, sq[:, 0:QB], start=True, stop=True),
                              reads=[sqR, RC], writes=[bankR[6]])
                        rn, rnR = tmp.next()
                        act.op(lambda: A_.activation(out=rn[:, 0:QB], in_=banks[6][:, 0:QB], func=AF.Ln, scale=1.0 / 128,
                                                     bias=eps_t), reads=[bankR[6], RC], writes=[rnR])
                        act.op(lambda: A_.activation(out=rn[:, 0:QB], in_=rn[:, 0:QB], func=AF.Exp, scale=-0.5),
                               reads=[rnR], writes=[rnR])
                        dve.op(lambda: V.scalar_tensor_tensor(out=o0[:, 0:QB], in0=o0[:, 0:QB], scalar=dnorm_s,
                                                              in1=rn[:, 0:QB], op0=ALU.mult, op1=ALU.mult),
                               reads=[rnR, RL], writes=[o0R])
                        ot, otR = outs.next()
                        pool.op(lambda: G.tensor_tensor(out=ot[:, 0:QB], in0=o0[:, 0:QB], in1=zt, op=ALU.mult),
                                reads=[o0R, ztR], writes=[otR])
                        sp.dma(YT[512 + h * 128: 512 + (h + 1) * 128, q0:q0 + QB], ot[:, 0:QB], reads=[otR])
                    pend2[0] = p2
            pend2[0]()
            kb.barrier()

        def phase_m(l, si):
            S = seqs[si]
            QB = min(512, S)
            kb.barrier()
            cv = Carve()
            wkv = cv.bf(16 * 1024, (16, 1024))
            wkvR = Res()
            nmem_b = cv.f32(D)
            xts = Rot([(cv.f32(D), Res()) for _ in range(2)])
            junk = cv.bf(D)
            hbs = Rot([(cv.bf(D), Res()) for _ in range(2)])
            memT = cv.bf(16 * 256, (16, 256))
            memTR = Res()
            small, smallR = cv.f32(8), Res()
            kmT, kmTR = cv.bf(4 * 256, (4, 256)), Res()
            vm, vmR = cv.bf(2 * 512, (2, 512)), Res()
            qts = Rot([(cv.bf(QB), cv.bf(QB), Res()) for _ in range(2)])
            pts = Rot([(cv.bf(512), Res()) for _ in range(3)])
            tmp = Rot([(cv.f32(512), Res()) for _ in range(4)])
            outs = Rot([(cv.bf(512), Res()) for _ in range(2)])
            sp.dma(wkv, WKV[l], reads=[RW], writes=[wkvR])
            sp.dma(nmem_b, norm_mem[l].partition_broadcast(128), writes=[smallR])
            for i in range(2):
                xt, xtR = xts.next()
                hb, hbR = hbs.next()
                sp.dma(xt, mem_in[si * N_MEM + i * 128: si * N_MEM + (i + 1) * 128, :], writes=[xtR])
                rms_rows((junk, small[:, 0:1], small[:, 1:2]), xt, xtR, small[:, 2:3], smallR, D)
                dve.op(lambda: V.scalar_tensor_tensor(out=hb, in0=xt, scalar=small[:, 2:3], in1=nmem_b,
                                                      op0=ALU.mult, op1=ALU.mult), reads=[xtR, smallR], writes=[hbR])
                for half in range(2):
                    tpb = bank_bf(half)
                    for kk in range(8):
                        kc = half * 8 + kk
                        pe.op(lambda kc=kc, kk=kk, tpb=tpb: T.transpose(out=tpb[:, kk * 128:(kk + 1) * 128],
                                                                        in_=hb[:, kc * 128:(kc + 1) * 128],
                                                                        identity=ident_bf), reads=[hbR, RC],
                              writes=[bankR[half]] if kk == 0 else (), dwrites=() if kk == 0 else [bankR[half]])
                    dve.op(lambda: V.tensor_copy(out=memT[:, half * 8:(half + 1) * 8, i * 128:(i + 1) * 128],
                                                 in_=tpb.rearrange("p (k t) -> p k t", k=8)), reads=[bankR[half]],
                           dwrites=[memTR])
            for h in range(4):
                bi = 2 + h
                for kc in range(16):
                    pe.op(lambda kc=kc: T.matmul(banks[bi][:, 0:256], wkv[:, kc, h * 128:(h + 1) * 128], memT[:, kc, :],
                                                 start=(kc == 0), stop=(kc == 15)), reads=[wkvR, memTR],
                          writes=[bankR[bi]] if kc == 0 else (), dwrites=() if kc == 0 else [bankR[bi]])
                act.op(lambda: A_.copy(out=kmT[:, h, :], in_=banks[bi][:, 0:256]), reads=[bankR[bi]], dwrites=[kmTR])
            for mt in range(2):
                bi = 6 + mt
                for kc in range(16):
                    pe.op(lambda kc=kc: T.matmul(banks[bi][:, :], memT[:, kc, mt * 128:(mt + 1) * 128],
                                                 wkv[:, kc, 512:1024], start=(kc == 0), stop=(kc == 15)),
                          reads=[wkvR, memTR], writes=[bankR[bi]] if kc == 0 else (),
                          dwrites=() if kc == 0 else [bankR[bi]])
                dve.op(lambda: V.tensor_copy(out=vm[:, mt, :], in_=banks[bi][:, :]), reads=[bankR[bi]], dwrites=[vmR])
            bsc = Rot([0, 1, 7])
            bo = Rot([2, 3])
            bdn = Rot([4, 5])
            scale = 128.0 ** -0.5
            for h in range(4):
                for qb in range(S // QB):
                    q0 = qb * QB
                    qt, zt, qR = qts.next()
                    sp.dma(qt, MQT[h * 128:(h + 1) * 128, q0:q0 + QB], writes=[qR])
                    sp.dma(zt, MZT[h * 128:(h + 1) * 128, q0:q0 + QB], dwrites=[qR])
                    b_o = bo.next()
                    b_d = bdn.next()
                    items = [dict(kT=kmT[:, h, kc * 128:(kc + 1) * 128], q=qt, v=vm[:, kc, h * 128:(h + 1) * 128],
                                  kR=kmTR, qR=qR, vR=vmR, bo=b_o, bd=b_d, first=(kc == 0), last=(kc == 1))
                             for kc in range(2)]
                    softmax_pv(items, QB, scale, bsc, pts)
                    r, rR = tmp.next()
                    dve.op(lambda: V.reciprocal(out=r[:, 0:QB], in_=banks[b_d][:, 0:QB]), reads=[bankR[b_d]],
                           writes=[rR])
                    o, oR = tmp.next()
                    dve.op(lambda: V.tensor_tensor(out=o[:, 0:QB], in0=banks[b_o][:, 0:QB], in1=r[:, 0:QB],
                                                   op=ALU.mult), reads=[bankR[b_o], rR], writes=[oR])
                    ot, otR = outs.next()
                    pool.op(lambda: G.tensor_tensor(out=ot[:, 0:QB], in0=o[:, 0:QB], in1=zt, op=ALU.mult),
                            reads=[oR, qR], writes=[otR])
                    sp.dma(YT[1536 + h * 128: 1536 + (h + 1) * 128, q0:q0 + QB], ot[:, 0:QB], reads=[otR])
            kb.barrier()

        def phase_c(l, si):
            S = seqs[si]
            TC = min(S, 1024)
            kb.barrier()
            cv = Carve()
            us = Rot([(cv.bf(TC + 2), Res()) for _ in range(2)])
            gs = Rot([(cv.bf(TC), Res()) for _ in range(2)])
            acc = Rot([(cv.f32(TC), Res()) for _ in range(2)])
            outs = Rot([(cv.bf(TC), Res()) for _ in range(2)])
            for c in range(4):
                for b0 in range(0, S, TC):
                    ut, utR = us.next()
                    gt, gtR = gs.next()
                    lo = max(0, b0 - 1)
                    hi = min(S, b0 + TC + 1)
                    pool.op(lambda: G.memset(ut, 0.0), writes=[utR])
                    sp.dma(ut[:, lo - (b0 - 1): hi - (b0 - 1)], CUT[c * 128:(c + 1) * 128, lo:hi], dwrites=[utR])
                    sp.dma(gt, CGT[c * 128:(c + 1) * 128, b0:b0 + TC], writes=[gtR])
                    a, aR = acc.next()
                    dve.op(lambda: V.tensor_scalar(out=a, in0=ut[:, 0:TC], scalar1=cconv[:, c * 3:c * 3 + 1],
                                                   scalar2=None, op0=ALU.mult), reads=[utR, RL], writes=[aR])
                    for wi in (1, 2):
                        dve.op(lambda wi=wi: V.scalar_tensor_tensor(out=a, in0=ut[:, wi:wi + TC],
                                                                    scalar=cconv[:, c * 3 + wi:c * 3 + wi + 1], in1=a,
                                                                    op0=ALU.mult, op1=ALU.add), reads=[utR, RL],
                               writes=[aR])
                    o, oR = outs.next()
                    pool.op(lambda: G.tensor_tensor(out=o, in0=a, in1=gt, op=ALU.mult), reads=[aR, gtR], writes=[oR])
                    sp.dma(YT[1024 + c * 128: 1024 + (c + 1) * 128, b0:b0 + TC], o, reads=[oR])
            kb.barrier()

        def phase_o(l, si, xsrc, xsrcR, xdst, xdstR):
            S = seqs[si]
            kb.barrier()
            cv = Carve()
            wo = cv.bf(16 * D, (16, D))
            woR = Res()
            yts = Rot([(cv.bf(16 * 128, (16, 128)), Res()) for _ in range(2)])
            y32 = Rot([(cv.f32(D), Res()) for _ in range(2)])
            xts = Rot([(cv.f32(D), Res()) for _ in range(2)])
            junk = cv.bf(D)
            small, smallR = cv.f32(8), Res()
            for q4 in range(4):
                sp.dma(wo[:, :, q4 * 512:(q4 + 1) * 512], WOUT[l][:, :, q4 * 512:(q4 + 1) * 512], reads=[RW],
                       dwrites=[woR])
            pbo = Rot([[0, 1, 2, 3], [4, 5, 6, 7]])
            def pre_o(ti):
                t0 = ti * 128
                yt, ytR = yts.next()
                sp.dma(yt, YT[:, t0:t0 + 128].rearrange("(k p) t -> p k t", p=128), writes=[ytR])
                xt, xtR = xts.next()
                sp.dma(xt, xsrc[offs[si] + t0: offs[si] + t0 + 128, :], reads=[xsrcR], writes=[xtR])
                return yt, ytR, xt, xtR
            pend = pre_o(0)
            for ti in range(S // 128):
                t0 = ti * 128
                yt, ytR, xt, xtR = pend
                if ti + 1 < S // 128:
                    pend = pre_o(ti + 1)
                bs = pbo.next()
                yv, yvR = y32.next()
                for cb in range(4):
                    bi = bs[cb]
                    for kc in range(16):
                        pe.op(lambda kc=kc: T.matmul(banks[bi][:, :], yt[:, kc, :], wo[:, kc, cb * 512:(cb + 1) * 512],
                                                     start=(kc == 0), stop=(kc == 15)), reads=[ytR, woR],
                              writes=[bankR[bi]] if kc == 0 else (), dwrites=() if kc == 0 else [bankR[bi]])
                    e = kb.ew.next()
                    if e is act:
                        act.op(lambda: A_.copy(out=yv[:, cb * 512:(cb + 1) * 512], in_=banks[bi][:, :]),
                               reads=[bankR[bi]], writes=[yvR] if cb == 0 else (), dwrites=() if cb == 0 else [yvR])
                    else:
                        dve.op(lambda: V.tensor_copy(out=yv[:, cb * 512:(cb + 1) * 512], in_=banks[bi][:, :]),
                               reads=[bankR[bi]], writes=[yvR] if cb == 0 else (), dwrites=() if cb == 0 else [yvR])
                rms_rows((junk, small[:, 0:1], small[:, 1:2]), yv, yvR, small[:, 2:3], smallR, D)
                dve.op(lambda: V.scalar_tensor_tensor(out=yv, in0=yv, scalar=small[:, 2:3], in1=npost_b, op0=ALU.mult,
                                                      op1=ALU.mult), reads=[smallR, RL], writes=[yvR])
                pool.op(lambda: G.tensor_tensor(out=yv, in0=yv, in1=xt, op=ALU.add), reads=[xtR], writes=[yvR])
                sp.dma(xdst[offs[si] + t0: offs[si] + t0 + 128, :], yv, reads=[yvR], dwrites=[xdstR])
            kb.barrier()

        XinR = Res("xin")
        X1R = Res("x1")
        YoR = Res("yout")
        for l in range(depth):
            load_layer_params(l)
            if depth == 1:
                xsrc, xsrcR, xdst, xdstR = x_in, XinR, y_out, YoR
            elif l == 0:
                xsrc, xsrcR, xdst, xdstR = x_in, XinR, X1, X1R
            else:
                xsrc, xsrcR, xdst, xdstR = X1, X1R, y_out, YoR
            for si in range(nseq):
                ph = phases.split(",")
                if "a" in ph:
                    phase_a(l, si, xsrc, xsrcR)
                if "g1" in ph:
                    phase_g1(l, si)
                if "g2" in ph:
                    phase_g2(l, si)
                if "g3" in ph:
                    phase_g3(l, si)
                if "b" in ph:
                    phase_b(l, si)
                if "m" in ph:
                    phase_m(l, si)
                if "c" in ph:
                    phase_c(l, si)
                if "o" in ph:
                    phase_o(l, si, xsrc, xsrcR, xdst, xdstR)
        kb.barrier()
    return nc


_CACHE = {}


def _run(seqs, depth, core_inputs, debug=False):
    key = (tuple(seqs), depth, debug)
    if key not in _CACHE:
        _CACHE[key] = build(list(seqs), depth, debug)
    nc = _CACHE[key]
    res = run_bass_kernel_spmd(nc, core_inputs, core_ids=list(range(len(core_inputs))))
    return res


def kernel(x_prompt, x_sample, mem_prompt, mem_sample, norm_pre, norm_post, norm_mem, w_in, gdn_conv,
           gdn_A_log, gdn_dt_bias, gdn_norm, diff_lambda, diff_norm, conv_w, w_mem_kv, w_out):
    n = 8
    f = lambda a: np.ascontiguousarray(np.asarray(a, dtype=np.float32))
    x_prompt, x_sample, mem_prompt, mem_sample = f(x_prompt), f(x_sample), f(mem_prompt), f(mem_sample)
    B, S, _ = x_prompt.shape
    DB, DS, _ = x_sample.shape
    pb = B // n
    db = DB // n
    seqs = [S] * pb + [DS] * db
    depth = np.asarray(norm_pre).shape[0]
    consts = _consts(max(seqs))
    shared = dict(norm_pre=f(norm_pre), norm_post=f(norm_post), norm_mem=f(norm_mem), w_in=f(w_in),
                  gdn_conv=f(gdn_conv), gdn_A_log=f(gdn_A_log), gdn_dt_bias=f(gdn_dt_bias), gdn_norm=f(gdn_norm),
                  diff_lambda=f(diff_lambda), diff_norm=f(diff_norm), conv_w=f(conv_w), w_mem_kv=f(w_mem_kv),
                  w_out=f(w_out), **consts)
    in_maps = []
    for c in range(n):
        xs = [x_prompt[c * pb + i] for i in range(pb)] + [x_sample[c * db + i] for i in range(db)]
        ms = [mem_prompt[c * pb + i] for i in range(pb)] + [mem_sample[c * db + i] for i in range(db)]
        m = dict(shared)
        m["x"] = np.ascontiguousarray(np.concatenate(xs, axis=0))
        m["mem"] = np.ascontiguousarray(np.concatenate(ms, axis=0))
        in_maps.append(m)
    res = _run(seqs, depth, in_maps)
    y_prompt = np.empty((B, S, D), np.float32)
    y_sample = np.empty((DB, DS, D), np.float32)
    for c in range(n):
        y = res.results[c]["y"]
        o = 0
        for i in range(pb):
            y_prompt[c * pb + i] = y[o:o + S]
            o += S
        for i in range(db):
            y_sample[c * db + i] = y[o:o + DS]
            o += DS
    return (y_prompt, y_sample)
```

```python
import math
from contextlib import ExitStack
import numpy as np
import ml_dtypes
import concourse.bass as bass
import concourse.mybir as mybir
from concourse.bass_utils import run_bass_kernel_spmd

F32 = mybir.dt.float32
BF16 = mybir.dt.bfloat16
F32R = mybir.dt.float32r
AF = mybir.ActivationFunctionType
ALU = mybir.AluOpType

D = 2048
W_BR = 512
IN_COLS = 7184
N_MEM = 256
EPS = 1e-6
NEG = 32768.0
C_AQKV, C_ADEC, C_AZ = 0, 1536, 1552
C_BQ, C_BK, C_BV, C_BZ = 2064, 2576, 3088, 3600
C_CB, C_CC, C_CX, C_CZ = 4112, 4624, 5136, 5648
C_MQ, C_MZ = 6160, 6672


class Res:
    __slots__ = ("w", "r", "wx", "name")

    def __init__(self, name=""):
        self.w = {}
        self.r = {}
        self.wx = {}
        self.name = name


class Slot:
    __slots__ = ("si", "val")


class Eng:
    RING = 12
    CAP = 30000

    def __init__(self, kb, name, obj):
        self.kb = kb
        self.name = name
        self.o = obj
        self.si = kb.newsem()
        self.cnt = 0
        self.known = {}
        self.ring = []
        self.ri = 0

    def _deps(self, reads, writes, dwrites):
        deps = {}

        def add(d):
            for k, v in d.items():
                if deps.get(k, 0) < v:
                    deps[k] = v
        for r in reads:
            add(r.w)
        for w in writes:
            add(w.w)
            add(w.r)
        for w in dwrites:
            add(w.r)
            add(w.wx)
        return deps

    def _wait(self, deps):
        for k, v in deps.items():
            if self.name == "pe" and k == self.si:
                continue
            if self.known.get(k, 0) < v:
                self.o.wait_ge(self.kb.sems[k], v)
                self.known[k] = v

    def _post(self, ev, reads, writes, dwrites):
        k, v = ev
        for r in reads:
            r.r[k] = v
        for w in writes:
            w.w = {k: v}
            w.wx = {k: v}
            w.r = {}
        for w in dwrites:
            w.w[k] = v

    def op(self, fn, reads=(), writes=(), dwrites=()):
        self._wait(self._deps(reads, writes, dwrites))
        ins = fn()
        self.cnt += 1
        ins.then_inc(self.kb.sems[self.si], 1)
        self._post((self.si, self.cnt), reads, writes, dwrites)
        if self.cnt >= self.CAP:
            self.si = self.kb.newsem()
            self.cnt = 0
        return ins

    def dma(self, out, in_, reads=(), writes=(), dwrites=()):
        self._wait(self._deps(reads, writes, dwrites))
        if len(self.ring) < self.RING:
            s = Slot()
            s.si = self.kb.newsem()
            s.val = 0
            self.ring.append(s)
        s = self.ring[self.ri % self.RING]
        self.ri += 1
        if s.val >= self.CAP:
            self._wait({s.si: s.val})
            s.si = self.kb.newsem()
            s.val = 0
        if s.val > 0:
            self._wait({s.si: s.val})
        ins = self.o.dma_start(out=out, in_=in_)
        s.val += 16
        ins.then_inc(self.kb.sems[s.si], 16)
        self._post((s.si, s.val), reads, writes, dwrites)
        return ins

    def last_events(self):
        ev = {}
        if self.cnt > 0:
            ev[self.si] = self.cnt
        for s in self.ring:
            if s.val > 0:
                ev[s.si] = s.val
        return ev


class Rot:
    def __init__(self, items):
        self.items = items
        self.i = 0

    def next(self):
        it = self.items[self.i % len(self.items)]
        self.i += 1
        return it


class KB:
    def __init__(self, nc, es):
        self.nc = nc
        self.es = es
        self.sems = []
        self.pe = Eng(self, "pe", nc.tensor)
        self.act = Eng(self, "act", nc.scalar)
        self.dve = Eng(self, "dve", nc.vector)
        self.pool = Eng(self, "pool", nc.gpsimd)
        self.sp = Eng(self, "sp", nc.sync)
        self.engs = [self.pe, self.act, self.dve, self.pool, self.sp]
        self.ew = Rot([self.act, self.dve])

    def newsem(self):
        h = self.es.enter_context(self.nc.semaphore(f"s{len(self.sems)}"))
        self.sems.append(h)
        return len(self.sems) - 1

    def barrier(self):
        ev = {}
        for e in self.engs:
            ev.update(e.last_events())
        for e in self.engs:
            e._wait(dict(ev))


def _consts(smax):
    c = {}
    eye = np.eye(128, dtype=np.float32)
    c["ident_bf"] = eye.astype(ml_dtypes.bfloat16)
    c["ones_bf"] = np.ones((128, 128), np.float32).astype(ml_dtypes.bfloat16)
    idx = np.arange(128)
    same = (idx[:, None] // 64) == (idx[None, :] // 64)
    lf = (same & (idx[:, None] <= idx[None, :])).astype(np.float32)
    lb = (same & (idx[:, None] >= idx[None, :])).astype(np.float32)
    bo = same.astype(np.float32)
    mf = np.where(same & (idx[:, None] >= idx[None, :]), 0.0, NEG).astype(np.float32)
    mb = np.where(same & (idx[:, None] <= idx[None, :]), 0.0, NEG).astype(np.float32)
    nodiag = (1.0 - eye).astype(np.float32)
    f32c = np.concatenate([eye, lf, lb, bo, np.tile(mf, (1, 4)), np.tile(mb, (1, 4)), np.tile(nodiag, (1, 4)),
                           np.tile(eye, (1, 4))], axis=1)
    c["f32c"] = np.ascontiguousarray(f32c)
    pt = np.zeros((128, 128), np.float32)
    for j in range(2):
        b = 64 * j
        for i in range(8):
            pt[b + 8 + i, b + i] = -1.0
            pt[b + i, b + 8 + i] = 1.0
    c["pt_bf"] = pt.astype(ml_dtypes.bfloat16)
    inv = (np.float32(500000.0) ** (-np.arange(0, 16, 2, dtype=np.float32) / np.float32(16))).astype(np.float32)
    ang = (np.arange(smax, dtype=np.float32)[:, None] * inv[None, :]).astype(np.float32)
    cos = np.cos(ang.astype(np.float64)).astype(np.float32).T
    sin = np.sin(ang.astype(np.float64)).astype(np.float32).T
    cf = np.ones((128, smax), np.float32)
    sf = np.zeros((128, smax), np.float32)
    for j in range(2):
        b = 64 * j
        cf[b:b + 8] = cos
        cf[b + 8:b + 16] = cos
        sf[b:b + 8] = sin
        sf[b + 8:b + 16] = sin
    c["rope_c"] = cf
    c["rope_s"] = sf
    return c


def build(seqs, depth, debug=False, phases="a,g1,g2,g3,b,m,c,o"):
    nc = bass.Bass("TRN2", target_bir_lowering=False)
    ntok = sum(seqs)
    nseq = len(seqs)
    smax = max(seqs)
    offs = [sum(seqs[:i]) for i in range(nseq)]

    def din(name, shape, dt=F32):
        return nc.dram_tensor(name, list(shape), dt, kind="ExternalInput").ap()

    def dscr(name, shape, dt):
        kind = "ExternalOutput" if debug else "Internal"
        return nc.dram_tensor(name, list(shape), dt, kind=kind).ap()

    x_in = din("x", [ntok, D])
    mem_in = din("mem", [nseq * N_MEM, D])
    norm_pre = din("norm_pre", [depth, D])
    norm_post = din("norm_post", [depth, D])
    norm_mem = din("norm_mem", [depth, D])
    w_in = din("w_in", [depth, D, IN_COLS])
    gdn_conv = din("gdn_conv", [depth, 5, 1536])
    gdn_A_log = din("gdn_A_log", [depth, 2, 4])
    gdn_dt_bias = din("gdn_dt_bias", [depth, 2, 4])
    gdn_norm = din("gdn_norm", [depth, 128])
    diff_lambda = din("diff_lambda", [depth, 4, 64])
    diff_norm = din("diff_norm", [depth, 128])
    conv_w = din("conv_w", [depth, 3, 512])
    w_mem_kv = din("w_mem_kv", [depth, D, 1024])
    w_out = din("w_out", [depth, D, D])
    c_ident_bf = din("ident_bf", [128, 128], BF16)
    c_ones_bf = din("ones_bf", [128, 128], BF16)
    c_pt_bf = din("pt_bf", [128, 128], BF16)
    c_f32c = din("f32c", [128, 4 * 128 + 4 * 512])
    c_rope_c = din("rope_c", [128, smax])
    c_rope_s = din("rope_s", [128, smax])
    y_out = nc.dram_tensor("y", [ntok, D], F32, kind="ExternalOutput").ap()

    WIN = dscr("WIN", [depth, 56, 128, 16, 128], BF16)
    WDB = dscr("WDB", [depth, 128, 16, 16], BF16)
    WOUT = dscr("WOUT", [depth, 128, 16, D], BF16)
    WKV = dscr("WKV", [depth, 128, 16, 1024], BF16)
    X1 = dscr("X1", [ntok, D], F32)
    QKVT = dscr("QKVT", [1536, smax], BF16)
    DBt = dscr("DBt", [smax, 16], F32)
    AZ = dscr("AZ", [smax, 512], BF16)
    BQT = dscr("BQT", [512, smax], BF16)
    BKT = dscr("BKT", [512, smax], BF16)
    BV = dscr("BV", [smax, 512], BF16)
    BZT = dscr("BZT", [512, smax], BF16)
    CUT = dscr("CUT", [512, smax], BF16)
    CGT = dscr("CGT", [512, smax], BF16)
    MQT = dscr("MQT", [512, smax], BF16)
    MZT = dscr("MZT", [512, smax], BF16)
    GQT = dscr("GQT", [512, smax], BF16)
    GKT = dscr("GKT", [512, smax], BF16)
    GK = dscr("GK", [smax, 512], BF16)
    GV = dscr("GV", [smax, 512], BF16)
    OD = [dscr("OF", [smax, 512], F32), dscr("OB", [smax, 512], F32)]
    YT = dscr("YT", [D, smax], BF16)

    es = ExitStack()
    with es:
        kb = KB(nc, es)
        pe, act, dve, pool, sp = kb.pe, kb.act, kb.dve, kb.pool, kb.sp
        ARENA_W = 38200
        arena = es.enter_context(nc.sbuf_tensor("arena", [128, ARENA_W], F32))
        rbuf = es.enter_context(nc.sbuf_tensor("rbuf", [128, 10 * 512 + 128 + 1424], F32R))
        cst = es.enter_context(nc.sbuf_tensor("cst", [128, 7800], F32))
        banks = [es.enter_context(nc.psum_tensor(f"bank{i}", [128, 512], F32)) for i in range(8)]
        bankR = [Res(f"bank{i}") for i in range(8)]

        coff = [0]

        def calloc(words):
            o = coff[0]
            coff[0] += words
            assert coff[0] <= 7800
            return cst[:, o:o + words]

        RC = Res("consts")
        f32c = calloc(4 * 128 + 4 * 512)
        ident_f = f32c[:, 0:128]
        Lf = f32c[:, 128:256]
        Lb = f32c[:, 256:384]
        mask4 = [f32c[:, 512:1024], f32c[:, 1024:1536]]
        nodiag4 = f32c[:, 1536:2048]
        ident4 = f32c[:, 2048:2560]
        Lmat = [Lf, Lb]
        ident_bf = calloc(64).bitcast(BF16)
        ones_bf = calloc(64).bitcast(BF16)
        pt_bf = calloc(64).bitcast(BF16)
        eps_t = calloc(1)
        eps128_t = calloc(1)
        one_t = calloc(1)
        sp.dma(f32c, c_f32c, writes=[RC])
        sp.dma(ident_bf, c_ident_bf, dwrites=[RC])
        sp.dma(ones_bf, c_ones_bf, dwrites=[RC])
        sp.dma(pt_bf, c_pt_bf, dwrites=[RC])
        pool.op(lambda: nc.gpsimd.memset(eps_t, EPS), dwrites=[RC])
        pool.op(lambda: nc.gpsimd.memset(eps128_t, EPS * 128.0), dwrites=[RC])
        pool.op(lambda: nc.gpsimd.memset(one_t, 1.0), dwrites=[RC])
        ind_t = calloc(2)
        RI = Res("ind")
        pool.op(lambda: nc.gpsimd.memset(ind_t, 0.0), writes=[RI])
        pool.op(lambda: nc.gpsimd.memset(ind_t[0:64, 0:1], 1.0), writes=[RI])
        pool.op(lambda: nc.gpsimd.memset(ind_t[64:128, 1:2], 1.0), writes=[RI])
        ident_r = rbuf[:, 10 * 512:10 * 512 + 128]
        dve.op(lambda: nc.vector.tensor_copy(out=ident_r, in_=ident_f), reads=[RC], dwrites=[RC])
        rb0 = 10 * 512 + 128
        Lr = [rbuf[:, rb0:rb0 + 128], rbuf[:, rb0 + 128:rb0 + 256]]
        Bor = rbuf[:, rb0 + 256:rb0 + 384]
        maskr = [rbuf[:, rb0 + 384:rb0 + 896], rbuf[:, rb0 + 896:rb0 + 1408]]
        g_r = [rbuf[:, rb0 + 1408:rb0 + 1412], rbuf[:, rb0 + 1412:rb0 + 1416]]
        dve.op(lambda: nc.vector.tensor_copy(out=rbuf[:, rb0:rb0 + 384], in_=f32c[:, 128:512]), reads=[RC], dwrites=[RC])
        dve.op(lambda: nc.vector.tensor_copy(out=rbuf[:, rb0 + 384:rb0 + 1408], in_=f32c[:, 512:1536]), reads=[RC],
               dwrites=[RC])
        RL = Res("layerparams")
        npre_b = calloc(D)
        npost_b = calloc(D)
        gconv = calloc(60)
        cconv = calloc(12)
        gnorm_b = calloc(128)
        alog_b = calloc(8)
        dtb_b = calloc(8)
        negA_b = calloc(8)
        dnorm_c = calloc(1)
        dnorm_s = calloc(1)
        lam_b = calloc(256)
        lam_t = calloc(256)
        lam_s = calloc(2)
        lam_e = calloc(2)
        neglam = calloc(1)

        RW = Res("weights")
        for l in range(depth):
            src = w_in[l][:, 0:1536].rearrange("(kc p) (ch c) -> ch p kc c", p=128, c=128)
            chunk_cols = [c0 for c0 in range(0, 1536, 128)] + [c0 for c0 in range(C_AZ, IN_COLS, 128)]
            assert len(chunk_cols) == 56
            for ci, c0 in enumerate(chunk_cols):
                srcc = w_in[l][:, c0:c0 + 128].rearrange("(kc p) c -> p kc c", p=128)
                pool.dma(WIN[l, ci], srcc, dwrites=[RW])
            pool.dma(WDB[l], w_in[l][:, C_ADEC:C_ADEC + 16].rearrange("(kc p) c -> p kc c", p=128), dwrites=[RW])
            for q4 in range(4):
                pool.dma(WOUT[l][:, :, q4 * 512:(q4 + 1) * 512],
                         w_out[l][:, q4 * 512:(q4 + 1) * 512].rearrange("(kc p) c -> p kc c", p=128), dwrites=[RW])
            for q4 in range(2):
                pool.dma(WKV[l][:, :, q4 * 512:(q4 + 1) * 512],
                         w_mem_kv[l][:, q4 * 512:(q4 + 1) * 512].rearrange("(kc p) c -> p kc c", p=128), dwrites=[RW])

        def chunk_index(col):
            if col < 1536:
                return col // 128
            return 12 + (col - C_AZ) // 128

        class Carve:
            def __init__(self):
                self.o = 0

            def f32(self, words, shape=None):
                ap = arena[:, self.o:self.o + words]
                self.o += words
                assert self.o <= ARENA_W, self.o
                if shape is not None:
                    ap = ap.rearrange("p (a b) -> p a b", a=shape[0]) if len(shape) == 2 else ap
                return ap

            def bf(self, elems, shape=None):
                words = (elems + 1) // 2
                ap = arena[:, self.o:self.o + words].bitcast(BF16)
                self.o += words
                assert self.o <= ARENA_W, self.o
                if shape is not None and len(shape) == 2:
                    ap = ap.rearrange("p (a b) -> p a b", a=shape[0])
                return ap

            def r32(self, words, shape=None):
                ap = arena[:, self.o:self.o + words].bitcast(F32R)
                self.o += words
                assert self.o <= ARENA_W, self.o
                if shape is not None and len(shape) == 2:
                    ap = ap.rearrange("p (a b) -> p a b", a=shape[0])
                return ap

        V = nc.vector
        G = nc.gpsimd
        A_ = nc.scalar
        T = nc.tensor

        def bank_bf(i):
            return banks[i][:, :].bitcast(BF16)

        def load_layer_params(l):
            kb.barrier()
            sp.dma(npre_b, norm_pre[l].partition_broadcast(128), writes=[RL])
            sp.dma(npost_b, norm_post[l].partition_broadcast(128), dwrites=[RL])
            sp.dma(gnorm_b, gdn_norm[l].partition_broadcast(128), dwrites=[RL])
            sp.dma(alog_b, gdn_A_log[l].rearrange("a h -> (a h)").partition_broadcast(128), dwrites=[RL])
            sp.dma(dtb_b, gdn_dt_bias[l].rearrange("a h -> (a h)").partition_broadcast(128), dwrites=[RL])
            sp.dma(lam_b, diff_lambda[l].rearrange("a d -> (a d)").partition_broadcast(128), dwrites=[RL])
            with nc.allow_non_contiguous_dma(reason="tiny parameter transposes"):
                for wi in range(5):
                    sp.dma(gconv.rearrange("p (c w) -> p c w", w=5)[:, :, wi],
                           gdn_conv[l, wi].rearrange("(c p) -> p c", p=128), dwrites=[RL])
                for wi in range(3):
                    sp.dma(cconv.rearrange("p (c w) -> p c w", w=3)[:, :, wi],
                           conv_w[l, wi].rearrange("(c p) -> p c", p=128), dwrites=[RL])
                sp.dma(dnorm_c, diff_norm[l].rearrange("(p o) -> p o", o=1), dwrites=[RL])
            lam_init = 0.8 - 0.6 * math.exp(-0.3 * l)
            act.op(lambda: A_.activation(out=negA_b, in_=alog_b, func=AF.Exp), reads=[RL], dwrites=[RL])
            dve.op(lambda: V.tensor_scalar(out=negA_b, in0=negA_b, scalar1=-1.0, scalar2=None, op0=ALU.mult),
                   reads=[RL], dwrites=[RL])
            dve.op(lambda: V.tensor_scalar(out=dnorm_s, in0=dnorm_c, scalar1=1.0 - lam_init, scalar2=None,
                                           op0=ALU.mult), reads=[RL], dwrites=[RL])
            lb3 = lam_b.rearrange("p (a d) -> p a d", d=64)
            lt3 = lam_t.rearrange("p (a d) -> p a d", d=64)
            dve.op(lambda: V.tensor_tensor(out=lt3[:, 0, :], in0=lb3[:, 0, :], in1=lb3[:, 1, :], op=ALU.mult),
                   reads=[RL], dwrites=[RL])
            dve.op(lambda: V.tensor_tensor(out=lt3[:, 1, :], in0=lb3[:, 2, :], in1=lb3[:, 3, :], op=ALU.mult),
                   reads=[RL], dwrites=[RL])
            dve.op(lambda: V.reduce_sum(out=lam_s[:, 0:1], in_=lt3[:, 0, :], axis=mybir.AxisListType.X),
                   reads=[RL], dwrites=[RL])
            dve.op(lambda: V.reduce_sum(out=lam_s[:, 1:2], in_=lt3[:, 1, :], axis=mybir.AxisListType.X),
                   reads=[RL], dwrites=[RL])
            act.op(lambda: A_.activation(out=lam_e, in_=lam_s, func=AF.Exp), reads=[RL], dwrites=[RL])
            dve.op(lambda: V.tensor_tensor(out=neglam, in0=lam_e[:, 1:2], in1=lam_e[:, 0:1], op=ALU.subtract),
                   reads=[RL], dwrites=[RL])
            dve.op(lambda: V.tensor_scalar(out=neglam, in0=neglam, scalar1=-lam_init, scalar2=None, op0=ALU.add),
                   reads=[RL], dwrites=[RL])
            kb.barrier()

        def rms_rows(cv, xt, xtR, rstd, tmpR, width):
            junk, ss, rms = cv
            act.op(lambda: A_.activation(out=junk, in_=xt, func=AF.Square, accum_out=ss),
                   reads=[xtR], writes=[tmpR])
            act.op(lambda: A_.activation(out=rms, in_=ss, func=AF.Sqrt, scale=1.0 / width, bias=eps_t),
                   reads=[tmpR, RC], dwrites=[tmpR])
            dve.op(lambda: V.reciprocal(out=rstd, in_=rms), reads=[tmpR], dwrites=[tmpR])

        def phase_a(l, si, xsrc, xsrcR):
            S = seqs[si]
            TB = min(S, 1024)
            NT = min(512, TB)
            kb.barrier()
            cv = Carve()
            xts = [(cv.f32(D), Res()) for _ in range(2)]
            junk = cv.bf(D)
            hbs = [(cv.bf(D), Res()) for _ in range(2)]
            hT = cv.bf(16 * TB, (16, TB))
            hTR = Res("hT")
            ring = Rot([(cv.bf(16 * 128, (16, 128)), Res()) for _ in range(8)])
            wides = [(cv.bf(16 * 512, (16, 512)), Res("wide")) for _ in range(2)]
            wdb = cv.bf(16 * 16, (16, 16))
            wdbR = Res("wdb")
            stg = Rot([(cv.f32(512), Res()) for _ in range(8)])
            ropeC = [(cv.f32(512), Res()) for _ in range(2)]
            ropeS = [(cv.f32(512), Res()) for _ in range(2)]
            small = cv.f32(8)
            smallR = Res()
            pb = Rot([2, 3, 4, 5, 6, 7])

            order = [c * 128 for c in range(12)]
            order += [col + h * 128 for col in (C_BQ, C_BK) for h in range(4)]
            order += [col + c * 128 for col in (C_BZ, C_MQ, C_MZ) for c in range(4)]
            order += [col + c * 128 for c in range(4) for col in (C_CB, C_CC, C_CX, C_CZ)]
            LOOK = 4
            pq = {"next": 0, "ready": []}

            def _issue():
                col = order[pq["next"] % len(order)]
                pq["next"] += 1
                w, wr = ring.next()
                sp.dma(w, WIN[l, chunk_index(col)], reads=[RW], writes=[wr])
                pq["ready"].append((col, w, wr))

            def load_chunk(col):
                while len(pq["ready"]) < 1:
                    _issue()
                c0, w, wr = pq["ready"].pop(0)
                assert c0 == col, (c0, col)
                while len(pq["ready"]) < LOOK and pq["next"] < pq["limit"]:
                    _issue()
                return w, wr
            pq["limit"] = len(order) * (S // TB)

            def fm_mm(w, wr, n, bi=None):
                bi = pb.next() if bi is None else bi
                for kc in range(16):
                    pe.op(lambda kc=kc: T.matmul(banks[bi][:, 0:NT], w[:, kc, :], hT[:, kc, n * NT:(n + 1) * NT],
                                                 start=(kc == 0), stop=(kc == 15)),
                          reads=[wr, hTR], writes=[bankR[bi]] if kc == 0 else (), dwrites=() if kc == 0 else [bankR[bi]])
                return bi

            for b0 in range(0, S, TB):
                tok0 = offs[si] + b0
                for i in range(TB // 128):
                    xt, xtR = xts[i % 2]
                    hb, hbR = hbs[i % 2]
                    sp.dma(xt, xsrc[tok0 + i * 128: tok0 + (i + 1) * 128, :], reads=[xsrcR], writes=[xtR])
                    rms_rows((junk, small[:, 0:1], small[:, 1:2]), xt, xtR, small[:, 2:3], smallR, D)
                    dve.op(lambda: V.scalar_tensor_tensor(out=hb, in0=xt, scalar=small[:, 2:3], in1=npre_b,
                                                          op0=ALU.mult, op1=ALU.mult),
                           reads=[xtR, smallR, RL], writes=[hbR])
                    for half in range(2):
                        tpb = bank_bf(half)
                        for kk in range(8):
                            kc = half * 8 + kk
                            pe.op(lambda kc=kc, kk=kk, tpb=tpb: T.transpose(out=tpb[:, kk * 128:(kk + 1) * 128],
                                                                            in_=hb[:, kc * 128:(kc + 1) * 128],
                                                                            identity=ident_bf),
                                  reads=[hbR, RC], writes=[bankR[half]] if kk == 0 else (),
                                  dwrites=() if kk == 0 else [bankR[half]])
                        e = kb.ew.next()
                        dst = hT[:, half * 8:(half + 1) * 8, i * 128:(i + 1) * 128]
                        srcv = tpb.rearrange("p (k t) -> p k t", k=8)
                        if e is act:
                            act.op(lambda: A_.copy(out=dst, in_=srcv), reads=[bankR[half]], dwrites=[hTR])
                        else:
                            dve.op(lambda: V.tensor_copy(out=dst, in_=srcv), reads=[bankR[half]], dwrites=[hTR])

                def store_fm(dst, row0, n, sv, svR):
                    sp.dma(dst[row0:row0 + 128, b0 + n * NT: b0 + (n + 1) * NT], sv, reads=[svR])

                def evac_copy_bf(bi, func=None):
                    sv, svR = stg.next()
                    svb = sv.bitcast(BF16)[:, 0:NT]
                    if func is not None:
                        act.op(lambda: A_.activation(out=svb, in_=banks[bi][:, 0:NT], func=func),
                               reads=[bankR[bi]], writes=[svR])
                    else:
                        e = kb.ew.next()
                        if e is act:
                            act.op(lambda: A_.copy(out=svb, in_=banks[bi][:, 0:NT]), reads=[bankR[bi]], writes=[svR])
                        else:
                            dve.op(lambda: V.tensor_copy(out=svb, in_=banks[bi][:, 0:NT]), reads=[bankR[bi]],
                                   writes=[svR])
                    return svb, svR

                nsub = TB // NT
                for c in range(12):
                    w, wr = load_chunk(c * 128)
                    for n in range(nsub):
                        bi = fm_mm(w, wr, n)
                        svb, svR = evac_copy_bf(bi)
                        store_fm(QKVT, c * 128, n, svb, svR)
                sp.dma(wdb, WDB[l], reads=[RW], writes=[wdbR])
                for wi_, col_ in enumerate((C_AZ, C_BV)):
                    wide_, wideR_ = wides[wi_]
                    for q4 in range(4):
                        sp.dma(wide_[:, :, q4 * 128:(q4 + 1) * 128], WIN[l, chunk_index(col_ + q4 * 128)], reads=[RW],
                               writes=[wideR_] if q4 == 0 else (), dwrites=() if q4 == 0 else [wideR_])
                for _ in range(LOOK):
                    if len(pq["ready"]) < LOOK and pq["next"] < pq["limit"]:
                        _issue()
                for i in range(TB // 128):
                    bi = pb.next()
                    for kc in range(16):
                        pe.op(lambda kc=kc: T.matmul(banks[bi][:, 0:16], hT[:, kc, i * 128:(i + 1) * 128],
                                                     wdb[:, kc, :], start=(kc == 0), stop=(kc == 15)),
                              reads=[wdbR, hTR], writes=[bankR[bi]] if kc == 0 else (),
                              dwrites=() if kc == 0 else [bankR[bi]])
                    sv, svR = stg.next()
                    dve.op(lambda: V.tensor_copy(out=sv[:, 0:16], in_=banks[bi][:, 0:16]), reads=[bankR[bi]],
                           writes=[svR])
                    sp.dma(DBt[b0 + i * 128: b0 + (i + 1) * 128, :], sv[:, 0:16], reads=[svR])

                def wide_tm(wsel, dst, func):
                    wide, wideR = wides[wsel]
                    for i in range(TB // 128):
                        bi = pb.next()
                        for kc in range(16):
                            pe.op(lambda kc=kc: T.matmul(banks[bi][:, :], hT[:, kc, i * 128:(i + 1) * 128],
                                                         wide[:, kc, :], start=(kc == 0), stop=(kc == 15)),
                                  reads=[wideR, hTR], writes=[bankR[bi]] if kc == 0 else (),
                                  dwrites=() if kc == 0 else [bankR[bi]])
                        sv, svR = stg.next()
                        svb = sv.bitcast(BF16)[:, 0:512]
                        if func is not None:
                            act.op(lambda: A_.activation(out=svb, in_=banks[bi][:, :], func=func), reads=[bankR[bi]],
                                   writes=[svR])
                        else:
                            dve.op(lambda: V.tensor_copy(out=svb, in_=banks[bi][:, :]), reads=[bankR[bi]], writes=[svR])
                        sp.dma(dst[b0 + i * 128: b0 + (i + 1) * 128, :], svb, reads=[svR])

                wide_tm(0, AZ, AF.Silu)
                wide_tm(1, BV, None)
                assert nsub <= 2
                for n in range(nsub):
                    p0 = b0 + n * NT
                    sp.dma(ropeC[n][0][:, 0:NT], c_rope_c[:, p0:p0 + NT], writes=[ropeC[n][1]])
                    sp.dma(ropeS[n][0][:, 0:NT], c_rope_s[:, p0:p0 + NT], writes=[ropeS[n][1]])
                pendr = [None]
                for qk, (col, dst) in enumerate(((C_BQ, BQT), (C_BK, BKT))):
                    for h in range(4):
                        w, wr = load_chunk(col + h * 128)
                        for n in range(nsub):
                            rc, rcR = ropeC[n]
                            rs, rsR = ropeS[n]
                            bi = fm_mm(w, wr, n)
                            if pendr[0] is not None:
                                pendr[0]()
                            sv, svR = stg.next()
                            qsb = sv.bitcast(BF16)[:, 0:NT]
                            act.op(lambda: A_.copy(out=qsb, in_=banks[bi][:, 0:NT]), reads=[bankR[bi]], writes=[svR])

                            def rot_rest(qsb=qsb, svR=svR, rc=rc, rcR=rcR, rs=rs, rsR=rsR, n=n, dst=dst, h=h):
                                b2 = pb.next()
                                pe.op(lambda: T.matmul(banks[b2][:, 0:NT], pt_bf, qsb, start=True, stop=True),
                                      reads=[svR, RC], writes=[bankR[b2]])
                                t1, t1R = stg.next()
                                dve.op(lambda: V.tensor_tensor(out=t1[:, 0:NT], in0=banks[b2][:, 0:NT], in1=rs[:, 0:NT],
                                                               op=ALU.mult), reads=[bankR[b2], rsR], writes=[t1R])
                                t2, t2R = stg.next()
                                pool.op(lambda: G.tensor_tensor(out=t2[:, 0:NT], in0=qsb, in1=rc[:, 0:NT], op=ALU.mult),
                                        reads=[svR, rcR], writes=[t2R])
                                o, oR = stg.next()
                                ob = o.bitcast(BF16)[:, 0:NT]
                                dve.op(lambda: V.tensor_tensor(out=ob, in0=t1[:, 0:NT], in1=t2[:, 0:NT], op=ALU.add),
                                       reads=[t1R, t2R], writes=[oR])
                                store_fm(dst, h * 128, n, ob, oR)
                            pendr[0] = rot_rest
                pendr[0]()
                for col, dst, func in ((C_BZ, BZT, AF.Silu), (C_MQ, MQT, None), (C_MZ, MZT, AF.Silu)):
                    for c in range(4):
                        w, wr = load_chunk(col + c * 128)
                        for n in range(nsub):
                            bi = fm_mm(w, wr, n)
                            svb, svR = evac_copy_bf(bi, func)
                            store_fm(dst, c * 128, n, svb, svR)
                for c in range(4):
                    ws = [load_chunk(col + c * 128) for col in (C_CB, C_CC, C_CX, C_CZ)]
                    for n in range(nsub):
                        bis = [fm_mm(w, wr, n) for (w, wr) in ws]
                        ccs, ccR = stg.next()
                        act.op(lambda: A_.copy(out=ccs[:, 0:NT], in_=banks[bis[1]][:, 0:NT]), reads=[bankR[bis[1]]],
                               writes=[ccR])
                        u, uR = stg.next()
                        ub = u.bitcast(BF16)[:, 0:NT]
                        dve.op(lambda: V.tensor_tensor(out=ub, in0=banks[bis[2]][:, 0:NT], in1=ccs[:, 0:NT],
                                                       op=ALU.mult), reads=[bankR[bis[2]], ccR], writes=[uR])
                        store_fm(CUT, c * 128, n, ub, uR)
                        sz, szR = stg.next()
                        act.op(lambda: A_.activation(out=sz[:, 0:NT], in_=banks[bis[3]][:, 0:NT], func=AF.Silu),
                               reads=[bankR[bis[3]]], writes=[szR])
                        g, gR = stg.next()
                        gb = g.bitcast(BF16)[:, 0:NT]
                        dve.op(lambda: V.tensor_tensor(out=gb, in0=banks[bis[0]][:, 0:NT], in1=sz[:, 0:NT],
                                                       op=ALU.mult), reads=[bankR[bis[0]], szR], writes=[gR])
                        store_fm(CGT, c * 128, n, gb, gR)
            kb.barrier()

        def phase_g1(l, si):
            S = seqs[si]
            TG = min(S, 512)
            kb.barrier()
            cv = Carve()
            xin = Rot([(cv.bf(TG + 4), Res()) for _ in range(2)])
            acc = Rot([(cv.f32(TG), Res()) for _ in range(2)])
            ys = Rot([(cv.f32(TG), Res()) for _ in range(2)])
            sq = Rot([(cv.bf(TG), Res()) for _ in range(2)])
            rn = Rot([(cv.f32(TG), Res()) for _ in range(2)])
            yn = Rot([(cv.bf(TG), Res()) for _ in range(3)])
            tk = Rot([(cv.bf(TG), Res()) for _ in range(2)])
            pb = Rot([0, 1, 2, 3])
            pt = Rot([4, 5, 6, 7])
            nt = TG // 128
            items = [(b0, c) for b0 in range(0, S, TG) for c in range(12)]

            def pre(b0, c):
                xi, xiR = xin.next()
                lo = max(0, b0 - 2)
                hi = min(S, b0 + TG + 2)
                pool.op(lambda: G.memset(xi, 0.0), writes=[xiR])
                sp.dma(xi[:, lo - (b0 - 2): hi - (b0 - 2)], QKVT[c * 128:(c + 1) * 128, lo:hi], dwrites=[xiR])
                return xi, xiR
            def chunk(b0, c, xi, xiR):
                a, aR = acc.next()
                dve.op(lambda: V.tensor_scalar(out=a, in0=xi[:, 0:TG], scalar1=gconv[:, c * 5:c * 5 + 1],
                                               scalar2=None, op0=ALU.mult), reads=[xiR, RL], writes=[aR])
                for wi in range(1, 5):
                    dve.op(lambda wi=wi: V.scalar_tensor_tensor(out=a, in0=xi[:, wi:wi + TG],
                                                                scalar=gconv[:, c * 5 + wi:c * 5 + wi + 1], in1=a,
                                                                op0=ALU.mult, op1=ALU.add),
                           reads=[xiR, RL], writes=[aR])
                h = c % 4
                if c < 8:
                    y, yR = ys.next()
                    act.op(lambda: A_.activation(out=y, in_=a, func=AF.Silu), reads=[aR], writes=[yR])
                    s2, s2R = sq.next()
                    act.op(lambda: A_.activation(out=s2, in_=y, func=AF.Square), reads=[yR], writes=[s2R])
                    bi = pb.next()
                    pe.op(lambda: T.matmul(banks[bi][:, 0:TG], ones_bf, s2, start=True, stop=True),
                          reads=[s2R, RC], writes=[bankR[bi]])
                    r, rR = rn.next()
                    if c < 4:
                        act.op(lambda: A_.activation(out=r, in_=banks[bi][:, 0:TG], func=AF.Sqrt, scale=128.0,
                                                     bias=eps128_t), reads=[bankR[bi], RC], writes=[rR])
                    else:
                        act.op(lambda: A_.activation(out=r, in_=banks[bi][:, 0:TG], func=AF.Sqrt, scale=1.0,
                                                     bias=eps_t), reads=[bankR[bi], RC], writes=[rR])
                    yield
                    dve.op(lambda: V.reciprocal(out=r, in_=r), reads=[rR], writes=[rR])
                    o, oR = yn.next()
                    pool.op(lambda: G.tensor_tensor(out=o, in0=y, in1=r, op=ALU.mult), reads=[yR, rR],
                            writes=[oR])
                    dst = GQT if c < 4 else GKT
                    sp.dma(dst[h * 128:(h + 1) * 128, b0:b0 + TG], o, reads=[oR])
                else:
                    o, oR = yn.next()
                    act.op(lambda: A_.activation(out=o, in_=a, func=AF.Silu), reads=[aR], writes=[oR])
                    yield
                if c >= 4:
                    bt = pt.next()
                    tpb = bank_bf(bt)
                    for i in range(nt):
                        pe.op(lambda i=i: T.transpose(out=tpb[:, i * 128:(i + 1) * 128],
                                                      in_=o[:, i * 128:(i + 1) * 128], identity=ident_bf),
                              reads=[oR, RC], writes=[bankR[bt]] if i == 0 else (),
                              dwrites=() if i == 0 else [bankR[bt]])
                    t, tR = tk.next()
                    act.op(lambda: A_.copy(out=t, in_=tpb[:, 0:TG]), reads=[bankR[bt]], writes=[tR])
                    dst = GK if c < 8 else GV
                    sp.dma(dst[b0:b0 + TG, h * 128:(h + 1) * 128].rearrange("(i p) d -> p i d", p=128),
                           t.rearrange("p (i d) -> p i d", d=128), reads=[tR])
            pend = pre(*items[0])
            prev = None
            for ii, (b0, c) in enumerate(items):
                xi, xiR = pend
                if ii + 1 < len(items):
                    pend = pre(*items[ii + 1])
                g_ = chunk(b0, c, xi, xiR)
                next(g_)
                if prev is not None:
                    for _ in prev:
                        pass
                prev = g_
            for _ in prev:
                pass
            kb.barrier()

        def phase_g2(l, si):
            S = seqs[si]
            ntile = S // 128
            kb.barrier()
            cv = Carve()

            def mk(fn, *a):
                return (fn(*a), Res())

            def alloc_set(d):
                B = {}
                B["qT"] = mk(cv.bf, 512, (4, 128))
                B["kT"] = mk(cv.bf, 512, (4, 128))
                B["ktok"] = mk(cv.bf, 512, (4, 128))
                B["vtok"] = mk(cv.bf, 512, (4, 128))
                B["db"] = mk(cv.f32, 16)
                B["sm"] = mk(cv.f32, 64)
                B["E"] = mk(cv.f32, 512, (4, 128))
                B["En"] = mk(cv.f32, 512, (4, 128))
                B["EG"] = mk(cv.f32, 512, (4, 128))
                rv = lambda i: rbuf[:, (5 * d + i) * 512:(5 * d + i + 1) * 512].rearrange("p (a b) -> p a b", a=4)
                B["Xs"] = [(rv(0), Res()), (rv(1), Res())]
                B["Ys"] = [(rv(2), Res()), (rv(3), Res())]
                B["R"] = rv(4), Res()
                B["Rb"] = mk(cv.bf, 512, (4, 128))
                B["attn"] = mk(cv.bf, 512, (4, 128))
                B["attnT"] = mk(cv.bf, 512, (4, 128))
                B["vb"] = mk(cv.bf, 512, (4, 128))
                B["kbg"] = mk(cv.bf, 512, (4, 128))
                B["kg"] = mk(cv.bf, 512, (4, 128))
                B["kg1"] = mk(cv.bf, 512, (4, 128))
                B["qgT"] = mk(cv.bf, 512, (4, 128))
                B["u"] = mk(cv.f32, 512, (4, 128))
                B["wT"] = mk(cv.bf, 512, (4, 128))
                B["vn"] = mk(cv.bf, 512, (4, 128))
                B["ost"] = mk(cv.f32, 512)
                B["S32"] = mk(cv.f32, 512, (4, 128))
                B["Sbf"] = mk(cv.bf, 512, (4, 128))
                return B
            bufs = [alloc_set(0), alloc_set(1)]
            for d in range(2):
                pool.op(lambda d=d: G.memset(bufs[d]["S32"][0], 0.0), writes=[bufs[d]["S32"][1]])
                pool.op(lambda d=d: G.memset(bufs[d]["Sbf"][0], 0.0), writes=[bufs[d]["Sbf"][1]])
                pool.op(lambda d=d: G.memset(bufs[d]["vn"][0], 0.0), writes=[bufs[d]["vn"][1]])
            def unit(d, ti, B):
                qT, qTR = B["qT"]
                kT, kTR = B["kT"]
                ktok, ktokR = B["ktok"]
                vtok, vtokR = B["vtok"]
                db, dbR = B["db"]
                sm, smR = B["sm"]
                E, ER = B["E"]
                En, EnR = B["En"]
                EG, EGR = B["EG"]
                Xs = B["Xs"]
                Ys = B["Ys"]
                R, RR = B["R"]
                Rb, RbR = B["Rb"]
                attn, attnR = B["attn"]
                attnT, attnTR = B["attnT"]
                vb, vbR = B["vb"]
                kbg, kbgR = B["kbg"]
                kg, kgR = B["kg"]
                kg1, kg1R = B["kg1"]
                qgT, qgTR = B["qgT"]
                u, uR = B["u"]
                wT, wTR = B["wT"]
                vn, vnR = B["vn"]
                ost, ostR = B["ost"]
                t0 = ti * 128
                bk = [4 * d + i_ for i_ in range(4)]
                s32, s32R = B["S32"]
                sbf, sbfR = B["Sbf"]
                sp.dma(qT, GQT[:, t0:t0 + 128].rearrange("(h p) t -> p h t", p=128), writes=[qTR])
                sp.dma(kT, GKT[:, t0:t0 + 128].rearrange("(h p) t -> p h t", p=128), writes=[kTR])
                sp.dma(ktok, GK[t0:t0 + 128, :].rearrange("t (h e) -> t h e", e=128), writes=[ktokR])
                sp.dma(vtok, GV[t0:t0 + 128, :].rearrange("t (h e) -> t h e", e=128), writes=[vtokR])
                sp.dma(db, DBt[t0:t0 + 128, :], writes=[dbR])
                dec = db[:, d * 4:d * 4 + 4]
                bet = db[:, 8 + d * 4:8 + d * 4 + 4]
                beta, nbeta, ee, g, gc, dl, kgf, egc, kbgf, spin = [sm[:, 4 * i:4 * i + 4] for i in range(10)]
                act.op(lambda: A_.activation(out=beta, in_=bet, func=AF.Sigmoid), reads=[dbR], writes=[smR])
                dve.op(lambda: V.tensor_scalar(out=nbeta, in0=beta, scalar1=-1.0, scalar2=None, op0=ALU.mult),
                       reads=[smR], dwrites=[smR])
                dve.op(lambda: V.tensor_tensor(out=spin, in0=dec, in1=dtb_b[:, d * 4:d * 4 + 4], op=ALU.add),
                       reads=[dbR, RL, smR], dwrites=[smR])
                act.op(lambda: A_.activation(out=ee, in_=spin, func=AF.Exp), reads=[smR], dwrites=[smR])
                act.op(lambda: A_.activation(out=ee, in_=ee, func=AF.Ln, bias=one_t), reads=[smR, RC],
                       dwrites=[smR])
                g = g_r[d]
                dve.op(lambda: V.tensor_tensor(out=g, in0=ee, in1=negA_b[:, d * 4:d * 4 + 4], op=ALU.mult),
                       reads=[smR, RL], dwrites=[smR])
                pe.op(lambda: T.matmul(banks[bk[0]][:, 0:4], Lr[d], g, start=True, stop=True), reads=[smR, RC],
                      writes=[bankR[bk[0]]])
                pe.op(lambda: T.matmul(banks[bk[0]][:, 4:8], Bor, g, start=True, stop=True),
                      reads=[smR, RC], dwrites=[bankR[bk[0]]])
                dve.op(lambda: V.tensor_copy(out=gc, in_=banks[bk[0]][:, 0:4]), reads=[bankR[bk[0]], smR], dwrites=[smR])
                dve.op(lambda: V.tensor_tensor(out=dl, in0=banks[bk[0]][:, 4:8], in1=gc, op=ALU.subtract),
                       reads=[bankR[bk[0]], smR], dwrites=[smR])
                act.op(lambda: A_.activation(out=kgf, in_=dl, func=AF.Exp), reads=[smR], dwrites=[smR])
                act.op(lambda: A_.activation(out=egc, in_=gc, func=AF.Exp), reads=[smR], dwrites=[smR])
                dve.op(lambda: V.tensor_tensor(out=kbgf, in0=egc, in1=beta, op=ALU.mult), reads=[smR],
                       dwrites=[smR])
                yield
                for h in range(4):
                    pe.op(lambda h=h: T.matmul(banks[bk[1]][:, h * 128:(h + 1) * 128],
                                               g[:, h:h + 1].to_broadcast([128, 128]), Lr[d],
                                               start=True, stop=True), reads=[smR, RC],
                          writes=[bankR[bk[1]]] if h == 0 else (), dwrites=() if h == 0 else [bankR[bk[1]]])
                pe.op(lambda: T.matmul(banks[bk[2]][:, :], ident_r, maskr[d], start=True, stop=False), reads=[RC],
                      writes=[bankR[bk[2]]])
                for h in range(4):
                    pe.op(lambda h=h: T.matmul(banks[bk[2]][:, h * 128:(h + 1) * 128],
                                               g[:, h:h + 1].to_broadcast([128, 128]), Lr[d],
                                               start=False, stop=(h == 3)), reads=[smR, RC], dwrites=[bankR[bk[2]]])
                for h in range(4):
                    act.op(lambda h=h: A_.activation(out=E[:, h, :], in_=banks[bk[2]][:, h * 128:(h + 1) * 128],
                                                     func=AF.Exp, scale=-1.0, bias=gc[:, h:h + 1]),
                           reads=[bankR[bk[2]], smR], writes=[ER] if h == 0 else (), dwrites=() if h == 0 else [ER])
                act.op(lambda: A_.activation(out=EG.rearrange("p h c -> p (h c)"), in_=banks[bk[1]][:, :], func=AF.Exp),
                       reads=[bankR[bk[1]]], writes=[EGR])
                pool.op(lambda: G.tensor_tensor(out=En.rearrange("p h c -> p (h c)"),
                                                in0=E.rearrange("p h c -> p (h c)"), in1=nodiag4, op=ALU.mult),
                        reads=[ER, RC], writes=[EnR])
                yield
                for h in range(4):
                    pe.op(lambda h=h: T.matmul(banks[bk[3]][:, h * 128:(h + 1) * 128], kT[:, h, :], kT[:, h, :],
                                               start=True, stop=True), reads=[kTR],
                          writes=[bankR[bk[3]]] if h == 0 else (), dwrites=() if h == 0 else [bankR[bk[3]]])
                for h in range(4):
                    pe.op(lambda h=h: T.matmul(banks[bk[0]][:, h * 128:(h + 1) * 128], qT[:, h, :], kT[:, h, :],
                                               start=True, stop=True), reads=[kTR, qTR],
                          writes=[bankR[bk[0]]] if h == 0 else (), dwrites=() if h == 0 else [bankR[bk[0]]])
                X0, X0R = Xs[0]
                Y0, Y0R = Ys[0]
                for h in range(4):
                    dve.op(lambda h=h: V.scalar_tensor_tensor(out=X0[:, h, :], in0=banks[bk[3]][:, h * 128:(h + 1) * 128],
                                                              scalar=nbeta[:, h:h + 1], in1=En[:, h, :],
                                                              op0=ALU.mult, op1=ALU.mult),
                           reads=[bankR[bk[3]], smR, EnR], writes=[X0R] if h == 0 else (),
                           dwrites=() if h == 0 else [X0R])
                dve.op(lambda: V.tensor_tensor(out=attn.rearrange("p h c -> p (h c)"), in0=banks[bk[0]][:, :],
                                               in1=E.rearrange("p h c -> p (h c)"), op=ALU.mult),
                       reads=[bankR[bk[0]], ER], writes=[attnR])
                yield
                b5r = banks[bk[1]][:, :].bitcast(F32R)
                for h in range(4):
                    pe.op(lambda h=h: T.matmul(banks[bk[1]][:, h * 128:(h + 1) * 128], X0[:, h, :], ident_r,
                                               start=True, stop=True),
                          reads=[X0R, RC],
                          writes=[bankR[bk[1]]] if h == 0 else (), dwrites=() if h == 0 else [bankR[bk[1]]])
                b6 = bank_bf(bk[2])
                for h in range(4):
                    pe.op(lambda h=h: T.transpose(out=b6[:, h * 128:(h + 1) * 128], in_=attn[:, h, :],
                                                  identity=ident_bf), reads=[attnR, RC],
                          writes=[bankR[bk[2]]] if h == 0 else (), dwrites=() if h == 0 else [bankR[bk[2]]])
                act.op(lambda: A_.copy(out=Y0.rearrange("p h c -> p (h c)"), in_=banks[bk[1]][:, :]),
                       reads=[bankR[bk[1]]], writes=[Y0R])
                dve.op(lambda: V.tensor_tensor(out=R.rearrange("p h c -> p (h c)"),
                                               in0=Y0.rearrange("p h c -> p (h c)").bitcast(F32),
                                               in1=ident4, op=ALU.add), reads=[Y0R, RC], writes=[RR])
                act.op(lambda: A_.copy(out=attnT.rearrange("p h c -> p (h c)"), in_=b6[:, 0:512]),
                       reads=[bankR[bk[2]]], writes=[attnTR])


                yield
                def mm4(bi, lhs, lhsR, rhs, rhsR):
                    for h in range(4):
                        pe.op(lambda h=h: T.matmul(banks[bi][:, h * 128:(h + 1) * 128], lhs[:, h, :], rhs[:, h, :],
                                                   start=True, stop=True), reads=[lhsR, rhsR],
                              writes=[bankR[bi]] if h == 0 else (), dwrites=() if h == 0 else [bankR[bi]])

                cur = 0
                for lev in range(1, 6):
                    Xc, XcR = Xs[cur]
                    Yc, YcR = Ys[cur]
                    Xn, XnR = Xs[1 - cur]
                    Yn, YnR = Ys[1 - cur]
                    if lev >= 2:
                        pass
                    if lev >= 2:
                        mm4(bk[3], Xc, XcR, R, RR)
                        dve.op(lambda: V.tensor_tensor(out=R.rearrange("p h c -> p (h c)"), in0=banks[bk[3]][:, :],
                                                       in1=R.rearrange("p h c -> p (h c)").bitcast(F32),
                                                       op=ALU.add), reads=[bankR[bk[3]], RR], writes=[RR])
                    mm4(bk[1], Yc, YcR, Xc, XcR)
                    act.op(lambda Xn=Xn: A_.copy(out=Xn.rearrange("p h c -> p (h c)"), in_=banks[bk[1]][:, :]),
                           reads=[bankR[bk[1]]], writes=[XnR])
                    if lev <= 4:
                        mm4(bk[2], Xc, XcR, Yc, YcR)
                        dve.op(lambda Yn=Yn: V.tensor_copy(out=Yn.rearrange("p h c -> p (h c)"), in_=banks[bk[2]][:, :]),
                               reads=[bankR[bk[2]]], writes=[YnR])
                    cur = 1 - cur
                    yield
                Xc, XcR = Xs[cur]
                mm4(bk[3], Xc, XcR, R, RR)
                dve.op(lambda: V.tensor_tensor(out=R.rearrange("p h c -> p (h c)"), in0=banks[bk[3]][:, :],
                                               in1=R.rearrange("p h c -> p (h c)").bitcast(F32), op=ALU.add),
                       reads=[bankR[bk[3]], RR], writes=[RR])
                act.op(lambda: A_.copy(out=Rb.rearrange("p h c -> p (h c)"),
                                       in_=R.rearrange("p h c -> p (h c)").bitcast(F32)), reads=[RR],
                       writes=[RbR])
                yield
                kgf0, kgf1 = sm[:, 40:44], sm[:, 44:48]
                dve.op(lambda: V.tensor_scalar(out=kgf0, in0=kgf, scalar1=ind_t[:, 0:1], scalar2=None, op0=ALU.mult),
                       reads=[smR, RI], dwrites=[smR])
                dve.op(lambda: V.tensor_scalar(out=kgf1, in0=kgf, scalar1=ind_t[:, 1:2], scalar2=None, op0=ALU.mult),
                       reads=[smR, RI], dwrites=[smR])
                for h in range(4):
                    act.op(lambda h=h: A_.activation(out=vb[:, h, :], in_=vtok[:, h, :], func=AF.Copy,
                                                     scale=beta[:, h:h + 1]), reads=[vtokR, smR],
                           writes=[vbR] if h == 0 else (), dwrites=() if h == 0 else [vbR])
                    dve.op(lambda h=h: V.tensor_scalar(out=kbg[:, h, :], in0=ktok[:, h, :],
                                                       scalar1=kbgf[:, h:h + 1], scalar2=None, op0=ALU.mult),
                           reads=[ktokR, smR], writes=[kbgR] if h == 0 else (), dwrites=() if h == 0 else [kbgR])
                    pool.op(lambda h=h: G.tensor_scalar(out=kg[:, h, :], in0=ktok[:, h, :],
                                                        scalar1=kgf0[:, h:h + 1], scalar2=1.0, op0=ALU.mult,
                                                        op1=ALU.mult),
                            reads=[ktokR, smR], writes=[kgR] if h == 0 else (), dwrites=() if h == 0 else [kgR])
                    dve.op(lambda h=h: V.tensor_scalar(out=kg1[:, h, :], in0=ktok[:, h, :],
                                                       scalar1=kgf1[:, h:h + 1], scalar2=None, op0=ALU.mult),
                           reads=[ktokR, smR], writes=[kg1R] if h == 0 else (), dwrites=() if h == 0 else [kg1R])
                pool.op(lambda: G.tensor_tensor(out=qgT.rearrange("p h c -> p (h c)"),
                                                in0=qT.rearrange("p h c -> p (h c)"),
                                                in1=EG.rearrange("p h c -> p (h c)"), op=ALU.mult),
                        reads=[qTR, EGR], writes=[qgTR])
                mm4(bk[3], Rb, RbR, vb, vbR)
                act.op(lambda: A_.copy(out=u.rearrange("p h c -> p (h c)"), in_=banks[bk[3]][:, :]), reads=[bankR[bk[3]]],
                       writes=[uR])
                mm4(bk[0], kbg, kbgR, Rb, RbR)
                dve.op(lambda: V.tensor_copy(out=wT.rearrange("p h c -> p (h c)"), in_=banks[bk[0]][:, :]),
                       reads=[bankR[bk[0]]], writes=[wTR])
                yield
                for step in range(2):
                    r0 = (0, 64)[step] if d == 0 else (64, 0)[step]
                    rows = slice(r0, r0 + 64)
                    for h in range(4):
                        pe.op(lambda h=h: T.matmul(banks[bk[0]][:, h * 128:(h + 1) * 128], wT[:, h, :],
                                                   sbf[:, h, :], start=True, stop=True), reads=[wTR, sbfR],
                              writes=[bankR[bk[0]]] if h == 0 else (), dwrites=() if h == 0 else [bankR[bk[0]]])
                    dve.op(lambda: V.tensor_tensor(out=vn[rows].rearrange("p h c -> p (h c)"),
                                                   in0=u[rows].rearrange("p h c -> p (h c)"),
                                                   in1=banks[bk[0]][rows, :], op=ALU.subtract),
                           reads=[bankR[bk[0]], uR], writes=[vnR])
                    for h in range(4):
                        pe.op(lambda h=h: T.matmul(banks[bk[1]][:, h * 128:(h + 1) * 128], qgT[:, h, :],
                                                   sbf[:, h, :], start=True, stop=False), reads=[qgTR, sbfR],
                              writes=[bankR[bk[1]]] if h == 0 else (), dwrites=() if h == 0 else [bankR[bk[1]]])
                        pe.op(lambda h=h: T.matmul(banks[bk[1]][:, h * 128:(h + 1) * 128], attnT[:, h, :],
                                                   vn[:, h, :], start=False, stop=True), reads=[attnTR, vnR],
                              dwrites=[bankR[bk[1]]])
                    for h in range(4):
                        pe.op(lambda h=h: T.matmul(banks[bk[2]][:, h * 128:(h + 1) * 128], (kg, kg1)[r0 // 64][:, h, :],
                                                   vn[:, h, :], start=True, stop=True), reads=[kgR, kg1R, vnR],
                              writes=[bankR[bk[2]]] if h == 0 else (), dwrites=() if h == 0 else [bankR[bk[2]]])
                    col = r0 + 63 if d == 0 else r0
                    for h in range(4):
                        dve.op(lambda h=h: V.scalar_tensor_tensor(out=s32[:, h, :], in0=s32[:, h, :],
                                                                  scalar=EG[:, h, col:col + 1],
                                                                  in1=banks[bk[2]][:, h * 128:(h + 1) * 128],
                                                                  op0=ALU.mult, op1=ALU.add),
                               reads=[bankR[bk[2]], EGR], writes=[s32R] if h == 0 else (),
                               dwrites=() if h == 0 else [s32R])
                    act.op(lambda: A_.copy(out=sbf.rearrange("p h c -> p (h c)"),
                                           in_=s32.rearrange("p h c -> p (h c)")), reads=[s32R], writes=[sbfR])
                    act.op(lambda: A_.copy(out=ost[rows, :], in_=banks[bk[1]][rows, :]), reads=[bankR[bk[1]]],
                           writes=[ostR] if step == 0 else (), dwrites=() if step == 0 else [ostR])
                    yield
                sp.dma(OD[d][t0:t0 + 128, :], ost, reads=[ostR])
            for it in range(ntile):
                alive = [unit(0, it, bufs[0]), unit(1, ntile - 1 - it, bufs[1])]
                while alive:
                    for g_ in list(alive):
                        try:
                            next(g_)
                        except StopIteration:
                            alive.remove(g_)
            kb.barrier()

        def phase_g3(l, si):
            S = seqs[si]
            kb.barrier()
            cv = Carve()
            ofs = Rot([(cv.f32(512), Res()) for _ in range(2)])
            obs = Rot([(cv.f32(512), Res()) for _ in range(2)])
            azs = Rot([(cv.bf(512), Res()) for _ in range(2)])
            junk = cv.f32(128)
            sm, smR = cv.f32(16), Res()
            ys = Rot([(cv.f32(512), Res()) for _ in range(2)])
            ybs = Rot([(cv.bf(512), Res()) for _ in range(2)])
            sts = Rot([(cv.bf(512), Res()) for _ in range(2)])
            pb = Rot([0, 1])
            def pre_g3(ti):
                t0 = ti * 128
                of, ofR = ofs.next()
                ob, obR = obs.next()
                az, azR = azs.next()
                sp.dma(of, OD[0][t0:t0 + 128, :], writes=[ofR])
                sp.dma(ob, OD[1][t0:t0 + 128, :], writes=[obR])
                sp.dma(az, AZ[t0:t0 + 128, :], writes=[azR])
                return of, ofR, ob, obR, az, azR
            pend = pre_g3(0)
            for ti in range(S // 128):
                t0 = ti * 128
                of, ofR, ob, obR, az, azR = pend
                if ti + 1 < S // 128:
                    pend = pre_g3(ti + 1)
                pool.op(lambda: G.tensor_tensor(out=of, in0=of, in1=ob, op=ALU.add), reads=[obR], writes=[ofR])
                for h in range(4):
                    act.op(lambda h=h: A_.activation(out=junk, in_=of[:, h * 128:(h + 1) * 128], func=AF.Square,
                                                     accum_out=sm[:, h:h + 1]), reads=[ofR], writes=[smR])
                act.op(lambda: A_.activation(out=sm[:, 4:8], in_=sm[:, 0:4], func=AF.Sqrt, scale=1.0 / 128, bias=eps_t),
                       reads=[smR, RC], dwrites=[smR])
                dve.op(lambda: V.reciprocal(out=sm[:, 8:12], in_=sm[:, 4:8]), reads=[smR], dwrites=[smR])
                y, yR = ys.next()
                for h in range(4):
                    dve.op(lambda h=h: V.scalar_tensor_tensor(out=y[:, h * 128:(h + 1) * 128],
                                                              in0=of[:, h * 128:(h + 1) * 128],
                                                              scalar=sm[:, 8 + h:9 + h], in1=gnorm_b, op0=ALU.mult,
                                                              op1=ALU.mult), reads=[ofR, smR, RL],
                           writes=[yR] if h == 0 else (), dwrites=() if h == 0 else [yR])
                yb, ybR = ybs.next()
                pool.op(lambda: G.tensor_tensor(out=yb, in0=y, in1=az, op=ALU.mult), reads=[yR, azR], writes=[ybR])
                bi = pb.next()
                tpb = bank_bf(bi)
                for h in range(4):
                    pe.op(lambda h=h: T.transpose(out=tpb[:, h * 128:(h + 1) * 128], in_=yb[:, h * 128:(h + 1) * 128],
                                                  identity=ident_bf), reads=[ybR, RC],
                          writes=[bankR[bi]] if h == 0 else (), dwrites=() if h == 0 else [bankR[bi]])
                st, stR = sts.next()
                act.op(lambda: A_.copy(out=st, in_=tpb[:, 0:512]), reads=[bankR[bi]], writes=[stR])
                sp.dma(YT[0:512, t0:t0 + 128].rearrange("(h e) t -> e h t", e=128),
                       st.rearrange("p (h t) -> p h t", t=128), reads=[stR])
            kb.barrier()

        def softmax_pv(items, QB, scale, bsc, pts, hook=None):
            n = len(items)
            sc = [None] * n

            def qk(i):
                it = items[i]
                bs = bsc.next()
                sc[i] = bs
                pe.op(lambda: T.matmul(banks[bs][:, 0:QB], it["kT"], it["q"], start=True, stop=True),
                      reads=[it["kR"], it["qR"]], writes=[bankR[bs]])
            for i in range(min(2, n)):
                qk(i)
            for i in range(n):
                it = items[i]
                bs = sc[i]
                bo, bd = it["bo"], it["bd"]
                p, pR = pts.next()
                act.op(lambda: A_.activation(out=p[:, 0:QB], in_=banks[bs][:, 0:QB], func=AF.Exp, scale=scale),
                       reads=[bankR[bs]], writes=[pR])
                pe.op(lambda: T.matmul(banks[bo][:, 0:QB], it["v"], p[:, 0:QB], start=it["first"], stop=it["last"]),
                      reads=[it["vR"], pR], writes=[bankR[bo]] if it["first"] else (),
                      dwrites=() if it["first"] else [bankR[bo]])
                pe.op(lambda: T.matmul(banks[bd][:, 0:QB], ones_bf, p[:, 0:QB], start=it["first"], stop=it["last"]),
                      reads=[RC, pR], writes=[bankR[bd]] if it["first"] else (),
                      dwrites=() if it["first"] else [bankR[bd]])
                if i + 2 < n:
                    qk(i + 2)
                if hook is not None and i == min(3, n - 1):
                    hook()

        def phase_b(l, si):
            S = seqs[si]
            QB = min(512, S)
            nkc = S // 128
            kb.barrier()
            cv = Carve()
            kTs = Rot([(cv.bf(S), (cv.bf(S), cv.bf(S)), cv.bf(S, (S // 128, 128)), Res()) for _ in range(2)])
            for (_k, (qz0, qz1), _v, kvR0) in kTs.items:
                pool.op(lambda: G.memset(qz0[64:128, :], 0.0), writes=[kvR0])
                pool.op(lambda: G.memset(qz1[0:64, :], 0.0), dwrites=[kvR0])
            zts = Rot([(cv.bf(QB), Res()) for _ in range(3)])
            pts = Rot([(cv.bf(512), Res()) for _ in range(4)])
            tmp = Rot([(cv.f32(512), Res()) for _ in range(6)])
            sqs = Rot([(cv.bf(512), Res()) for _ in range(2)])
            outs = Rot([(cv.bf(512), Res()) for _ in range(2)])
            bsc = Rot([0, 1, 7])

            def load_head(h):
                kTh, qTh, vh, kvR = kTs.next()
                sp.dma(kTh, BKT[h * 128:(h + 1) * 128, 0:S], writes=[kvR])
                sp.dma(qTh[0][0:64, :], BQT[h * 128:h * 128 + 64, 0:S], dwrites=[kvR])
                sp.dma(qTh[1][64:128, :], BQT[h * 128 + 64:(h + 1) * 128, 0:S], dwrites=[kvR])
                sp.dma(vh, BV[0:S, h * 128:(h + 1) * 128].rearrange("(i p) e -> p i e", p=128), dwrites=[kvR])
                return kTh, qTh, vh, kvR

            def load_z(h, qb):
                zt, ztR = zts.next()
                sp.dma(zt, BZT[h * 128:(h + 1) * 128, qb * QB:(qb + 1) * QB], writes=[ztR])
                return zt, ztR
            nqb = S // QB
            hq = [(h, qb) for h in range(4) for qb in range(nqb)]
            head_next = load_head(0)
            z_next = load_z(0, 0)
            pend2 = [None]
            for ii, (h, qb) in enumerate(hq):
                if qb == 0:
                    kTh, qTh, vh, kvR = head_next
                    if h + 1 < 4:
                        head_next = load_head(h + 1)
                zt, ztR = z_next
                if ii + 1 < len(hq):
                    z_next = load_z(*hq[ii + 1])
                if True:
                    q0 = qb * QB
                    items = []
                    for j in range(2):
                        for kc in range(nkc):
                            items.append(dict(kT=kTh[:, kc * 128:(kc + 1) * 128], q=qTh[j][:, q0:q0 + QB],
                                              v=vh[:, kc, :], kR=kvR, qR=kvR, vR=kvR, bo=2 + j, bd=4 + j,
                                              first=(kc == 0), last=(kc == nkc - 1)))
                    softmax_pv(items, QB, 0.125, bsc, pts, hook=pend2[0])
                    r0, r0R = tmp.next()
                    r1, r1R = tmp.next()
                    dve.op(lambda: V.reciprocal(out=r0[:, 0:QB], in_=banks[4][:, 0:QB]), reads=[bankR[4]], writes=[r0R])
                    dve.op(lambda: V.reciprocal(out=r1[:, 0:QB], in_=banks[5][:, 0:QB]), reads=[bankR[5]], writes=[r1R])
                    o0, o0R = tmp.next()
                    o1, o1R = tmp.next()
                    dve.op(lambda: V.tensor_tensor(out=o0[:, 0:QB], in0=banks[2][:, 0:QB], in1=r0[:, 0:QB], op=ALU.mult),
                           reads=[bankR[2], r0R], writes=[o0R])
                    dve.op(lambda: V.tensor_tensor(out=o1[:, 0:QB], in0=banks[3][:, 0:QB], in1=r1[:, 0:QB], op=ALU.mult),
                           reads=[bankR[3], r1R], writes=[o1R])
                    dve.op(lambda: V.scalar_tensor_tensor(out=o0[:, 0:QB], in0=o1[:, 0:QB], scalar=neglam,
                                                          in1=o0[:, 0:QB], op0=ALU.mult, op1=ALU.add),
                           reads=[o1R, RL], writes=[o0R])
                    sq, sqR = sqs.next()
                    pool.op(lambda: G.tensor_tensor(out=sq[:, 0:QB], in0=o0[:, 0:QB], in1=o0[:, 0:QB], op=ALU.mult),
                            reads=[o0R], writes=[sqR])
                    def p2(sq=sq, sqR=sqR, o0=o0, o0R=o0R, zt=zt, ztR=ztR, h=h, q0=q0):
                        pe.op(lambda: T.matmul(banks[6][:, 0:QB], ones_bf, sq[:, 0:QB], start=True, stop=True),
                              reads=[sqR, RC], writes=[bankR[6]])
                        rn, rnR = tmp.next()
                        act.op(lambda: A_.activation(out=rn[:, 0:QB], in_=banks[6][:, 0:QB], func=AF.Ln, scale=1.0 / 128,
                                                     bias=eps_t), reads=[bankR[6], RC], writes=[rnR])
                        act.op(lambda: A_.activation(out=rn[:, 0:QB], in_=rn[:, 0:QB], func=AF.Exp, scale=-0.5),
                               reads=[rnR], writes=[rnR])
                        dve.op(lambda: V.scalar_tensor_tensor(out=o0[:, 0:QB], in0=o0[:, 0:QB], scalar=dnorm_s,
                                                              in1=rn[:, 0:QB], op0=ALU.mult, op1=ALU.mult),
                               reads=[rnR, RL], writes=[o0R])
                        ot, otR = outs.next()
                        pool.op(lambda: G.tensor_tensor(out=ot[:, 0:QB], in0=o0[:, 0:QB], in1=zt, op=ALU.mult),
                                reads=[o0R, ztR], writes=[otR])
                        sp.dma(YT[512 + h * 128: 512 + (h + 1) * 128, q0:q0 + QB], ot[:, 0:QB], reads=[otR])
                    pend2[0] = p2
            pend2[0]()
            kb.barrier()

        def phase_m(l, si):
            S = seqs[si]
            QB = min(512, S)
            kb.barrier()
            cv = Carve()
            wkv = cv.bf(16 * 1024, (16, 1024))
            wkvR = Res()
            nmem_b = cv.f32(D)
            xts = Rot([(cv.f32(D), Res()) for _ in range(2)])
            junk = cv.bf(D)
            hbs = Rot([(cv.bf(D), Res()) for _ in range(2)])
            memT = cv.bf(16 * 256, (16, 256))
            memTR = Res()
            small, smallR = cv.f32(8), Res()
            kmT, kmTR = cv.bf(4 * 256, (4, 256)), Res()
            vm, vmR = cv.bf(2 * 512, (2, 512)), Res()
            qts = Rot([(cv.bf(QB), cv.bf(QB), Res()) for _ in range(2)])
            pts = Rot([(cv.bf(512), Res()) for _ in range(3)])
            tmp = Rot([(cv.f32(512), Res()) for _ in range(4)])
            outs = Rot([(cv.bf(512), Res()) for _ in range(2)])
            sp.dma(wkv, WKV[l], reads=[RW], writes=[wkvR])
            sp.dma(nmem_b, norm_mem[l].partition_broadcast(128), writes=[smallR])
            for i in range(2):
                xt, xtR = xts.next()
                hb, hbR = hbs.next()
                sp.dma(xt, mem_in[si * N_MEM + i * 128: si * N_MEM + (i + 1) * 128, :], writes=[xtR])
                rms_rows((junk, small[:, 0:1], small[:, 1:2]), xt, xtR, small[:, 2:3], smallR, D)
                dve.op(lambda: V.scalar_tensor_tensor(out=hb, in0=xt, scalar=small[:, 2:3], in1=nmem_b,
                                                      op0=ALU.mult, op1=ALU.mult), reads=[xtR, smallR], writes=[hbR])
                for half in range(2):
                    tpb = bank_bf(half)
                    for kk in range(8):
                        kc = half * 8 + kk
                        pe.op(lambda kc=kc, kk=kk, tpb=tpb: T.transpose(out=tpb[:, kk * 128:(kk + 1) * 128],
                                                                        in_=hb[:, kc * 128:(kc + 1) * 128],
                                                                        identity=ident_bf), reads=[hbR, RC],
                              writes=[bankR[half]] if kk == 0 else (), dwrites=() if kk == 0 else [bankR[half]])
                    dve.op(lambda: V.tensor_copy(out=memT[:, half * 8:(half + 1) * 8, i * 128:(i + 1) * 128],
                                                 in_=tpb.rearrange("p (k t) -> p k t", k=8)), reads=[bankR[half]],
                           dwrites=[memTR])
            for h in range(4):
                bi = 2 + h
                for kc in range(16):
                    pe.op(lambda kc=kc: T.matmul(banks[bi][:, 0:256], wkv[:, kc, h * 128:(h + 1) * 128], memT[:, kc, :],
                                                 start=(kc == 0), stop=(kc == 15)), reads=[wkvR, memTR],
                          writes=[bankR[bi]] if kc == 0 else (), dwrites=() if kc == 0 else [bankR[bi]])
                act.op(lambda: A_.copy(out=kmT[:, h, :], in_=banks[bi][:, 0:256]), reads=[bankR[bi]], dwrites=[kmTR])
            for mt in range(2):
                bi = 6 + mt
                for kc in range(16):
                    pe.op(lambda kc=kc: T.matmul(banks[bi][:, :], memT[:, kc, mt * 128:(mt + 1) * 128],
                                                 wkv[:, kc, 512:1024], start=(kc == 0), stop=(kc == 15)),
                          reads=[wkvR, memTR], writes=[bankR[bi]] if kc == 0 else (),
                          dwrites=() if kc == 0 else [bankR[bi]])
                dve.op(lambda: V.tensor_copy(out=vm[:, mt, :], in_=banks[bi][:, :]), reads=[bankR[bi]], dwrites=[vmR])
            bsc = Rot([0, 1, 7])
            bo = Rot([2, 3])
            bdn = Rot([4, 5])
            scale = 128.0 ** -0.5
            def load_q(h, qb):
                qt, zt, qR = qts.next()
                sp.dma(qt, MQT[h * 128:(h + 1) * 128, qb * QB:(qb + 1) * QB], writes=[qR])
                sp.dma(zt, MZT[h * 128:(h + 1) * 128, qb * QB:(qb + 1) * QB], dwrites=[qR])
                return qt, zt, qR
            hq = [(h, qb) for h in range(4) for qb in range(S // QB)]
            q_next = load_q(0, 0)
            for ii, (h, qb) in enumerate(hq):
                if True:
                    q0 = qb * QB
                    qt, zt, qR = q_next
                    if ii + 1 < len(hq):
                        q_next = load_q(*hq[ii + 1])
                    b_o = bo.next()
                    b_d = bdn.next()
                    items = [dict(kT=kmT[:, h, kc * 128:(kc + 1) * 128], q=qt, v=vm[:, kc, h * 128:(h + 1) * 128],
                                  kR=kmTR, qR=qR, vR=vmR, bo=b_o, bd=b_d, first=(kc == 0), last=(kc == 1))
                             for kc in range(2)]
                    softmax_pv(items, QB, scale, bsc, pts)
                    r, rR = tmp.next()
                    dve.op(lambda: V.reciprocal(out=r[:, 0:QB], in_=banks[b_d][:, 0:QB]), reads=[bankR[b_d]],
                           writes=[rR])
                    o, oR = tmp.next()
                    dve.op(lambda: V.tensor_tensor(out=o[:, 0:QB], in0=banks[b_o][:, 0:QB], in1=r[:, 0:QB],
                                                   op=ALU.mult), reads=[bankR[b_o], rR], writes=[oR])
                    ot, otR = outs.next()
                    pool.op(lambda: G.tensor_tensor(out=ot[:, 0:QB], in0=o[:, 0:QB], in1=zt, op=ALU.mult),
                            reads=[oR, qR], writes=[otR])
                    sp.dma(YT[1536 + h * 128: 1536 + (h + 1) * 128, q0:q0 + QB], ot[:, 0:QB], reads=[otR])
            kb.barrier()

        def phase_c(l, si):
            S = seqs[si]
            TC = min(S, 1024)
            kb.barrier()
            cv = Carve()
            us = Rot([(cv.bf(TC + 2), Res()) for _ in range(2)])
            gs = Rot([(cv.bf(TC), Res()) for _ in range(2)])
            acc = Rot([(cv.f32(TC), Res()) for _ in range(2)])
            outs = Rot([(cv.bf(TC), Res()) for _ in range(2)])
            def load_c(c, b0):
                ut, utR = us.next()
                gt, gtR = gs.next()
                lo = max(0, b0 - 1)
                hi = min(S, b0 + TC + 1)
                pool.op(lambda: G.memset(ut, 0.0), writes=[utR])
                sp.dma(ut[:, lo - (b0 - 1): hi - (b0 - 1)], CUT[c * 128:(c + 1) * 128, lo:hi], dwrites=[utR])
                sp.dma(gt, CGT[c * 128:(c + 1) * 128, b0:b0 + TC], writes=[gtR])
                return ut, utR, gt, gtR
            citems = [(c, b0) for c in range(4) for b0 in range(0, S, TC)]
            c_next = load_c(*citems[0])
            for ii, (c, b0) in enumerate(citems):
                if True:
                    ut, utR, gt, gtR = c_next
                    if ii + 1 < len(citems):
                        c_next = load_c(*citems[ii + 1])
                    a, aR = acc.next()
                    dve.op(lambda: V.tensor_scalar(out=a, in0=ut[:, 0:TC], scalar1=cconv[:, c * 3:c * 3 + 1],
                                                   scalar2=None, op0=ALU.mult), reads=[utR, RL], writes=[aR])
                    for wi in (1, 2):
                        dve.op(lambda wi=wi: V.scalar_tensor_tensor(out=a, in0=ut[:, wi:wi + TC],
                                                                    scalar=cconv[:, c * 3 + wi:c * 3 + wi + 1], in1=a,
                                                                    op0=ALU.mult, op1=ALU.add), reads=[utR, RL],
                               writes=[aR])
                    o, oR = outs.next()
                    pool.op(lambda: G.tensor_tensor(out=o, in0=a, in1=gt, op=ALU.mult), reads=[aR, gtR], writes=[oR])
                    sp.dma(YT[1024 + c * 128: 1024 + (c + 1) * 128, b0:b0 + TC], o, reads=[oR])
            kb.barrier()

        def phase_o(l, si, xsrc, xsrcR, xdst, xdstR):
            S = seqs[si]
            kb.barrier()
            cv = Carve()
            wo = cv.bf(16 * D, (16, D))
            woR = Res()
            yts = Rot([(cv.bf(16 * 128, (16, 128)), Res()) for _ in range(2)])
            y32 = Rot([(cv.f32(D), Res()) for _ in range(2)])
            xts = Rot([(cv.f32(D), Res()) for _ in range(2)])
            junk = cv.bf(D)
            small, smallR = cv.f32(8), Res()
            for q4 in range(4):
                sp.dma(wo[:, :, q4 * 512:(q4 + 1) * 512], WOUT[l][:, :, q4 * 512:(q4 + 1) * 512], reads=[RW],
                       dwrites=[woR])
            pbo = Rot([[0, 1, 2, 3], [4, 5, 6, 7]])
            def pre_o(ti):
                t0 = ti * 128
                yt, ytR = yts.next()
                sp.dma(yt, YT[:, t0:t0 + 128].rearrange("(k p) t -> p k t", p=128), writes=[ytR])
                xt, xtR = xts.next()
                sp.dma(xt, xsrc[offs[si] + t0: offs[si] + t0 + 128, :], reads=[xsrcR], writes=[xtR])
                return yt, ytR, xt, xtR
            pend = pre_o(0)
            for ti in range(S // 128):
                t0 = ti * 128
                yt, ytR, xt, xtR = pend
                if ti + 1 < S // 128:
                    pend = pre_o(ti + 1)
                bs = pbo.next()
                yv, yvR = y32.next()
                for cb in range(4):
                    bi = bs[cb]
                    for kc in range(16):
                        pe.op(lambda kc=kc: T.matmul(banks[bi][:, :], yt[:, kc, :], wo[:, kc, cb * 512:(cb + 1) * 512],
                                                     start=(kc == 0), stop=(kc == 15)), reads=[ytR, woR],
                              writes=[bankR[bi]] if kc == 0 else (), dwrites=() if kc == 0 else [bankR[bi]])
                    e = kb.ew.next()
                    if e is act:
                        act.op(lambda: A_.copy(out=yv[:, cb * 512:(cb + 1) * 512], in_=banks[bi][:, :]),
                               reads=[bankR[bi]], writes=[yvR] if cb == 0 else (), dwrites=() if cb == 0 else [yvR])
                    else:
                        dve.op(lambda: V.tensor_copy(out=yv[:, cb * 512:(cb + 1) * 512], in_=banks[bi][:, :]),
                               reads=[bankR[bi]], writes=[yvR] if cb == 0 else (), dwrites=() if cb == 0 else [yvR])
                rms_rows((junk, small[:, 0:1], small[:, 1:2]), yv, yvR, small[:, 2:3], smallR, D)
                dve.op(lambda: V.scalar_tensor_tensor(out=yv, in0=yv, scalar=small[:, 2:3], in1=npost_b, op0=ALU.mult,
                                                      op1=ALU.mult), reads=[smallR, RL], writes=[yvR])
                pool.op(lambda: G.tensor_tensor(out=yv, in0=yv, in1=xt, op=ALU.add), reads=[xtR], writes=[yvR])
                sp.dma(xdst[offs[si] + t0: offs[si] + t0 + 128, :], yv, reads=[yvR], dwrites=[xdstR])
            kb.barrier()

        XinR = Res("xin")
        X1R = Res("x1")
        YoR = Res("yout")
        for l in range(depth):
            load_layer_params(l)
            if depth == 1:
                xsrc, xsrcR, xdst, xdstR = x_in, XinR, y_out, YoR
            elif l == 0:
                xsrc, xsrcR, xdst, xdstR = x_in, XinR, X1, X1R
            else:
                xsrc, xsrcR, xdst, xdstR = X1, X1R, y_out, YoR
            for si in range(nseq):
                ph = phases.split(",")
                if "a" in ph:
                    phase_a(l, si, xsrc, xsrcR)
                if "g1" in ph:
                    phase_g1(l, si)
                if "g2" in ph:
                    phase_g2(l, si)
                if "g3" in ph:
                    phase_g3(l, si)
                if "b" in ph:
                    phase_b(l, si)
                if "m" in ph:
                    phase_m(l, si)
                if "c" in ph:
                    phase_c(l, si)
                if "o" in ph:
                    phase_o(l, si, xsrc, xsrcR, xdst, xdstR)
        kb.barrier()
    return nc


_CACHE = {}


def _run(seqs, depth, core_inputs, debug=False):
    key = (tuple(seqs), depth, debug)
    if key not in _CACHE:
        _CACHE[key] = build(list(seqs), depth, debug)
    nc = _CACHE[key]
    res = run_bass_kernel_spmd(nc, core_inputs, core_ids=list(range(len(core_inputs))))
    return res


def kernel(x_prompt, x_sample, mem_prompt, mem_sample, norm_pre, norm_post, norm_mem, w_in, gdn_conv,
           gdn_A_log, gdn_dt_bias, gdn_norm, diff_lambda, diff_norm, conv_w, w_mem_kv, w_out):
    n = 8
    f = lambda a: np.ascontiguousarray(np.asarray(a, dtype=np.float32))
    x_prompt, x_sample, mem_prompt, mem_sample = f(x_prompt), f(x_sample), f(mem_prompt), f(mem_sample)
    B, S, _ = x_prompt.shape
    DB, DS, _ = x_sample.shape
    pb = B // n
    db = DB // n
    seqs = [S] * pb + [DS] * db
    depth = np.asarray(norm_pre).shape[0]
    consts = _consts(max(seqs))
    shared = dict(norm_pre=f(norm_pre), norm_post=f(norm_post), norm_mem=f(norm_mem), w_in=f(w_in),
                  gdn_conv=f(gdn_conv), gdn_A_log=f(gdn_A_log), gdn_dt_bias=f(gdn_dt_bias), gdn_norm=f(gdn_norm),
                  diff_lambda=f(diff_lambda), diff_norm=f(diff_norm), conv_w=f(conv_w), w_mem_kv=f(w_mem_kv),
                  w_out=f(w_out), **consts)
    in_maps = []
    for c in range(n):
        xs = [x_prompt[c * pb + i] for i in range(pb)] + [x_sample[c * db + i] for i in range(db)]
        ms = [mem_prompt[c * pb + i] for i in range(pb)] + [mem_sample[c * db + i] for i in range(db)]
        m = dict(shared)
        m["x"] = np.ascontiguousarray(np.concatenate(xs, axis=0))
        m["mem"] = np.ascontiguousarray(np.concatenate(ms, axis=0))
        in_maps.append(m)
    res = _run(seqs, depth, in_maps)
    y_prompt = np.empty((B, S, D), np.float32)
    y_sample = np.empty((DB, DS, D), np.float32)
    for c in range(n):
        y = res.results[c]["y"]
        o = 0
        for i in range(pb):
            y_prompt[c * pb + i] = y[o:o + S]
            o += S
        for i in range(db):
            y_sample[c * db + i] = y[o:o + DS]
            o += DS
    return (y_prompt, y_sample)
```

```python
import math
from contextlib import ExitStack
import numpy as np
import ml_dtypes
import concourse.bass as bass
import concourse.mybir as mybir
from concourse.bass_utils import run_bass_kernel_spmd

F32 = mybir.dt.float32
BF16 = mybir.dt.bfloat16
F32R = mybir.dt.float32r
AF = mybir.ActivationFunctionType
ALU = mybir.AluOpType

D = 2048
W_BR = 512
IN_COLS = 7184
N_MEM = 256
EPS = 1e-6
NEG = 32768.0
C_AQKV, C_ADEC, C_AZ = 0, 1536, 1552
C_BQ, C_BK, C_BV, C_BZ = 2064, 2576, 3088, 3600
C_CB, C_CC, C_CX, C_CZ = 4112, 4624, 5136, 5648
C_MQ, C_MZ = 6160, 6672


class Res:
    __slots__ = ("w", "r", "wx", "name")

    def __init__(self, name=""):
        self.w = {}
        self.r = {}
        self.wx = {}
        self.name = name


class Slot:
    __slots__ = ("si", "val")


class Eng:
    RING = 12
    CAP = 30000

    def __init__(self, kb, name, obj):
        self.kb = kb
        self.name = name
        self.o = obj
        self.si = kb.newsem()
        self.cnt = 0
        self.known = {}
        self.ring = []
        self.ri = 0

    def _deps(self, reads, writes, dwrites):
        deps = {}

        def add(d):
            for k, v in d.items():
                if deps.get(k, 0) < v:
                    deps[k] = v
        for r in reads:
            add(r.w)
        for w in writes:
            add(w.w)
            add(w.r)
        for w in dwrites:
            add(w.r)
            add(w.wx)
        return deps

    def _wait(self, deps):
        for k, v in deps.items():
            if self.name == "pe" and k == self.si:
                continue
            if self.known.get(k, 0) < v:
                self.o.wait_ge(self.kb.sems[k], v)
                self.known[k] = v

    def _post(self, ev, reads, writes, dwrites):
        k, v = ev
        for r in reads:
            r.r[k] = v
        for w in writes:
            w.w = {k: v}
            w.wx = {k: v}
            w.r = {}
        for w in dwrites:
            w.w[k] = v

    def op(self, fn, reads=(), writes=(), dwrites=()):
        self._wait(self._deps(reads, writes, dwrites))
        ins = fn()
        self.cnt += 1
        ins.then_inc(self.kb.sems[self.si], 1)
        self._post((self.si, self.cnt), reads, writes, dwrites)
        if self.cnt >= self.CAP:
            self.si = self.kb.newsem()
            self.cnt = 0
        return ins

    def dma(self, out, in_, reads=(), writes=(), dwrites=()):
        self._wait(self._deps(reads, writes, dwrites))
        if len(self.ring) < self.RING:
            s = Slot()
            s.si = self.kb.newsem()
            s.val = 0
            self.ring.append(s)
        s = self.ring[self.ri % self.RING]
        self.ri += 1
        if s.val >= self.CAP:
            self._wait({s.si: s.val})
            s.si = self.kb.newsem()
            s.val = 0
        if s.val > 0:
            self._wait({s.si: s.val})
        ins = self.o.dma_start(out=out, in_=in_)
        s.val += 16
        ins.then_inc(self.kb.sems[s.si], 16)
        self._post((s.si, s.val), reads, writes, dwrites)
        return ins

    def last_events(self):
        ev = {}
        if self.cnt > 0:
            ev[self.si] = self.cnt
        for s in self.ring:
            if s.val > 0:
                ev[s.si] = s.val
        return ev


class _DmaRouter:
    def __init__(self, load_eng, store_eng):
        self.ld = load_eng
        self.st = store_eng

    def dma(self, out, in_, reads=(), writes=(), dwrites=()):
        is_store = "DRam" in type(out.tensor).__name__ and "DRam" not in type(in_.tensor).__name__
        eng = self.st if is_store else self.ld
        return eng.dma(out, in_, reads=reads, writes=writes, dwrites=dwrites)


class Rot:
    def __init__(self, items):
        self.items = items
        self.i = 0

    def next(self):
        it = self.items[self.i % len(self.items)]
        self.i += 1
        return it


class KB:
    def __init__(self, nc, es):
        self.nc = nc
        self.es = es
        self.sems = []
        self.pe = Eng(self, "pe", nc.tensor)
        self.act = Eng(self, "act", nc.scalar)
        self.dve = Eng(self, "dve", nc.vector)
        self.pool = Eng(self, "pool", nc.gpsimd)
        self.sp = Eng(self, "sp", nc.sync)
        self.engs = [self.pe, self.act, self.dve, self.pool, self.sp]
        self.ew = Rot([self.act, self.dve])
        self.sp = _DmaRouter(self.sp, self.pool)

    def newsem(self):
        h = self.es.enter_context(self.nc.semaphore(f"s{len(self.sems)}"))
        self.sems.append(h)
        return len(self.sems) - 1

    def barrier(self):
        ev = {}
        for e in self.engs:
            ev.update(e.last_events())
        for e in self.engs:
            e._wait(dict(ev))


def _consts(smax):
    c = {}
    eye = np.eye(128, dtype=np.float32)
    c["ident_bf"] = eye.astype(ml_dtypes.bfloat16)
    c["ones_bf"] = np.ones((128, 128), np.float32).astype(ml_dtypes.bfloat16)
    idx = np.arange(128)
    same = (idx[:, None] // 64) == (idx[None, :] // 64)
    lf = (same & (idx[:, None] <= idx[None, :])).astype(np.float32)
    lb = (same & (idx[:, None] >= idx[None, :])).astype(np.float32)
    bo = same.astype(np.float32)
    mf = np.where(same & (idx[:, None] >= idx[None, :]), 0.0, NEG).astype(np.float32)
    mb = np.where(same & (idx[:, None] <= idx[None, :]), 0.0, NEG).astype(np.float32)
    nodiag = (1.0 - eye).astype(np.float32)
    f32c = np.concatenate([eye, lf, lb, bo, np.tile(mf, (1, 4)), np.tile(mb, (1, 4)), np.tile(nodiag, (1, 4)),
                           np.tile(eye, (1, 4))], axis=1)
    c["f32c"] = np.ascontiguousarray(f32c)
    pt = np.zeros((128, 128), np.float32)
    for j in range(2):
        b = 64 * j
        for i in range(8):
            pt[b + 8 + i, b + i] = -1.0
            pt[b + i, b + 8 + i] = 1.0
    c["pt_bf"] = pt.astype(ml_dtypes.bfloat16)
    inv = (np.float32(500000.0) ** (-np.arange(0, 16, 2, dtype=np.float32) / np.float32(16))).astype(np.float32)
    ang = (np.arange(smax, dtype=np.float32)[:, None] * inv[None, :]).astype(np.float32)
    cos = np.cos(ang.astype(np.float64)).astype(np.float32).T
    sin = np.sin(ang.astype(np.float64)).astype(np.float32).T
    cf = np.ones((128, smax), np.float32)
    sf = np.zeros((128, smax), np.float32)
    for j in range(2):
        b = 64 * j
        cf[b:b + 8] = cos
        cf[b + 8:b + 16] = cos
        sf[b:b + 8] = sin
        sf[b + 8:b + 16] = sin
    c["rope_c"] = cf
    c["rope_s"] = sf
    return c


def build(seqs, depth, debug=False, phases="a,g1,g2,g3,b,m,c,o"):
    nc = bass.Bass("TRN2", target_bir_lowering=False)
    ntok = sum(seqs)
    nseq = len(seqs)
    smax = max(seqs)
    offs = [sum(seqs[:i]) for i in range(nseq)]

    def din(name, shape, dt=F32):
        return nc.dram_tensor(name, list(shape), dt, kind="ExternalInput").ap()

    def dscr(name, shape, dt):
        kind = "ExternalOutput" if debug else "Internal"
        return nc.dram_tensor(name, list(shape), dt, kind=kind).ap()

    x_in = din("x", [ntok, D])
    mem_in = din("mem", [nseq * N_MEM, D])
    norm_pre = din("norm_pre", [depth, D])
    norm_post = din("norm_post", [depth, D])
    norm_mem = din("norm_mem", [depth, D])
    w_in = din("w_in", [depth, D, IN_COLS])
    gdn_conv = din("gdn_conv", [depth, 5, 1536])
    gdn_A_log = din("gdn_A_log", [depth, 2, 4])
    gdn_dt_bias = din("gdn_dt_bias", [depth, 2, 4])
    gdn_norm = din("gdn_norm", [depth, 128])
    diff_lambda = din("diff_lambda", [depth, 4, 64])
    diff_norm = din("diff_norm", [depth, 128])
    conv_w = din("conv_w", [depth, 3, 512])
    w_mem_kv = din("w_mem_kv", [depth, D, 1024])
    w_out = din("w_out", [depth, D, D])
    c_ident_bf = din("ident_bf", [128, 128], BF16)
    c_ones_bf = din("ones_bf", [128, 128], BF16)
    c_pt_bf = din("pt_bf", [128, 128], BF16)
    c_f32c = din("f32c", [128, 4 * 128 + 4 * 512])
    c_rope_c = din("rope_c", [128, smax])
    c_rope_s = din("rope_s", [128, smax])
    y_out = nc.dram_tensor("y", [ntok, D], F32, kind="ExternalOutput").ap()

    WIN = dscr("WIN", [depth, 56, 128, 16, 128], BF16)
    WDB = dscr("WDB", [depth, 128, 16, 16], BF16)
    WOUT = dscr("WOUT", [depth, 128, 16, D], BF16)
    WKV = dscr("WKV", [depth, 128, 16, 1024], BF16)
    X1 = dscr("X1", [ntok, D], F32)
    QKVT = dscr("QKVT", [1536, smax], BF16)
    DBt = dscr("DBt", [smax, 16], F32)
    AZ = dscr("AZ", [smax, 512], BF16)
    BQT = dscr("BQT", [512, smax], BF16)
    BKT = dscr("BKT", [512, smax], BF16)
    BV = dscr("BV", [smax, 512], BF16)
    BZT = dscr("BZT", [512, smax], BF16)
    CUT = dscr("CUT", [512, smax], BF16)
    CGT = dscr("CGT", [512, smax], BF16)
    MQT = dscr("MQT", [512, smax], BF16)
    MZT = dscr("MZT", [512, smax], BF16)
    GQT = dscr("GQT", [512, smax], BF16)
    GKT = dscr("GKT", [512, smax], BF16)
    GK = dscr("GK", [smax, 512], BF16)
    GV = dscr("GV", [smax, 512], BF16)
    OD = [dscr("OF", [smax, 512], F32), dscr("OB", [smax, 512], F32)]
    YT = dscr("YT", [D, smax], BF16)

    es = ExitStack()
    with es:
        kb = KB(nc, es)
        pe, act, dve, pool, sp = kb.pe, kb.act, kb.dve, kb.pool, kb.sp
        ARENA_W = 38200
        arena = es.enter_context(nc.sbuf_tensor("arena", [128, ARENA_W], F32))
        rbuf = es.enter_context(nc.sbuf_tensor("rbuf", [128, 10 * 512 + 128 + 1424], F32R))
        cst = es.enter_context(nc.sbuf_tensor("cst", [128, 7800], F32))
        banks = [es.enter_context(nc.psum_tensor(f"bank{i}", [128, 512], F32)) for i in range(8)]
        bankR = [Res(f"bank{i}") for i in range(8)]

        coff = [0]

        def calloc(words):
            o = coff[0]
            coff[0] += words
            assert coff[0] <= 7800
            return cst[:, o:o + words]

        RC = Res("consts")
        f32c = calloc(4 * 128 + 4 * 512)
        ident_f = f32c[:, 0:128]
        Lf = f32c[:, 128:256]
        Lb = f32c[:, 256:384]
        mask4 = [f32c[:, 512:1024], f32c[:, 1024:1536]]
        nodiag4 = f32c[:, 1536:2048]
        ident4 = f32c[:, 2048:2560]
        Lmat = [Lf, Lb]
        ident_bf = calloc(64).bitcast(BF16)
        ones_bf = calloc(64).bitcast(BF16)
        pt_bf = calloc(64).bitcast(BF16)
        eps_t = calloc(1)
        eps128_t = calloc(1)
        one_t = calloc(1)
        sp.dma(f32c, c_f32c, writes=[RC])
        sp.dma(ident_bf, c_ident_bf, dwrites=[RC])
        sp.dma(ones_bf, c_ones_bf, dwrites=[RC])
        sp.dma(pt_bf, c_pt_bf, dwrites=[RC])
        pool.op(lambda: nc.gpsimd.memset(eps_t, EPS), dwrites=[RC])
        pool.op(lambda: nc.gpsimd.memset(eps128_t, EPS * 128.0), dwrites=[RC])
        pool.op(lambda: nc.gpsimd.memset(one_t, 1.0), dwrites=[RC])
        ind_t = calloc(2)
        RI = Res("ind")
        pool.op(lambda: nc.gpsimd.memset(ind_t, 0.0), writes=[RI])
        pool.op(lambda: nc.gpsimd.memset(ind_t[0:64, 0:1], 1.0), writes=[RI])
        pool.op(lambda: nc.gpsimd.memset(ind_t[64:128, 1:2], 1.0), writes=[RI])
        ident_r = rbuf[:, 10 * 512:10 * 512 + 128]
        dve.op(lambda: nc.vector.tensor_copy(out=ident_r, in_=ident_f), reads=[RC], dwrites=[RC])
        rb0 = 10 * 512 + 128
        Lr = [rbuf[:, rb0:rb0 + 128], rbuf[:, rb0 + 128:rb0 + 256]]
        Bor = rbuf[:, rb0 + 256:rb0 + 384]
        maskr = [rbuf[:, rb0 + 384:rb0 + 896], rbuf[:, rb0 + 896:rb0 + 1408]]
        g_r = [rbuf[:, rb0 + 1408:rb0 + 1412], rbuf[:, rb0 + 1412:rb0 + 1416]]
        dve.op(lambda: nc.vector.tensor_copy(out=rbuf[:, rb0:rb0 + 384], in_=f32c[:, 128:512]), reads=[RC], dwrites=[RC])
        dve.op(lambda: nc.vector.tensor_copy(out=rbuf[:, rb0 + 384:rb0 + 1408], in_=f32c[:, 512:1536]), reads=[RC],
               dwrites=[RC])
        RL = Res("layerparams")
        npre_b = calloc(D)
        npost_b = calloc(D)
        gconv = calloc(60)
        cconv = calloc(12)
        gnorm_b = calloc(128)
        alog_b = calloc(8)
        dtb_b = calloc(8)
        negA_b = calloc(8)
        dnorm_c = calloc(1)
        dnorm_s = calloc(1)
        lam_b = calloc(256)
        lam_t = calloc(256)
        lam_s = calloc(2)
        lam_e = calloc(2)
        neglam = calloc(1)

        RW = Res("weights")
        for l in range(depth):
            src = w_in[l][:, 0:1536].rearrange("(kc p) (ch c) -> ch p kc c", p=128, c=128)
            chunk_cols = [c0 for c0 in range(0, 1536, 128)] + [c0 for c0 in range(C_AZ, IN_COLS, 128)]
            assert len(chunk_cols) == 56
            for ci, c0 in enumerate(chunk_cols):
                srcc = w_in[l][:, c0:c0 + 128].rearrange("(kc p) c -> p kc c", p=128)
                pool.dma(WIN[l, ci], srcc, dwrites=[RW])
            pool.dma(WDB[l], w_in[l][:, C_ADEC:C_ADEC + 16].rearrange("(kc p) c -> p kc c", p=128), dwrites=[RW])
            for q4 in range(4):
                pool.dma(WOUT[l][:, :, q4 * 512:(q4 + 1) * 512],
                         w_out[l][:, q4 * 512:(q4 + 1) * 512].rearrange("(kc p) c -> p kc c", p=128), dwrites=[RW])
            for q4 in range(2):
                pool.dma(WKV[l][:, :, q4 * 512:(q4 + 1) * 512],
                         w_mem_kv[l][:, q4 * 512:(q4 + 1) * 512].rearrange("(kc p) c -> p kc c", p=128), dwrites=[RW])

        def chunk_index(col):
            if col < 1536:
                return col // 128
            return 12 + (col - C_AZ) // 128

        class Carve:
            def __init__(self):
                self.o = 0

            def f32(self, words, shape=None):
                ap = arena[:, self.o:self.o + words]
                self.o += words
                assert self.o <= ARENA_W, self.o
                if shape is not None:
                    ap = ap.rearrange("p (a b) -> p a b", a=shape[0]) if len(shape) == 2 else ap
                return ap

            def bf(self, elems, shape=None):
                words = (elems + 1) // 2
                ap = arena[:, self.o:self.o + words].bitcast(BF16)
                self.o += words
                assert self.o <= ARENA_W, self.o
                if shape is not None and len(shape) == 2:
                    ap = ap.rearrange("p (a b) -> p a b", a=shape[0])
                return ap

            def r32(self, words, shape=None):
                ap = arena[:, self.o:self.o + words].bitcast(F32R)
                self.o += words
                assert self.o <= ARENA_W, self.o
                if shape is not None and len(shape) == 2:
                    ap = ap.rearrange("p (a b) -> p a b", a=shape[0])
                return ap

        V = nc.vector
        G = nc.gpsimd
        A_ = nc.scalar
        T = nc.tensor

        def bank_bf(i):
            return banks[i][:, :].bitcast(BF16)

        def load_layer_params(l):
            kb.barrier()
            sp.dma(npre_b, norm_pre[l].partition_broadcast(128), writes=[RL])
            sp.dma(npost_b, norm_post[l].partition_broadcast(128), dwrites=[RL])
            sp.dma(gnorm_b, gdn_norm[l].partition_broadcast(128), dwrites=[RL])
            sp.dma(alog_b, gdn_A_log[l].rearrange("a h -> (a h)").partition_broadcast(128), dwrites=[RL])
            sp.dma(dtb_b, gdn_dt_bias[l].rearrange("a h -> (a h)").partition_broadcast(128), dwrites=[RL])
            sp.dma(lam_b, diff_lambda[l].rearrange("a d -> (a d)").partition_broadcast(128), dwrites=[RL])
            with nc.allow_non_contiguous_dma(reason="tiny parameter transposes"):
                for wi in range(5):
                    sp.dma(gconv.rearrange("p (c w) -> p c w", w=5)[:, :, wi],
                           gdn_conv[l, wi].rearrange("(c p) -> p c", p=128), dwrites=[RL])
                for wi in range(3):
                    sp.dma(cconv.rearrange("p (c w) -> p c w", w=3)[:, :, wi],
                           conv_w[l, wi].rearrange("(c p) -> p c", p=128), dwrites=[RL])
                sp.dma(dnorm_c, diff_norm[l].rearrange("(p o) -> p o", o=1), dwrites=[RL])
            lam_init = 0.8 - 0.6 * math.exp(-0.3 * l)
            act.op(lambda: A_.activation(out=negA_b, in_=alog_b, func=AF.Exp), reads=[RL], dwrites=[RL])
            dve.op(lambda: V.tensor_scalar(out=negA_b, in0=negA_b, scalar1=-1.0, scalar2=None, op0=ALU.mult),
                   reads=[RL], dwrites=[RL])
            dve.op(lambda: V.tensor_scalar(out=dnorm_s, in0=dnorm_c, scalar1=1.0 - lam_init, scalar2=None,
                                           op0=ALU.mult), reads=[RL], dwrites=[RL])
            lb3 = lam_b.rearrange("p (a d) -> p a d", d=64)
            lt3 = lam_t.rearrange("p (a d) -> p a d", d=64)
            dve.op(lambda: V.tensor_tensor(out=lt3[:, 0, :], in0=lb3[:, 0, :], in1=lb3[:, 1, :], op=ALU.mult),
                   reads=[RL], dwrites=[RL])
            dve.op(lambda: V.tensor_tensor(out=lt3[:, 1, :], in0=lb3[:, 2, :], in1=lb3[:, 3, :], op=ALU.mult),
                   reads=[RL], dwrites=[RL])
            dve.op(lambda: V.reduce_sum(out=lam_s[:, 0:1], in_=lt3[:, 0, :], axis=mybir.AxisListType.X),
                   reads=[RL], dwrites=[RL])
            dve.op(lambda: V.reduce_sum(out=lam_s[:, 1:2], in_=lt3[:, 1, :], axis=mybir.AxisListType.X),
                   reads=[RL], dwrites=[RL])
            act.op(lambda: A_.activation(out=lam_e, in_=lam_s, func=AF.Exp), reads=[RL], dwrites=[RL])
            dve.op(lambda: V.tensor_tensor(out=neglam, in0=lam_e[:, 1:2], in1=lam_e[:, 0:1], op=ALU.subtract),
                   reads=[RL], dwrites=[RL])
            dve.op(lambda: V.tensor_scalar(out=neglam, in0=neglam, scalar1=-lam_init, scalar2=None, op0=ALU.add),
                   reads=[RL], dwrites=[RL])
            kb.barrier()

        def rms_rows(cv, xt, xtR, rstd, tmpR, width):
            junk, ss, rms = cv
            act.op(lambda: A_.activation(out=junk, in_=xt, func=AF.Square, accum_out=ss),
                   reads=[xtR], writes=[tmpR])
            act.op(lambda: A_.activation(out=rms, in_=ss, func=AF.Sqrt, scale=1.0 / width, bias=eps_t),
                   reads=[tmpR, RC], dwrites=[tmpR])
            dve.op(lambda: V.reciprocal(out=rstd, in_=rms), reads=[tmpR], dwrites=[tmpR])

        def phase_a(l, si, xsrc, xsrcR):
            S = seqs[si]
            TB = min(S, 1024)
            NT = min(512, TB)
            kb.barrier()
            cv = Carve()
            xts = [(cv.f32(D), Res()) for _ in range(2)]
            junk = cv.bf(D)
            hbs = [(cv.bf(D), Res()) for _ in range(2)]
            hT = cv.bf(16 * TB, (16, TB))
            hTR = Res("hT")
            ring = Rot([(cv.bf(16 * 128, (16, 128)), Res()) for _ in range(8)])
            wides = [(cv.bf(16 * 512, (16, 512)), Res("wide")) for _ in range(2)]
            wdb = cv.bf(16 * 16, (16, 16))
            wdbR = Res("wdb")
            stg = Rot([(cv.f32(512), Res()) for _ in range(8)])
            ropeC = [(cv.f32(512), Res()) for _ in range(2)]
            ropeS = [(cv.f32(512), Res()) for _ in range(2)]
            small = cv.f32(8)
            smallR = Res()
            pb = Rot([2, 3, 4, 5, 6, 7])

            order = [c * 128 for c in range(12)]
            order += [col + h * 128 for col in (C_BQ, C_BK) for h in range(4)]
            order += [col + c * 128 for col in (C_BZ, C_MQ, C_MZ) for c in range(4)]
            order += [col + c * 128 for c in range(4) for col in (C_CB, C_CC, C_CX, C_CZ)]
            LOOK = 4
            pq = {"next": 0, "ready": []}

            def _issue():
                col = order[pq["next"] % len(order)]
                pq["next"] += 1
                w, wr = ring.next()
                sp.dma(w, WIN[l, chunk_index(col)], reads=[RW], writes=[wr])
                pq["ready"].append((col, w, wr))

            def load_chunk(col):
                while len(pq["ready"]) < 1:
                    _issue()
                c0, w, wr = pq["ready"].pop(0)
                assert c0 == col, (c0, col)
                while len(pq["ready"]) < LOOK and pq["next"] < pq["limit"]:
                    _issue()
                return w, wr
            pq["limit"] = len(order) * (S // TB)

            def fm_mm(w, wr, n, bi=None):
                bi = pb.next() if bi is None else bi
                for kc in range(16):
                    pe.op(lambda kc=kc: T.matmul(banks[bi][:, 0:NT], w[:, kc, :], hT[:, kc, n * NT:(n + 1) * NT],
                                                 start=(kc == 0), stop=(kc == 15)),
                          reads=[wr, hTR], writes=[bankR[bi]] if kc == 0 else (), dwrites=() if kc == 0 else [bankR[bi]])
                return bi

            for b0 in range(0, S, TB):
                tok0 = offs[si] + b0
                for i in range(TB // 128):
                    xt, xtR = xts[i % 2]
                    hb, hbR = hbs[i % 2]
                    sp.dma(xt, xsrc[tok0 + i * 128: tok0 + (i + 1) * 128, :], reads=[xsrcR], writes=[xtR])
                    rms_rows((junk, small[:, 0:1], small[:, 1:2]), xt, xtR, small[:, 2:3], smallR, D)
                    dve.op(lambda: V.scalar_tensor_tensor(out=hb, in0=xt, scalar=small[:, 2:3], in1=npre_b,
                                                          op0=ALU.mult, op1=ALU.mult),
                           reads=[xtR, smallR, RL], writes=[hbR])
                    for half in range(2):
                        tpb = bank_bf(half)
                        for kk in range(8):
                            kc = half * 8 + kk
                            pe.op(lambda kc=kc, kk=kk, tpb=tpb: T.transpose(out=tpb[:, kk * 128:(kk + 1) * 128],
                                                                            in_=hb[:, kc * 128:(kc + 1) * 128],
                                                                            identity=ident_bf),
                                  reads=[hbR, RC], writes=[bankR[half]] if kk == 0 else (),
                                  dwrites=() if kk == 0 else [bankR[half]])
                        e = kb.ew.next()
                        dst = hT[:, half * 8:(half + 1) * 8, i * 128:(i + 1) * 128]
                        srcv = tpb.rearrange("p (k t) -> p k t", k=8)
                        if e is act:
                            act.op(lambda: A_.copy(out=dst, in_=srcv), reads=[bankR[half]], dwrites=[hTR])
                        else:
                            dve.op(lambda: V.tensor_copy(out=dst, in_=srcv), reads=[bankR[half]], dwrites=[hTR])

                def store_fm(dst, row0, n, sv, svR):
                    sp.dma(dst[row0:row0 + 128, b0 + n * NT: b0 + (n + 1) * NT], sv, reads=[svR])

                def evac_copy_bf(bi, func=None):
                    sv, svR = stg.next()
                    svb = sv.bitcast(BF16)[:, 0:NT]
                    if func is not None:
                        act.op(lambda: A_.activation(out=svb, in_=banks[bi][:, 0:NT], func=func),
                               reads=[bankR[bi]], writes=[svR])
                    else:
                        e = kb.ew.next()
                        if e is act:
                            act.op(lambda: A_.copy(out=svb, in_=banks[bi][:, 0:NT]), reads=[bankR[bi]], writes=[svR])
                        else:
                            dve.op(lambda: V.tensor_copy(out=svb, in_=banks[bi][:, 0:NT]), reads=[bankR[bi]],
                                   writes=[svR])
                    return svb, svR

                nsub = TB // NT
                for c in range(12):
                    w, wr = load_chunk(c * 128)
                    for n in range(nsub):
                        bi = fm_mm(w, wr, n)
                        svb, svR = evac_copy_bf(bi)
                        store_fm(QKVT, c * 128, n, svb, svR)
                sp.dma(wdb, WDB[l], reads=[RW], writes=[wdbR])
                for wi_, col_ in enumerate((C_AZ, C_BV)):
                    wide_, wideR_ = wides[wi_]
                    for q4 in range(4):
                        sp.dma(wide_[:, :, q4 * 128:(q4 + 1) * 128], WIN[l, chunk_index(col_ + q4 * 128)], reads=[RW],
                               writes=[wideR_] if q4 == 0 else (), dwrites=() if q4 == 0 else [wideR_])
                for _ in range(LOOK):
                    if len(pq["ready"]) < LOOK and pq["next"] < pq["limit"]:
                        _issue()
                for i in range(TB // 128):
                    bi = pb.next()
                    for kc in range(16):
                        pe.op(lambda kc=kc: T.matmul(banks[bi][:, 0:16], hT[:, kc, i * 128:(i + 1) * 128],
                                                     wdb[:, kc, :], start=(kc == 0), stop=(kc == 15)),
                              reads=[wdbR, hTR], writes=[bankR[bi]] if kc == 0 else (),
                              dwrites=() if kc == 0 else [bankR[bi]])
                    sv, svR = stg.next()
                    dve.op(lambda: V.tensor_copy(out=sv[:, 0:16], in_=banks[bi][:, 0:16]), reads=[bankR[bi]],
                           writes=[svR])
                    sp.dma(DBt[b0 + i * 128: b0 + (i + 1) * 128, :], sv[:, 0:16], reads=[svR])

                def wide_tm(wsel, dst, func):
                    wide, wideR = wides[wsel]
                    for i in range(TB // 128):
                        bi = pb.next()
                        for kc in range(16):
                            pe.op(lambda kc=kc: T.matmul(banks[bi][:, :], hT[:, kc, i * 128:(i + 1) * 128],
                                                         wide[:, kc, :], start=(kc == 0), stop=(kc == 15)),
                                  reads=[wideR, hTR], writes=[bankR[bi]] if kc == 0 else (),
                                  dwrites=() if kc == 0 else [bankR[bi]])
                        sv, svR = stg.next()
                        svb = sv.bitcast(BF16)[:, 0:512]
                        if func is not None:
                            act.op(lambda: A_.activation(out=svb, in_=banks[bi][:, :], func=func), reads=[bankR[bi]],
                                   writes=[svR])
                        else:
                            dve.op(lambda: V.tensor_copy(out=svb, in_=banks[bi][:, :]), reads=[bankR[bi]], writes=[svR])
                        sp.dma(dst[b0 + i * 128: b0 + (i + 1) * 128, :], svb, reads=[svR])

                wide_tm(0, AZ, AF.Silu)
                wide_tm(1, BV, None)
                assert nsub <= 2
                for n in range(nsub):
                    p0 = b0 + n * NT
                    sp.dma(ropeC[n][0][:, 0:NT], c_rope_c[:, p0:p0 + NT], writes=[ropeC[n][1]])
                    sp.dma(ropeS[n][0][:, 0:NT], c_rope_s[:, p0:p0 + NT], writes=[ropeS[n][1]])
                pendr = [None]
                for qk, (col, dst) in enumerate(((C_BQ, BQT), (C_BK, BKT))):
                    for h in range(4):
                        w, wr = load_chunk(col + h * 128)
                        for n in range(nsub):
                            rc, rcR = ropeC[n]
                            rs, rsR = ropeS[n]
                            bi = fm_mm(w, wr, n)
                            if pendr[0] is not None:
                                pendr[0]()
                            sv, svR = stg.next()
                            qsb = sv.bitcast(BF16)[:, 0:NT]
                            act.op(lambda: A_.copy(out=qsb, in_=banks[bi][:, 0:NT]), reads=[bankR[bi]], writes=[svR])

                            def rot_rest(qsb=qsb, svR=svR, rc=rc, rcR=rcR, rs=rs, rsR=rsR, n=n, dst=dst, h=h):
                                b2 = pb.next()
                                pe.op(lambda: T.matmul(banks[b2][:, 0:NT], pt_bf, qsb, start=True, stop=True),
                                      reads=[svR, RC], writes=[bankR[b2]])
                                t1, t1R = stg.next()
                                dve.op(lambda: V.tensor_tensor(out=t1[:, 0:NT], in0=banks[b2][:, 0:NT], in1=rs[:, 0:NT],
                                                               op=ALU.mult), reads=[bankR[b2], rsR], writes=[t1R])
                                t2, t2R = stg.next()
                                pool.op(lambda: G.tensor_tensor(out=t2[:, 0:NT], in0=qsb, in1=rc[:, 0:NT], op=ALU.mult),
                                        reads=[svR, rcR], writes=[t2R])
                                o, oR = stg.next()
                                ob = o.bitcast(BF16)[:, 0:NT]
                                dve.op(lambda: V.tensor_tensor(out=ob, in0=t1[:, 0:NT], in1=t2[:, 0:NT], op=ALU.add),
                                       reads=[t1R, t2R], writes=[oR])
                                store_fm(dst, h * 128, n, ob, oR)
                            pendr[0] = rot_rest
                pendr[0]()
                for col, dst, func in ((C_BZ, BZT, AF.Silu), (C_MQ, MQT, None), (C_MZ, MZT, AF.Silu)):
                    for c in range(4):
                        w, wr = load_chunk(col + c * 128)
                        for n in range(nsub):
                            bi = fm_mm(w, wr, n)
                            svb, svR = evac_copy_bf(bi, func)
                            store_fm(dst, c * 128, n, svb, svR)
                for c in range(4):
                    ws = [load_chunk(col + c * 128) for col in (C_CB, C_CC, C_CX, C_CZ)]
                    for n in range(nsub):
                        bis = [fm_mm(w, wr, n) for (w, wr) in ws]
                        ccs, ccR = stg.next()
                        act.op(lambda: A_.copy(out=ccs[:, 0:NT], in_=banks[bis[1]][:, 0:NT]), reads=[bankR[bis[1]]],
                               writes=[ccR])
                        u, uR = stg.next()
                        ub = u.bitcast(BF16)[:, 0:NT]
                        dve.op(lambda: V.tensor_tensor(out=ub, in0=banks[bis[2]][:, 0:NT], in1=ccs[:, 0:NT],
                                                       op=ALU.mult), reads=[bankR[bis[2]], ccR], writes=[uR])
                        store_fm(CUT, c * 128, n, ub, uR)
                        sz, szR = stg.next()
                        act.op(lambda: A_.activation(out=sz[:, 0:NT], in_=banks[bis[3]][:, 0:NT], func=AF.Silu),
                               reads=[bankR[bis[3]]], writes=[szR])
                        g, gR = stg.next()
                        gb = g.bitcast(BF16)[:, 0:NT]
                        dve.op(lambda: V.tensor_tensor(out=gb, in0=banks[bis[0]][:, 0:NT], in1=sz[:, 0:NT],
                                                       op=ALU.mult), reads=[bankR[bis[0]], szR], writes=[gR])
                        store_fm(CGT, c * 128, n, gb, gR)
            kb.barrier()

        def phase_g1(l, si):
            S = seqs[si]
            TG = min(S, 512)
            kb.barrier()
            cv = Carve()
            xin = Rot([(cv.bf(TG + 4), Res()) for _ in range(2)])
            acc = Rot([(cv.f32(TG), Res()) for _ in range(2)])
            ys = Rot([(cv.f32(TG), Res()) for _ in range(2)])
            sq = Rot([(cv.bf(TG), Res()) for _ in range(2)])
            rn = Rot([(cv.f32(TG), Res()) for _ in range(2)])
            yn = Rot([(cv.bf(TG), Res()) for _ in range(3)])
            tk = Rot([(cv.bf(TG), Res()) for _ in range(2)])
            pb = Rot([0, 1, 2, 3])
            pt = Rot([4, 5, 6, 7])
            nt = TG // 128
            items = [(b0, c) for b0 in range(0, S, TG) for c in range(12)]

            def pre(b0, c):
                xi, xiR = xin.next()
                lo = max(0, b0 - 2)
                hi = min(S, b0 + TG + 2)
                pool.op(lambda: G.memset(xi, 0.0), writes=[xiR])
                sp.dma(xi[:, lo - (b0 - 2): hi - (b0 - 2)], QKVT[c * 128:(c + 1) * 128, lo:hi], dwrites=[xiR])
                return xi, xiR
            def chunk(b0, c, xi, xiR):
                a, aR = acc.next()
                dve.op(lambda: V.tensor_scalar(out=a, in0=xi[:, 0:TG], scalar1=gconv[:, c * 5:c * 5 + 1],
                                               scalar2=None, op0=ALU.mult), reads=[xiR, RL], writes=[aR])
                for wi in range(1, 5):
                    dve.op(lambda wi=wi: V.scalar_tensor_tensor(out=a, in0=xi[:, wi:wi + TG],
                                                                scalar=gconv[:, c * 5 + wi:c * 5 + wi + 1], in1=a,
                                                                op0=ALU.mult, op1=ALU.add),
                           reads=[xiR, RL], writes=[aR])
                h = c % 4
                if c < 8:
                    y, yR = ys.next()
                    act.op(lambda: A_.activation(out=y, in_=a, func=AF.Silu), reads=[aR], writes=[yR])
                    s2, s2R = sq.next()
                    act.op(lambda: A_.activation(out=s2, in_=y, func=AF.Square), reads=[yR], writes=[s2R])
                    bi = pb.next()
                    pe.op(lambda: T.matmul(banks[bi][:, 0:TG], ones_bf, s2, start=True, stop=True),
                          reads=[s2R, RC], writes=[bankR[bi]])
                    r, rR = rn.next()
                    if c < 4:
                        act.op(lambda: A_.activation(out=r, in_=banks[bi][:, 0:TG], func=AF.Sqrt, scale=128.0,
                                                     bias=eps128_t), reads=[bankR[bi], RC], writes=[rR])
                    else:
                        act.op(lambda: A_.activation(out=r, in_=banks[bi][:, 0:TG], func=AF.Sqrt, scale=1.0,
                                                     bias=eps_t), reads=[bankR[bi], RC], writes=[rR])
                    yield
                    dve.op(lambda: V.reciprocal(out=r, in_=r), reads=[rR], writes=[rR])
                    o, oR = yn.next()
                    pool.op(lambda: G.tensor_tensor(out=o, in0=y, in1=r, op=ALU.mult), reads=[yR, rR],
                            writes=[oR])
                    dst = GQT if c < 4 else GKT
                    sp.dma(dst[h * 128:(h + 1) * 128, b0:b0 + TG], o, reads=[oR])
                else:
                    o, oR = yn.next()
                    act.op(lambda: A_.activation(out=o, in_=a, func=AF.Silu), reads=[aR], writes=[oR])
                    yield
                if c >= 4:
                    bt = pt.next()
                    tpb = bank_bf(bt)
                    for i in range(nt):
                        pe.op(lambda i=i: T.transpose(out=tpb[:, i * 128:(i + 1) * 128],
                                                      in_=o[:, i * 128:(i + 1) * 128], identity=ident_bf),
                              reads=[oR, RC], writes=[bankR[bt]] if i == 0 else (),
                              dwrites=() if i == 0 else [bankR[bt]])
                    t, tR = tk.next()
                    act.op(lambda: A_.copy(out=t, in_=tpb[:, 0:TG]), reads=[bankR[bt]], writes=[tR])
                    dst = GK if c < 8 else GV
                    sp.dma(dst[b0:b0 + TG, h * 128:(h + 1) * 128].rearrange("(i p) d -> p i d", p=128),
                           t.rearrange("p (i d) -> p i d", d=128), reads=[tR])
            pend = pre(*items[0])
            prev = None
            for ii, (b0, c) in enumerate(items):
                xi, xiR = pend
                if ii + 1 < len(items):
                    pend = pre(*items[ii + 1])
                g_ = chunk(b0, c, xi, xiR)
                next(g_)
                if prev is not None:
                    for _ in prev:
                        pass
                prev = g_
            for _ in prev:
                pass
            kb.barrier()

        def phase_g2(l, si):
            S = seqs[si]
            ntile = S // 128
            kb.barrier()
            cv = Carve()

            def mk(fn, *a):
                return (fn(*a), Res())

            def alloc_set(d):
                B = {}
                B["qT"] = mk(cv.bf, 512, (4, 128))
                B["kT"] = mk(cv.bf, 512, (4, 128))
                B["ktok"] = mk(cv.bf, 512, (4, 128))
                B["vtok"] = mk(cv.bf, 512, (4, 128))
                B["db"] = mk(cv.f32, 16)
                B["sm"] = mk(cv.f32, 64)
                B["E"] = mk(cv.f32, 512, (4, 128))
                B["En"] = mk(cv.f32, 512, (4, 128))
                B["EG"] = mk(cv.f32, 512, (4, 128))
                rv = lambda i: rbuf[:, (5 * d + i) * 512:(5 * d + i + 1) * 512].rearrange("p (a b) -> p a b", a=4)
                B["Xs"] = [(rv(0), Res()), (rv(1), Res())]
                B["Ys"] = [(rv(2), Res()), (rv(3), Res())]
                B["R"] = rv(4), Res()
                B["Rb"] = mk(cv.bf, 512, (4, 128))
                B["attn"] = mk(cv.bf, 512, (4, 128))
                B["attnT"] = mk(cv.bf, 512, (4, 128))
                B["vb"] = mk(cv.bf, 512, (4, 128))
                B["kbg"] = mk(cv.bf, 512, (4, 128))
                B["kg"] = mk(cv.bf, 512, (4, 128))
                B["kg1"] = mk(cv.bf, 512, (4, 128))
                B["qgT"] = mk(cv.bf, 512, (4, 128))
                B["u"] = mk(cv.f32, 512, (4, 128))
                B["wT"] = mk(cv.bf, 512, (4, 128))
                B["vn"] = mk(cv.bf, 512, (4, 128))
                B["ost"] = mk(cv.f32, 512)
                B["S32"] = mk(cv.f32, 512, (4, 128))
                B["Sbf"] = mk(cv.bf, 512, (4, 128))
                return B
            bufs = [alloc_set(0), alloc_set(1)]
            for d in range(2):
                pool.op(lambda d=d: G.memset(bufs[d]["S32"][0], 0.0), writes=[bufs[d]["S32"][1]])
                pool.op(lambda d=d: G.memset(bufs[d]["Sbf"][0], 0.0), writes=[bufs[d]["Sbf"][1]])
                pool.op(lambda d=d: G.memset(bufs[d]["vn"][0], 0.0), writes=[bufs[d]["vn"][1]])
            def unit(d, ti, B):
                qT, qTR = B["qT"]
                kT, kTR = B["kT"]
                ktok, ktokR = B["ktok"]
                vtok, vtokR = B["vtok"]
                db, dbR = B["db"]
                sm, smR = B["sm"]
                E, ER = B["E"]
                En, EnR = B["En"]
                EG, EGR = B["EG"]
                Xs = B["Xs"]
                Ys = B["Ys"]
                R, RR = B["R"]
                Rb, RbR = B["Rb"]
                attn, attnR = B["attn"]
                attnT, attnTR = B["attnT"]
                vb, vbR = B["vb"]
                kbg, kbgR = B["kbg"]
                kg, kgR = B["kg"]
                kg1, kg1R = B["kg1"]
                qgT, qgTR = B["qgT"]
                u, uR = B["u"]
                wT, wTR = B["wT"]
                vn, vnR = B["vn"]
                ost, ostR = B["ost"]
                t0 = ti * 128
                bk = [4 * d + i_ for i_ in range(4)]
                s32, s32R = B["S32"]
                sbf, sbfR = B["Sbf"]
                sp.dma(qT, GQT[:, t0:t0 + 128].rearrange("(h p) t -> p h t", p=128), writes=[qTR])
                sp.dma(kT, GKT[:, t0:t0 + 128].rearrange("(h p) t -> p h t", p=128), writes=[kTR])
                sp.dma(ktok, GK[t0:t0 + 128, :].rearrange("t (h e) -> t h e", e=128), writes=[ktokR])
                sp.dma(vtok, GV[t0:t0 + 128, :].rearrange("t (h e) -> t h e", e=128), writes=[vtokR])
                sp.dma(db, DBt[t0:t0 + 128, :], writes=[dbR])
                dec = db[:, d * 4:d * 4 + 4]
                bet = db[:, 8 + d * 4:8 + d * 4 + 4]
                beta, nbeta, ee, g, gc, dl, kgf, egc, kbgf, spin = [sm[:, 4 * i:4 * i + 4] for i in range(10)]
                act.op(lambda: A_.activation(out=beta, in_=bet, func=AF.Sigmoid), reads=[dbR], writes=[smR])
                dve.op(lambda: V.tensor_scalar(out=nbeta, in0=beta, scalar1=-1.0, scalar2=None, op0=ALU.mult),
                       reads=[smR], dwrites=[smR])
                dve.op(lambda: V.tensor_tensor(out=spin, in0=dec, in1=dtb_b[:, d * 4:d * 4 + 4], op=ALU.add),
                       reads=[dbR, RL, smR], dwrites=[smR])
                act.op(lambda: A_.activation(out=ee, in_=spin, func=AF.Exp), reads=[smR], dwrites=[smR])
                act.op(lambda: A_.activation(out=ee, in_=ee, func=AF.Ln, bias=one_t), reads=[smR, RC],
                       dwrites=[smR])
                g = g_r[d]
                dve.op(lambda: V.tensor_tensor(out=g, in0=ee, in1=negA_b[:, d * 4:d * 4 + 4], op=ALU.mult),
                       reads=[smR, RL], dwrites=[smR])
                pe.op(lambda: T.matmul(banks[bk[0]][:, 0:4], Lr[d], g, start=True, stop=True), reads=[smR, RC],
                      writes=[bankR[bk[0]]])
                pe.op(lambda: T.matmul(banks[bk[0]][:, 4:8], Bor, g, start=True, stop=True),
                      reads=[smR, RC], dwrites=[bankR[bk[0]]])
                dve.op(lambda: V.tensor_copy(out=gc, in_=banks[bk[0]][:, 0:4]), reads=[bankR[bk[0]], smR], dwrites=[smR])
                dve.op(lambda: V.tensor_tensor(out=dl, in0=banks[bk[0]][:, 4:8], in1=gc, op=ALU.subtract),
                       reads=[bankR[bk[0]], smR], dwrites=[smR])
                act.op(lambda: A_.activation(out=kgf, in_=dl, func=AF.Exp), reads=[smR], dwrites=[smR])
                act.op(lambda: A_.activation(out=egc, in_=gc, func=AF.Exp), reads=[smR], dwrites=[smR])
                dve.op(lambda: V.tensor_tensor(out=kbgf, in0=egc, in1=beta, op=ALU.mult), reads=[smR],
                       dwrites=[smR])
                yield
                for h in range(4):
                    pe.op(lambda h=h: T.matmul(banks[bk[1]][:, h * 128:(h + 1) * 128],
                                               g[:, h:h + 1].to_broadcast([128, 128]), Lr[d],
                                               start=True, stop=True), reads=[smR, RC],
                          writes=[bankR[bk[1]]] if h == 0 else (), dwrites=() if h == 0 else [bankR[bk[1]]])
                pe.op(lambda: T.matmul(banks[bk[2]][:, :], ident_r, maskr[d], start=True, stop=False), reads=[RC],
                      writes=[bankR[bk[2]]])
                for h in range(4):
                    pe.op(lambda h=h: T.matmul(banks[bk[2]][:, h * 128:(h + 1) * 128],
                                               g[:, h:h + 1].to_broadcast([128, 128]), Lr[d],
                                               start=False, stop=(h == 3)), reads=[smR, RC], dwrites=[bankR[bk[2]]])
                for h in range(4):
                    act.op(lambda h=h: A_.activation(out=E[:, h, :], in_=banks[bk[2]][:, h * 128:(h + 1) * 128],
                                                     func=AF.Exp, scale=-1.0, bias=gc[:, h:h + 1]),
                           reads=[bankR[bk[2]], smR], writes=[ER] if h == 0 else (), dwrites=() if h == 0 else [ER])
                act.op(lambda: A_.activation(out=EG.rearrange("p h c -> p (h c)"), in_=banks[bk[1]][:, :], func=AF.Exp),
                       reads=[bankR[bk[1]]], writes=[EGR])
                pool.op(lambda: G.tensor_tensor(out=En.rearrange("p h c -> p (h c)"),
                                                in0=E.rearrange("p h c -> p (h c)"), in1=nodiag4, op=ALU.mult),
                        reads=[ER, RC], writes=[EnR])
                yield
                for h in range(4):
                    pe.op(lambda h=h: T.matmul(banks[bk[3]][:, h * 128:(h + 1) * 128], kT[:, h, :], kT[:, h, :],
                                               start=True, stop=True), reads=[kTR],
                          writes=[bankR[bk[3]]] if h == 0 else (), dwrites=() if h == 0 else [bankR[bk[3]]])
                for h in range(4):
                    pe.op(lambda h=h: T.matmul(banks[bk[0]][:, h * 128:(h + 1) * 128], qT[:, h, :], kT[:, h, :],
                                               start=True, stop=True), reads=[kTR, qTR],
                          writes=[bankR[bk[0]]] if h == 0 else (), dwrites=() if h == 0 else [bankR[bk[0]]])
                X0, X0R = Xs[0]
                Y0, Y0R = Ys[0]
                for h in range(4):
                    dve.op(lambda h=h: V.scalar_tensor_tensor(out=X0[:, h, :], in0=banks[bk[3]][:, h * 128:(h + 1) * 128],
                                                              scalar=nbeta[:, h:h + 1], in1=En[:, h, :],
                                                              op0=ALU.mult, op1=ALU.mult),
                           reads=[bankR[bk[3]], smR, EnR], writes=[X0R] if h == 0 else (),
                           dwrites=() if h == 0 else [X0R])
                dve.op(lambda: V.tensor_tensor(out=attn.rearrange("p h c -> p (h c)"), in0=banks[bk[0]][:, :],
                                               in1=E.rearrange("p h c -> p (h c)"), op=ALU.mult),
                       reads=[bankR[bk[0]], ER], writes=[attnR])
                yield
                b5r = banks[bk[1]][:, :].bitcast(F32R)
                for h in range(4):
                    pe.op(lambda h=h: T.matmul(banks[bk[1]][:, h * 128:(h + 1) * 128], X0[:, h, :], ident_r,
                                               start=True, stop=True),
                          reads=[X0R, RC],
                          writes=[bankR[bk[1]]] if h == 0 else (), dwrites=() if h == 0 else [bankR[bk[1]]])
                b6 = bank_bf(bk[2])
                for h in range(4):
                    pe.op(lambda h=h: T.transpose(out=b6[:, h * 128:(h + 1) * 128], in_=attn[:, h, :],
                                                  identity=ident_bf), reads=[attnR, RC],
                          writes=[bankR[bk[2]]] if h == 0 else (), dwrites=() if h == 0 else [bankR[bk[2]]])
                act.op(lambda: A_.copy(out=Y0.rearrange("p h c -> p (h c)"), in_=banks[bk[1]][:, :]),
                       reads=[bankR[bk[1]]], writes=[Y0R])
                dve.op(lambda: V.tensor_tensor(out=R.rearrange("p h c -> p (h c)"),
                                               in0=Y0.rearrange("p h c -> p (h c)").bitcast(F32),
                                               in1=ident4, op=ALU.add), reads=[Y0R, RC], writes=[RR])
                act.op(lambda: A_.copy(out=attnT.rearrange("p h c -> p (h c)"), in_=b6[:, 0:512]),
                       reads=[bankR[bk[2]]], writes=[attnTR])


                yield
                def mm4(bi, lhs, lhsR, rhs, rhsR):
                    for h in range(4):
                        pe.op(lambda h=h: T.matmul(banks[bi][:, h * 128:(h + 1) * 128], lhs[:, h, :], rhs[:, h, :],
                                                   start=True, stop=True), reads=[lhsR, rhsR],
                              writes=[bankR[bi]] if h == 0 else (), dwrites=() if h == 0 else [bankR[bi]])

                cur = 0
                for lev in range(1, 6):
                    Xc, XcR = Xs[cur]
                    Yc, YcR = Ys[cur]
                    Xn, XnR = Xs[1 - cur]
                    Yn, YnR = Ys[1 - cur]
                    if lev >= 2:
                        pass
                    if lev >= 2:
                        mm4(bk[3], Xc, XcR, R, RR)
                        dve.op(lambda: V.tensor_tensor(out=R.rearrange("p h c -> p (h c)"), in0=banks[bk[3]][:, :],
                                                       in1=R.rearrange("p h c -> p (h c)").bitcast(F32),
                                                       op=ALU.add), reads=[bankR[bk[3]], RR], writes=[RR])
                    mm4(bk[1], Yc, YcR, Xc, XcR)
                    act.op(lambda Xn=Xn: A_.copy(out=Xn.rearrange("p h c -> p (h c)"), in_=banks[bk[1]][:, :]),
                           reads=[bankR[bk[1]]], writes=[XnR])
                    if lev <= 4:
                        mm4(bk[2], Xc, XcR, Yc, YcR)
                        dve.op(lambda Yn=Yn: V.tensor_copy(out=Yn.rearrange("p h c -> p (h c)"), in_=banks[bk[2]][:, :]),
                               reads=[bankR[bk[2]]], writes=[YnR])
                    cur = 1 - cur
                    yield
                Xc, XcR = Xs[cur]
                mm4(bk[3], Xc, XcR, R, RR)
                dve.op(lambda: V.tensor_tensor(out=R.rearrange("p h c -> p (h c)"), in0=banks[bk[3]][:, :],
                                               in1=R.rearrange("p h c -> p (h c)").bitcast(F32), op=ALU.add),
                       reads=[bankR[bk[3]], RR], writes=[RR])
                act.op(lambda: A_.copy(out=Rb.rearrange("p h c -> p (h c)"),
                                       in_=R.rearrange("p h c -> p (h c)").bitcast(F32)), reads=[RR],
                       writes=[RbR])
                yield
                kgf0, kgf1 = sm[:, 40:44], sm[:, 44:48]
                dve.op(lambda: V.tensor_scalar(out=kgf0, in0=kgf, scalar1=ind_t[:, 0:1], scalar2=None, op0=ALU.mult),
                       reads=[smR, RI], dwrites=[smR])
                dve.op(lambda: V.tensor_scalar(out=kgf1, in0=kgf, scalar1=ind_t[:, 1:2], scalar2=None, op0=ALU.mult),
                       reads=[smR, RI], dwrites=[smR])
                for h in range(4):
                    act.op(lambda h=h: A_.activation(out=vb[:, h, :], in_=vtok[:, h, :], func=AF.Copy,
                                                     scale=beta[:, h:h + 1]), reads=[vtokR, smR],
                           writes=[vbR] if h == 0 else (), dwrites=() if h == 0 else [vbR])
                    dve.op(lambda h=h: V.tensor_scalar(out=kbg[:, h, :], in0=ktok[:, h, :],
                                                       scalar1=kbgf[:, h:h + 1], scalar2=None, op0=ALU.mult),
                           reads=[ktokR, smR], writes=[kbgR] if h == 0 else (), dwrites=() if h == 0 else [kbgR])
                    pool.op(lambda h=h: G.tensor_scalar(out=kg[:, h, :], in0=ktok[:, h, :],
                                                        scalar1=kgf0[:, h:h + 1], scalar2=1.0, op0=ALU.mult,
                                                        op1=ALU.mult),
                            reads=[ktokR, smR], writes=[kgR] if h == 0 else (), dwrites=() if h == 0 else [kgR])
                    dve.op(lambda h=h: V.tensor_scalar(out=kg1[:, h, :], in0=ktok[:, h, :],
                                                       scalar1=kgf1[:, h:h + 1], scalar2=None, op0=ALU.mult),
                           reads=[ktokR, smR], writes=[kg1R] if h == 0 else (), dwrites=() if h == 0 else [kg1R])
                pool.op(lambda: G.tensor_tensor(out=qgT.rearrange("p h c -> p (h c)"),
                                                in0=qT.rearrange("p h c -> p (h c)"),
                                                in1=EG.rearrange("p h c -> p (h c)"), op=ALU.mult),
                        reads=[qTR, EGR], writes=[qgTR])
                mm4(bk[3], Rb, RbR, vb, vbR)
                act.op(lambda: A_.copy(out=u.rearrange("p h c -> p (h c)"), in_=banks[bk[3]][:, :]), reads=[bankR[bk[3]]],
                       writes=[uR])
                mm4(bk[0], kbg, kbgR, Rb, RbR)
                dve.op(lambda: V.tensor_copy(out=wT.rearrange("p h c -> p (h c)"), in_=banks[bk[0]][:, :]),
                       reads=[bankR[bk[0]]], writes=[wTR])
                yield
                for step in range(2):
                    r0 = (0, 64)[step] if d == 0 else (64, 0)[step]
                    rows = slice(r0, r0 + 64)
                    for h in range(4):
                        pe.op(lambda h=h: T.matmul(banks[bk[0]][:, h * 128:(h + 1) * 128], wT[:, h, :],
                                                   sbf[:, h, :], start=True, stop=True), reads=[wTR, sbfR],
                              writes=[bankR[bk[0]]] if h == 0 else (), dwrites=() if h == 0 else [bankR[bk[0]]])
                    dve.op(lambda: V.tensor_tensor(out=vn[rows].rearrange("p h c -> p (h c)"),
                                                   in0=u[rows].rearrange("p h c -> p (h c)"),
                                                   in1=banks[bk[0]][rows, :], op=ALU.subtract),
                           reads=[bankR[bk[0]], uR], writes=[vnR])
                    for h in range(4):
                        pe.op(lambda h=h: T.matmul(banks[bk[1]][:, h * 128:(h + 1) * 128], qgT[:, h, :],
                                                   sbf[:, h, :], start=True, stop=False), reads=[qgTR, sbfR],
                              writes=[bankR[bk[1]]] if h == 0 else (), dwrites=() if h == 0 else [bankR[bk[1]]])
                        pe.op(lambda h=h: T.matmul(banks[bk[1]][:, h * 128:(h + 1) * 128], attnT[:, h, :],
                                                   vn[:, h, :], start=False, stop=True), reads=[attnTR, vnR],
                              dwrites=[bankR[bk[1]]])
                    for h in range(4):
                        pe.op(lambda h=h: T.matmul(banks[bk[2]][:, h * 128:(h + 1) * 128], (kg, kg1)[r0 // 64][:, h, :],
                                                   vn[:, h, :], start=True, stop=True), reads=[kgR, kg1R, vnR],
                              writes=[bankR[bk[2]]] if h == 0 else (), dwrites=() if h == 0 else [bankR[bk[2]]])
                    col = r0 + 63 if d == 0 else r0
                    for h in range(4):
                        dve.op(lambda h=h: V.scalar_tensor_tensor(out=s32[:, h, :], in0=s32[:, h, :],
                                                                  scalar=EG[:, h, col:col + 1],
                                                                  in1=banks[bk[2]][:, h * 128:(h + 1) * 128],
                                                                  op0=ALU.mult, op1=ALU.add),
                               reads=[bankR[bk[2]], EGR], writes=[s32R] if h == 0 else (),
                               dwrites=() if h == 0 else [s32R])
                    act.op(lambda: A_.copy(out=sbf.rearrange("p h c -> p (h c)"),
                                           in_=s32.rearrange("p h c -> p (h c)")), reads=[s32R], writes=[sbfR])
                    act.op(lambda: A_.copy(out=ost[rows, :], in_=banks[bk[1]][rows, :]), reads=[bankR[bk[1]]],
                           writes=[ostR] if step == 0 else (), dwrites=() if step == 0 else [ostR])
                    yield
                sp.dma(OD[d][t0:t0 + 128, :], ost, reads=[ostR])
            for it in range(ntile):
                alive = [unit(0, it, bufs[0]), unit(1, ntile - 1 - it, bufs[1])]
                while alive:
                    for g_ in list(alive):
                        try:
                            next(g_)
                        except StopIteration:
                            alive.remove(g_)
            kb.barrier()

        def phase_g3(l, si):
            S = seqs[si]
            kb.barrier()
            cv = Carve()
            ofs = Rot([(cv.f32(512), Res()) for _ in range(2)])
            obs = Rot([(cv.f32(512), Res()) for _ in range(2)])
            azs = Rot([(cv.bf(512), Res()) for _ in range(2)])
            junk = cv.f32(128)
            sm, smR = cv.f32(16), Res()
            ys = Rot([(cv.f32(512), Res()) for _ in range(2)])
            ybs = Rot([(cv.bf(512), Res()) for _ in range(2)])
            sts = Rot([(cv.bf(512), Res()) for _ in range(2)])
            pb = Rot([0, 1])
            def pre_g3(ti):
                t0 = ti * 128
                of, ofR = ofs.next()
                ob, obR = obs.next()
                az, azR = azs.next()
                sp.dma(of, OD[0][t0:t0 + 128, :], writes=[ofR])
                sp.dma(ob, OD[1][t0:t0 + 128, :], writes=[obR])
                sp.dma(az, AZ[t0:t0 + 128, :], writes=[azR])
                return of, ofR, ob, obR, az, azR
            pend = pre_g3(0)
            for ti in range(S // 128):
                t0 = ti * 128
                of, ofR, ob, obR, az, azR = pend
                if ti + 1 < S // 128:
                    pend = pre_g3(ti + 1)
                pool.op(lambda: G.tensor_tensor(out=of, in0=of, in1=ob, op=ALU.add), reads=[obR], writes=[ofR])
                for h in range(4):
                    act.op(lambda h=h: A_.activation(out=junk, in_=of[:, h * 128:(h + 1) * 128], func=AF.Square,
                                                     accum_out=sm[:, h:h + 1]), reads=[ofR], writes=[smR])
                act.op(lambda: A_.activation(out=sm[:, 4:8], in_=sm[:, 0:4], func=AF.Sqrt, scale=1.0 / 128, bias=eps_t),
                       reads=[smR, RC], dwrites=[smR])
                dve.op(lambda: V.reciprocal(out=sm[:, 8:12], in_=sm[:, 4:8]), reads=[smR], dwrites=[smR])
                y, yR = ys.next()
                for h in range(4):
                    dve.op(lambda h=h: V.scalar_tensor_tensor(out=y[:, h * 128:(h + 1) * 128],
                                                              in0=of[:, h * 128:(h + 1) * 128],
                                                              scalar=sm[:, 8 + h:9 + h], in1=gnorm_b, op0=ALU.mult,
                                                              op1=ALU.mult), reads=[ofR, smR, RL],
                           writes=[yR] if h == 0 else (), dwrites=() if h == 0 else [yR])
                yb, ybR = ybs.next()
                pool.op(lambda: G.tensor_tensor(out=yb, in0=y, in1=az, op=ALU.mult), reads=[yR, azR], writes=[ybR])
                bi = pb.next()
                tpb = bank_bf(bi)
                for h in range(4):
                    pe.op(lambda h=h: T.transpose(out=tpb[:, h * 128:(h + 1) * 128], in_=yb[:, h * 128:(h + 1) * 128],
                                                  identity=ident_bf), reads=[ybR, RC],
                          writes=[bankR[bi]] if h == 0 else (), dwrites=() if h == 0 else [bankR[bi]])
                st, stR = sts.next()
                act.op(lambda: A_.copy(out=st, in_=tpb[:, 0:512]), reads=[bankR[bi]], writes=[stR])
                sp.dma(YT[0:512, t0:t0 + 128].rearrange("(h e) t -> e h t", e=128),
                       st.rearrange("p (h t) -> p h t", t=128), reads=[stR])
            kb.barrier()

        def softmax_pv(items, QB, scale, bsc, pts, hook=None):
            n = len(items)
            sc = [None] * n

            def qk(i):
                it = items[i]
                bs = bsc.next()
                sc[i] = bs
                pe.op(lambda: T.matmul(banks[bs][:, 0:QB], it["kT"], it["q"], start=True, stop=True),
                      reads=[it["kR"], it["qR"]], writes=[bankR[bs]])
            for i in range(min(2, n)):
                qk(i)
            for i in range(n):
                it = items[i]
                bs = sc[i]
                bo, bd = it["bo"], it["bd"]
                p, pR = pts.next()
                act.op(lambda: A_.activation(out=p[:, 0:QB], in_=banks[bs][:, 0:QB], func=AF.Exp, scale=scale),
                       reads=[bankR[bs]], writes=[pR])
                pe.op(lambda: T.matmul(banks[bo][:, 0:QB], it["v"], p[:, 0:QB], start=it["first"], stop=it["last"]),
                      reads=[it["vR"], pR], writes=[bankR[bo]] if it["first"] else (),
                      dwrites=() if it["first"] else [bankR[bo]])
                pe.op(lambda: T.matmul(banks[bd][:, 0:QB], ones_bf, p[:, 0:QB], start=it["first"], stop=it["last"]),
                      reads=[RC, pR], writes=[bankR[bd]] if it["first"] else (),
                      dwrites=() if it["first"] else [bankR[bd]])
                if i + 2 < n:
                    qk(i + 2)
                if hook is not None and i == min(9, n - 1):
                    hook()

        def phase_b(l, si):
            S = seqs[si]
            QB = min(512, S)
            nkc = S // 128
            kb.barrier()
            cv = Carve()
            kTs = Rot([(cv.bf(S), (cv.bf(S), cv.bf(S)), cv.bf(S, (S // 128, 128)), Res()) for _ in range(2)])
            for (_k, (qz0, qz1), _v, kvR0) in kTs.items:
                pool.op(lambda: G.memset(qz0[64:128, :], 0.0), writes=[kvR0])
                pool.op(lambda: G.memset(qz1[0:64, :], 0.0), dwrites=[kvR0])
            zts = Rot([(cv.bf(QB), Res()) for _ in range(3)])
            pts = Rot([(cv.bf(512), Res()) for _ in range(4)])
            tmp = Rot([(cv.f32(512), Res()) for _ in range(6)])
            sqs = Rot([(cv.bf(512), Res()) for _ in range(2)])
            outs = Rot([(cv.bf(512), Res()) for _ in range(2)])
            bsc = Rot([0, 1, 7])

            def load_head(h):
                kTh, qTh, vh, kvR = kTs.next()
                sp.dma(kTh, BKT[h * 128:(h + 1) * 128, 0:S], writes=[kvR])
                sp.dma(qTh[0][0:64, :], BQT[h * 128:h * 128 + 64, 0:S], dwrites=[kvR])
                sp.dma(qTh[1][64:128, :], BQT[h * 128 + 64:(h + 1) * 128, 0:S], dwrites=[kvR])
                sp.dma(vh, BV[0:S, h * 128:(h + 1) * 128].rearrange("(i p) e -> p i e", p=128), dwrites=[kvR])
                return kTh, qTh, vh, kvR

            def load_z(h, qb):
                zt, ztR = zts.next()
                sp.dma(zt, BZT[h * 128:(h + 1) * 128, qb * QB:(qb + 1) * QB], writes=[ztR])
                return zt, ztR
            nqb = S // QB
            hq = [(h, qb) for h in range(4) for qb in range(nqb)]
            head_next = load_head(0)
            z_next = load_z(0, 0)
            pend2 = [None]
            for ii, (h, qb) in enumerate(hq):
                if qb == 0:
                    kTh, qTh, vh, kvR = head_next
                    if h + 1 < 4:
                        head_next = load_head(h + 1)
                zt, ztR = z_next
                if ii + 1 < len(hq):
                    z_next = load_z(*hq[ii + 1])
                if True:
                    q0 = qb * QB
                    items = []
                    for j in range(2):
                        for kc in range(nkc):
                            items.append(dict(kT=kTh[:, kc * 128:(kc + 1) * 128], q=qTh[j][:, q0:q0 + QB],
                                              v=vh[:, kc, :], kR=kvR, qR=kvR, vR=kvR, bo=2 + j, bd=4 + j,
                                              first=(kc == 0), last=(kc == nkc - 1)))
                    softmax_pv(items, QB, 0.125, bsc, pts, hook=pend2[0])
                    r0, r0R = tmp.next()
                    r1, r1R = tmp.next()
                    o0, o0R = tmp.next()
                    o1, o1R = tmp.next()
                    dve.op(lambda: V.tensor_copy(out=o0[:, 0:QB], in_=banks[2][:, 0:QB]), reads=[bankR[2]], writes=[o0R])
                    dve.op(lambda: V.tensor_copy(out=r0[:, 0:QB], in_=banks[4][:, 0:QB]), reads=[bankR[4]], writes=[r0R])
                    dve.op(lambda: V.reciprocal(out=r0[:, 0:QB], in_=r0[:, 0:QB]), reads=[r0R], writes=[r0R])
                    dve.op(lambda: V.tensor_tensor(out=o0[:, 0:QB], in0=o0[:, 0:QB], in1=r0[:, 0:QB], op=ALU.mult),
                           reads=[r0R], writes=[o0R])
                    dve.op(lambda: V.reciprocal(out=r1[:, 0:QB], in_=banks[5][:, 0:QB]), reads=[bankR[5]], writes=[r1R])
                    dve.op(lambda: V.tensor_tensor(out=o1[:, 0:QB], in0=banks[3][:, 0:QB], in1=r1[:, 0:QB], op=ALU.mult),
                           reads=[bankR[3], r1R], writes=[o1R])
                    dve.op(lambda: V.scalar_tensor_tensor(out=o0[:, 0:QB], in0=o1[:, 0:QB], scalar=neglam,
                                                          in1=o0[:, 0:QB], op0=ALU.mult, op1=ALU.add),
                           reads=[o1R, RL], writes=[o0R])
                    sq, sqR = sqs.next()
                    pool.op(lambda: G.tensor_tensor(out=sq[:, 0:QB], in0=o0[:, 0:QB], in1=o0[:, 0:QB], op=ALU.mult),
                            reads=[o0R], writes=[sqR])
                    def p2(sq=sq, sqR=sqR, o0=o0, o0R=o0R, zt=zt, ztR=ztR, h=h, q0=q0):
                        pe.op(lambda: T.matmul(banks[6][:, 0:QB], ones_bf, sq[:, 0:QB], start=True, stop=True),
                              reads=[sqR, RC], writes=[bankR[6]])
                        rn, rnR = tmp.next()
                        act.op(lambda: A_.activation(out=rn[:, 0:QB], in_=banks[6][:, 0:QB], func=AF.Ln, scale=1.0 / 128,
                                                     bias=eps_t), reads=[bankR[6], RC], writes=[rnR])
                        act.op(lambda: A_.activation(out=rn[:, 0:QB], in_=rn[:, 0:QB], func=AF.Exp, scale=-0.5),
                               reads=[rnR], writes=[rnR])
                        dve.op(lambda: V.scalar_tensor_tensor(out=o0[:, 0:QB], in0=o0[:, 0:QB], scalar=dnorm_s,
                                                              in1=rn[:, 0:QB], op0=ALU.mult, op1=ALU.mult),
                               reads=[rnR, RL], writes=[o0R])
                        ot, otR = outs.next()
                        pool.op(lambda: G.tensor_tensor(out=ot[:, 0:QB], in0=o0[:, 0:QB], in1=zt, op=ALU.mult),
                                reads=[o0R, ztR], writes=[otR])
                        sp.dma(YT[512 + h * 128: 512 + (h + 1) * 128, q0:q0 + QB], ot[:, 0:QB], reads=[otR])
                    pend2[0] = p2
            pend2[0]()
            kb.barrier()

        def phase_m(l, si):
            S = seqs[si]
            QB = min(512, S)
            kb.barrier()
            cv = Carve()
            wkv = cv.bf(16 * 1024, (16, 1024))
            wkvR = Res()
            nmem_b = cv.f32(D)
            xts = Rot([(cv.f32(D), Res()) for _ in range(2)])
            junk = cv.bf(D)
            hbs = Rot([(cv.bf(D), Res()) for _ in range(2)])
            memT = cv.bf(16 * 256, (16, 256))
            memTR = Res()
            small, smallR = cv.f32(8), Res()
            kmT, kmTR = cv.bf(4 * 256, (4, 256)), Res()
            vm, vmR = cv.bf(2 * 512, (2, 512)), Res()
            qts = Rot([(cv.bf(QB), cv.bf(QB), Res()) for _ in range(2)])
            pts = Rot([(cv.bf(512), Res()) for _ in range(3)])
            tmp = Rot([(cv.f32(512), Res()) for _ in range(4)])
            outs = Rot([(cv.bf(512), Res()) for _ in range(2)])
            sp.dma(wkv, WKV[l], reads=[RW], writes=[wkvR])
            sp.dma(nmem_b, norm_mem[l].partition_broadcast(128), writes=[smallR])
            for i in range(2):
                xt, xtR = xts.next()
                hb, hbR = hbs.next()
                sp.dma(xt, mem_in[si * N_MEM + i * 128: si * N_MEM + (i + 1) * 128, :], writes=[xtR])
                rms_rows((junk, small[:, 0:1], small[:, 1:2]), xt, xtR, small[:, 2:3], smallR, D)
                dve.op(lambda: V.scalar_tensor_tensor(out=hb, in0=xt, scalar=small[:, 2:3], in1=nmem_b,
                                                      op0=ALU.mult, op1=ALU.mult), reads=[xtR, smallR], writes=[hbR])
                for half in range(2):
                    tpb = bank_bf(half)
                    for kk in range(8):
                        kc = half * 8 + kk
                        pe.op(lambda kc=kc, kk=kk, tpb=tpb: T.transpose(out=tpb[:, kk * 128:(kk + 1) * 128],
                                                                        in_=hb[:, kc * 128:(kc + 1) * 128],
                                                                        identity=ident_bf), reads=[hbR, RC],
                              writes=[bankR[half]] if kk == 0 else (), dwrites=() if kk == 0 else [bankR[half]])
                    dve.op(lambda: V.tensor_copy(out=memT[:, half * 8:(half + 1) * 8, i * 128:(i + 1) * 128],
                                                 in_=tpb.rearrange("p (k t) -> p k t", k=8)), reads=[bankR[half]],
                           dwrites=[memTR])
            for h in range(4):
                bi = 2 + h
                for kc in range(16):
                    pe.op(lambda kc=kc: T.matmul(banks[bi][:, 0:256], wkv[:, kc, h * 128:(h + 1) * 128], memT[:, kc, :],
                                                 start=(kc == 0), stop=(kc == 15)), reads=[wkvR, memTR],
                          writes=[bankR[bi]] if kc == 0 else (), dwrites=() if kc == 0 else [bankR[bi]])
                act.op(lambda: A_.copy(out=kmT[:, h, :], in_=banks[bi][:, 0:256]), reads=[bankR[bi]], dwrites=[kmTR])
            for mt in range(2):
                bi = 6 + mt
                for kc in range(16):
                    pe.op(lambda kc=kc: T.matmul(banks[bi][:, :], memT[:, kc, mt * 128:(mt + 1) * 128],
                                                 wkv[:, kc, 512:1024], start=(kc == 0), stop=(kc == 15)),
                          reads=[wkvR, memTR], writes=[bankR[bi]] if kc == 0 else (),
                          dwrites=() if kc == 0 else [bankR[bi]])
                dve.op(lambda: V.tensor_copy(out=vm[:, mt, :], in_=banks[bi][:, :]), reads=[bankR[bi]], dwrites=[vmR])
            bsc = Rot([0, 1, 7])
            bo = Rot([2, 3])
            bdn = Rot([4, 5])
            scale = 128.0 ** -0.5
            def load_q(h, qb):
                qt, zt, qR = qts.next()
                sp.dma(qt, MQT[h * 128:(h + 1) * 128, qb * QB:(qb + 1) * QB], writes=[qR])
                sp.dma(zt, MZT[h * 128:(h + 1) * 128, qb * QB:(qb + 1) * QB], dwrites=[qR])
                return qt, zt, qR
            hq = [(h, qb) for h in range(4) for qb in range(S // QB)]
            q_next = load_q(0, 0)
            for ii, (h, qb) in enumerate(hq):
                if True:
                    q0 = qb * QB
                    qt, zt, qR = q_next
                    if ii + 1 < len(hq):
                        q_next = load_q(*hq[ii + 1])
                    b_o = bo.next()
                    b_d = bdn.next()
                    items = [dict(kT=kmT[:, h, kc * 128:(kc + 1) * 128], q=qt, v=vm[:, kc, h * 128:(h + 1) * 128],
                                  kR=kmTR, qR=qR, vR=vmR, bo=b_o, bd=b_d, first=(kc == 0), last=(kc == 1))
                             for kc in range(2)]
                    softmax_pv(items, QB, scale, bsc, pts)
                    r, rR = tmp.next()
                    dve.op(lambda: V.reciprocal(out=r[:, 0:QB], in_=banks[b_d][:, 0:QB]), reads=[bankR[b_d]],
                           writes=[rR])
                    o, oR = tmp.next()
                    dve.op(lambda: V.tensor_tensor(out=o[:, 0:QB], in0=banks[b_o][:, 0:QB], in1=r[:, 0:QB],
                                                   op=ALU.mult), reads=[bankR[b_o], rR], writes=[oR])
                    ot, otR = outs.next()
                    pool.op(lambda: G.tensor_tensor(out=ot[:, 0:QB], in0=o[:, 0:QB], in1=zt, op=ALU.mult),
                            reads=[oR, qR], writes=[otR])
                    sp.dma(YT[1536 + h * 128: 1536 + (h + 1) * 128, q0:q0 + QB], ot[:, 0:QB], reads=[otR])
            kb.barrier()

        def phase_c(l, si):
            S = seqs[si]
            TC = min(S, 1024)
            kb.barrier()
            cv = Carve()
            us = Rot([(cv.bf(TC + 2), Res()) for _ in range(2)])
            gs = Rot([(cv.bf(TC), Res()) for _ in range(2)])
            acc = Rot([(cv.f32(TC), Res()) for _ in range(2)])
            outs = Rot([(cv.bf(TC), Res()) for _ in range(2)])
            def load_c(c, b0):
                ut, utR = us.next()
                gt, gtR = gs.next()
                lo = max(0, b0 - 1)
                hi = min(S, b0 + TC + 1)
                pool.op(lambda: G.memset(ut, 0.0), writes=[utR])
                sp.dma(ut[:, lo - (b0 - 1): hi - (b0 - 1)], CUT[c * 128:(c + 1) * 128, lo:hi], dwrites=[utR])
                sp.dma(gt, CGT[c * 128:(c + 1) * 128, b0:b0 + TC], writes=[gtR])
                return ut, utR, gt, gtR
            citems = [(c, b0) for c in range(4) for b0 in range(0, S, TC)]
            c_next = load_c(*citems[0])
            for ii, (c, b0) in enumerate(citems):
                if True:
                    ut, utR, gt, gtR = c_next
                    if ii + 1 < len(citems):
                        c_next = load_c(*citems[ii + 1])
                    a, aR = acc.next()
                    dve.op(lambda: V.tensor_scalar(out=a, in0=ut[:, 0:TC], scalar1=cconv[:, c * 3:c * 3 + 1],
                                                   scalar2=None, op0=ALU.mult), reads=[utR, RL], writes=[aR])
                    for wi in (1, 2):
                        dve.op(lambda wi=wi: V.scalar_tensor_tensor(out=a, in0=ut[:, wi:wi + TC],
                                                                    scalar=cconv[:, c * 3 + wi:c * 3 + wi + 1], in1=a,
                                                                    op0=ALU.mult, op1=ALU.add), reads=[utR, RL],
                               writes=[aR])
                    o, oR = outs.next()
                    pool.op(lambda: G.tensor_tensor(out=o, in0=a, in1=gt, op=ALU.mult), reads=[aR, gtR], writes=[oR])
                    sp.dma(YT[1024 + c * 128: 1024 + (c + 1) * 128, b0:b0 + TC], o, reads=[oR])
            kb.barrier()

        def phase_o(l, si, xsrc, xsrcR, xdst, xdstR):
            S = seqs[si]
            kb.barrier()
            cv = Carve()
            wo = cv.bf(16 * D, (16, D))
            woR = Res()
            yts = Rot([(cv.bf(16 * 128, (16, 128)), Res()) for _ in range(2)])
            y32 = Rot([(cv.f32(D), Res()) for _ in range(2)])
            xts = Rot([(cv.f32(D), Res()) for _ in range(2)])
            junk = cv.bf(D)
            small, smallR = cv.f32(8), Res()
            for q4 in range(4):
                sp.dma(wo[:, :, q4 * 512:(q4 + 1) * 512], WOUT[l][:, :, q4 * 512:(q4 + 1) * 512], reads=[RW],
                       dwrites=[woR])
            pbo = Rot([[0, 1, 2, 3], [4, 5, 6, 7]])
            def pre_o(ti):
                t0 = ti * 128
                yt, ytR = yts.next()
                sp.dma(yt, YT[:, t0:t0 + 128].rearrange("(k p) t -> p k t", p=128), writes=[ytR])
                xt, xtR = xts.next()
                sp.dma(xt, xsrc[offs[si] + t0: offs[si] + t0 + 128, :], reads=[xsrcR], writes=[xtR])
                return yt, ytR, xt, xtR
            pend = pre_o(0)
            for ti in range(S // 128):
                t0 = ti * 128
                yt, ytR, xt, xtR = pend
                if ti + 1 < S // 128:
                    pend = pre_o(ti + 1)
                bs = pbo.next()
                yv, yvR = y32.next()
                for cb in range(4):
                    bi = bs[cb]
                    for kc in range(16):
                        pe.op(lambda kc=kc: T.matmul(banks[bi][:, :], yt[:, kc, :], wo[:, kc, cb * 512:(cb + 1) * 512],
                                                     start=(kc == 0), stop=(kc == 15)), reads=[ytR, woR],
                              writes=[bankR[bi]] if kc == 0 else (), dwrites=() if kc == 0 else [bankR[bi]])
                    e = kb.ew.next()
                    if e is act:
                        act.op(lambda: A_.copy(out=yv[:, cb * 512:(cb + 1) * 512], in_=banks[bi][:, :]),
                               reads=[bankR[bi]], writes=[yvR] if cb == 0 else (), dwrites=() if cb == 0 else [yvR])
                    else:
                        dve.op(lambda: V.tensor_copy(out=yv[:, cb * 512:(cb + 1) * 512], in_=banks[bi][:, :]),
                               reads=[bankR[bi]], writes=[yvR] if cb == 0 else (), dwrites=() if cb == 0 else [yvR])
                rms_rows((junk, small[:, 0:1], small[:, 1:2]), yv, yvR, small[:, 2:3], smallR, D)
                dve.op(lambda: V.scalar_tensor_tensor(out=yv, in0=yv, scalar=small[:, 2:3], in1=npost_b, op0=ALU.mult,
                                                      op1=ALU.mult), reads=[smallR, RL], writes=[yvR])
                pool.op(lambda: G.tensor_tensor(out=yv, in0=yv, in1=xt, op=ALU.add), reads=[xtR], writes=[yvR])
                sp.dma(xdst[offs[si] + t0: offs[si] + t0 + 128, :], yv, reads=[yvR], dwrites=[xdstR])
            kb.barrier()

        XinR = Res("xin")
        X1R = Res("x1")
        YoR = Res("yout")
        for l in range(depth):
            load_layer_params(l)
            if depth == 1:
                xsrc, xsrcR, xdst, xdstR = x_in, XinR, y_out, YoR
            elif l == 0:
                xsrc, xsrcR, xdst, xdstR = x_in, XinR, X1, X1R
            else:
                xsrc, xsrcR, xdst, xdstR = X1, X1R, y_out, YoR
            for si in range(nseq):
                ph = phases.split(",")
                if "a" in ph:
                    phase_a(l, si, xsrc, xsrcR)
                if "g1" in ph:
                    phase_g1(l, si)
                if "g2" in ph:
                    phase_g2(l, si)
                if "g3" in ph:
                    phase_g3(l, si)
                if "b" in ph:
                    phase_b(l, si)
                if "m" in ph:
                    phase_m(l, si)
                if "c" in ph:
                    phase_c(l, si)
                if "o" in ph:
                    phase_o(l, si, xsrc, xsrcR, xdst, xdstR)
        kb.barrier()
    return nc


_CACHE = {}


def _run(seqs, depth, core_inputs, debug=False):
    key = (tuple(seqs), depth, debug)
    if key not in _CACHE:
        _CACHE[key] = build(list(seqs), depth, debug)
    nc = _CACHE[key]
    res = run_bass_kernel_spmd(nc, core_inputs, core_ids=list(range(len(core_inputs))))
    return res


def kernel(x_prompt, x_sample, mem_prompt, mem_sample, norm_pre, norm_post, norm_mem, w_in, gdn_conv,
           gdn_A_log, gdn_dt_bias, gdn_norm, diff_lambda, diff_norm, conv_w, w_mem_kv, w_out):
    n = 8
    f = lambda a: np.ascontiguousarray(np.asarray(a, dtype=np.float32))
    x_prompt, x_sample, mem_prompt, mem_sample = f(x_prompt), f(x_sample), f(mem_prompt), f(mem_sample)
    B, S, _ = x_prompt.shape
    DB, DS, _ = x_sample.shape
    pb = B // n
    db = DB // n
    seqs = [S] * pb + [DS] * db
    depth = np.asarray(norm_pre).shape[0]
    consts = _consts(max(seqs))
    shared = dict(norm_pre=f(norm_pre), norm_post=f(norm_post), norm_mem=f(norm_mem), w_in=f(w_in),
                  gdn_conv=f(gdn_conv), gdn_A_log=f(gdn_A_log), gdn_dt_bias=f(gdn_dt_bias), gdn_norm=f(gdn_norm),
                  diff_lambda=f(diff_lambda), diff_norm=f(diff_norm), conv_w=f(conv_w), w_mem_kv=f(w_mem_kv),
                  w_out=f(w_out), **consts)
    in_maps = []
    for c in range(n):
        xs = [x_prompt[c * pb + i] for i in range(pb)] + [x_sample[c * db + i] for i in range(db)]
        ms = [mem_prompt[c * pb + i] for i in range(pb)] + [mem_sample[c * db + i] for i in range(db)]
        m = dict(shared)
        m["x"] = np.ascontiguousarray(np.concatenate(xs, axis=0))
        m["mem"] = np.ascontiguousarray(np.concatenate(ms, axis=0))
        in_maps.append(m)
    res = _run(seqs, depth, in_maps)
    y_prompt = np.empty((B, S, D), np.float32)
    y_sample = np.empty((DB, DS, D), np.float32)
    for c in range(n):
        y = res.results[c]["y"]
        o = 0
        for i in range(pb):
            y_prompt[c * pb + i] = y[o:o + S]
            o += S
        for i in range(db):
            y_sample[c * db + i] = y[o:o + DS]
            o += DS
    return (y_prompt, y_sample)
```
